# Optimizing a Trainium2 kernel written in Bass

```python
import math
import jax, jax.numpy as jnp
from jax import lax
import numpy as np

D_MODEL = 1024
BATCH = 8
SEQ = 2048
DEPTH = 4

N_MIXERS = 3
EPS = 1e-6
SB_HEAD_DIM = 64
SB_HEADS = D_MODEL // SB_HEAD_DIM
Q_BLOCK = 128
CHUNK = 128
SG_WIDTH = D_MODEL
SG_GROUPS = 8
SG_HEAD_DIM = SG_WIDTH // SG_GROUPS
SSM_GROUP = 16
SSM_GROUPS = D_MODEL // SSM_GROUP
SSM_STATE = 64
DT_MIN = 1e-3
DT_MAX = 1e-1
D_FF = 2816
CONV_K = 3
N_A = (DEPTH + 2) // 3
N_B = (DEPTH + 1) // 3
N_C = DEPTH // 3

kernel_name = "hybrid_stickbreak_gmlp_s5_trunk"


def rmsnorm(x, g):
    xf = x.astype(jnp.float32)
    y = xf * lax.rsqrt(jnp.mean(xf * xf, axis=-1, keepdims=True) + EPS) * g.astype(jnp.float32)
    return y.astype(x.dtype)


def stick_breaking_attention(xn, w_qkv, w_o):
    B, L, _ = xn.shape
    qkv = (xn @ w_qkv).reshape(B, L, 3, SB_HEADS, SB_HEAD_DIM).astype(jnp.float32)
    q = qkv[:, :, 0] * (SB_HEAD_DIM ** -0.5)
    k = qkv[:, :, 1]
    v = qkv[:, :, 2]
    blocks = []
    for i in range(L // Q_BLOCK):
        t0 = i * Q_BLOCK
        t1 = t0 + Q_BLOCK
        qb, kb, vb = q[:, t0:t1], k[:, :t1], v[:, :t1]
        z = jnp.einsum('bthd,bshd->bhts', qb, kb)
        t_pos = t0 + jnp.arange(Q_BLOCK)
        s_pos = jnp.arange(t1)
        mask = s_pos[None, :] < t_pos[:, None]
        log_1m = jnp.where(mask, jax.nn.log_sigmoid(-z), 0.0)
        log_w = jax.nn.log_sigmoid(z) + lax.cumsum(log_1m, axis=3, reverse=True) - log_1m
        w = jnp.where(mask, jnp.exp(log_w), 0.0)
        blocks.append(jnp.einsum('bhts,bshd->bthd', w, vb))
    o = jnp.concatenate(blocks, axis=1).reshape(B, L, D_MODEL).astype(xn.dtype)
    return o @ w_o


def chunked_spatial_gating(xn, w_in, v_norm_g, w_s, b_s, w_o):
    B, L, _ = xn.shape
    h = jax.nn.gelu(xn @ w_in)
    u, v = jnp.split(h, 2, axis=-1)
    v = rmsnorm(v, v_norm_g)
    v = v.reshape(B, L // CHUNK, CHUNK, SG_GROUPS, SG_HEAD_DIM)
    w_causal = jnp.tril(w_s)
    sv = jnp.einsum('gts,bcsgd->bctgd', w_causal, v) + b_s.T[:, :, None]
    return (u * sv.reshape(B, L, SG_WIDTH)) @ w_o


def _complex_affine_combine(e1, e2):
    a1r, a1i, b1r, b1i = e1
    a2r, a2i, b2r, b2i = e2
    ar = a2r * a1r - a2i * a1i
    ai = a2r * a1i + a2i * a1r
    br = a2r * b1r - a2i * b1i + b2r
    bi = a2r * b1i + a2i * b1r + b2i
    return (ar, ai, br, bi)


def s5_layer(xn, w_in, lam_re, lam_im, log_dt, b_re, b_im, c_re, c_im, d_skip, w_glu):
    B, L, _ = xn.shape
    u = (xn @ w_in).astype(jnp.float32)
    ug = u.reshape(B, L, SSM_GROUPS, SSM_GROUP)
    lr = jnp.minimum(lam_re.astype(jnp.float32), -1e-4)
    li = lam_im.astype(jnp.float32)
    dt = jnp.exp(log_dt.astype(jnp.float32))[:, None]
    mag = jnp.exp(dt * lr)
    ar = mag * jnp.cos(dt * li)
    ai = mag * jnp.sin(dt * li)
    den = lr * lr + li * li
    coef_re = ((ar - 1.0) * lr + ai * li) / den
    coef_im = (ai * lr - (ar - 1.0) * li) / den
    br32, bi32 = b_re.astype(jnp.float32), b_im.astype(jnp.float32)
    bbar_re = coef_re[..., None] * br32 - coef_im[..., None] * bi32
    bbar_im = coef_re[..., None] * bi32 + coef_im[..., None] * br32
    bu_re = jnp.einsum('gph,blgh->blgp', bbar_re, ug)
    bu_im = jnp.einsum('gph,blgh->blgp', bbar_im, ug)
    a_re = jnp.broadcast_to(ar, (1, L, SSM_GROUPS, SSM_STATE))
    a_im = jnp.broadcast_to(ai, (1, L, SSM_GROUPS, SSM_STATE))
    _, _, xr, xi = lax.associative_scan(_complex_affine_combine, (a_re, a_im, bu_re, bu_im), axis=1)
    y = (jnp.einsum('ghp,blgp->blgh', c_re.astype(jnp.float32), xr)
         - jnp.einsum('ghp,blgp->blgh', c_im.astype(jnp.float32), xi))
    y = y.reshape(B, L, D_MODEL) + d_skip.astype(jnp.float32) * u
    hg = jax.nn.gelu(y).astype(xn.dtype) @ w_glu
    a, g = jnp.split(hg, 2, axis=-1)
    return a * jax.nn.sigmoid(g)


def causal_depthwise_conv(h, w, b):
    L = h.shape[1]
    hp = jnp.pad(h, ((0, 0), (CONV_K - 1, 0), (0, 0)))
    y = hp[:, 0:L] * w[0]
    for kk in range(1, CONV_K):
        y = y + hp[:, kk:kk + L] * w[kk]
    return y + b


def conv_gated_ffn(xn, w_up, conv_w, conv_b, w_down):
    h = causal_depthwise_conv(xn @ w_up, conv_w, conv_b)
    a, g = jnp.split(h, 2, axis=-1)
    return (jax.nn.silu(g) * a) @ w_down


def setup_inputs(seed: int = 0) -> dict:
    key = jax.random.key(seed)
    ks = jax.random.split(key, 32)
    f32 = jnp.float32

    def nrm(k, shape, scale):
        return jax.random.normal(k, shape, f32) * scale

    x = nrm(ks[0], (BATCH, SEQ, D_MODEL), 1.0)
    norm_g = 1.0 + nrm(ks[1], (DEPTH, 2, D_MODEL), 0.05)
    final_norm_g = 1.0 + nrm(ks[2], (D_MODEL,), 0.05)
    sb_w_qkv = nrm(ks[3], (N_A, D_MODEL, 3 * D_MODEL), D_MODEL ** -0.5)
    sb_w_o = nrm(ks[4], (N_A, D_MODEL, D_MODEL), D_MODEL ** -0.5)
    sg_w_in = nrm(ks[5], (N_B, D_MODEL, 2 * SG_WIDTH), D_MODEL ** -0.5)
    sg_norm_g = 1.0 + nrm(ks[6], (N_B, SG_WIDTH), 0.05)
    sg_w_s = nrm(ks[7], (N_B, SG_GROUPS, CHUNK, CHUNK), CHUNK ** -0.5)
    sg_b = 1.0 + nrm(ks[8], (N_B, SG_GROUPS, CHUNK), 0.1)
    sg_w_o = nrm(ks[9], (N_B, SG_WIDTH, D_MODEL), SG_WIDTH ** -0.5)
    ssm_w_in = nrm(ks[10], (N_C, D_MODEL, D_MODEL), D_MODEL ** -0.5)
    ssm_lam_re = -0.5 + nrm(ks[11], (N_C, SSM_GROUPS, SSM_STATE), 0.01)
    ssm_lam_im = (jnp.pi * jnp.arange(SSM_STATE, dtype=f32)
                  + nrm(ks[12], (N_C, SSM_GROUPS, SSM_STATE), 0.01))
    ssm_log_dt = jax.random.uniform(ks[13], (N_C, SSM_GROUPS), f32,
                                    minval=math.log(DT_MIN), maxval=math.log(DT_MAX))
    ssm_b_re = nrm(ks[14], (N_C, SSM_GROUPS, SSM_STATE, SSM_GROUP), (2 * SSM_GROUP) ** -0.5)
    ssm_b_im = nrm(ks[15], (N_C, SSM_GROUPS, SSM_STATE, SSM_GROUP), (2 * SSM_GROUP) ** -0.5)
    ssm_c_re = nrm(ks[16], (N_C, SSM_GROUPS, SSM_GROUP, SSM_STATE), SSM_STATE ** -0.5)
    ssm_c_im = nrm(ks[17], (N_C, SSM_GROUPS, SSM_GROUP, SSM_STATE), SSM_STATE ** -0.5)
    ssm_d = nrm(ks[18], (N_C, D_MODEL), 1.0)
    ssm_w_glu = nrm(ks[19], (N_C, D_MODEL, 2 * D_MODEL), D_MODEL ** -0.5)
    ffn_w_up = nrm(ks[20], (DEPTH, D_MODEL, 2 * D_FF), D_MODEL ** -0.5)
    ffn_conv_w = nrm(ks[21], (DEPTH, CONV_K, 2 * D_FF), CONV_K ** -0.5)
    ffn_conv_b = nrm(ks[22], (DEPTH, 2 * D_FF), 0.01)
    ffn_w_down = nrm(ks[23], (DEPTH, D_FF, D_MODEL), D_FF ** -0.5)
    return {
        "x": x, "norm_g": norm_g, "final_norm_g": final_norm_g,
        "sb_w_qkv": sb_w_qkv, "sb_w_o": sb_w_o,
        "sg_w_in": sg_w_in, "sg_norm_g": sg_norm_g, "sg_w_s": sg_w_s, "sg_b": sg_b, "sg_w_o": sg_w_o,
        "ssm_w_in": ssm_w_in, "ssm_lam_re": ssm_lam_re, "ssm_lam_im": ssm_lam_im,
        "ssm_log_dt": ssm_log_dt, "ssm_b_re": ssm_b_re, "ssm_b_im": ssm_b_im,
        "ssm_c_re": ssm_c_re, "ssm_c_im": ssm_c_im, "ssm_d": ssm_d, "ssm_w_glu": ssm_w_glu,
        "ffn_w_up": ffn_w_up, "ffn_conv_w": ffn_conv_w, "ffn_conv_b": ffn_conv_b,
        "ffn_w_down": ffn_w_down,
    }


def reference(x, norm_g, final_norm_g, sb_w_qkv, sb_w_o, sg_w_in, sg_norm_g, sg_w_s, sg_b, sg_w_o,
              ssm_w_in, ssm_lam_re, ssm_lam_im, ssm_log_dt, ssm_b_re, ssm_b_im, ssm_c_re, ssm_c_im,
              ssm_d, ssm_w_glu, ffn_w_up, ffn_conv_w, ffn_conv_b, ffn_w_down):
    for i in range(DEPTH):
        mixer = i % N_MIXERS
        j = i // N_MIXERS
        xn = rmsnorm(x, norm_g[i, 0])
        if mixer == 0:
            m = stick_breaking_attention(xn, sb_w_qkv[j], sb_w_o[j])
        elif mixer == 1:
            m = chunked_spatial_gating(xn, sg_w_in[j], sg_norm_g[j], sg_w_s[j], sg_b[j], sg_w_o[j])
        else:
            m = s5_layer(xn, ssm_w_in[j], ssm_lam_re[j], ssm_lam_im[j], ssm_log_dt[j],
                         ssm_b_re[j], ssm_b_im[j], ssm_c_re[j], ssm_c_im[j], ssm_d[j], ssm_w_glu[j])
        x = x + m.astype(x.dtype)
        f = conv_gated_ffn(rmsnorm(x, norm_g[i, 1]), ffn_w_up[i], ffn_conv_w[i], ffn_conv_b[i],
                           ffn_w_down[i])
        x = x + f.astype(x.dtype)
    return rmsnorm(x, final_norm_g)
```

```python
import math
import os
from contextlib import ExitStack

import numpy as np
import concourse.bass as bass
import concourse.mybir as mybir
from concourse.bass_utils import run_bass_kernel_spmd

F32 = mybir.dt.float32
BF16 = mybir.dt.bfloat16
AF = mybir.ActivationFunctionType
ALU = mybir.AluOpType

P = 128
L = 2048
D = 1024
KD = 8
DFF = 2816
NJ = 22
EPS = 1e-6
NCORES = 8

OFF_NG = 0
OFF_FNG = 64
OFF_CW = 72
OFF_CB = 600
OFF_SSMD = 776
NVROWS = 896

C_IDENT = 0
C_MASKL = 128
C_TRILT = 256
C_ONES = 384
C_EPS = 512
C_NEGPI = 513
C_HALFPI = 514
MAGIC = 12582912.0
CW1 = 6.28125
CW2 = 2.0 * math.pi - CW1
NCONST1 = 640
C2_BMASK = 0
C2_CMASK = 128
C2_RMASK = 640
C2_IOTA = 704
C2_REP = 1216
NCONST2 = 2240


def _host_consts():
    c = np.zeros((P, NCONST1), np.float32)
    r = np.arange(P)
    c[:, C_IDENT:C_IDENT + 128] = np.eye(P, dtype=np.float32)
    c[:, C_MASKL:C_MASKL + 128] = (r[None, :] < r[:, None]).astype(np.float32)
    c[:, C_TRILT:C_TRILT + 128] = (r[:, None] <= r[None, :]).astype(np.float32)
    c[:, C_ONES:C_ONES + 128] = 1.0
    c[:, C_EPS] = EPS
    c[:, C_NEGPI] = -math.pi
    c[:, C_HALFPI] = math.pi / 2
    c2 = np.zeros((P, NCONST2), np.float32)
    for g in range(64):
        q, gl = divmod(g, 8)
        c2[g, C2_REP + q * 128 + gl * 16: C2_REP + q * 128 + gl * 16 + 16] = 1.0
    for row in range(P):
        gl8 = row // 16
        c2[row, C2_BMASK + (gl8 % 2) * 64: C2_BMASK + (gl8 % 2) * 64 + 64] = 1.0
        c2[row, C2_RMASK + gl8 // 2] = 1.0
    for m in range(4):
        for row in range(P):
            gl = row // 64
            g8 = 2 * m + gl
            c2[row, C2_CMASK + m * 128 + g8 * 16: C2_CMASK + m * 128 + g8 * 16 + 16] = 1.0
    c2[:, C2_IOTA:C2_IOTA + 512] = np.arange(512, dtype=np.float32)[None, :]
    return c, c2


class Dep:
    __slots__ = ("w", "r", "name", "excl")

    def __init__(self, name, w=None, excl=False):
        self.name = name
        self.w = w
        self.r = []
        self.excl = excl


class Op:
    __slots__ = ("eng", "fn", "idx", "deps", "waits", "signal", "sigval", "dkey", "dval", "gidx")


ENGS = ("pe", "act", "dve", "pool", "sp")


class KB:
    def __init__(self, nc):
        self.nc = nc
        self.ops = {e: [] for e in ENGS}
        self.all_ops = []
        self.tiles = []
        self.fence_op = None
        self.dma_count = {}
        self.dma_group = set()

    def tile(self, name, excl=False):
        t = Dep(name, self.fence_op, excl)
        self.tiles.append(t)
        return t

    def tiles_n(self, name, *dims):
        if len(dims) == 1:
            return [self.tile(f"{name}{i}") for i in range(dims[0])]
        return [self.tiles_n(f"{name}{i}_", *dims[1:]) for i in range(dims[0])]

    def op(self, eng, fn, reads=(), writes=(), dkey=None, nd=1):
        o = Op()
        o.eng = eng
        o.fn = fn
        o.idx = len(self.ops[eng])
        o.gidx = len(self.all_ops)
        o.deps = set()
        o.waits = []
        o.signal = False
        o.sigval = 0
        o.dkey = dkey
        o.dval = 0
        if dkey is not None:
            self.dma_count[dkey] = self.dma_count.get(dkey, 0) + 16 * nd
            o.dval = self.dma_count[dkey]
        for t in reads:
            if t.w is not None:
                o.deps.add(t.w)
            if t.excl:
                for r in t.r:
                    if r.eng != eng:
                        o.deps.add(r)
        for t in writes:
            if t.w is not None:
                o.deps.add(t.w)
            for r in t.r:
                o.deps.add(r)
        for t in reads:
            t.r.append(o)
        for t in writes:
            t.w = o
            t.r = []
        o.deps.discard(o)
        self.ops[eng].append(o)
        self.all_ops.append(o)
        return o

    def dma(self, queue, out, in_, reads, writes, key, group=False):
        if group:
            self.dma_group.add(key)
        return self.op(queue, lambda e: [e.dma_start(out=out, in_=in_)], reads, writes, dkey=key)

    def dma_multi(self, queue, pieces, reads, writes, key):
        return self.op(queue, lambda e: [e.dma_start(out=o, in_=i) for (o, i) in pieces], reads, writes, dkey=key,
                       nd=len(pieces))

    def fence(self):
        deps_r, deps_w = [], []
        o = self.op("sp", lambda e: e.nop(), reads=(), writes=())
        for t in self.tiles:
            if t.w is not None:
                o.deps.add(t.w)
            for r in t.r:
                o.deps.add(r)
        o.deps.discard(o)
        self.fence_op = o
        return o

    def finalize(self, es):
        nc = self.nc
        seen = {e: {} for e in ENGS}
        for o in self.all_ops:
            sn = seen[o.eng]
            need = {}
            for d in o.deps:
                if d.dkey is not None:
                    key = ("d", d.dkey)
                    val = self.dma_count[d.dkey] if d.dkey in self.dma_group else d.dval
                    if sn.get(key, 0) >= val:
                        continue
                    if need.get(key, (0, None))[0] < val:
                        need[key] = (val, d)
                else:
                    if d.eng == o.eng:
                        if o.eng == "pe" or (o.idx - d.idx) > 3:
                            continue
                    key = ("e", d.eng)
                    if sn.get(key, -1) >= d.idx:
                        continue
                    if need.get(key, (-1, None))[0] < d.idx:
                        need[key] = (d.idx, d)
            for key, (val, d) in need.items():
                sn[key] = val
                if key[0] == "e":
                    d.signal = True
                o.waits.append((key, d))
        for e in ENGS:
            cnt = 0
            for o in self.ops[e]:
                if o.signal:
                    cnt += 1
                    o.sigval = cnt
        esem = {e: es.enter_context(nc.semaphore(f"sem_{e}")) for e in ENGS}
        dsem = {k: es.enter_context(nc.semaphore(f"dsem_{k}")) for k in self.dma_count}
        block = es.enter_context(nc.Block())
        kb = self

        def run(ename, eng):
            for o in kb.ops[ename]:
                for key, d in o.waits:
                    if key[0] == "d":
                        val = kb.dma_count[d.dkey] if d.dkey in kb.dma_group else d.dval
                        eng.wait_ge(dsem[d.dkey], val)
                    else:
                        eng.wait_ge(esem[d.eng], d.sigval)
                ins = o.fn(eng)
                if o.dkey is not None:
                    for di in ins:
                        di.then_inc(dsem[o.dkey], 16)
                elif o.signal:
                    ins.then_inc(esem[ename], 1)

        @block.tensor
        def _(e):
            run("pe", e)

        @block.scalar
        def _(e):
            run("act", e)

        @block.vector
        def _(e):
            run("dve", e)

        @block.gpsimd
        def _(e):
            run("pool", e)

        @block.sync
        def _(e):
            run("sp", e)


class Prog:
    def __init__(self, layers, do_final, x_in_tokmajor=True):
        self.layers = layers
        self.do_final = do_final

    def build(self):
        nc = bass.Bass("TRN2", target_bir_lowering=False)
        self.nc = nc
        dt = nc.dram_tensor
        self.d = {}
        shapes = {
            "x": [L, D], "consts": [P, NCONST1], "consts2": [P, NCONST2], "vecs": [NVROWS, 128],
            "sb_w_qkv": [2, D, 3 * D], "sb_w_o": [2, D, D],
            "sg_w_in": [1, D, 2 * D], "sg_norm_g": [1, D], "sg_w_s": [1, 8, 128, 128],
            "sg_b": [1, 8, 128], "sg_w_o": [1, D, D],
            "ssm_w_in": [1, D, D], "ssm_lam_re": [1, 64, 64], "ssm_lam_im": [1, 64, 64],
            "ssm_log_dt": [1, 64], "ssm_b_re": [1, 64, 64, 16], "ssm_b_im": [1, 64, 64, 16],
            "ssm_c_re": [1, 64, 16, 64], "ssm_c_im": [1, 64, 16, 64], "ssm_w_glu": [1, D, 2 * D],
            "ffn_w_up": [4, D, 2 * DFF], "ffn_w_down": [4, DFF, D],
        }
        for n, s in shapes.items():
            self.d[n] = dt(n, s, F32, kind="ExternalInput").ap()
        self.y = dt("y", [L, D], F32, kind="ExternalOutput").ap()
        self.scr = dt("s5_scratch", [2, 64, 1024], F32, kind="Internal").ap()

        with ExitStack() as es:
            self.es = es
            kb = KB(nc)
            self.kb = kb
            self.ptr = (nc._sbuf_addr_for_side("left") + 63) // 64 * 64
            self.sb_end = nc._sbuf_addr_for_side("right")
            self.uid = 0
            sb = self.alloc
            self.X = sb("X", [P, KD, L], F32)
            self.Xd = kb.tiles_n("X", KD, 4)
            self.cst = sb("cst", [P, NCONST1], F32)
            self.cst_d = kb.tile("cst")
            self.cstb = sb("cstb", [P, 512], BF16)
            self.cstb_d = kb.tile("cstb")
            self.cols = sb("cols", [P, NVROWS], F32)
            self.cols_d = kb.tile("cols")
            self.NSLOT = 3
            self.slots = [sb(f"slot{i}", [P, 3072], BF16) for i in range(self.NSLOT)]
            self.slot_d = [kb.tile(f"slot{i}") for i in range(self.NSLOT)]
            self.slot_i = 0
            self.PF = es.enter_context(nc.psum_tensor("pf", [P, 6 * 512], F32))
            self.PB = es.enter_context(nc.psum_tensor("pb", [P, 2 * 1024], BF16))
            self.pf_d = [kb.tile(f"pf{i}", excl=True) for i in range(6)]
            self.pb_d = [kb.tile(f"pb{i}", excl=True) for i in range(2)]

            self.setup()
            self.load_x()
            for l in self.layers:
                m = l % 3
                if m == 0:
                    self.attention(l, l // 3)
                elif m == 1:
                    self.gmlp(l)
                else:
                    self.s5(l)
                if os.environ.get("NOFFN") != "1":
                    self.ffn(l)
            self.store(self.do_final)
            kb.finalize(es)
        return nc

    def alloc(self, name, shape, dtype):
        nbytes = int(np.prod(shape[1:])) * (4 if dtype == F32 else 2)
        off = (self.ptr + 63) // 64 * 64
        assert off + nbytes <= self.sb_end, f"SBUF overflow allocating {name}: {off + nbytes} > {self.sb_end}"
        self.ptr = off + nbytes
        self.uid += 1
        return self.nc.alloc_sbuf_tensor_at(f"{name}_{self.uid}", list(shape), dtype, offset=off)

    def pf(self, b, n=512, off=0):
        return self.PF[:, b * 512 + off: b * 512 + off + n]

    def ident_f(self):
        return self.cst[:, C_IDENT:C_IDENT + 128]

    def col(self, r):
        return self.cols[:, r:r + 1]

    def phase_scope(self):
        self.kb.fence()
        ps = ExitStack()
        return ps

    def setup(self):
        kb, nc = self.kb, self.nc
        kb.dma("sp", self.cst[:], self.d["consts"], [], [self.cst_d], "cst")
        kb.dma("pool", self.cstb[:], self.d["consts"][:, 0:512], [], [self.cstb_d], "cstb")
        mark = self.ptr
        vst = self.alloc("vstage", [P, 7, 128], F32)
        if True:
            vd = kb.tile("vstage")
            kb.dma("sp", vst[:], self.d["vecs"].rearrange("(a p) c -> p a c", p=P), [], [vd], "vst")
            for a in range(7):
                b = a // 4
                o = (a % 4) * 128
                kb.op("pe", lambda e, a=a, b=b, o=o: e.transpose(self.pf(b, 128, o), vst[:, a, :], self.ident_f()),
                      [vd, self.cst_d], [self.pf_d[b]])
            kb.op("dve", lambda e: e.tensor_copy(out=self.cols[:, 0:512], in_=self.pf(0)), [self.pf_d[0]], [self.cols_d])
            kb.op("dve", lambda e: e.tensor_copy(out=self.cols[:, 512:896], in_=self.pf(1, 384)), [self.pf_d[1]], [self.cols_d])
            kb.fence()
        self.ptr = mark

    def load_x(self):
        kb, nc = self.kb, self.nc
        x = self.d["x"]
        mark = self.ptr
        xst = self.alloc("xst", [P, 2, D], F32)
        if True:
            xd = [kb.tile("xst0"), kb.tile("xst1")]
            for i in range(16):
                s = i % 2
                kb.dma("sp", xst[:, s, :], x[i * 128:(i + 1) * 128, :], [], [xd[s]], f"xst{s}")
                for half in range(2):
                    b = 2 * s + half
                    for kk in range(4):
                        k = half * 4 + kk
                        kb.op("pe", lambda e, s=s, k=k, b=b, kk=kk: e.transpose(
                            self.pf(b, 128, kk * 128), xst[:, s, k * 128:(k + 1) * 128], self.ident_f()),
                            [xd[s], self.cst_d], [self.pf_d[b]])
                    eng = "dve" if half == 0 else "act"
                    outap = self.X[:, half * 4:half * 4 + 4, i * 128:(i + 1) * 128]
                    inap = self.pf(b).rearrange("p (k t) -> p k t", k=4)
                    if eng == "dve":
                        fn = lambda e, outap=outap, inap=inap: e.tensor_copy(out=outap, in_=inap)
                    else:
                        fn = lambda e, outap=outap, inap=inap: e.activation(out=outap, in_=inap, func=AF.Copy)
                    kb.op(eng, fn, [self.pf_d[b]], [self.Xd[k][i // 4] for k in range(half * 4, half * 4 + 4)])
            kb.fence()
        self.ptr = mark

    def store(self, do_final):
        kb, nc = self.kb, self.nc
        mark = self.ptr
        ps = None
        if True:
            sbt = self.alloc
            xo = sbt("xo", [P, KD, 512], F32)
            xo_d = kb.tiles_n("xo", KD)
            yst = sbt("yst", [P, 2, D], F32)
            yd = [kb.tile("yst0"), kb.tile("yst1")]
            nt = self.norm_temps(ps) if do_final else None
            cnt = 0
            for tt in range(4):
                if do_final:
                    self.rmsnorm(OFF_FNG, tt, xo, 0, xo_d, nt)
                    src = lambda k, c: xo[:, k, c * 128:(c + 1) * 128]
                    srcd = lambda k: xo_d[k]
                else:
                    src = lambda k, c, tt=tt: self.X[:, k, tt * 512 + c * 128: tt * 512 + (c + 1) * 128]
                    srcd = lambda k, tt=tt: self.Xd[k][tt]
                for c in range(4):
                    i = tt * 4 + c
                    s = cnt % 2
                    cnt += 1
                    for half in range(2):
                        b = 2 * s + half
                        for kk in range(4):
                            k = half * 4 + kk
                            kb.op("pe", lambda e, k=k, c=c, b=b, kk=kk, src=src: e.transpose(
                                self.pf(b, 128, kk * 128), src(k, c), self.ident_f()),
                                [srcd(k), self.cst_d], [self.pf_d[b]])
                        outap = yst[:, s, half * 512:(half + 1) * 512]
                        if half == 0:
                            kb.op("dve", lambda e, outap=outap, b=b: e.tensor_copy(out=outap, in_=self.pf(b)),
                                  [self.pf_d[b]], [yd[s]])
                        else:
                            kb.op("act", lambda e, outap=outap, b=b: e.activation(out=outap, in_=self.pf(b), func=AF.Copy),
                                  [self.pf_d[b]], [yd[s]])
                    kb.dma("sp", self.y[i * 128:(i + 1) * 128, :], yst[:, s, :], [yd[s]], [], f"yst{s}")
            fin = kb.tile("fin")
            kb.op("sp", lambda e: e.nop(), [], [yd[0], yd[1], fin])
            kb.fence()
        self.ptr = mark

    def norm_temps(self, ps):
        nc, kb = self.nc, self.kb
        sq = self.alloc("n_sq", [P, KD, 512], BF16)
        r1 = self.alloc("n_r1", [P, 512], F32)
        rs = self.alloc("n_rs", [P, 512], F32)
        return dict(sq=sq, r1=r1, rs=rs, sq_d=kb.tile("n_sq"), r1_d=kb.tile("n_r1"), rs_d=kb.tile("n_rs"))

    def rmsnorm(self, goff, tt, out, ocol0, out_d, nt, bank=5, out_scale_eng="dve"):
        kb = self.kb
        X = self.X
        ts = slice(tt * 512, (tt + 1) * 512)
        xr = [self.Xd[k][tt] for k in range(KD)]
        kb.op("act", lambda e: e.activation(out=nt["sq"][:], in_=X[:, :, ts], func=AF.Square), xr, [nt["sq_d"]])
        ones_b = self.cstb[:, 384:512]

        def mm(e):
            ins = None
            for k in range(KD):
                ins = e.matmul(self.pf(bank), lhsT=ones_b, rhs=nt["sq"][:, k, :], start=(k == 0), stop=(k == KD - 1))
            return ins
        kb.op("pe", mm, [nt["sq_d"], self.cstb_d], [self.pf_d[bank]])
        kb.op("act", lambda e: e.activation(out=nt["r1"][:], in_=self.pf(bank), func=AF.Sqrt, bias=self.cst[:, C_EPS:C_EPS + 1],
                                            scale=1.0 / D), [self.pf_d[bank], self.cst_d], [nt["r1_d"]])
        kb.op("dve", lambda e: e.reciprocal(out=nt["rs"][:], in_=nt["r1"][:]), [nt["r1_d"]], [nt["rs_d"]])
        for k in range(KD):
            kb.op("dve", lambda e, k=k: e.scalar_tensor_tensor(
                out=out[:, k, ocol0:ocol0 + 512], in0=X[:, k, ts], scalar=self.col(goff + k), in1=nt["rs"][:],
                op0=ALU.mult, op1=ALU.mult), [self.Xd[k][tt], nt["rs_d"], self.cols_d], [out_d[k]])

    def next_slot(self):
        s = self.slot_i % self.NSLOT
        self.slot_i += 1
        return s

    def load_w(self, s, pieces):
        self.kb.dma_multi("pool", pieces, [], [self.slot_d[s]], f"slot{s}")

    def slot3(self, s, k, n):
        return self.slots[s][:, 0:k * n].rearrange("p (k n) -> p k n", k=k)

    def ffn(self, l):
        kb, nc = self.kb, self.nc
        wup = self.d["ffn_w_up"][l]
        wdn = self.d["ffn_w_down"][l]
        kb.fence()
        mark = self.ptr
        ps = None
        if True:
            sbt = self.alloc
            xn = sbt("f_xn", [P, KD, 1024], BF16)
            xn_d = [kb.tiles_n("f_xn", KD) for _ in range(2)]
            act = sbt("f_act", [P, NJ, 1024], BF16)
            act_d = kb.tiles_n("f_act", NJ, 2)
            hs = sbt("f_hs", [P, 2, 2, 2 + 1024], F32)
            hs_d = kb.tiles_n("f_hs", 2, 2, 2)
            hsh_d = kb.tiles_n("f_hsh", 2, 2)
            y0 = sbt("f_y0", [P, 2, 2, 512], F32)
            y0_d = kb.tiles_n("f_y0", 2, 2)
            sg = sbt("f_sg", [P, 2, 512], F32)
            sg_d = kb.tiles_n("f_sg", 2)
            halo = sbt("f_halo", [P, NJ, 2, 2], F32)
            halo_d = kb.tiles_n("f_halo", NJ)
            nt = self.norm_temps(ps)
            ucount = 0
            for half in range(2):
                for t2 in range(2):
                    self.rmsnorm(OFF_NG + (l * 2 + 1) * 8, half * 2 + t2, xn, t2 * 512, xn_d[t2], nt)
                pend = []

                def issue_up(j):
                    s = self.next_slot()
                    v = self.slot3(s, KD, 256)
                    self.load_w(s, [
                        (v[:, :, 0:128], wup[:, j * 128:(j + 1) * 128].rearrange("(k p) n -> p k n", p=P)),
                        (v[:, :, 128:256], wup[:, DFF + j * 128: DFF + (j + 1) * 128].rearrange("(k p) n -> p k n", p=P)),
                    ])
                    return s

                def issue_dn(dc):
                    s = self.next_slot()
                    v = self.slot3(s, NJ, 128)
                    self.load_w(s, [(v, wdn[:, dc * 128:(dc + 1) * 128].rearrange("(k p) n -> p k n", p=P))])
                    return s
                seq = [("u", j) for j in range(NJ)] + [("d", dc) for dc in range(KD)]
                PRE = self.NSLOT - 1
                slots_of = {}
                for q in range(min(PRE, len(seq))):
                    slots_of[q] = issue_up(seq[q][1]) if seq[q][0] == "u" else issue_dn(seq[q][1])
                for qi, (kind, j) in enumerate(seq):
                    s = slots_of[qi]
                    if kind == "u":
                        wv = self.slot3(s, KD, 256)
                        hb = j % 2
                        for ag in range(2):
                            if half == 0:
                                kb.op("pool", lambda e, hb=hb, ag=ag: e.memset(hs[:, hb, ag, 0:2], 0.0), [], [hsh_d[hb][ag]])
                            else:
                                kb.op("pool", lambda e, hb=hb, ag=ag, j=j: e.tensor_copy(out=hs[:, hb, ag, 0:2], in_=halo[:, j, ag, :]),
                                      [halo_d[j]], [hsh_d[hb][ag]])
                        for t2 in range(2):
                            pb = (ucount % 2) * 2
                            yb = ucount % 2
                            ucount += 1
                            for ag in range(2):
                                def mm(e, ag=ag, t2=t2, pb=pb, wv=wv):
                                    ins = None
                                    for k in range(KD):
                                        ins = e.matmul(self.pf(pb + ag), lhsT=wv[:, k, ag * 128:(ag + 1) * 128],
                                                       rhs=xn[:, k, t2 * 512:(t2 + 1) * 512], start=(k == 0), stop=(k == KD - 1))
                                    return ins
                                kb.op("pe", mm, [self.slot_d[s]] + xn_d[t2], [self.pf_d[pb + ag]])
                            c0 = 2 + t2 * 512
                            for ag in range(2):
                                kk = j + ag * NJ
                                w0 = self.col(OFF_CW + (l * 3 + 0) * 44 + kk)
                                w1 = self.col(OFF_CW + (l * 3 + 1) * 44 + kk)
                                w2 = self.col(OFF_CW + (l * 3 + 2) * 44 + kk)
                                bb = self.col(OFF_CB + l * 44 + kk)
                                kb.op("act", lambda e, hb=hb, ag=ag, c0=c0, pb=pb: e.activation(
                                    out=hs[:, hb, ag, c0:c0 + 512], in_=self.pf(pb + ag), func=AF.Copy),
                                    [self.pf_d[pb + ag]], [hs_d[hb][ag][t2]])
                                kb.op("act", lambda e, yb=yb, ag=ag, pb=pb, w2=w2, bb=bb: e.activation(
                                    out=y0[:, yb, ag, :], in_=self.pf(pb + ag), func=AF.Identity, bias=bb, scale=w2),
                                    [self.pf_d[pb + ag], self.cols_d], [y0_d[yb][ag]])
                                rd = [hs_d[hb][ag][t2], self.cols_d] + ([hsh_d[hb][ag]] if t2 == 0 else [hs_d[hb][ag][0]])
                                kb.op("dve", lambda e, yb=yb, ag=ag, hb=hb, c0=c0, w1=w1: e.scalar_tensor_tensor(
                                    out=y0[:, yb, ag, :], in0=hs[:, hb, ag, c0 - 1:c0 + 511], scalar=w1, in1=y0[:, yb, ag, :],
                                    op0=ALU.mult, op1=ALU.add), rd + [y0_d[yb][ag]], [y0_d[yb][ag]])
                                kb.op("dve", lambda e, yb=yb, ag=ag, hb=hb, c0=c0, w0=w0: e.scalar_tensor_tensor(
                                    out=y0[:, yb, ag, :], in0=hs[:, hb, ag, c0 - 2:c0 + 510], scalar=w0, in1=y0[:, yb, ag, :],
                                    op0=ALU.mult, op1=ALU.add), rd + [y0_d[yb][ag]], [y0_d[yb][ag]])
                            kb.op("act", lambda e, yb=yb: e.activation(out=sg[:, yb, :], in_=y0[:, yb, 1, :], func=AF.Silu),
                                  [y0_d[yb][1]], [sg_d[yb]])
                            kb.op("dve", lambda e, yb=yb, j=j, t2=t2: e.tensor_tensor(
                                out=act[:, j, t2 * 512:(t2 + 1) * 512], in0=sg[:, yb, :], in1=y0[:, yb, 0, :], op=ALU.mult),
                                [sg_d[yb], y0_d[yb][0]], [act_d[j][t2]])
                        if half == 0:
                            kb.op("pool", lambda e, hb=hb, j=j: e.tensor_copy(out=halo[:, j, :, :], in_=hs[:, hb, :, 1024:1026]),
                                  [hs_d[hb][0][1], hs_d[hb][1][1]], [halo_d[j]])
                    else:
                        dc = j
                        wv = self.slot3(s, NJ, 128)
                        for t2 in range(2):
                            bank = 4 + (t2 % 2)
                            def mm(e, t2=t2, bank=bank, wv=wv):
                                ins = None
                                for jj in range(NJ):
                                    ins = e.matmul(self.pf(bank), lhsT=wv[:, jj, :], rhs=act[:, jj, t2 * 512:(t2 + 1) * 512],
                                                   start=(jj == 0), stop=(jj == NJ - 1))
                                return ins
                            kb.op("pe", mm, [self.slot_d[s]] + [act_d[jj][t2] for jj in range(NJ)], [self.pf_d[bank]])
                            tt = half * 2 + t2
                            kb.op("dve", lambda e, dc=dc, tt=tt, bank=bank: e.tensor_tensor(
                                out=self.X[:, dc, tt * 512:(tt + 1) * 512], in0=self.X[:, dc, tt * 512:(tt + 1) * 512],
                                in1=self.pf(bank), op=ALU.add), [self.pf_d[bank], self.Xd[dc][tt]], [self.Xd[dc][tt]])
                    nq = qi + PRE
                    if nq < len(seq):
                        slots_of[nq] = issue_up(seq[nq][1]) if seq[nq][0] == "u" else issue_dn(seq[nq][1])
            kb.fence()
        self.ptr = mark

    def attention(self, l, ja):
        kb, nc = self.kb, self.nc
        wqkv = self.d["sb_w_qkv"][ja]
        wo = self.d["sb_w_o"][ja]
        kb.fence()
        mark = self.ptr
        ps = None
        if True:
            sbt = self.alloc
            xn = sbt("a_xn", [P, KD, L], BF16)
            xn_d = kb.tiles_n("a_xn", 4, KD)
            ao = sbt("a_o", [P, KD, L], BF16)
            ao_d = kb.tiles_n("a_o", KD, 4)
            qT = sbt("a_q", [P, L], BF16)
            kT = sbt("a_k", [P, L], BF16)
            vv = sbt("a_v", [P, 16, 128], BF16)
            q_d, k_d, v_d = kb.tile("a_q"), kb.tile("a_k"), kb.tile("a_v")
            NB = 4
            PW = 512
            mark2 = self.ptr
            nt = self.norm_temps(None)
            for tt in range(4):
                self.rmsnorm(OFF_NG + (l * 2 + 0) * 8, tt, xn, tt * 512, xn_d[tt], nt)
            kb.fence()
            self.ptr = mark2
            ee = sbt("a_e", [P, NB, PW], F32)
            spt = sbt("a_sp", [P, NB, PW], F32)
            lw = sbt("a_lw", [P, NB, PW], F32)
            ww = sbt("a_w", [P, NB, PW], BF16)
            wT = sbt("a_wT", [P, NB, PW], BF16)
            ones = sbt("a_ones", [P, PW], BF16)
            carry = sbt("a_carry", [P, 8], F32)
            e_d, sp_d, lw_d, w_d, wT_d = (kb.tiles_n(n, NB) for n in ("a_e", "a_sp", "a_lw", "a_w", "a_wT"))
            ones_d = kb.tile("a_ones")
            carry_d = kb.tiles_n("a_carry", 8)
            kb.op("pool", lambda e: e.memset(ones[:], 1.0), [], [ones_d])
            maskL_f = self.cst[:, C_MASKL:C_MASKL + 128]
            maskL_b = self.cstb[:, 128:256]
            ident_b = self.cstb[:, 0:128]
            pcount = 0
            ocount = 0
            ccount = 0
            gen = 0
            for pair in range(8):
                s = self.next_slot()
                wv = self.slot3(s, KD, 384)
                self.load_w(s, [(wv[:, :, c * 128:(c + 1) * 128],
                                 wqkv[:, c * D + pair * 128: c * D + (pair + 1) * 128].rearrange("(k p) n -> p k n", p=P))
                                for c in range(3)])
                for tt in range(4):
                    for c in range(2):
                        bank = 4 + (gen % 2)
                        gen += 1
                        def mm(e, c=c, tt=tt, bank=bank, wv=wv):
                            ins = None
                            for k in range(KD):
                                ins = e.matmul(self.pf(bank), lhsT=wv[:, k, c * 128:(c + 1) * 128],
                                               rhs=xn[:, k, tt * 512:(tt + 1) * 512], start=(k == 0), stop=(k == KD - 1))
                            return ins
                        kb.op("pe", mm, [self.slot_d[s]] + xn_d[tt], [self.pf_d[bank]])
                        dst = qT if c == 0 else kT
                        dd = q_d if c == 0 else k_d
                        sc = 0.125 if c == 0 else 1.0
                        kb.op("act", lambda e, dst=dst, tt=tt, bank=bank, sc=sc: e.activation(
                            out=dst[:, tt * 512:(tt + 1) * 512], in_=self.pf(bank), func=AF.Copy, scale=sc),
                            [self.pf_d[bank]], [dd])
                for g4 in range(4):
                    bank = 4 + (gen % 2)
                    gen += 1
                    def mmv(e, g4=g4, bank=bank, wv=wv):
                        ins = None
                        for c4 in range(4):
                            i = g4 * 4 + c4
                            for k in range(KD):
                                ins = e.matmul(self.pf(bank, 128, c4 * 128), lhsT=xn[:, k, i * 128:(i + 1) * 128],
                                               rhs=wv[:, k, 256:384], start=(k == 0), stop=(k == KD - 1))
                        return ins
                    kb.op("pe", mmv, [self.slot_d[s]] + xn_d[g4], [self.pf_d[bank]])
                    kb.op("dve", lambda e, g4=g4, bank=bank: e.tensor_copy(
                        out=vv[:, g4 * 4:(g4 + 1) * 4, :], in_=self.pf(bank).rearrange("p (c n) -> p c n", c=4)),
                        [self.pf_d[bank]], [v_d])
                items = []
                for hh in range(2):
                    for i in range(16):
                        t1 = (i + 1) * 128
                        pcs = []
                        ke = t1
                        while ke > 0:
                            ks = ((ke - 1) // PW) * PW
                            pcs.append((ks, ke))
                            ke = ks
                        ob = 4 + (ocount % 2)
                        ocount += 1
                        prev_cc = None
                        for pi, (ks, ke) in enumerate(pcs):
                            it = dict(hh=hh, i=i, ks=ks, ke=ke, first=(pi == 0), last=(pi == len(pcs) - 1), ob=ob,
                                      p=pcount, cin=prev_cc, cout=None)
                            pcount += 1
                            if pi < len(pcs) - 1:
                                it["cout"] = ccount % 8
                                ccount += 1
                            prev_cc = it["cout"]
                            items.append(it)

                def geo(it):
                    hh, i, ks, ke, p = it["hh"], it["i"], it["ks"], it["ke"], it["p"]
                    return 64 * hh, i * 128, (i + 1) * 128, ke - ks, p % NB, p % 4

                def f_z(it):
                    pb0, t0, t1, n, bi, zb = geo(it)
                    ks, ke = it["ks"], it["ke"]
                    zap = self.pf(zb, n)
                    kb.op("pe", lambda e: e.matmul(zap, lhsT=qT[pb0:pb0 + 64, t0:t1], rhs=kT[pb0:pb0 + 64, ks:ke], start=True, stop=True),
                          [q_d, k_d], [self.pf_d[zb]])

                def f_exp(it):
                    pb0, t0, t1, n, bi, zb = geo(it)
                    zap = self.pf(zb, n)
                    kb.op("act", lambda e: e.activation(out=ee[:, bi, 0:n], in_=zap, func=AF.Exp), [self.pf_d[zb]], [e_d[bi]])

                def f_mask(it):
                    pb0, t0, t1, n, bi, zb = geo(it)
                    if it["first"]:
                        kb.op("pool", lambda e: e.tensor_tensor(out=ee[:, bi, n - 128:n], in0=ee[:, bi, n - 128:n], in1=maskL_f, op=ALU.mult),
                              [e_d[bi], self.cst_d], [e_d[bi]])

                def f_ln(it):
                    pb0, t0, t1, n, bi, zb = geo(it)
                    kb.op("act", lambda e: e.activation(out=spt[:, bi, 0:n], in_=ee[:, bi, 0:n], func=AF.Ln, bias=1.0), [e_d[bi]], [sp_d[bi]])

                def f_scan(it):
                    pb0, t0, t1, n, bi, zb = geo(it)
                    cin, cout = it["cin"], it["cout"]
                    init = 0.0 if cin is None else carry[:, cin:cin + 1]
                    rd = [sp_d[bi], ones_d] + ([] if cin is None else [carry_d[cin]])
                    kb.op("dve", lambda e: e.tensor_tensor_scan(out=lw[:, bi, 0:n][:, ::-1], data0=ones[:, 0:n], data1=spt[:, bi, 0:n][:, ::-1],
                                                                initial=init, op0=ALU.mult, op1=ALU.add), rd, [lw_d[bi]])
                    if cout is not None:
                        kb.op("dve", lambda e: e.tensor_copy(out=carry[:, cout:cout + 1], in_=lw[:, bi, 0:1]), [lw_d[bi]], [carry_d[cout]])

                def f_er(it):
                    pb0, t0, t1, n, bi, zb = geo(it)
                    kb.op("act", lambda e: e.activation(out=lw[:, bi, 0:n], in_=lw[:, bi, 0:n], func=AF.Exp, scale=-1.0), [lw_d[bi]], [lw_d[bi]])

                def f_mult(it):
                    pb0, t0, t1, n, bi, zb = geo(it)
                    kb.op("dve", lambda e: e.tensor_tensor(out=ww[:, bi, 0:n], in0=ee[:, bi, 0:n], in1=lw[:, bi, 0:n], op=ALU.mult),
                          [e_d[bi], lw_d[bi]], [w_d[bi]])

                def f_tr(it):
                    pb0, t0, t1, n, bi, zb = geo(it)
                    tb = it["p"] % 2
                    nb_ = n // 128

                    def trs(e):
                        ins = None
                        for b in range(nb_):
                            ins = e.transpose(self.PB[:, tb * 1024 + b * 128: tb * 1024 + (b + 1) * 128], ww[:, bi, b * 128:(b + 1) * 128], ident_b)
                        return ins
                    kb.op("pe", trs, [w_d[bi], self.cstb_d], [self.pb_d[tb]])

                def f_copy(it):
                    pb0, t0, t1, n, bi, zb = geo(it)
                    tb = it["p"] % 2
                    kb.op("act", lambda e: e.activation(out=wT[:, bi, 0:n], in_=self.PB[:, tb * 1024: tb * 1024 + n], func=AF.Copy),
                          [self.pb_d[tb]], [wT_d[bi]])

                def f_mmo(it, pair=pair):
                    pb0, t0, t1, n, bi, zb = geo(it)
                    ob, i = it["ob"], it["i"]
                    nb_ = n // 128
                    kb0 = it["ks"] // 128
                    first, last = it["first"], it["last"]

                    def mmo(e):
                        ins = None
                        for b in range(nb_):
                            ins = e.matmul(self.pf(ob, 128), lhsT=vv[:, kb0 + b, :], rhs=wT[:, bi, b * 128:(b + 1) * 128],
                                           start=(first and b == 0), stop=(last and b == nb_ - 1))
                        return ins
                    kb.op("pe", mmo, [wT_d[bi], v_d], [self.pf_d[ob]])
                    if last:
                        kb.op("act", lambda e: e.activation(out=ao[pb0:pb0 + 64, pair, t0:t1], in_=self.PF[pb0:pb0 + 64, ob * 512: ob * 512 + 128],
                                                            func=AF.Copy), [self.pf_d[ob]], [ao_d[pair][i // 4]])
                sched = [(0, f_z), (0, f_exp), (0, f_mask), (2, f_er), (5, f_copy), (0, f_ln), (1, f_scan), (3, f_mult), (4, f_tr), (6, f_mmo)]
                nit = len(items)
                for step in range(nit + 6):
                    for lag, fn in sched:
                        j = step - lag
                        if 0 <= j < nit:
                            fn(items[j])
            self.proj_residual(ao, lambda k, tt: ao_d[k][tt], wo)
            kb.fence()
        self.ptr = mark

    def proj_residual(self, src, src_d, wdram):
        kb = self.kb
        for dc in range(KD):
            s = self.next_slot()
            wv = self.slot3(s, KD, 128)
            self.load_w(s, [(wv, wdram[:, dc * 128:(dc + 1) * 128].rearrange("(k p) n -> p k n", p=P))])
            for tt in range(4):
                bank = tt % 4

                def mm(e, tt=tt, bank=bank, wv=wv):
                    ins = None
                    for k in range(KD):
                        ins = e.matmul(self.pf(bank), lhsT=wv[:, k, :], rhs=src[:, k, tt * 512:(tt + 1) * 512],
                                       start=(k == 0), stop=(k == KD - 1))
                    return ins
                kb.op("pe", mm, [self.slot_d[s]] + [src_d(k, tt) for k in range(KD)], [self.pf_d[bank]])
                kb.op("dve", lambda e, dc=dc, tt=tt, bank=bank: e.tensor_tensor(
                    out=self.X[:, dc, tt * 512:(tt + 1) * 512], in0=self.X[:, dc, tt * 512:(tt + 1) * 512],
                    in1=self.pf(bank), op=ALU.add), [self.pf_d[bank], self.Xd[dc][tt]], [self.Xd[dc][tt]])

    def gmlp(self, l):
        kb, nc = self.kb, self.nc
        win = self.d["sg_w_in"][0]
        wo = self.d["sg_w_o"][0]
        kb.fence()
        mark = self.ptr
        sbt = self.alloc
        xn = sbt("g_xn", [P, KD, L], BF16)
        xn_d = kb.tiles_n("g_xn", 4, KD)
        vn = sbt("g_vn", [P, 16, D], BF16)
        vn_d = kb.tiles_n("g_vn", 16)
        wsT = sbt("g_wsT", [P, 8, 128], BF16)
        wsT_d = kb.tile("g_wsT")
        bsr = sbt("g_bsr", [1, D], BF16)
        bsr_d = kb.tile("g_bsr")
        ssq = sbt("g_ssq", [P, 16], F32)
        rstd = sbt("g_rstd", [P, 16], F32)
        ssq_d, rstd_d = kb.tile("g_ssq"), kb.tiles_n("g_rstd", 16)
        mark2 = self.ptr
        nt = self.norm_temps(None)
        for tt in range(4):
            self.rmsnorm(OFF_NG + (l * 2 + 0) * 8, tt, xn, tt * 512, xn_d[tt], nt)
        kb.fence()
        self.ptr = mark2
        wv_ = sbt("g_wv", [P, KD, D], BF16)
        wv_d = kb.tile("g_wv")
        vg = sbt("g_vg", [P, 2, D], F32)
        vg_d = kb.tiles_n("g_vg", 2)
        junk = sbt("g_junk", [P, D], BF16)
        junk_d = kb.tile("g_junk")
        gbc = sbt("g_gbc", [P, D], F32)
        gbc_d = kb.tile("g_gbc")
        grow = sbt("g_grow", [1, D], F32)
        grow_d = kb.tile("g_grow")
        wsf = sbt("g_wsf", [P, 8, 128], F32)
        wsf_d = kb.tile("g_wsf")
        kb.dma_multi("pool", [(wv_[:, :, c * 256:(c + 1) * 256],
                               win[:, D + c * 256: D + (c + 1) * 256].rearrange("(k p) n -> p k n", p=P)) for c in range(4)],
                     [], [wv_d], "g_wv")
        kb.dma("pool", bsr[:], self.d["sg_b"].rearrange("a g t -> a (g t)"), [], [bsr_d], "g_bsr")
        kb.dma("sp", grow[:], self.d["sg_norm_g"], [], [grow_d], "g_grow")
        kb.dma("sp", wsf[:], self.d["sg_w_s"][0].rearrange("g t s -> t g s"), [], [wsf_d], "g_wsf")
        kb.op("dve", lambda e: e.memset(ssq[:], 0.0), [], [ssq_d])
        ones_row_f = self.cst[0:1, C_ONES:C_ONES + 128]
        for nh in range(2):
            kb.op("pe", lambda e, nh=nh: e.matmul(self.pf(4 + nh), lhsT=ones_row_f, rhs=grow[0:1, nh * 512:(nh + 1) * 512],
                                                 start=True, stop=True), [grow_d, self.cst_d], [self.pf_d[4 + nh]])
            kb.op("dve", lambda e, nh=nh: e.tensor_copy(out=gbc[:, nh * 512:(nh + 1) * 512], in_=self.pf(4 + nh)),
                  [self.pf_d[4 + nh]], [gbc_d])
        trilT = self.cst[:, C_TRILT:C_TRILT + 128]
        for g in range(8):
            bank = 4 + g // 4
            kb.op("pe", lambda e, g=g, bank=bank: e.transpose(self.pf(bank, 128, (g % 4) * 128), wsf[:, g, :], self.ident_f()),
                  [wsf_d, self.cst_d], [self.pf_d[bank]])
            kb.op("dve", lambda e, g=g, bank=bank: e.tensor_tensor(out=wsT[:, g, :], in0=self.pf(bank, 128, (g % 4) * 128), in1=trilT,
                                                                   op=ALU.mult), [self.pf_d[bank], self.cst_d], [wsT_d])
        for i in range(16):
            b0 = 2 * (i % 2)
            vb = i % 2
            for nh in range(2):
                def mm(e, i=i, nh=nh, b0=b0):
                    ins = None
                    for k in range(KD):
                        ins = e.matmul(self.pf(b0 + nh), lhsT=xn[:, k, i * 128:(i + 1) * 128], rhs=wv_[:, k, nh * 512:(nh + 1) * 512],
                                       start=(k == 0), stop=(k == KD - 1))
                    return ins
                kb.op("pe", mm, [wv_d] + xn_d[i // 4], [self.pf_d[b0 + nh]])
            kb.op("act", lambda e, vb=vb, b0=b0: e.activation(out=vg[:, vb, :], in_=self.PF[:, b0 * 512: b0 * 512 + 1024],
                                                              func=AF.Gelu_apprx_tanh), [self.pf_d[b0], self.pf_d[b0 + 1]], [vg_d[vb]])
            kb.op("act", lambda e, vb=vb, i=i: e.activation(out=junk[:], in_=vg[:, vb, :], func=AF.Square, accum_out=ssq[:, i:i + 1]),
                  [vg_d[vb], ssq_d], [junk_d, rstd_d[i]])
            kb.op("act", lambda e, i=i: e.activation(out=rstd[:, i:i + 1], in_=ssq[:, i:i + 1], func=AF.Sqrt,
                                                     bias=self.cst[:, C_EPS:C_EPS + 1], scale=1.0 / D), [rstd_d[i], self.cst_d], [rstd_d[i]])
            kb.op("dve", lambda e, i=i: e.reciprocal(out=rstd[:, i:i + 1], in_=rstd[:, i:i + 1]), [rstd_d[i]], [rstd_d[i]])
            kb.op("dve", lambda e, i=i, vb=vb: e.scalar_tensor_tensor(out=vn[:, i, :], in0=vg[:, vb, :], scalar=rstd[:, i:i + 1], in1=gbc[:],
                                                                    op0=ALU.mult, op1=ALU.mult), [vg_d[vb], rstd_d[i], gbc_d], [vn_d[i]])
        kb.fence()
        self.ptr = mark2
        gated = sbt("g_gated", [P, KD, L], BF16)
        gated_d = kb.tiles_n("g_gated", KD, 4)
        ug = sbt("g_ug", [P, 2, 512], F32)
        ug_d = kb.tiles_n("g_ug", 2)
        ones_row_b = self.cstb[0:1, 384:512]
        cnt = 0
        for g in range(8):
            s = self.next_slot()
            wu = self.slot3(s, KD, 128)
            self.load_w(s, [(wu, win[:, g * 128:(g + 1) * 128].rearrange("(k p) n -> p k n", p=P))])
            for tt in range(4):
                sb_ = (cnt % 2) * 2
                ub_ = sb_ + 1
                ui = cnt % 2
                cnt += 1

                def mmsv(e, g=g, tt=tt, sb_=sb_):
                    ins = None
                    for c in range(4):
                        o = self.pf(sb_, 128, c * 128)
                        e.matmul(o, lhsT=vn[:, 4 * tt + c, g * 128:(g + 1) * 128], rhs=wsT[:, g, :], start=True, stop=False)
                        ins = e.matmul(o, lhsT=ones_row_b, rhs=bsr[0:1, g * 128:(g + 1) * 128], start=False, stop=True)
                    return ins
                kb.op("pe", mmsv, [vn_d[4 * tt + c] for c in range(4)] + [wsT_d, bsr_d, self.cstb_d], [self.pf_d[sb_]])

                def mmu(e, tt=tt, ub_=ub_, wu=wu):
                    ins = None
                    for k in range(KD):
                        ins = e.matmul(self.pf(ub_), lhsT=wu[:, k, :], rhs=xn[:, k, tt * 512:(tt + 1) * 512], start=(k == 0), stop=(k == KD - 1))
                    return ins
                kb.op("pe", mmu, [self.slot_d[s]] + xn_d[tt], [self.pf_d[ub_]])
                kb.op("act", lambda e, ui=ui, ub_=ub_: e.activation(out=ug[:, ui, :], in_=self.pf(ub_), func=AF.Gelu_apprx_tanh),
                      [self.pf_d[ub_]], [ug_d[ui]])
                kb.op("dve", lambda e, g=g, tt=tt, ui=ui, sb_=sb_: e.tensor_tensor(
                    out=gated[:, g, tt * 512:(tt + 1) * 512], in0=ug[:, ui, :], in1=self.pf(sb_), op=ALU.mult),
                    [ug_d[ui], self.pf_d[sb_]], [gated_d[g][tt]])
        self.proj_residual(gated, lambda k, tt: gated_d[k][tt], wo)
        kb.fence()
        self.ptr = mark

    def s5(self, l):
        kb, nc = self.kb, self.nc
        d = self.d
        win = d["ssm_w_in"][0]
        wglu = d["ssm_w_glu"][0]
        PI = math.pi
        kb.fence()
        mark = self.ptr
        sbt = self.alloc
        xn = sbt("s_xn", [P, KD, L], BF16)
        xn_d = kb.tiles_n("s_xn", 4, KD)
        yg = sbt("s_yg", [P, KD, L], BF16)
        yg_d = kb.tiles_n("s_yg", KD, 4)
        c2 = sbt("s_c2", [P, C2_REP], F32)
        c2_d = kb.tile("s_c2")
        bbt = sbt("s_bbt", [P, 2, 8, 64], F32)
        bbt_d = kb.tile("s_bbt")
        ct = sbt("s_ct", [P, 2, 8, 64], F32)
        ct_d = kb.tile("s_ct")
        svr = sbt("s_svr", [P, 32], F32)
        svt = sbt("s_svt", [P, 32], F32)
        sv_d = kb.tile("s_sv")
        carry = sbt("s_carry", [P, 2, 32], F32)
        carry_d = kb.tiles_n("s_carry", 32)
        kb.dma("sp", c2[:], d["consts2"][:, 0:C2_REP], [], [c2_d], "s_c2")
        kb.dma("sp", ct[:, 0, :, :], d["ssm_c_re"][0].rearrange("(q gl) h p -> (gl h) q p", q=8), [], [ct_d], "s_ct")
        kb.dma("sp", ct[:, 1, :, :], d["ssm_c_im"][0].rearrange("(q gl) h p -> (gl h) q p", q=8), [], [ct_d], "s_ct")
        mark2 = self.ptr
        nt = self.norm_temps(None)
        for tt in range(4):
            self.rmsnorm(OFF_NG + (l * 2 + 0) * 8, tt, xn, tt * 512, xn_d[tt], nt)
        kb.fence()
        self.ptr = mark2
        bn = sbt("s_bn", [64, 2, 1024], F32)
        bn_d = kb.tile("s_bn")
        kb.dma("sp", bn[:, 0, :], d["ssm_b_re"][0].rearrange("g p h -> g (p h)"), [], [bn_d], "s_bn")
        kb.dma("sp", bn[:, 1, :], d["ssm_b_im"][0].rearrange("g p h -> g (p h)"), [], [bn_d], "s_bn")
        NT = 26
        tp = sbt("s_tp", [64, NT, 64], F32)
        tp_d = kb.tiles_n("s_tp", NT)
        dtc = sbt("s_dt", [64, 2], F32)
        dt_d = kb.tile("s_dt")
        (LR, LI, DLR, DLI, MAG, SA, CA, SN, CS, AR, AI, T1, T2, DEN, RDEN, AM1, U1, U2, CR, CI) = range(20)
        T = lambda i: tp[:, i, :]
        kb.dma("sp", T(LR), d["ssm_lam_re"][0], [], [tp_d[LR]], "s_p0")
        kb.dma("sp", T(LI), d["ssm_lam_im"][0], [], [tp_d[LI]], "s_p1")
        kb.dma("sp", dtc[:, 0:1], d["ssm_log_dt"].rearrange("a g -> g a"), [], [dt_d], "s_p2")
        kb.op("act", lambda e: e.activation(out=dtc[:, 1:2], in_=dtc[:, 0:1], func=AF.Exp), [dt_d], [dt_d])
        dtcol = dtc[:, 1:2]

        def v1(eng, fn, r, w):
            kb.op(eng, fn, [tp_d[i] for i in r] + [dt_d, self.cst_d], [tp_d[i] for i in w])
        v1("dve", lambda e: e.tensor_scalar_min(out=T(LR), in0=T(LR), scalar1=-1e-4), [LR], [LR])
        v1("dve", lambda e: e.tensor_scalar_mul(out=T(DLR), in0=T(LR), scalar1=dtcol), [LR], [DLR])
        v1("dve", lambda e: e.tensor_scalar_mul(out=T(DLI), in0=T(LI), scalar1=dtcol), [LI], [DLI])
        v1("act", lambda e: e.activation(out=T(MAG), in_=T(DLR), func=AF.Exp), [DLR], [MAG])
        hpi64 = self.cst[0:64, C_HALFPI:C_HALFPI + 1]
        v1("dve", lambda e: e.tensor_scalar(out=T(T1), in0=T(DLI), scalar1=1.0 / (2 * PI), scalar2=MAGIC, op0=ALU.mult, op1=ALU.add), [DLI], [T1])
        v1("dve", lambda e: e.tensor_scalar_add(out=T(T1), in0=T(T1), scalar1=-MAGIC), [T1], [T1])
        v1("dve", lambda e: e.scalar_tensor_tensor(out=T(SA), in0=T(T1), scalar=-CW1, in1=T(DLI), op0=ALU.mult, op1=ALU.add), [T1, DLI], [SA])
        v1("dve", lambda e: e.scalar_tensor_tensor(out=T(SA), in0=T(T1), scalar=-CW2, in1=T(SA), op0=ALU.mult, op1=ALU.add), [T1, SA], [SA])
        v1("dve", lambda e: e.scalar_tensor_tensor(out=T(CA), in0=T(SA), scalar=-1.0, in1=T(SA), op0=ALU.mult, op1=ALU.max), [SA], [CA])
        v1("act", lambda e: e.activation(out=T(SN), in_=T(SA), func=AF.Sin, scale=0.999999), [SA], [SN])
        v1("act", lambda e: e.activation(out=T(CS), in_=T(CA), func=AF.Sin, scale=-0.999999, bias=hpi64), [CA], [CS])
        v1("dve", lambda e: e.tensor_tensor(out=T(AR), in0=T(MAG), in1=T(CS), op=ALU.mult), [MAG, CS], [AR])
        v1("dve", lambda e: e.tensor_tensor(out=T(AI), in0=T(MAG), in1=T(SN), op=ALU.mult), [MAG, SN], [AI])
        v1("dve", lambda e: e.tensor_tensor(out=T(T1), in0=T(LR), in1=T(LR), op=ALU.mult), [LR], [T1])
        v1("dve", lambda e: e.tensor_tensor(out=T(T2), in0=T(LI), in1=T(LI), op=ALU.mult), [LI], [T2])
        v1("dve", lambda e: e.tensor_tensor(out=T(DEN), in0=T(T1), in1=T(T2), op=ALU.add), [T1, T2], [DEN])
        v1("dve", lambda e: e.reciprocal(out=T(RDEN), in_=T(DEN)), [DEN], [RDEN])
        v1("dve", lambda e: e.tensor_scalar_add(out=T(AM1), in0=T(AR), scalar1=-1.0), [AR], [AM1])
        v1("dve", lambda e: e.tensor_tensor(out=T(U1), in0=T(AM1), in1=T(LR), op=ALU.mult), [AM1, LR], [U1])
        v1("dve", lambda e: e.tensor_tensor(out=T(U2), in0=T(AI), in1=T(LI), op=ALU.mult), [AI, LI], [U2])
        v1("dve", lambda e: e.tensor_tensor(out=T(U1), in0=T(U1), in1=T(U2), op=ALU.add), [U1, U2], [U1])
        v1("dve", lambda e: e.tensor_tensor(out=T(CR), in0=T(U1), in1=T(RDEN), op=ALU.mult), [U1, RDEN], [CR])
        v1("dve", lambda e: e.tensor_tensor(out=T(U1), in0=T(AI), in1=T(LR), op=ALU.mult), [AI, LR, CR], [U1])
        v1("dve", lambda e: e.tensor_tensor(out=T(U2), in0=T(AM1), in1=T(LI), op=ALU.mult), [AM1, LI], [U2])
        v1("dve", lambda e: e.tensor_tensor(out=T(U1), in0=T(U1), in1=T(U2), op=ALU.subtract), [U1, U2], [U1])
        v1("dve", lambda e: e.tensor_tensor(out=T(CI), in0=T(U1), in1=T(RDEN), op=ALU.mult), [U1, RDEN], [CI])
        bbn = sbt("s_bbn", [64, 2, 1024], F32)
        bbn_d = kb.tile("s_bbn")
        tmpn = sbt("s_tmpn", [64, 1024], F32)
        tmpn_d = kb.tile("s_tmpn")
        R_, I_ = 0, 1
        cb = lambda i: T(i).unsqueeze(2).broadcast_to([64, 64, 16])
        nat = lambda ri: bn[:, ri, :].rearrange("g (p h) -> g p h", h=16)
        outv = lambda t_: t_.rearrange("g (h p) -> g p h", h=16)
        kb.op("dve", lambda e: e.tensor_tensor(out=outv(bbn[:, R_, :]), in0=cb(CR), in1=nat(R_), op=ALU.mult), [tp_d[CR], bn_d], [bbn_d])
        kb.op("dve", lambda e: e.tensor_tensor(out=outv(tmpn[:]), in0=cb(CI), in1=nat(I_), op=ALU.mult), [tp_d[CI], bn_d], [tmpn_d])
        kb.op("dve", lambda e: e.tensor_tensor(out=bbn[:, R_, :], in0=bbn[:, R_, :], in1=tmpn[:], op=ALU.subtract), [bbn_d, tmpn_d], [bbn_d])
        kb.op("dve", lambda e: e.tensor_tensor(out=outv(bbn[:, I_, :]), in0=cb(CR), in1=nat(I_), op=ALU.mult), [tp_d[CR], bn_d], [bbn_d])
        kb.op("dve", lambda e: e.tensor_tensor(out=outv(tmpn[:]), in0=cb(CI), in1=nat(R_), op=ALU.mult), [tp_d[CI], bn_d, bbn_d], [tmpn_d])
        kb.op("dve", lambda e: e.tensor_tensor(out=bbn[:, I_, :], in0=bbn[:, I_, :], in1=tmpn[:], op=ALU.add), [bbn_d, tmpn_d], [bbn_d])
        scr_d = kb.tile("s_scr")
        kb.dma("sp", self.scr.rearrange("r g n -> g r n"), bbn[:], [bbn_d], [scr_d], "s_scr_w")
        for ri in range(2):
            kb.dma("sp", bbt[:, ri, :, :], self.scr[ri].rearrange("(q gl) (h p) -> (gl h) q p", q=8, h=16), [scr_d], [bbt_d], "s_scr_r")
        dup = sbt("s_dup", [64, 2, 128], F32)
        dup_d = kb.tile("s_dup")
        for qi, (src, dst) in enumerate(((MAG, svr), (DLI, svt))):
            kb.op("dve", lambda e, qi=qi, src=src: e.tensor_copy(out=dup[:, qi, 0:64], in_=T(src)), [tp_d[src]], [dup_d])
            kb.op("dve", lambda e, qi=qi, src=src: e.tensor_copy(out=dup[:, qi, 64:128], in_=T(src)), [tp_d[src]], [dup_d])
            bank = 2 + qi
            kb.op("pe", lambda e, qi=qi, bank=bank: e.transpose(self.pf(bank, 64), dup[:, qi, :], self.cst[0:64, C_IDENT:C_IDENT + 64]),
                  [dup_d, self.cst_d], [self.pf_d[bank]])
            kb.op("dve", lambda e, dst=dst, bank=bank: e.tensor_copy(out=dst[0:64, :], in_=self.PF[0:64, bank * 512: bank * 512 + 64: 2]),
                  [self.pf_d[bank]], [sv_d])
            kb.op("dve", lambda e, dst=dst, bank=bank: e.tensor_copy(out=dst[64:128, :], in_=self.PF[64:128, bank * 512 + 1: bank * 512 + 64: 2]),
                  [self.pf_d[bank]], [sv_d])
        kb.op("dve", lambda e: e.memset(carry[:], 0.0), [], carry_d)
        kb.fence()
        self.ptr = mark2
        if os.environ.get("S5_STOP") == "1":
            self.ptr = mark
            return
        u32 = sbt("s_u32", [P, L], F32)
        u32_d = kb.tiles_n("s_u32", 4)
        ub = sbt("s_ub", [P, L], BF16)
        ub_d = kb.tiles_n("s_ub", 4)
        bwm = sbt("s_bwm", [P, 2, 4, 128], BF16)
        bwm_d = kb.tile("s_bwm")
        bwf = sbt("s_bwf", [P, 2, 128], F32)
        bwf_d = kb.tile("s_bwf")
        cw = sbt("s_cw", [P, 2, 4, 128], BF16)
        cw_d = kb.tile("s_cw")
        ctd = sbt("s_ctd", [P, 2, 128], F32)
        ctd_d = kb.tile("s_ctd")
        NTMP = 9
        tm = sbt("s_tm", [P, NTMP, 512], F32)
        tm_d = kb.tiles_n("s_tm", NTMP)
        A_, S_, C_, BR, BI, T2_, T3_, T4_, RB = range(9)
        xrb = sbt("s_xr", [P, 2, 512], BF16)
        xr_d = kb.tiles_n("s_xr", 2)
        ysum = sbt("s_ysum", [P, 512], F32)
        ysum_d = kb.tile("s_ysum")
        M = lambda i: tm[:, i, :]
        iota = c2[:, C2_IOTA:C2_IOTA + 512]
        bmask = c2[:, C2_BMASK:C2_BMASK + 128]
        bankc = 0
        for q in range(8):
            s = self.next_slot()
            wv = self.slot3(s, KD, 128)
            self.load_w(s, [(wv, win[:, q * 128:(q + 1) * 128].rearrange("(k p) n -> p k n", p=P))])
            for tt in range(4):
                bank = 4 + (tt % 2)

                def mm(e, tt=tt, bank=bank, wv=wv):
                    ins = None
                    for k in range(KD):
                        ins = e.matmul(self.pf(bank), lhsT=wv[:, k, :], rhs=xn[:, k, tt * 512:(tt + 1) * 512], start=(k == 0), stop=(k == KD - 1))
                    return ins
                if "m" not in os.environ.get("S5_SKIP", ""):
                    kb.op("pe", mm, [self.slot_d[s]] + xn_d[tt], [self.pf_d[bank]])
                if "c" not in os.environ.get("S5_SKIP", ""):
                    kb.op("dve", lambda e, tt=tt, bank=bank: e.tensor_copy(out=u32[:, tt * 512:(tt + 1) * 512], in_=self.pf(bank)),
                          [self.pf_d[bank]], [u32_d[tt]])
                if "d" not in os.environ.get("S5_SKIP", ""):
                    kb.op("dve", lambda e, tt=tt, bank=bank: e.tensor_copy(out=ub[:, tt * 512:(tt + 1) * 512], in_=self.pf(bank)),
                          [self.pf_d[bank]], [ub_d[tt]])
            SKIP = os.environ.get("S5_SKIP", "")
            for ri in range(2):
                if "a" in SKIP:
                    continue
                kb.op("dve", lambda e, ri=ri, q=q: e.tensor_tensor(
                    out=bwf[:, ri, :].rearrange("p (a n) -> p a n", a=2), in0=bbt[:, ri, q:q + 1, :].broadcast_to([P, 2, 64]),
                    in1=bmask.rearrange("p (a n) -> p a n", a=2), op=ALU.mult), [bbt_d, c2_d], [bwf_d])
                for jj in range(4):
                    kb.op("dve", lambda e, ri=ri, jj=jj: e.tensor_scalar_mul(
                        out=bwm[:, ri, jj, :], in0=bwf[:, ri, :], scalar1=c2[:, C2_RMASK + jj:C2_RMASK + jj + 1]), [bwf_d, c2_d], [bwm_d])
                if "b" in SKIP:
                    continue
                kb.op("dve", lambda e, ri=ri, q=q: e.tensor_copy(
                    out=ctd[:, ri, :].rearrange("p (a n) -> p a n", a=2), in_=ct[:, ri, q:q + 1, :].broadcast_to([P, 2, 64])), [ct_d], [ctd_d])
                bank = 4 + ri
                kb.op("pe", lambda e, ri=ri, bank=bank: e.transpose(self.pf(bank, 128), ctd[:, ri, :], self.ident_f()),
                      [ctd_d, self.cst_d], [self.pf_d[bank]])
                for jj in range(4):
                    cm = c2[:, C2_CMASK + jj * 128: C2_CMASK + (jj + 1) * 128]
                    if ri == 0:
                        kb.op("dve", lambda e, jj=jj, bank=bank, cm=cm: e.tensor_tensor(out=cw[:, 0, jj, :], in0=self.pf(bank, 128), in1=cm, op=ALU.mult),
                              [self.pf_d[bank], c2_d], [cw_d])
                    else:
                        kb.op("dve", lambda e, jj=jj, bank=bank, cm=cm: e.scalar_tensor_tensor(
                            out=cw[:, 1, jj, :], in0=self.pf(bank, 128), scalar=-1.0, in1=cm, op0=ALU.mult, op1=ALU.mult),
                            [self.pf_d[bank], c2_d], [cw_d])
            if os.environ.get("S5_STOP") == "2":
                break
            for tt in range(4):
                yb = 4 + (tt % 2)
                ts = slice(tt * 512, (tt + 1) * 512)
                if os.environ.get("S5_STOP") == "3" and (q > 0 or tt > 0):
                    break
                if os.environ.get("S5_STOP") == "4" and q > 0:
                    break
                for jj in range(4):
                    j = 4 * q + jj
                    th = svt[:, j:j + 1]
                    rho = svr[:, j:j + 1]
                    b0 = (bankc % 2) * 2
                    bankc += 1
                    kb.op("dve", lambda e, tt=tt, th=th: e.tensor_scalar(out=M(A_), in0=iota, scalar1=float(tt * 512), scalar2=th,
                                                                        op0=ALU.add, op1=ALU.mult), [c2_d, sv_d], [tm_d[A_]])
                    kb.op("dve", lambda e: e.tensor_scalar(out=M(T4_), in0=M(A_), scalar1=1.0 / (2 * PI), scalar2=MAGIC, op0=ALU.mult, op1=ALU.add),
                          [tm_d[A_]], [tm_d[T4_]])
                    kb.op("dve", lambda e: e.tensor_scalar_add(out=M(T4_), in0=M(T4_), scalar1=-MAGIC), [tm_d[T4_]], [tm_d[T4_]])
                    kb.op("dve", lambda e: e.scalar_tensor_tensor(out=M(S_), in0=M(T4_), scalar=-CW1, in1=M(A_), op0=ALU.mult, op1=ALU.add),
                          [tm_d[T4_], tm_d[A_]], [tm_d[S_]])
                    kb.op("dve", lambda e: e.scalar_tensor_tensor(out=M(S_), in0=M(T4_), scalar=-CW2, in1=M(S_), op0=ALU.mult, op1=ALU.add),
                          [tm_d[T4_], tm_d[S_]], [tm_d[S_]])
                    kb.op("dve", lambda e: e.scalar_tensor_tensor(out=M(C_), in0=M(S_), scalar=-1.0, in1=M(S_), op0=ALU.mult, op1=ALU.max),
                          [tm_d[S_]], [tm_d[C_]])
                    hpi = self.cst[:, C_HALFPI:C_HALFPI + 1]
                    kb.op("act", lambda e: e.activation(out=M(S_), in_=M(S_), func=AF.Sin, scale=0.999999), [tm_d[S_]], [tm_d[S_]])
                    kb.op("act", lambda e, hpi=hpi: e.activation(out=M(C_), in_=M(C_), func=AF.Sin, scale=-0.999999, bias=hpi),
                          [tm_d[C_], self.cst_d], [tm_d[C_]])
                    kb.op("act", lambda e, rho=rho: e.activation(out=M(RB), in_=iota, func=AF.Identity, bias=rho, scale=0.0),
                          [c2_d, sv_d], [tm_d[RB]])
                    for ri in range(2):
                        kb.op("pe", lambda e, ri=ri, jj=jj, b0=b0, ts=ts: e.matmul(self.pf(b0 + ri), lhsT=bwm[:, ri, jj, :], rhs=ub[:, ts],
                                                                                  start=True, stop=True), [bwm_d, ub_d[tt]], [self.pf_d[b0 + ri]])
                        kb.op("act", lambda e, ri=ri, b0=b0: e.activation(out=M(BR + ri), in_=self.pf(b0 + ri), func=AF.Copy),
                              [self.pf_d[b0 + ri]], [tm_d[BR + ri]])
                    tt_ = lambda eng, o, a, b, op: kb.op(eng, lambda e: e.tensor_tensor(out=M(o), in0=M(a), in1=M(b), op=op),
                                                         [tm_d[a], tm_d[b]], [tm_d[o]])
                    tt_("dve", A_, C_, BR, ALU.mult)
                    tt_("pool", T2_, S_, BI, ALU.mult)
                    tt_("dve", A_, A_, T2_, ALU.add)
                    tt_("pool", T3_, C_, BI, ALU.mult)
                    tt_("dve", T4_, S_, BR, ALU.mult)
                    tt_("pool", T3_, T3_, T4_, ALU.subtract)
                    for ri, src in ((0, A_), (1, T3_)):
                        kb.op("dve", lambda e, ri=ri, src=src, j=j: e.tensor_tensor_scan(
                            out=M(BR + ri), data0=M(RB), data1=M(src), initial=carry[:, ri, j:j + 1], op0=ALU.mult, op1=ALU.add),
                            [tm_d[RB], tm_d[src], carry_d[j]], [tm_d[BR + ri]])
                    for ri in range(2):
                        kb.op("dve", lambda e, ri=ri, j=j: e.tensor_copy(out=carry[:, ri, j:j + 1], in_=tm[:, BR + ri, 511:512]),
                              [tm_d[BR + ri]], [carry_d[j]])
                    xi_ = bankc % 2
                    tt_("dve", A_, C_, BR, ALU.mult)
                    tt_("pool", T2_, S_, BI, ALU.mult)
                    kb.op("dve", lambda e: e.tensor_tensor(out=xrb[:, 0, :], in0=M(A_), in1=M(T2_), op=ALU.subtract),
                          [tm_d[A_], tm_d[T2_]], [xr_d[0]])
                    tt_("dve", T4_, S_, BR, ALU.mult)
                    tt_("pool", T3_, C_, BI, ALU.mult)
                    kb.op("pool", lambda e: e.tensor_tensor(out=xrb[:, 1, :], in0=M(T4_), in1=M(T3_), op=ALU.add),
                          [tm_d[T4_], tm_d[T3_]], [xr_d[1]])
                    for ri in range(2):
                        kb.op("pe", lambda e, ri=ri, jj=jj, yb=yb: e.matmul(self.pf(yb), lhsT=cw[:, ri, jj, :], rhs=xrb[:, ri, :],
                                                                           start=(jj == 0 and ri == 0), stop=(jj == 3 and ri == 1)),
                              [cw_d, xr_d[ri]], [self.pf_d[yb]])
                kb.op("dve", lambda e, tt=tt, q=q, yb=yb, ts=ts: e.scalar_tensor_tensor(
                    out=ysum[:], in0=u32[:, ts], scalar=self.col(OFF_SSMD + q), in1=self.pf(yb), op0=ALU.mult, op1=ALU.add),
                    [u32_d[tt], self.cols_d, self.pf_d[yb]], [ysum_d])
                kb.op("act", lambda e, q=q, ts=ts: e.activation(out=yg[:, q, ts], in_=ysum[:], func=AF.Gelu_apprx_tanh), [ysum_d], [yg_d[q][tt]])
        kb.fence()
        self.ptr = mark2
        if os.environ.get("S5_STOP") in ("2", "3", "4", "5"):
            self.ptr = mark
            return
        sgm = sbt("s_sgm", [P, 2, 512], F32)
        sgm_d = kb.tiles_n("s_sgm", 2)
        gcnt = 0
        for dc in range(KD):
            s = self.next_slot()
            wv = self.slot3(s, KD, 256)
            self.load_w(s, [(wv[:, :, 0:128], wglu[:, dc * 128:(dc + 1) * 128].rearrange("(k p) n -> p k n", p=P)),
                            (wv[:, :, 128:256], wglu[:, D + dc * 128: D + (dc + 1) * 128].rearrange("(k p) n -> p k n", p=P))])
            for tt in range(4):
                b0 = (gcnt % 3) * 2
                gi = gcnt % 2
                gcnt += 1
                for ag in range(2):
                    def mm(e, ag=ag, tt=tt, b0=b0, wv=wv):
                        ins = None
                        for k in range(KD):
                            ins = e.matmul(self.pf(b0 + ag), lhsT=wv[:, k, ag * 128:(ag + 1) * 128], rhs=yg[:, k, tt * 512:(tt + 1) * 512],
                                           start=(k == 0), stop=(k == KD - 1))
                        return ins
                    kb.op("pe", mm, [self.slot_d[s]] + [yg_d[k][tt] for k in range(KD)], [self.pf_d[b0 + ag]])
                kb.op("act", lambda e, gi=gi, b0=b0: e.activation(out=sgm[:, gi, :], in_=self.pf(b0 + 1), func=AF.Sigmoid),
                      [self.pf_d[b0 + 1]], [sgm_d[gi]])
                kb.op("dve", lambda e, gi=gi, b0=b0: e.tensor_tensor(out=sgm[:, gi, :], in0=self.pf(b0), in1=sgm[:, gi, :], op=ALU.mult),
                      [self.pf_d[b0], sgm_d[gi]], [sgm_d[gi]])
                kb.op("dve", lambda e, dc=dc, tt=tt, gi=gi: e.tensor_tensor(
                    out=self.X[:, dc, tt * 512:(tt + 1) * 512], in0=self.X[:, dc, tt * 512:(tt + 1) * 512], in1=sgm[:, gi, :], op=ALU.add),
                    [sgm_d[gi], self.Xd[dc][tt]], [self.Xd[dc][tt]])
        kb.fence()
        self.ptr = mark


_CACHE = {}


def _pack_vecs(norm_g, final_norm_g, ffn_conv_w, ffn_conv_b, ssm_d):
    v = np.zeros((NVROWS, 128), np.float32)
    v[OFF_NG:OFF_NG + 64] = np.asarray(norm_g, np.float32).reshape(64, 128)
    v[OFF_FNG:OFF_FNG + 8] = np.asarray(final_norm_g, np.float32).reshape(8, 128)
    v[OFF_CW:OFF_CW + 528] = np.asarray(ffn_conv_w, np.float32).reshape(528, 128)
    v[OFF_CB:OFF_CB + 176] = np.asarray(ffn_conv_b, np.float32).reshape(176, 128)
    v[OFF_SSMD:OFF_SSMD + 8] = np.asarray(ssm_d, np.float32).reshape(8, 128)
    return v


def run_layers(x, inputs, layers, do_final, cores=NCORES):
    key = (tuple(layers), do_final)
    if key not in _CACHE:
        _CACHE[key] = Prog(layers, do_final).build()
    nc = _CACHE[key]
    consts, consts2 = _host_consts()
    vecs = _pack_vecs(inputs["norm_g"], inputs["final_norm_g"], inputs["ffn_conv_w"], inputs["ffn_conv_b"], inputs["ssm_d"])
    shared = {"consts": consts, "consts2": consts2, "vecs": vecs}
    for n in ("sb_w_qkv", "sb_w_o", "sg_w_in", "sg_norm_g", "sg_w_s", "sg_b", "sg_w_o", "ssm_w_in", "ssm_lam_re",
              "ssm_lam_im", "ssm_log_dt", "ssm_b_re", "ssm_b_im", "ssm_c_re", "ssm_c_im", "ssm_w_glu", "ffn_w_up",
              "ffn_w_down"):
        shared[n] = np.ascontiguousarray(np.asarray(inputs[n], np.float32))
    in_maps = []
    for c in range(cores):
        m = dict(shared)
        m["x"] = np.ascontiguousarray(np.asarray(x[c], np.float32))
        in_maps.append(m)
    res = run_bass_kernel_spmd(nc, in_maps, core_ids=list(range(cores)))
    return np.stack([np.asarray(r["y"]) for r in res.results], axis=0)


def kernel(**inputs):
    x = np.asarray(inputs["x"], np.float32)
    out = run_layers(x, inputs, [0, 1, 2, 3], True)
    return out.astype(np.float32)
```

```python
import math
import os
from contextlib import ExitStack

import numpy as np
import concourse.bass as bass
import concourse.mybir as mybir
from concourse.bass_utils import run_bass_kernel_spmd

F32 = mybir.dt.float32
BF16 = mybir.dt.bfloat16
AF = mybir.ActivationFunctionType
ALU = mybir.AluOpType

P = 128
L = 2048
D = 1024
KD = 8
DFF = 2816
NJ = 22
EPS = 1e-6
NCORES = 8

OFF_NG = 0
OFF_FNG = 64
OFF_CW = 72
OFF_CB = 600
OFF_SSMD = 776
NVROWS = 896

C_IDENT = 0
C_MASKL = 128
C_TRILT = 256
C_ONES = 384
C_EPS = 512
C_NEGPI = 513
C_HALFPI = 514
MAGIC = 12582912.0
CW1 = 6.28125
CW2 = 2.0 * math.pi - CW1
NCONST1 = 640
C2_BMASK = 0
C2_CMASK = 128
C2_RMASK = 640
C2_IOTA = 704
C2_REP = 1216
NCONST2 = 2240


def _host_consts():
    c = np.zeros((P, NCONST1), np.float32)
    r = np.arange(P)
    c[:, C_IDENT:C_IDENT + 128] = np.eye(P, dtype=np.float32)
    c[:, C_MASKL:C_MASKL + 128] = (r[None, :] < r[:, None]).astype(np.float32)
    c[:, C_TRILT:C_TRILT + 128] = (r[:, None] <= r[None, :]).astype(np.float32)
    c[:, C_ONES:C_ONES + 128] = 1.0
    c[:, C_EPS] = EPS
    c[:, C_NEGPI] = -math.pi
    c[:, C_HALFPI] = math.pi / 2
    c2 = np.zeros((P, NCONST2), np.float32)
    for g in range(64):
        q, gl = divmod(g, 8)
        c2[g, C2_REP + q * 128 + gl * 16: C2_REP + q * 128 + gl * 16 + 16] = 1.0
    for row in range(P):
        gl8 = row // 16
        c2[row, C2_BMASK + (gl8 % 2) * 64: C2_BMASK + (gl8 % 2) * 64 + 64] = 1.0
        c2[row, C2_RMASK + gl8 // 2] = 1.0
    for m in range(4):
        for row in range(P):
            gl = row // 64
            g8 = 2 * m + gl
            c2[row, C2_CMASK + m * 128 + g8 * 16: C2_CMASK + m * 128 + g8 * 16 + 16] = 1.0
    c2[:, C2_IOTA:C2_IOTA + 512] = np.arange(512, dtype=np.float32)[None, :]
    return c, c2


class Dep:
    __slots__ = ("w", "r", "name", "excl")

    def __init__(self, name, w=None, excl=False):
        self.name = name
        self.w = w
        self.r = []
        self.excl = excl


class Op:
    __slots__ = ("eng", "fn", "idx", "deps", "waits", "signal", "sigval", "dkey", "dval", "gidx")


ENGS = ("pe", "act", "dve", "pool", "sp")


class KB:
    def __init__(self, nc):
        self.nc = nc
        self.ops = {e: [] for e in ENGS}
        self.all_ops = []
        self.tiles = []
        self.fence_op = None
        self.dma_count = {}
        self.dma_group = set()

    def tile(self, name, excl=False):
        t = Dep(name, self.fence_op, excl)
        self.tiles.append(t)
        return t

    def tiles_n(self, name, *dims):
        if len(dims) == 1:
            return [self.tile(f"{name}{i}") for i in range(dims[0])]
        return [self.tiles_n(f"{name}{i}_", *dims[1:]) for i in range(dims[0])]

    def op(self, eng, fn, reads=(), writes=(), dkey=None, nd=1):
        o = Op()
        o.eng = eng
        o.fn = fn
        o.idx = len(self.ops[eng])
        o.gidx = len(self.all_ops)
        o.deps = set()
        o.waits = []
        o.signal = False
        o.sigval = 0
        o.dkey = dkey
        o.dval = 0
        if dkey is not None:
            self.dma_count[dkey] = self.dma_count.get(dkey, 0) + 16 * nd
            o.dval = self.dma_count[dkey]
        for t in reads:
            if t.w is not None:
                o.deps.add(t.w)
            if t.excl:
                for r in t.r:
                    if r.eng != eng:
                        o.deps.add(r)
        for t in writes:
            if t.w is not None:
                o.deps.add(t.w)
            for r in t.r:
                o.deps.add(r)
        for t in reads:
            t.r.append(o)
        for t in writes:
            t.w = o
            t.r = []
        o.deps.discard(o)
        self.ops[eng].append(o)
        self.all_ops.append(o)
        return o

    def dma(self, queue, out, in_, reads, writes, key, group=False):
        if group:
            self.dma_group.add(key)
        return self.op(queue, lambda e: [e.dma_start(out=out, in_=in_)], reads, writes, dkey=key)

    def dma_multi(self, queue, pieces, reads, writes, key):
        return self.op(queue, lambda e: [e.dma_start(out=o, in_=i) for (o, i) in pieces], reads, writes, dkey=key,
                       nd=len(pieces))

    def fence(self):
        deps_r, deps_w = [], []
        o = self.op("sp", lambda e: e.nop(), reads=(), writes=())
        for t in self.tiles:
            if t.w is not None:
                o.deps.add(t.w)
            for r in t.r:
                o.deps.add(r)
        o.deps.discard(o)
        self.fence_op = o
        return o

    def finalize(self, es):
        nc = self.nc
        seen = {e: {} for e in ENGS}
        for o in self.all_ops:
            sn = seen[o.eng]
            need = {}
            for d in o.deps:
                if d.dkey is not None:
                    key = ("d", d.dkey)
                    val = self.dma_count[d.dkey] if d.dkey in self.dma_group else d.dval
                    if sn.get(key, 0) >= val:
                        continue
                    if need.get(key, (0, None))[0] < val:
                        need[key] = (val, d)
                else:
                    if d.eng == o.eng:
                        if o.eng == "pe" or (o.idx - d.idx) > 3:
                            continue
                    key = ("e", d.eng)
                    if sn.get(key, -1) >= d.idx:
                        continue
                    if need.get(key, (-1, None))[0] < d.idx:
                        need[key] = (d.idx, d)
            for key, (val, d) in need.items():
                sn[key] = val
                if key[0] == "e":
                    d.signal = True
                o.waits.append((key, d))
        for e in ENGS:
            cnt = 0
            for o in self.ops[e]:
                if o.signal:
                    cnt += 1
                    o.sigval = cnt
        esem = {e: es.enter_context(nc.semaphore(f"sem_{e}")) for e in ENGS}
        dsem = {k: es.enter_context(nc.semaphore(f"dsem_{k}")) for k in self.dma_count}
        block = es.enter_context(nc.Block())
        kb = self

        def run(ename, eng):
            for o in kb.ops[ename]:
                for key, d in o.waits:
                    if key[0] == "d":
                        val = kb.dma_count[d.dkey] if d.dkey in kb.dma_group else d.dval
                        eng.wait_ge(dsem[d.dkey], val)
                    else:
                        eng.wait_ge(esem[d.eng], d.sigval)
                ins = o.fn(eng)
                if o.dkey is not None:
                    for di in ins:
                        di.then_inc(dsem[o.dkey], 16)
                elif o.signal:
                    ins.then_inc(esem[ename], 1)

        @block.tensor
        def _(e):
            run("pe", e)

        @block.scalar
        def _(e):
            run("act", e)

        @block.vector
        def _(e):
            run("dve", e)

        @block.gpsimd
        def _(e):
            run("pool", e)

        @block.sync
        def _(e):
            run("sp", e)


class Prog:
    def __init__(self, layers, do_final, x_in_tokmajor=True):
        self.layers = layers
        self.do_final = do_final

    def build(self):
        nc = bass.Bass("TRN2", target_bir_lowering=False)
        self.nc = nc
        dt = nc.dram_tensor
        self.d = {}
        shapes = {
            "x": [L, D], "consts": [P, NCONST1], "consts2": [P, NCONST2], "vecs": [NVROWS, 128],
            "sb_w_qkv": [2, D, 3 * D], "sb_w_o": [2, D, D],
            "sg_w_in": [1, D, 2 * D], "sg_norm_g": [1, D], "sg_w_s": [1, 8, 128, 128],
            "sg_b": [1, 8, 128], "sg_w_o": [1, D, D],
            "ssm_w_in": [1, D, D], "ssm_lam_re": [1, 64, 64], "ssm_lam_im": [1, 64, 64],
            "ssm_log_dt": [1, 64], "ssm_b_re": [1, 64, 64, 16], "ssm_b_im": [1, 64, 64, 16],
            "ssm_c_re": [1, 64, 16, 64], "ssm_c_im": [1, 64, 16, 64], "ssm_w_glu": [1, D, 2 * D],
            "ffn_w_up": [4, D, 2 * DFF], "ffn_w_down": [4, DFF, D],
        }
        for n, s in shapes.items():
            self.d[n] = dt(n, s, F32, kind="ExternalInput").ap()
        self.y = dt("y", [L, D], F32, kind="ExternalOutput").ap()
        self.scr = dt("s5_scratch", [2, 64, 1024], F32, kind="Internal").ap()

        with ExitStack() as es:
            self.es = es
            kb = KB(nc)
            self.kb = kb
            self.ptr = (nc._sbuf_addr_for_side("left") + 63) // 64 * 64
            self.sb_end = nc._sbuf_addr_for_side("right")
            self.uid = 0
            sb = self.alloc
            self.X = sb("X", [P, KD, L], F32)
            self.Xd = kb.tiles_n("X", KD, 4)
            self.cst = sb("cst", [P, NCONST1], F32)
            self.cst_d = kb.tile("cst")
            self.cstb = sb("cstb", [P, 512], BF16)
            self.cstb_d = kb.tile("cstb")
            self.cols = sb("cols", [P, NVROWS], F32)
            self.cols_d = kb.tile("cols")
            self.NSLOT = 3
            self.slots = [sb(f"slot{i}", [P, 3072], BF16) for i in range(self.NSLOT)]
            self.slot_d = [kb.tile(f"slot{i}") for i in range(self.NSLOT)]
            self.slot_i = 0
            self.PF = es.enter_context(nc.psum_tensor("pf", [P, 6 * 512], F32))
            self.PB = es.enter_context(nc.psum_tensor("pb", [P, 2 * 1024], BF16))
            self.pf_d = [kb.tile(f"pf{i}", excl=True) for i in range(6)]
            self.pb_d = [kb.tile(f"pb{i}", excl=True) for i in range(2)]

            self.setup()
            self.load_x()
            for l in self.layers:
                m = l % 3
                if m == 0:
                    self.attention(l, l // 3)
                elif m == 1:
                    self.gmlp(l)
                else:
                    self.s5(l)
                if os.environ.get("NOFFN") != "1":
                    self.ffn(l)
            self.store(self.do_final)
            kb.finalize(es)
        return nc

    def alloc(self, name, shape, dtype):
        nbytes = int(np.prod(shape[1:])) * (4 if dtype == F32 else 2)
        off = (self.ptr + 63) // 64 * 64
        assert off + nbytes <= self.sb_end, f"SBUF overflow allocating {name}: {off + nbytes} > {self.sb_end}"
        self.ptr = off + nbytes
        self.uid += 1
        return self.nc.alloc_sbuf_tensor_at(f"{name}_{self.uid}", list(shape), dtype, offset=off)

    def pf(self, b, n=512, off=0):
        return self.PF[:, b * 512 + off: b * 512 + off + n]

    def pfx(self, b):
        if b < 6:
            return self.pf(b)
        return self.PB[:, (b - 6) * 1024:(b - 5) * 1024].bitcast(F32)

    def pfx_d(self, b):
        return self.pf_d[b] if b < 6 else self.pb_d[b - 6]

    def ident_f(self):
        return self.cst[:, C_IDENT:C_IDENT + 128]

    def col(self, r):
        return self.cols[:, r:r + 1]

    def phase_scope(self):
        self.kb.fence()
        ps = ExitStack()
        return ps

    def setup(self):
        kb, nc = self.kb, self.nc
        kb.dma("sp", self.cst[:], self.d["consts"], [], [self.cst_d], "cst")
        kb.dma("pool", self.cstb[:], self.d["consts"][:, 0:512], [], [self.cstb_d], "cstb")
        mark = self.ptr
        vst = self.alloc("vstage", [P, 7, 128], F32)
        if True:
            vd = kb.tile("vstage")
            kb.dma("sp", vst[:], self.d["vecs"].rearrange("(a p) c -> p a c", p=P), [], [vd], "vst")
            for a in range(7):
                b = a // 4
                o = (a % 4) * 128
                kb.op("pe", lambda e, a=a, b=b, o=o: e.transpose(self.pf(b, 128, o), vst[:, a, :], self.ident_f()),
                      [vd, self.cst_d], [self.pf_d[b]])
            kb.op("dve", lambda e: e.tensor_copy(out=self.cols[:, 0:512], in_=self.pf(0)), [self.pf_d[0]], [self.cols_d])
            kb.op("dve", lambda e: e.tensor_copy(out=self.cols[:, 512:896], in_=self.pf(1, 384)), [self.pf_d[1]], [self.cols_d])
            kb.fence()
        self.ptr = mark

    def load_x(self):
        kb, nc = self.kb, self.nc
        x = self.d["x"]
        mark = self.ptr
        xst = self.alloc("xst", [P, 2, D], F32)
        if True:
            xd = [kb.tile("xst0"), kb.tile("xst1")]
            for i in range(16):
                s = i % 2
                kb.dma("sp", xst[:, s, :], x[i * 128:(i + 1) * 128, :], [], [xd[s]], f"xst{s}")
                for half in range(2):
                    b = 2 * s + half
                    for kk in range(4):
                        k = half * 4 + kk
                        kb.op("pe", lambda e, s=s, k=k, b=b, kk=kk: e.transpose(
                            self.pf(b, 128, kk * 128), xst[:, s, k * 128:(k + 1) * 128], self.ident_f()),
                            [xd[s], self.cst_d], [self.pf_d[b]])
                    eng = "dve" if half == 0 else "act"
                    outap = self.X[:, half * 4:half * 4 + 4, i * 128:(i + 1) * 128]
                    inap = self.pf(b).rearrange("p (k t) -> p k t", k=4)
                    if eng == "dve":
                        fn = lambda e, outap=outap, inap=inap: e.tensor_copy(out=outap, in_=inap)
                    else:
                        fn = lambda e, outap=outap, inap=inap: e.activation(out=outap, in_=inap, func=AF.Copy)
                    kb.op(eng, fn, [self.pf_d[b]], [self.Xd[k][i // 4] for k in range(half * 4, half * 4 + 4)])
            kb.fence()
        self.ptr = mark

    def store(self, do_final):
        kb, nc = self.kb, self.nc
        mark = self.ptr
        ps = None
        if True:
            sbt = self.alloc
            xo = sbt("xo", [P, KD, 512], F32)
            xo_d = kb.tiles_n("xo", KD)
            yst = sbt("yst", [P, 2, D], F32)
            yd = [kb.tile("yst0"), kb.tile("yst1")]
            nt = self.norm_temps(ps) if do_final else None
            cnt = 0
            for tt in range(4):
                if do_final:
                    self.rmsnorm(OFF_FNG, tt, xo, 0, xo_d, nt)
                    src = lambda k, c: xo[:, k, c * 128:(c + 1) * 128]
                    srcd = lambda k: xo_d[k]
                else:
                    src = lambda k, c, tt=tt: self.X[:, k, tt * 512 + c * 128: tt * 512 + (c + 1) * 128]
                    srcd = lambda k, tt=tt: self.Xd[k][tt]
                for c in range(4):
                    i = tt * 4 + c
                    s = cnt % 2
                    cnt += 1
                    for half in range(2):
                        b = 2 * s + half
                        for kk in range(4):
                            k = half * 4 + kk
                            kb.op("pe", lambda e, k=k, c=c, b=b, kk=kk, src=src: e.transpose(
                                self.pf(b, 128, kk * 128), src(k, c), self.ident_f()),
                                [srcd(k), self.cst_d], [self.pf_d[b]])
                        outap = yst[:, s, half * 512:(half + 1) * 512]
                        if half == 0:
                            kb.op("dve", lambda e, outap=outap, b=b: e.tensor_copy(out=outap, in_=self.pf(b)),
                                  [self.pf_d[b]], [yd[s]])
                        else:
                            kb.op("act", lambda e, outap=outap, b=b: e.activation(out=outap, in_=self.pf(b), func=AF.Copy),
                                  [self.pf_d[b]], [yd[s]])
                    kb.dma("sp", self.y[i * 128:(i + 1) * 128, :], yst[:, s, :], [yd[s]], [], f"yst{s}")
            fin = kb.tile("fin")
            kb.op("sp", lambda e: e.nop(), [], [yd[0], yd[1], fin])
            kb.fence()
        self.ptr = mark

    def norm_temps(self, ps):
        nc, kb = self.nc, self.kb
        sq = self.alloc("n_sq", [P, KD, 512], BF16)
        r1 = self.alloc("n_r1", [P, 512], F32)
        rs = self.alloc("n_rs", [P, 512], F32)
        return dict(sq=sq, r1=r1, rs=rs, sq_d=kb.tile("n_sq"), r1_d=kb.tile("n_r1"), rs_d=kb.tile("n_rs"))

    def rmsnorm(self, goff, tt, out, ocol0, out_d, nt, bank=5, out_scale_eng="dve"):
        kb = self.kb
        X = self.X
        ts = slice(tt * 512, (tt + 1) * 512)
        xr = [self.Xd[k][tt] for k in range(KD)]
        kb.op("act", lambda e: e.activation(out=nt["sq"][:], in_=X[:, :, ts], func=AF.Square), xr, [nt["sq_d"]])
        ones_b = self.cstb[:, 384:512]

        def mm(e):
            ins = None
            for k in range(KD):
                ins = e.matmul(self.pf(bank), lhsT=ones_b, rhs=nt["sq"][:, k, :], start=(k == 0), stop=(k == KD - 1))
            return ins
        kb.op("pe", mm, [nt["sq_d"], self.cstb_d], [self.pf_d[bank]])
        kb.op("act", lambda e: e.activation(out=nt["r1"][:], in_=self.pf(bank), func=AF.Sqrt, bias=self.cst[:, C_EPS:C_EPS + 1],
                                            scale=1.0 / D), [self.pf_d[bank], self.cst_d], [nt["r1_d"]])
        kb.op("dve", lambda e: e.reciprocal(out=nt["rs"][:], in_=nt["r1"][:]), [nt["r1_d"]], [nt["rs_d"]])
        for k in range(KD):
            kb.op("dve", lambda e, k=k: e.scalar_tensor_tensor(
                out=out[:, k, ocol0:ocol0 + 512], in0=X[:, k, ts], scalar=self.col(goff + k), in1=nt["rs"][:],
                op0=ALU.mult, op1=ALU.mult), [self.Xd[k][tt], nt["rs_d"], self.cols_d], [out_d[k]])

    def next_slot(self):
        s = self.slot_i % self.NSLOT
        self.slot_i += 1
        return s

    def load_w(self, s, pieces):
        self.kb.dma_multi("pool", pieces, [], [self.slot_d[s]], f"slot{s}")

    def slot3(self, s, k, n):
        return self.slots[s][:, 0:k * n].rearrange("p (k n) -> p k n", k=k)

    def ffn(self, l):
        kb, nc = self.kb, self.nc
        wup = self.d["ffn_w_up"][l]
        wdn = self.d["ffn_w_down"][l]
        kb.fence()
        mark = self.ptr
        ps = None
        if True:
            sbt = self.alloc
            xn = sbt("f_xn", [P, KD, 1024], BF16)
            xn_d = [kb.tiles_n("f_xn", KD) for _ in range(2)]
            act = sbt("f_act", [P, NJ, 1024], BF16)
            act_d = kb.tiles_n("f_act", NJ, 2)
            hs = sbt("f_hs", [P, 2, 2, 2 + 1024], F32)
            hs_d = kb.tiles_n("f_hs", 2, 2, 2)
            hsh_d = kb.tiles_n("f_hsh", 2, 2)
            y0 = sbt("f_y0", [P, 2, 2, 512], F32)
            y0_d = kb.tiles_n("f_y0", 2, 2)
            sg = sbt("f_sg", [P, 2, 512], F32)
            sg_d = kb.tiles_n("f_sg", 2)
            halo = sbt("f_halo", [P, NJ, 2, 2], F32)
            halo_d = kb.tiles_n("f_halo", NJ)
            nt = self.norm_temps(ps)
            ucount = 0
            for half in range(2):
                for t2 in range(2):
                    self.rmsnorm(OFF_NG + (l * 2 + 1) * 8, half * 2 + t2, xn, t2 * 512, xn_d[t2], nt)
                pend = []

                def issue_up(j):
                    s = self.next_slot()
                    v = self.slot3(s, KD, 256)
                    self.load_w(s, [
                        (v[:, :, 0:128], wup[:, j * 128:(j + 1) * 128].rearrange("(k p) n -> p k n", p=P)),
                        (v[:, :, 128:256], wup[:, DFF + j * 128: DFF + (j + 1) * 128].rearrange("(k p) n -> p k n", p=P)),
                    ])
                    return s

                def issue_dn(dc):
                    s = self.next_slot()
                    v = self.slot3(s, NJ, 128)
                    self.load_w(s, [(v, wdn[:, dc * 128:(dc + 1) * 128].rearrange("(k p) n -> p k n", p=P))])
                    return s
                seq = [("u", j) for j in range(NJ)] + [("d", dc) for dc in range(KD)]
                PRE = self.NSLOT - 1
                slots_of = {}
                for q in range(min(PRE, len(seq))):
                    slots_of[q] = issue_up(seq[q][1]) if seq[q][0] == "u" else issue_dn(seq[q][1])
                for qi, (kind, j) in enumerate(seq):
                    s = slots_of[qi]
                    if kind == "u":
                        wv = self.slot3(s, KD, 256)
                        hb = j % 2
                        for ag in range(2):
                            if half == 0:
                                kb.op("pool", lambda e, hb=hb, ag=ag: e.memset(hs[:, hb, ag, 0:2], 0.0), [], [hsh_d[hb][ag]])
                            else:
                                kb.op("pool", lambda e, hb=hb, ag=ag, j=j: e.tensor_copy(out=hs[:, hb, ag, 0:2], in_=halo[:, j, ag, :]),
                                      [halo_d[j]], [hsh_d[hb][ag]])
                        for t2 in range(2):
                            pb = (ucount % 2) * 2
                            yb = ucount % 2
                            ucount += 1
                            for ag in range(2):
                                def mm(e, ag=ag, t2=t2, pb=pb, wv=wv):
                                    ins = None
                                    for k in range(KD):
                                        ins = e.matmul(self.pf(pb + ag), lhsT=wv[:, k, ag * 128:(ag + 1) * 128],
                                                       rhs=xn[:, k, t2 * 512:(t2 + 1) * 512], start=(k == 0), stop=(k == KD - 1))
                                    return ins
                                kb.op("pe", mm, [self.slot_d[s]] + xn_d[t2], [self.pf_d[pb + ag]])
                            c0 = 2 + t2 * 512
                            for ag in range(2):
                                kk = j + ag * NJ
                                w0 = self.col(OFF_CW + (l * 3 + 0) * 44 + kk)
                                w1 = self.col(OFF_CW + (l * 3 + 1) * 44 + kk)
                                w2 = self.col(OFF_CW + (l * 3 + 2) * 44 + kk)
                                bb = self.col(OFF_CB + l * 44 + kk)
                                kb.op("act", lambda e, hb=hb, ag=ag, c0=c0, pb=pb: e.activation(
                                    out=hs[:, hb, ag, c0:c0 + 512], in_=self.pf(pb + ag), func=AF.Copy),
                                    [self.pf_d[pb + ag]], [hs_d[hb][ag][t2]])
                                kb.op("act", lambda e, yb=yb, ag=ag, pb=pb, w2=w2, bb=bb: e.activation(
                                    out=y0[:, yb, ag, :], in_=self.pf(pb + ag), func=AF.Identity, bias=bb, scale=w2),
                                    [self.pf_d[pb + ag], self.cols_d], [y0_d[yb][ag]])
                                rd = [hs_d[hb][ag][t2], self.cols_d] + ([hsh_d[hb][ag]] if t2 == 0 else [hs_d[hb][ag][0]])
                                kb.op("dve", lambda e, yb=yb, ag=ag, hb=hb, c0=c0, w1=w1: e.scalar_tensor_tensor(
                                    out=y0[:, yb, ag, :], in0=hs[:, hb, ag, c0 - 1:c0 + 511], scalar=w1, in1=y0[:, yb, ag, :],
                                    op0=ALU.mult, op1=ALU.add), rd + [y0_d[yb][ag]], [y0_d[yb][ag]])
                                kb.op("dve", lambda e, yb=yb, ag=ag, hb=hb, c0=c0, w0=w0: e.scalar_tensor_tensor(
                                    out=y0[:, yb, ag, :], in0=hs[:, hb, ag, c0 - 2:c0 + 510], scalar=w0, in1=y0[:, yb, ag, :],
                                    op0=ALU.mult, op1=ALU.add), rd + [y0_d[yb][ag]], [y0_d[yb][ag]])
                            kb.op("act", lambda e, yb=yb: e.activation(out=sg[:, yb, :], in_=y0[:, yb, 1, :], func=AF.Silu),
                                  [y0_d[yb][1]], [sg_d[yb]])
                            kb.op("dve", lambda e, yb=yb, j=j, t2=t2: e.tensor_tensor(
                                out=act[:, j, t2 * 512:(t2 + 1) * 512], in0=sg[:, yb, :], in1=y0[:, yb, 0, :], op=ALU.mult),
                                [sg_d[yb], y0_d[yb][0]], [act_d[j][t2]])
                        if half == 0:
                            kb.op("pool", lambda e, hb=hb, j=j: e.tensor_copy(out=halo[:, j, :, :], in_=hs[:, hb, :, 1024:1026]),
                                  [hs_d[hb][0][1], hs_d[hb][1][1]], [halo_d[j]])
                    else:
                        dc = j
                        wv = self.slot3(s, NJ, 128)
                        for t2 in range(2):
                            bank = 4 + (t2 % 2)
                            def mm(e, t2=t2, bank=bank, wv=wv):
                                ins = None
                                for jj in range(NJ):
                                    ins = e.matmul(self.pf(bank), lhsT=wv[:, jj, :], rhs=act[:, jj, t2 * 512:(t2 + 1) * 512],
                                                   start=(jj == 0), stop=(jj == NJ - 1))
                                return ins
                            kb.op("pe", mm, [self.slot_d[s]] + [act_d[jj][t2] for jj in range(NJ)], [self.pf_d[bank]])
                            tt = half * 2 + t2
                            kb.op("dve", lambda e, dc=dc, tt=tt, bank=bank: e.tensor_tensor(
                                out=self.X[:, dc, tt * 512:(tt + 1) * 512], in0=self.X[:, dc, tt * 512:(tt + 1) * 512],
                                in1=self.pf(bank), op=ALU.add), [self.pf_d[bank], self.Xd[dc][tt]], [self.Xd[dc][tt]])
                    nq = qi + PRE
                    if nq < len(seq):
                        slots_of[nq] = issue_up(seq[nq][1]) if seq[nq][0] == "u" else issue_dn(seq[nq][1])
            kb.fence()
        self.ptr = mark

    def attention(self, l, ja):
        kb, nc = self.kb, self.nc
        wqkv = self.d["sb_w_qkv"][ja]
        wo = self.d["sb_w_o"][ja]
        kb.fence()
        mark = self.ptr
        ps = None
        if True:
            sbt = self.alloc
            xn = sbt("a_xn", [P, KD, L], BF16)
            xn_d = kb.tiles_n("a_xn", 4, KD)
            ao = sbt("a_o", [P, KD, L], BF16)
            ao_d = kb.tiles_n("a_o", KD, 4)
            qT = sbt("a_q", [P, L], BF16)
            kT = sbt("a_k", [P, L], BF16)
            vv = sbt("a_v", [P, 16, 128], BF16)
            q_d, k_d, v_d = kb.tile("a_q"), kb.tile("a_k"), kb.tile("a_v")
            NB = 4
            PW = 512
            mark2 = self.ptr
            nt = self.norm_temps(None)
            for tt in range(4):
                self.rmsnorm(OFF_NG + (l * 2 + 0) * 8, tt, xn, tt * 512, xn_d[tt], nt)
            kb.fence()
            self.ptr = mark2
            ee = sbt("a_e", [P, NB, PW], F32)
            spt = sbt("a_sp", [P, NB, PW], F32)
            lw = sbt("a_lw", [P, NB, PW], F32)
            ww = sbt("a_w", [P, NB, PW], BF16)
            wT = sbt("a_wT", [P, NB, PW], BF16)
            ones = sbt("a_ones", [P, PW], BF16)
            carry = sbt("a_carry", [P, 8], F32)
            e_d, sp_d, lw_d, w_d, wT_d = (kb.tiles_n(n, NB) for n in ("a_e", "a_sp", "a_lw", "a_w", "a_wT"))
            ones_d = kb.tile("a_ones")
            carry_d = kb.tiles_n("a_carry", 8)
            kb.op("pool", lambda e: e.memset(ones[:], 1.0), [], [ones_d])
            maskL_f = self.cst[:, C_MASKL:C_MASKL + 128]
            maskL_b = self.cstb[:, 128:256]
            ident_b = self.cstb[:, 0:128]
            pcount = 0
            ocount = 0
            ccount = 0
            gen = 0
            for pair in range(8):
                s = self.next_slot()
                wv = self.slot3(s, KD, 384)
                self.load_w(s, [(wv[:, :, c * 128:(c + 1) * 128],
                                 wqkv[:, c * D + pair * 128: c * D + (pair + 1) * 128].rearrange("(k p) n -> p k n", p=P))
                                for c in range(3)])
                for tt in range(4):
                    for c in range(2):
                        bank = 4 + (gen % 2)
                        gen += 1
                        def mm(e, c=c, tt=tt, bank=bank, wv=wv):
                            ins = None
                            for k in range(KD):
                                ins = e.matmul(self.pf(bank), lhsT=wv[:, k, c * 128:(c + 1) * 128],
                                               rhs=xn[:, k, tt * 512:(tt + 1) * 512], start=(k == 0), stop=(k == KD - 1))
                            return ins
                        kb.op("pe", mm, [self.slot_d[s]] + xn_d[tt], [self.pf_d[bank]])
                        dst = qT if c == 0 else kT
                        dd = q_d if c == 0 else k_d
                        sc = 0.125 if c == 0 else 1.0
                        kb.op("act", lambda e, dst=dst, tt=tt, bank=bank, sc=sc: e.activation(
                            out=dst[:, tt * 512:(tt + 1) * 512], in_=self.pf(bank), func=AF.Copy, scale=sc),
                            [self.pf_d[bank]], [dd])
                for g4 in range(4):
                    bank = 4 + (gen % 2)
                    gen += 1
                    def mmv(e, g4=g4, bank=bank, wv=wv):
                        ins = None
                        for c4 in range(4):
                            i = g4 * 4 + c4
                            for k in range(KD):
                                ins = e.matmul(self.pf(bank, 128, c4 * 128), lhsT=xn[:, k, i * 128:(i + 1) * 128],
                                               rhs=wv[:, k, 256:384], start=(k == 0), stop=(k == KD - 1))
                        return ins
                    kb.op("pe", mmv, [self.slot_d[s]] + xn_d[g4], [self.pf_d[bank]])
                    kb.op("dve", lambda e, g4=g4, bank=bank: e.tensor_copy(
                        out=vv[:, g4 * 4:(g4 + 1) * 4, :], in_=self.pf(bank).rearrange("p (c n) -> p c n", c=4)),
                        [self.pf_d[bank]], [v_d])
                items = []
                for hh in range(2):
                    for i in range(16):
                        t1 = (i + 1) * 128
                        pcs = []
                        ke = t1
                        while ke > 0:
                            ks = ((ke - 1) // PW) * PW
                            pcs.append((ks, ke))
                            ke = ks
                        ob = 4 + (ocount % 2)
                        ocount += 1
                        prev_cc = None
                        for pi, (ks, ke) in enumerate(pcs):
                            it = dict(hh=hh, i=i, ks=ks, ke=ke, first=(pi == 0), last=(pi == len(pcs) - 1), ob=ob,
                                      p=pcount, cin=prev_cc, cout=None)
                            pcount += 1
                            if pi < len(pcs) - 1:
                                it["cout"] = ccount % 8
                                ccount += 1
                            prev_cc = it["cout"]
                            items.append(it)

                def geo(it):
                    hh, i, ks, ke, p = it["hh"], it["i"], it["ks"], it["ke"], it["p"]
                    return 64 * hh, i * 128, (i + 1) * 128, ke - ks, p % NB, p % 4

                def f_z(it):
                    pb0, t0, t1, n, bi, zb = geo(it)
                    ks, ke = it["ks"], it["ke"]
                    zap = self.pf(zb, n)
                    kb.op("pe", lambda e: e.matmul(zap, lhsT=qT[pb0:pb0 + 64, t0:t1], rhs=kT[pb0:pb0 + 64, ks:ke], start=True, stop=True),
                          [q_d, k_d], [self.pf_d[zb]])

                def f_exp(it):
                    pb0, t0, t1, n, bi, zb = geo(it)
                    zap = self.pf(zb, n)
                    kb.op("act", lambda e: e.activation(out=ee[:, bi, 0:n], in_=zap, func=AF.Exp), [self.pf_d[zb]], [e_d[bi]])

                def f_mask(it):
                    pb0, t0, t1, n, bi, zb = geo(it)
                    if it["first"]:
                        kb.op("pool", lambda e: e.tensor_tensor(out=ee[:, bi, n - 128:n], in0=ee[:, bi, n - 128:n], in1=maskL_f, op=ALU.mult),
                              [e_d[bi], self.cst_d], [e_d[bi]])

                def f_ln(it):
                    pb0, t0, t1, n, bi, zb = geo(it)
                    kb.op("act", lambda e: e.activation(out=spt[:, bi, 0:n], in_=ee[:, bi, 0:n], func=AF.Ln, bias=1.0), [e_d[bi]], [sp_d[bi]])

                def f_scan(it):
                    pb0, t0, t1, n, bi, zb = geo(it)
                    cin, cout = it["cin"], it["cout"]
                    init = 0.0 if cin is None else carry[:, cin:cin + 1]
                    rd = [sp_d[bi], ones_d] + ([] if cin is None else [carry_d[cin]])
                    kb.op("dve", lambda e: e.tensor_tensor_scan(out=lw[:, bi, 0:n][:, ::-1], data0=ones[:, 0:n], data1=spt[:, bi, 0:n][:, ::-1],
                                                                initial=init, op0=ALU.mult, op1=ALU.add), rd, [lw_d[bi]])
                    if cout is not None:
                        kb.op("dve", lambda e: e.tensor_copy(out=carry[:, cout:cout + 1], in_=lw[:, bi, 0:1]), [lw_d[bi]], [carry_d[cout]])

                def f_er(it):
                    pb0, t0, t1, n, bi, zb = geo(it)
                    kb.op("act", lambda e: e.activation(out=lw[:, bi, 0:n], in_=lw[:, bi, 0:n], func=AF.Exp, scale=-1.0), [lw_d[bi]], [lw_d[bi]])

                def f_mult(it):
                    pb0, t0, t1, n, bi, zb = geo(it)
                    kb.op("dve", lambda e: e.tensor_tensor(out=ww[:, bi, 0:n], in0=ee[:, bi, 0:n], in1=lw[:, bi, 0:n], op=ALU.mult),
                          [e_d[bi], lw_d[bi]], [w_d[bi]])

                def f_tr(it):
                    pb0, t0, t1, n, bi, zb = geo(it)
                    tb = it["p"] % 2
                    nb_ = n // 128

                    def trs(e):
                        ins = None
                        for b in range(nb_):
                            ins = e.transpose(self.PB[:, tb * 1024 + b * 128: tb * 1024 + (b + 1) * 128], ww[:, bi, b * 128:(b + 1) * 128], ident_b)
                        return ins
                    kb.op("pe", trs, [w_d[bi], self.cstb_d], [self.pb_d[tb]])

                def f_copy(it):
                    pb0, t0, t1, n, bi, zb = geo(it)
                    tb = it["p"] % 2
                    kb.op("act", lambda e: e.activation(out=wT[:, bi, 0:n], in_=self.PB[:, tb * 1024: tb * 1024 + n], func=AF.Copy),
                          [self.pb_d[tb]], [wT_d[bi]])

                def f_mmo(it, pair=pair):
                    pb0, t0, t1, n, bi, zb = geo(it)
                    ob, i = it["ob"], it["i"]
                    nb_ = n // 128
                    kb0 = it["ks"] // 128
                    first, last = it["first"], it["last"]

                    def mmo(e):
                        ins = None
                        for b in range(nb_):
                            ins = e.matmul(self.pf(ob, 128), lhsT=vv[:, kb0 + b, :], rhs=wT[:, bi, b * 128:(b + 1) * 128],
                                           start=(first and b == 0), stop=(last and b == nb_ - 1))
                        return ins
                    kb.op("pe", mmo, [wT_d[bi], v_d], [self.pf_d[ob]])
                    if last:
                        kb.op("act", lambda e: e.activation(out=ao[pb0:pb0 + 64, pair, t0:t1], in_=self.PF[pb0:pb0 + 64, ob * 512: ob * 512 + 128],
                                                            func=AF.Copy), [self.pf_d[ob]], [ao_d[pair][i // 4]])
                sched = [(0, f_z), (0, f_exp), (0, f_mask), (2, f_er), (5, f_copy), (0, f_ln), (1, f_scan), (3, f_mult), (4, f_tr), (6, f_mmo)]
                nit = len(items)
                for step in range(nit + 6):
                    for lag, fn in sched:
                        j = step - lag
                        if 0 <= j < nit:
                            fn(items[j])
            self.proj_residual(ao, lambda k, tt: ao_d[k][tt], wo)
            kb.fence()
        self.ptr = mark

    def proj_residual(self, src, src_d, wdram):
        kb = self.kb
        for dc in range(KD):
            s = self.next_slot()
            wv = self.slot3(s, KD, 128)
            self.load_w(s, [(wv, wdram[:, dc * 128:(dc + 1) * 128].rearrange("(k p) n -> p k n", p=P))])
            for tt in range(4):
                bank = tt % 4

                def mm(e, tt=tt, bank=bank, wv=wv):
                    ins = None
                    for k in range(KD):
                        ins = e.matmul(self.pf(bank), lhsT=wv[:, k, :], rhs=src[:, k, tt * 512:(tt + 1) * 512],
                                       start=(k == 0), stop=(k == KD - 1))
                    return ins
                kb.op("pe", mm, [self.slot_d[s]] + [src_d(k, tt) for k in range(KD)], [self.pf_d[bank]])
                kb.op("dve", lambda e, dc=dc, tt=tt, bank=bank: e.tensor_tensor(
                    out=self.X[:, dc, tt * 512:(tt + 1) * 512], in0=self.X[:, dc, tt * 512:(tt + 1) * 512],
                    in1=self.pf(bank), op=ALU.add), [self.pf_d[bank], self.Xd[dc][tt]], [self.Xd[dc][tt]])

    def gmlp(self, l):
        kb, nc = self.kb, self.nc
        win = self.d["sg_w_in"][0]
        wo = self.d["sg_w_o"][0]
        kb.fence()
        mark = self.ptr
        sbt = self.alloc
        xn = sbt("g_xn", [P, KD, L], BF16)
        xn_d = kb.tiles_n("g_xn", 4, KD)
        vn = sbt("g_vn", [P, 16, D], BF16)
        vn_d = kb.tiles_n("g_vn", 16)
        wsT = sbt("g_wsT", [P, 8, 128], BF16)
        wsT_d = kb.tile("g_wsT")
        bsr = sbt("g_bsr", [1, D], BF16)
        bsr_d = kb.tile("g_bsr")
        ssq = sbt("g_ssq", [P, 16], F32)
        rstd = sbt("g_rstd", [P, 16], F32)
        ssq_d, rstd_d = kb.tile("g_ssq"), kb.tiles_n("g_rstd", 16)
        mark2 = self.ptr
        nt = self.norm_temps(None)
        for tt in range(4):
            self.rmsnorm(OFF_NG + (l * 2 + 0) * 8, tt, xn, tt * 512, xn_d[tt], nt)
        kb.fence()
        self.ptr = mark2
        wv_ = sbt("g_wv", [P, KD, D], BF16)
        wv_d = kb.tile("g_wv")
        vg = sbt("g_vg", [P, 2, D], F32)
        vg_d = kb.tiles_n("g_vg", 2)
        junk = sbt("g_junk", [P, D], BF16)
        junk_d = kb.tile("g_junk")
        gbc = sbt("g_gbc", [P, D], F32)
        gbc_d = kb.tile("g_gbc")
        grow = sbt("g_grow", [1, D], F32)
        grow_d = kb.tile("g_grow")
        wsf = sbt("g_wsf", [P, 8, 128], F32)
        wsf_d = kb.tile("g_wsf")
        kb.dma_multi("pool", [(wv_[:, :, c * 256:(c + 1) * 256],
                               win[:, D + c * 256: D + (c + 1) * 256].rearrange("(k p) n -> p k n", p=P)) for c in range(4)],
                     [], [wv_d], "g_wv")
        kb.dma("pool", bsr[:], self.d["sg_b"].rearrange("a g t -> a (g t)"), [], [bsr_d], "g_bsr")
        kb.dma("sp", grow[:], self.d["sg_norm_g"], [], [grow_d], "g_grow")
        kb.dma("sp", wsf[:], self.d["sg_w_s"][0].rearrange("g t s -> t g s"), [], [wsf_d], "g_wsf")
        kb.op("dve", lambda e: e.memset(ssq[:], 0.0), [], [ssq_d])
        ones_row_f = self.cst[0:1, C_ONES:C_ONES + 128]
        for nh in range(2):
            kb.op("pe", lambda e, nh=nh: e.matmul(self.pf(4 + nh), lhsT=ones_row_f, rhs=grow[0:1, nh * 512:(nh + 1) * 512],
                                                 start=True, stop=True), [grow_d, self.cst_d], [self.pf_d[4 + nh]])
            kb.op("dve", lambda e, nh=nh: e.tensor_copy(out=gbc[:, nh * 512:(nh + 1) * 512], in_=self.pf(4 + nh)),
                  [self.pf_d[4 + nh]], [gbc_d])
        trilT = self.cst[:, C_TRILT:C_TRILT + 128]
        for g in range(8):
            bank = 4 + g // 4
            kb.op("pe", lambda e, g=g, bank=bank: e.transpose(self.pf(bank, 128, (g % 4) * 128), wsf[:, g, :], self.ident_f()),
                  [wsf_d, self.cst_d], [self.pf_d[bank]])
            kb.op("dve", lambda e, g=g, bank=bank: e.tensor_tensor(out=wsT[:, g, :], in0=self.pf(bank, 128, (g % 4) * 128), in1=trilT,
                                                                   op=ALU.mult), [self.pf_d[bank], self.cst_d], [wsT_d])
        for i in range(16):
            b0 = 2 * (i % 2)
            vb = i % 2
            for nh in range(2):
                def mm(e, i=i, nh=nh, b0=b0):
                    ins = None
                    for k in range(KD):
                        ins = e.matmul(self.pf(b0 + nh), lhsT=xn[:, k, i * 128:(i + 1) * 128], rhs=wv_[:, k, nh * 512:(nh + 1) * 512],
                                       start=(k == 0), stop=(k == KD - 1))
                    return ins
                kb.op("pe", mm, [wv_d] + xn_d[i // 4], [self.pf_d[b0 + nh]])
            kb.op("act", lambda e, vb=vb, b0=b0: e.activation(out=vg[:, vb, :], in_=self.PF[:, b0 * 512: b0 * 512 + 1024],
                                                              func=AF.Gelu_apprx_tanh), [self.pf_d[b0], self.pf_d[b0 + 1]], [vg_d[vb]])
            kb.op("act", lambda e, vb=vb, i=i: e.activation(out=junk[:], in_=vg[:, vb, :], func=AF.Square, accum_out=ssq[:, i:i + 1]),
                  [vg_d[vb], ssq_d], [junk_d, rstd_d[i]])
            kb.op("act", lambda e, i=i: e.activation(out=rstd[:, i:i + 1], in_=ssq[:, i:i + 1], func=AF.Sqrt,
                                                     bias=self.cst[:, C_EPS:C_EPS + 1], scale=1.0 / D), [rstd_d[i], self.cst_d], [rstd_d[i]])
            kb.op("dve", lambda e, i=i: e.reciprocal(out=rstd[:, i:i + 1], in_=rstd[:, i:i + 1]), [rstd_d[i]], [rstd_d[i]])
            kb.op("dve", lambda e, i=i, vb=vb: e.scalar_tensor_tensor(out=vn[:, i, :], in0=vg[:, vb, :], scalar=rstd[:, i:i + 1], in1=gbc[:],
                                                                    op0=ALU.mult, op1=ALU.mult), [vg_d[vb], rstd_d[i], gbc_d], [vn_d[i]])
        kb.fence()
        self.ptr = mark2
        gated = sbt("g_gated", [P, KD, L], BF16)
        gated_d = kb.tiles_n("g_gated", KD, 4)
        ug = sbt("g_ug", [P, 2, 512], F32)
        ug_d = kb.tiles_n("g_ug", 2)
        ones_row_b = self.cstb[0:1, 384:512]
        cnt = 0
        for g in range(8):
            s = self.next_slot()
            wu = self.slot3(s, KD, 128)
            self.load_w(s, [(wu, win[:, g * 128:(g + 1) * 128].rearrange("(k p) n -> p k n", p=P))])
            for tt in range(4):
                sb_ = (cnt % 2) * 2
                ub_ = sb_ + 1
                ui = cnt % 2
                cnt += 1

                def mmsv(e, g=g, tt=tt, sb_=sb_):
                    ins = None
                    for c in range(4):
                        o = self.pf(sb_, 128, c * 128)
                        e.matmul(o, lhsT=vn[:, 4 * tt + c, g * 128:(g + 1) * 128], rhs=wsT[:, g, :], start=True, stop=False)
                        ins = e.matmul(o, lhsT=ones_row_b, rhs=bsr[0:1, g * 128:(g + 1) * 128], start=False, stop=True)
                    return ins
                kb.op("pe", mmsv, [vn_d[4 * tt + c] for c in range(4)] + [wsT_d, bsr_d, self.cstb_d], [self.pf_d[sb_]])

                def mmu(e, tt=tt, ub_=ub_, wu=wu):
                    ins = None
                    for k in range(KD):
                        ins = e.matmul(self.pf(ub_), lhsT=wu[:, k, :], rhs=xn[:, k, tt * 512:(tt + 1) * 512], start=(k == 0), stop=(k == KD - 1))
                    return ins
                kb.op("pe", mmu, [self.slot_d[s]] + xn_d[tt], [self.pf_d[ub_]])
                kb.op("act", lambda e, ui=ui, ub_=ub_: e.activation(out=ug[:, ui, :], in_=self.pf(ub_), func=AF.Gelu_apprx_tanh),
                      [self.pf_d[ub_]], [ug_d[ui]])
                kb.op("dve", lambda e, g=g, tt=tt, ui=ui, sb_=sb_: e.tensor_tensor(
                    out=gated[:, g, tt * 512:(tt + 1) * 512], in0=ug[:, ui, :], in1=self.pf(sb_), op=ALU.mult),
                    [ug_d[ui], self.pf_d[sb_]], [gated_d[g][tt]])
        self.proj_residual(gated, lambda k, tt: gated_d[k][tt], wo)
        kb.fence()
        self.ptr = mark

    def s5(self, l):
        kb, nc = self.kb, self.nc
        d = self.d
        win = d["ssm_w_in"][0]
        wglu = d["ssm_w_glu"][0]
        PI = math.pi
        kb.fence()
        mark = self.ptr
        sbt = self.alloc
        xn = sbt("s_xn", [P, KD, L], BF16)
        xn_d = kb.tiles_n("s_xn", 4, KD)
        yg = sbt("s_yg", [P, KD, L], BF16)
        yg_d = kb.tiles_n("s_yg", KD, 4)
        c2 = sbt("s_c2", [P, C2_REP], F32)
        c2_d = kb.tile("s_c2")
        bbt = sbt("s_bbt", [P, 2, 8, 64], BF16)
        bbt_d = kb.tile("s_bbt")
        ct = sbt("s_ct", [P, 2, 8, 64], BF16)
        ct_d = kb.tile("s_ct")
        svr = sbt("s_svr", [P, 32], F32)
        svt = sbt("s_svt", [P, 32], F32)
        svc = sbt("s_svc", [P, 32], F32)
        svs = sbt("s_svs", [P, 32], F32)
        svn = sbt("s_svn", [P, 32], F32)
        ctmp = sbt("s_ctmp", [P, 2], F32)
        ctmp_d = kb.tile("s_ctmp")
        sv_d = kb.tile("s_sv")
        carry = sbt("s_carry", [P, 2, 32], F32)
        carry_d = kb.tiles_n("s_carry", 32)
        kb.dma("sp", c2[:], d["consts2"][:, 0:C2_REP], [], [c2_d], "s_c2")
        kb.dma("pool", ct[:, 0, :, :], d["ssm_c_re"][0].rearrange("(q gl) h p -> (gl h) q p", q=8), [], [ct_d], "s_ct")
        kb.dma("pool", ct[:, 1, :, :], d["ssm_c_im"][0].rearrange("(q gl) h p -> (gl h) q p", q=8), [], [ct_d], "s_ct")
        mark2 = self.ptr
        nt = self.norm_temps(None)
        for tt in range(4):
            self.rmsnorm(OFF_NG + (l * 2 + 0) * 8, tt, xn, tt * 512, xn_d[tt], nt)
        kb.fence()
        self.ptr = mark2
        bn = sbt("s_bn", [64, 2, 1024], F32)
        bn_d = kb.tile("s_bn")
        kb.dma("sp", bn[:, 0, :], d["ssm_b_re"][0].rearrange("g p h -> g (p h)"), [], [bn_d], "s_bn")
        kb.dma("sp", bn[:, 1, :], d["ssm_b_im"][0].rearrange("g p h -> g (p h)"), [], [bn_d], "s_bn")
        NT = 26
        tp = sbt("s_tp", [64, NT, 64], F32)
        tp_d = kb.tiles_n("s_tp", NT)
        dtc = sbt("s_dt", [64, 2], F32)
        dt_d = kb.tile("s_dt")
        (LR, LI, DLR, DLI, MAG, SA, CA, SN, CS, AR, AI, T1, T2, DEN, RDEN, AM1, U1, U2, CR, CI) = range(20)
        T = lambda i: tp[:, i, :]
        kb.dma("sp", T(LR), d["ssm_lam_re"][0], [], [tp_d[LR]], "s_p0")
        kb.dma("sp", T(LI), d["ssm_lam_im"][0], [], [tp_d[LI]], "s_p1")
        kb.dma("sp", dtc[:, 0:1], d["ssm_log_dt"].rearrange("a g -> g a"), [], [dt_d], "s_p2")
        kb.op("act", lambda e: e.activation(out=dtc[:, 1:2], in_=dtc[:, 0:1], func=AF.Exp), [dt_d], [dt_d])
        dtcol = dtc[:, 1:2]

        def v1(eng, fn, r, w):
            kb.op(eng, fn, [tp_d[i] for i in r] + [dt_d, self.cst_d], [tp_d[i] for i in w])
        v1("dve", lambda e: e.tensor_scalar_min(out=T(LR), in0=T(LR), scalar1=-1e-4), [LR], [LR])
        v1("dve", lambda e: e.tensor_scalar_mul(out=T(DLR), in0=T(LR), scalar1=dtcol), [LR], [DLR])
        v1("dve", lambda e: e.tensor_scalar_mul(out=T(DLI), in0=T(LI), scalar1=dtcol), [LI], [DLI])
        v1("act", lambda e: e.activation(out=T(MAG), in_=T(DLR), func=AF.Exp), [DLR], [MAG])
        hpi64 = self.cst[0:64, C_HALFPI:C_HALFPI + 1]
        v1("dve", lambda e: e.tensor_scalar(out=T(T1), in0=T(DLI), scalar1=1.0 / (2 * PI), scalar2=MAGIC, op0=ALU.mult, op1=ALU.add), [DLI], [T1])
        v1("dve", lambda e: e.tensor_scalar_add(out=T(T1), in0=T(T1), scalar1=-MAGIC), [T1], [T1])
        v1("dve", lambda e: e.scalar_tensor_tensor(out=T(SA), in0=T(T1), scalar=-CW1, in1=T(DLI), op0=ALU.mult, op1=ALU.add), [T1, DLI], [SA])
        v1("dve", lambda e: e.scalar_tensor_tensor(out=T(SA), in0=T(T1), scalar=-CW2, in1=T(SA), op0=ALU.mult, op1=ALU.add), [T1, SA], [SA])
        v1("dve", lambda e: e.scalar_tensor_tensor(out=T(CA), in0=T(SA), scalar=-1.0, in1=T(SA), op0=ALU.mult, op1=ALU.max), [SA], [CA])
        v1("act", lambda e: e.activation(out=T(SN), in_=T(SA), func=AF.Sin, scale=0.999999), [SA], [SN])
        v1("act", lambda e: e.activation(out=T(CS), in_=T(CA), func=AF.Sin, scale=-0.999999, bias=hpi64), [CA], [CS])
        v1("dve", lambda e: e.tensor_tensor(out=T(AR), in0=T(MAG), in1=T(CS), op=ALU.mult), [MAG, CS], [AR])
        v1("dve", lambda e: e.tensor_tensor(out=T(AI), in0=T(MAG), in1=T(SN), op=ALU.mult), [MAG, SN], [AI])
        v1("dve", lambda e: e.tensor_tensor(out=T(T1), in0=T(LR), in1=T(LR), op=ALU.mult), [LR], [T1])
        v1("dve", lambda e: e.tensor_tensor(out=T(T2), in0=T(LI), in1=T(LI), op=ALU.mult), [LI], [T2])
        v1("dve", lambda e: e.tensor_tensor(out=T(DEN), in0=T(T1), in1=T(T2), op=ALU.add), [T1, T2], [DEN])
        v1("dve", lambda e: e.reciprocal(out=T(RDEN), in_=T(DEN)), [DEN], [RDEN])
        v1("dve", lambda e: e.tensor_scalar_add(out=T(AM1), in0=T(AR), scalar1=-1.0), [AR], [AM1])
        v1("dve", lambda e: e.tensor_tensor(out=T(U1), in0=T(AM1), in1=T(LR), op=ALU.mult), [AM1, LR], [U1])
        v1("dve", lambda e: e.tensor_tensor(out=T(U2), in0=T(AI), in1=T(LI), op=ALU.mult), [AI, LI], [U2])
        v1("dve", lambda e: e.tensor_tensor(out=T(U1), in0=T(U1), in1=T(U2), op=ALU.add), [U1, U2], [U1])
        v1("dve", lambda e: e.tensor_tensor(out=T(CR), in0=T(U1), in1=T(RDEN), op=ALU.mult), [U1, RDEN], [CR])
        v1("dve", lambda e: e.tensor_tensor(out=T(U1), in0=T(AI), in1=T(LR), op=ALU.mult), [AI, LR, CR], [U1])
        v1("dve", lambda e: e.tensor_tensor(out=T(U2), in0=T(AM1), in1=T(LI), op=ALU.mult), [AM1, LI], [U2])
        v1("dve", lambda e: e.tensor_tensor(out=T(U1), in0=T(U1), in1=T(U2), op=ALU.subtract), [U1, U2], [U1])
        v1("dve", lambda e: e.tensor_tensor(out=T(CI), in0=T(U1), in1=T(RDEN), op=ALU.mult), [U1, RDEN], [CI])
        bbn = sbt("s_bbn", [64, 2, 1024], F32)
        bbn_d = kb.tile("s_bbn")
        tmpn = sbt("s_tmpn", [64, 1024], F32)
        tmpn_d = kb.tile("s_tmpn")
        R_, I_ = 0, 1
        cb = lambda i: T(i).unsqueeze(2).broadcast_to([64, 64, 16])
        nat = lambda ri: bn[:, ri, :].rearrange("g (p h) -> g p h", h=16)
        outv = lambda t_: t_.rearrange("g (h p) -> g p h", h=16)
        kb.op("dve", lambda e: e.tensor_tensor(out=outv(bbn[:, R_, :]), in0=cb(CR), in1=nat(R_), op=ALU.mult), [tp_d[CR], bn_d], [bbn_d])
        kb.op("dve", lambda e: e.tensor_tensor(out=outv(tmpn[:]), in0=cb(CI), in1=nat(I_), op=ALU.mult), [tp_d[CI], bn_d], [tmpn_d])
        kb.op("dve", lambda e: e.tensor_tensor(out=bbn[:, R_, :], in0=bbn[:, R_, :], in1=tmpn[:], op=ALU.subtract), [bbn_d, tmpn_d], [bbn_d])
        kb.op("dve", lambda e: e.tensor_tensor(out=outv(bbn[:, I_, :]), in0=cb(CR), in1=nat(I_), op=ALU.mult), [tp_d[CR], bn_d], [bbn_d])
        kb.op("dve", lambda e: e.tensor_tensor(out=outv(tmpn[:]), in0=cb(CI), in1=nat(R_), op=ALU.mult), [tp_d[CI], bn_d, bbn_d], [tmpn_d])
        kb.op("dve", lambda e: e.tensor_tensor(out=bbn[:, I_, :], in0=bbn[:, I_, :], in1=tmpn[:], op=ALU.add), [bbn_d, tmpn_d], [bbn_d])
        scr_d = kb.tile("s_scr")
        kb.dma("sp", self.scr.rearrange("r g n -> g r n"), bbn[:], [bbn_d], [scr_d], "s_scr_w")
        for ri in range(2):
            kb.dma("pool", bbt[:, ri, :, :], self.scr[ri].rearrange("(q gl) (h p) -> (gl h) q p", q=8, h=16), [scr_d], [bbt_d], "s_scr_r")
        A5, K5, R5, B5, C5, S5_ = 20, 21, 22, 23, 24, 25
        v1("dve", lambda e: e.tensor_scalar_mul(out=T(A5), in0=T(DLI), scalar1=512.0), [DLI], [A5])
        v1("dve", lambda e: e.tensor_scalar(out=T(K5), in0=T(A5), scalar1=1.0 / (2 * PI), scalar2=MAGIC, op0=ALU.mult, op1=ALU.add), [A5], [K5])
        v1("dve", lambda e: e.tensor_scalar_add(out=T(K5), in0=T(K5), scalar1=-MAGIC), [K5], [K5])
        v1("dve", lambda e: e.scalar_tensor_tensor(out=T(R5), in0=T(K5), scalar=-CW1, in1=T(A5), op0=ALU.mult, op1=ALU.add), [K5, A5], [R5])
        v1("dve", lambda e: e.scalar_tensor_tensor(out=T(R5), in0=T(K5), scalar=-CW2, in1=T(R5), op0=ALU.mult, op1=ALU.add), [K5, R5], [R5])
        v1("dve", lambda e: e.scalar_tensor_tensor(out=T(B5), in0=T(R5), scalar=-1.0, in1=T(R5), op0=ALU.mult, op1=ALU.max), [R5], [B5])
        v1("act", lambda e: e.activation(out=T(S5_), in_=T(R5), func=AF.Sin, scale=0.999999), [R5], [S5_])
        v1("act", lambda e: e.activation(out=T(C5), in_=T(B5), func=AF.Sin, scale=-0.999999, bias=hpi64), [B5], [C5])
        dup = sbt("s_dup", [64, 4, 128], F32)
        dup_d = kb.tile("s_dup")
        for qi, (src, dst) in enumerate(((MAG, svr), (DLI, svt), (C5, svc), (S5_, svs))):
            kb.op("dve", lambda e, qi=qi, src=src: e.tensor_copy(out=dup[:, qi, 0:64], in_=T(src)), [tp_d[src]], [dup_d])
            kb.op("dve", lambda e, qi=qi, src=src: e.tensor_copy(out=dup[:, qi, 64:128], in_=T(src)), [tp_d[src]], [dup_d])
            bank = 2 + qi
            kb.op("pe", lambda e, qi=qi, bank=bank: e.transpose(self.pf(bank, 64), dup[:, qi, :], self.cst[0:64, C_IDENT:C_IDENT + 64]),
                  [dup_d, self.cst_d], [self.pf_d[bank]])
            kb.op("dve", lambda e, dst=dst, bank=bank: e.tensor_copy(out=dst[0:64, :], in_=self.PF[0:64, bank * 512: bank * 512 + 64: 2]),
                  [self.pf_d[bank]], [sv_d])
            kb.op("dve", lambda e, dst=dst, bank=bank: e.tensor_copy(out=dst[64:128, :], in_=self.PF[64:128, bank * 512 + 1: bank * 512 + 64: 2]),
                  [self.pf_d[bank]], [sv_d])
        kb.op("dve", lambda e: e.tensor_scalar_mul(out=svn[:], in0=svs[:], scalar1=-1.0), [sv_d], [sv_d])
        kb.op("dve", lambda e: e.memset(carry[:], 0.0), [], carry_d)
        kb.fence()
        self.ptr = mark2
        u32 = sbt("s_u32", [P, L], F32)
        u32_d = kb.tiles_n("s_u32", 4)
        ub = sbt("s_ub", [P, L], BF16)
        ub_d = kb.tiles_n("s_ub", 4)
        bwm = sbt("s_bwm", [P, 2, 4, 128], BF16)
        bwm_d = kb.tile("s_bwm")
        bwf = sbt("s_bwf", [P, 2, 128], F32)
        bwf_d = kb.tile("s_bwf")
        cw = sbt("s_cw", [P, 2, 4, 128], BF16)
        cw_d = kb.tile("s_cw")
        ctd, ctd_d = bwf, bwf_d
        tabs = sbt("s_tabs", [P, 2, 2, 512], F32)
        tabs_d = kb.tiles_n("s_tabs", 2, 2)
        rbt = sbt("s_rbt", [P, 512], F32)
        rbt_d = kb.tile("s_rbt")
        un = sbt("s_un", [P, 2, 512], F32)
        un_d = kb.tiles_n("s_un", 2)
        wk = sbt("s_wk", [P, 4, 512], F32)
        wk_d = kb.tiles_n("s_wk", 4)
        xrb = sbt("s_xr", [P, 2, 512], BF16)
        xr_d = kb.tiles_n("s_xr", 2)
        iota = c2[:, C2_IOTA:C2_IOTA + 512]
        bmask = c2[:, C2_BMASK:C2_BMASK + 128]
        hpi = self.cst[:, C_HALFPI:C_HALFPI + 1]
        W = lambda i: wk[:, i, :]
        ysum, ysum_d = wk[:, 0, :], wk_d[0]
        YB = [4, 5, 6, 7]

        def tables(j):
            par = j % 2
            tS, tC, tK = tabs[:, par, 0, :], tabs[:, par, 1, :], wk[:, 3, :]
            dS, dC = tabs_d[par]
            dK = wk_d[3]
            th = svt[:, j:j + 1]
            kb.op("dve", lambda e: e.tensor_scalar_mul(out=tC, in0=iota, scalar1=th), [c2_d, sv_d], [dC])
            kb.op("dve", lambda e: e.tensor_scalar(out=tK, in0=tC, scalar1=1.0 / (2 * PI), scalar2=MAGIC, op0=ALU.mult, op1=ALU.add), [dC], [dK])
            kb.op("dve", lambda e: e.tensor_scalar_add(out=tK, in0=tK, scalar1=-MAGIC), [dK], [dK])
            kb.op("dve", lambda e: e.scalar_tensor_tensor(out=tS, in0=tK, scalar=-CW1, in1=tC, op0=ALU.mult, op1=ALU.add), [dK, dC], [dS])
            kb.op("dve", lambda e: e.scalar_tensor_tensor(out=tS, in0=tK, scalar=-CW2, in1=tS, op0=ALU.mult, op1=ALU.add), [dK, dS], [dS])
            kb.op("dve", lambda e: e.scalar_tensor_tensor(out=tC, in0=tS, scalar=-1.0, in1=tS, op0=ALU.mult, op1=ALU.max), [dS], [dC])
            kb.op("act", lambda e: e.activation(out=tS, in_=tS, func=AF.Sin, scale=0.999999), [dS], [dS])
            kb.op("act", lambda e: e.activation(out=tC, in_=tC, func=AF.Sin, scale=-0.999999, bias=hpi), [dC, self.cst_d], [dC])

        def rho_table(j):
            rho = svr[:, j:j + 1]
            kb.op("act", lambda e: e.activation(out=rbt[:], in_=iota, func=AF.Identity, bias=rho, scale=0.0), [c2_d, sv_d], [rbt_d])

        ucount = [0]

        def stage_a(jj, tt, up):
            ts = slice(tt * 512, (tt + 1) * 512)
            for ri in range(2):
                bk = 2 * up + ri
                kb.op("pe", lambda e, ri=ri, bk=bk: e.matmul(self.pf(bk), lhsT=bwm[:, ri, jj, :], rhs=ub[:, ts], start=True, stop=True),
                      [bwm_d, ub_d[tt]], [self.pf_d[bk]])

        def stage_bcd(j, jj, tt, up):
            par = j % 2
            tS, tC, tK = tabs[:, par, 0, :], tabs[:, par, 1, :], rbt[:]
            dS, dC = tabs_d[par]
            dK = rbt_d
            brp, bip = self.pf(2 * up), self.pf(2 * up + 1)
            dbr, dbi = self.pf_d[2 * up], self.pf_d[2 * up + 1]
            YR, YI = un[:, 0, :], un[:, 1, :]

            def tt_(eng, o, od, a, ad, b, bd, op):
                kb.op(eng, lambda e: e.tensor_tensor(out=o, in0=a, in1=b, op=op), [ad, bd], [od])
            tt_("dve", W(0), wk_d[0], brp, dbr, tC, dC, ALU.mult)
            tt_("dve", W(1), wk_d[1], bip, dbi, tS, dS, ALU.mult)
            tt_("pool", W(0), wk_d[0], W(0), wk_d[0], W(1), wk_d[1], ALU.add)
            tt_("dve", W(2), wk_d[2], bip, dbi, tC, dC, ALU.mult)
            tt_("dve", W(3), wk_d[3], brp, dbr, tS, dS, ALU.mult)
            tt_("dve", W(2), wk_d[2], W(2), wk_d[2], W(3), wk_d[3], ALU.subtract)
            for ri, src in ((0, 0), (1, 2)):
                kb.op("dve", lambda e, ri=ri, src=src: e.tensor_tensor_scan(
                    out=un[:, ri, :], data0=tK, data1=W(src), initial=carry[:, ri, j:j + 1], op0=ALU.mult, op1=ALU.add),
                    [dK, wk_d[src], carry_d[j]], [un_d[ri]])
            if tt < 3:
                yrl, yil = un[:, 0, 511:512], un[:, 1, 511:512]
                cc, ss, ns = svc[:, j:j + 1], svs[:, j:j + 1], svn[:, j:j + 1]
                kb.op("dve", lambda e: e.tensor_scalar_mul(out=ctmp[:, 0:1], in0=yrl, scalar1=cc), [un_d[0], sv_d], [ctmp_d])
                kb.op("dve", lambda e: e.tensor_scalar_mul(out=ctmp[:, 1:2], in0=yrl, scalar1=ss), [un_d[0], sv_d, ctmp_d], [ctmp_d])
                kb.op("dve", lambda e: e.scalar_tensor_tensor(out=carry[:, 0, j:j + 1], in0=yil, scalar=ns, in1=ctmp[:, 0:1], op0=ALU.mult, op1=ALU.add),
                      [un_d[1], sv_d, ctmp_d], [carry_d[j]])
                kb.op("dve", lambda e: e.scalar_tensor_tensor(out=carry[:, 1, j:j + 1], in0=yil, scalar=cc, in1=ctmp[:, 1:2], op0=ALU.mult, op1=ALU.add),
                      [un_d[1], sv_d, ctmp_d, carry_d[j]], [carry_d[j]])
            tt_("dve", W(0), wk_d[0], tC, dC, YR, un_d[0], ALU.mult)
            tt_("dve", W(1), wk_d[1], tS, dS, YI, un_d[1], ALU.mult)
            kb.op("dve", lambda e: e.tensor_tensor(out=xrb[:, 0, :], in0=W(0), in1=W(1), op=ALU.subtract), [wk_d[0], wk_d[1]], [xr_d[0]])
            tt_("dve", W(3), wk_d[3], tS, dS, YR, un_d[0], ALU.mult)
            tt_("dve", W(2), wk_d[2], tC, dC, YI, un_d[1], ALU.mult)
            kb.op("dve", lambda e: e.tensor_tensor(out=xrb[:, 1, :], in0=W(3), in1=W(2), op=ALU.add), [wk_d[3], wk_d[2]], [xr_d[1]])
            yb = YB[tt]
            for ri in range(2):
                kb.op("pe", lambda e, ri=ri: e.matmul(self.pfx(yb), lhsT=cw[:, ri, jj, :], rhs=xrb[:, ri, :],
                                                     start=(jj == 0 and ri == 0), stop=(jj == 3 and ri == 1)),
                      [cw_d, xr_d[ri]], [self.pfx_d(yb)])

        for q in range(8):
            s = self.next_slot()
            wv = self.slot3(s, KD, 128)
            self.load_w(s, [(wv, win[:, q * 128:(q + 1) * 128].rearrange("(k p) n -> p k n", p=P))])
            for tt in range(4):
                bank = tt % 2

                def mm(e, tt=tt, bank=bank, wv=wv):
                    ins = None
                    for k in range(KD):
                        ins = e.matmul(self.pf(bank), lhsT=wv[:, k, :], rhs=xn[:, k, tt * 512:(tt + 1) * 512], start=(k == 0), stop=(k == KD - 1))
                    return ins
                kb.op("pe", mm, [self.slot_d[s]] + xn_d[tt], [self.pf_d[bank]])
                kb.op("dve", lambda e, tt=tt, bank=bank: e.tensor_copy(out=u32[:, tt * 512:(tt + 1) * 512], in_=self.pf(bank)),
                      [self.pf_d[bank]], [u32_d[tt]])
                kb.op("dve", lambda e, tt=tt, bank=bank: e.tensor_copy(out=ub[:, tt * 512:(tt + 1) * 512], in_=self.pf(bank)),
                      [self.pf_d[bank]], [ub_d[tt]])
            for ri in range(2):
                kb.op("dve", lambda e, ri=ri, q=q: e.tensor_tensor(
                    out=bwf[:, ri, :].rearrange("p (a n) -> p a n", a=2), in0=bbt[:, ri, q:q + 1, :].broadcast_to([P, 2, 64]),
                    in1=bmask.rearrange("p (a n) -> p a n", a=2), op=ALU.mult), [bbt_d, c2_d], [bwf_d])
                for jj in range(4):
                    kb.op("dve", lambda e, ri=ri, jj=jj: e.tensor_scalar_mul(
                        out=bwm[:, ri, jj, :], in0=bwf[:, ri, :], scalar1=c2[:, C2_RMASK + jj:C2_RMASK + jj + 1]), [bwf_d, c2_d], [bwm_d])
                kb.op("dve", lambda e, ri=ri, q=q: e.tensor_copy(
                    out=ctd[:, ri, :].rearrange("p (a n) -> p a n", a=2), in_=ct[:, ri, q:q + 1, :].broadcast_to([P, 2, 64])), [ct_d], [ctd_d])
                bank = ri
                kb.op("pe", lambda e, ri=ri, bank=bank: e.transpose(self.pf(bank, 128), ctd[:, ri, :], self.ident_f()),
                      [ctd_d, self.cst_d], [self.pf_d[bank]])
                for jj in range(4):
                    cm = c2[:, C2_CMASK + jj * 128: C2_CMASK + (jj + 1) * 128]
                    if ri == 0:
                        kb.op("dve", lambda e, jj=jj, bank=bank, cm=cm: e.tensor_tensor(out=cw[:, 0, jj, :], in0=self.pf(bank, 128), in1=cm, op=ALU.mult),
                              [self.pf_d[bank], c2_d], [cw_d])
                    else:
                        kb.op("dve", lambda e, jj=jj, bank=bank, cm=cm: e.scalar_tensor_tensor(
                            out=cw[:, 1, jj, :], in0=self.pf(bank, 128), scalar=-1.0, in1=cm, op0=ALU.mult, op1=ALU.mult),
                            [self.pf_d[bank], c2_d], [cw_d])
            units = [(jj, tt) for jj in range(4) for tt in range(4)]
            tables(4 * q)
            stage_a(units[0][0], units[0][1], ucount[0] % 2)
            for ui, (jj, tt) in enumerate(units):
                up = ucount[0] % 2
                ucount[0] += 1
                if ui + 1 < len(units):
                    stage_a(units[ui + 1][0], units[ui + 1][1], ucount[0] % 2)
                if tt == 0:
                    rho_table(4 * q + jj)
                stage_bcd(4 * q + jj, jj, tt, up)
                if tt == 1 and jj < 3:
                    tables(4 * q + jj + 1)
            for tt in range(4):
                yb = YB[tt]
                ts = slice(tt * 512, (tt + 1) * 512)
                kb.op("dve", lambda e, tt=tt, q=q, yb=yb, ts=ts: e.scalar_tensor_tensor(
                    out=ysum, in0=u32[:, ts], scalar=self.col(OFF_SSMD + q), in1=self.pfx(yb), op0=ALU.mult, op1=ALU.add),
                    [u32_d[tt], self.cols_d, self.pfx_d(yb)], [ysum_d])
                kb.op("act", lambda e, q=q, ts=ts: e.activation(out=yg[:, q, ts], in_=ysum, func=AF.Gelu_apprx_tanh), [ysum_d], [yg_d[q][tt]])
        kb.fence()
        self.ptr = mark2
        if os.environ.get("S5_STOP") in ("2", "3", "4", "5"):
            self.ptr = mark
            return
        sgm = sbt("s_sgm", [P, 2, 512], F32)
        sgm_d = kb.tiles_n("s_sgm", 2)
        gcnt = 0
        for dc in range(KD):
            s = self.next_slot()
            wv = self.slot3(s, KD, 256)
            self.load_w(s, [(wv[:, :, 0:128], wglu[:, dc * 128:(dc + 1) * 128].rearrange("(k p) n -> p k n", p=P)),
                            (wv[:, :, 128:256], wglu[:, D + dc * 128: D + (dc + 1) * 128].rearrange("(k p) n -> p k n", p=P))])
            for tt in range(4):
                b0 = (gcnt % 3) * 2
                gi = gcnt % 2
                gcnt += 1
                for ag in range(2):
                    def mm(e, ag=ag, tt=tt, b0=b0, wv=wv):
                        ins = None
                        for k in range(KD):
                            ins = e.matmul(self.pf(b0 + ag), lhsT=wv[:, k, ag * 128:(ag + 1) * 128], rhs=yg[:, k, tt * 512:(tt + 1) * 512],
                                           start=(k == 0), stop=(k == KD - 1))
                        return ins
                    kb.op("pe", mm, [self.slot_d[s]] + [yg_d[k][tt] for k in range(KD)], [self.pf_d[b0 + ag]])
                kb.op("act", lambda e, gi=gi, b0=b0: e.activation(out=sgm[:, gi, :], in_=self.pf(b0 + 1), func=AF.Sigmoid),
                      [self.pf_d[b0 + 1]], [sgm_d[gi]])
                kb.op("dve", lambda e, gi=gi, b0=b0: e.tensor_tensor(out=sgm[:, gi, :], in0=self.pf(b0), in1=sgm[:, gi, :], op=ALU.mult),
                      [self.pf_d[b0], sgm_d[gi]], [sgm_d[gi]])
                kb.op("dve", lambda e, dc=dc, tt=tt, gi=gi: e.tensor_tensor(
                    out=self.X[:, dc, tt * 512:(tt + 1) * 512], in0=self.X[:, dc, tt * 512:(tt + 1) * 512], in1=sgm[:, gi, :], op=ALU.add),
                    [sgm_d[gi], self.Xd[dc][tt]], [self.Xd[dc][tt]])
        kb.fence()
        self.ptr = mark


_CACHE = {}


def _pack_vecs(norm_g, final_norm_g, ffn_conv_w, ffn_conv_b, ssm_d):
    v = np.zeros((NVROWS, 128), np.float32)
    v[OFF_NG:OFF_NG + 64] = np.asarray(norm_g, np.float32).reshape(64, 128)
    v[OFF_FNG:OFF_FNG + 8] = np.asarray(final_norm_g, np.float32).reshape(8, 128)
    v[OFF_CW:OFF_CW + 528] = np.asarray(ffn_conv_w, np.float32).reshape(528, 128)
    v[OFF_CB:OFF_CB + 176] = np.asarray(ffn_conv_b, np.float32).reshape(176, 128)
    v[OFF_SSMD:OFF_SSMD + 8] = np.asarray(ssm_d, np.float32).reshape(8, 128)
    return v


def run_layers(x, inputs, layers, do_final, cores=NCORES):
    key = (tuple(layers), do_final)
    if key not in _CACHE:
        _CACHE[key] = Prog(layers, do_final).build()
    nc = _CACHE[key]
    consts, consts2 = _host_consts()
    vecs = _pack_vecs(inputs["norm_g"], inputs["final_norm_g"], inputs["ffn_conv_w"], inputs["ffn_conv_b"], inputs["ssm_d"])
    shared = {"consts": consts, "consts2": consts2, "vecs": vecs}
    for n in ("sb_w_qkv", "sb_w_o", "sg_w_in", "sg_norm_g", "sg_w_s", "sg_b", "sg_w_o", "ssm_w_in", "ssm_lam_re",
              "ssm_lam_im", "ssm_log_dt", "ssm_b_re", "ssm_b_im", "ssm_c_re", "ssm_c_im", "ssm_w_glu", "ffn_w_up",
              "ffn_w_down"):
        shared[n] = np.ascontiguousarray(np.asarray(inputs[n], np.float32))
    in_maps = []
    for c in range(cores):
        m = dict(shared)
        m["x"] = np.ascontiguousarray(np.asarray(x[c], np.float32))
        in_maps.append(m)
    res = run_bass_kernel_spmd(nc, in_maps, core_ids=list(range(cores)))
    return np.stack([np.asarray(r["y"]) for r in res.results], axis=0)


def kernel(**inputs):
    x = np.asarray(inputs["x"], np.float32)
    out = run_layers(x, inputs, [0, 1, 2, 3], True)
    return out.astype(np.float32)
```

```python
import math
import os
from contextlib import ExitStack

import numpy as np
import concourse.bass as bass
import concourse.mybir as mybir
from concourse.bass_utils import run_bass_kernel_spmd

F32 = mybir.dt.float32
BF16 = mybir.dt.bfloat16
AF = mybir.ActivationFunctionType
ALU = mybir.AluOpType

P = 128
L = 2048
D = 1024
KD = 8
DFF = 2816
NJ = 22
EPS = 1e-6
NCORES = 8

OFF_NG = 0
OFF_FNG = 64
OFF_CW = 72
OFF_CB = 600
OFF_SSMD = 776
NVROWS = 896

C_IDENT = 0
C_MASKL = 128
C_TRILT = 256
C_ONES = 384
C_EPS = 512
C_NEGPI = 513
C_HALFPI = 514
MAGIC = 12582912.0
CW1 = 6.28125
CW2 = 2.0 * math.pi - CW1
PI_LO = 3.1415925
HALFPI_LO = 1.5707962
NCONST1 = 640
C2_BMASK = 0
C2_CMASK = 128
C2_RMASK = 640
C2_IOTA = 704
C2_REP = 1216
NCONST2 = 2240


def _host_consts():
    c = np.zeros((P, NCONST1), np.float32)
    r = np.arange(P)
    c[:, C_IDENT:C_IDENT + 128] = np.eye(P, dtype=np.float32)
    c[:, C_MASKL:C_MASKL + 128] = (r[None, :] < r[:, None]).astype(np.float32)
    c[:, C_TRILT:C_TRILT + 128] = (r[:, None] <= r[None, :]).astype(np.float32)
    c[:, C_ONES:C_ONES + 128] = 1.0
    c[:, C_EPS] = EPS
    c[:, C_NEGPI] = -math.pi
    c[:, C_HALFPI] = HALFPI_LO
    c2 = np.zeros((P, NCONST2), np.float32)
    for g in range(64):
        q, gl = divmod(g, 8)
        c2[g, C2_REP + q * 128 + gl * 16: C2_REP + q * 128 + gl * 16 + 16] = 1.0
    for row in range(P):
        gl8 = row // 16
        c2[row, C2_BMASK + (gl8 % 2) * 64: C2_BMASK + (gl8 % 2) * 64 + 64] = 1.0
        c2[row, C2_RMASK + gl8 // 2] = 1.0
    for m in range(4):
        for row in range(P):
            gl = row // 64
            g8 = 2 * m + gl
            c2[row, C2_CMASK + m * 128 + g8 * 16: C2_CMASK + m * 128 + g8 * 16 + 16] = 1.0
    c2[:, C2_IOTA:C2_IOTA + 512] = np.arange(512, dtype=np.float32)[None, :]
    return c, c2


class Dep:
    __slots__ = ("w", "r", "name", "excl")

    def __init__(self, name, w=None, excl=False):
        self.name = name
        self.w = w
        self.r = []
        self.excl = excl


class Op:
    __slots__ = ("eng", "fn", "idx", "deps", "waits", "signal", "sigval", "dkey", "dval", "gidx")


ENGS = ("pe", "act", "dve", "pool", "sp")


class KB:
    def __init__(self, nc):
        self.nc = nc
        self.ops = {e: [] for e in ENGS}
        self.all_ops = []
        self.tiles = []
        self.fence_op = None
        self.dma_count = {}
        self.dma_group = set()

    def tile(self, name, excl=False):
        t = Dep(name, self.fence_op, excl)
        self.tiles.append(t)
        return t

    def tiles_n(self, name, *dims):
        if len(dims) == 1:
            return [self.tile(f"{name}{i}") for i in range(dims[0])]
        return [self.tiles_n(f"{name}{i}_", *dims[1:]) for i in range(dims[0])]

    def op(self, eng, fn, reads=(), writes=(), dkey=None, nd=1):
        o = Op()
        o.eng = eng
        o.fn = fn
        o.idx = len(self.ops[eng])
        o.gidx = len(self.all_ops)
        o.deps = set()
        o.waits = []
        o.signal = False
        o.sigval = 0
        o.dkey = dkey
        o.dval = 0
        if dkey is not None:
            self.dma_count[dkey] = self.dma_count.get(dkey, 0) + 16 * nd
            o.dval = self.dma_count[dkey]
        for t in reads:
            if t.w is not None:
                o.deps.add(t.w)
            if t.excl:
                for r in t.r:
                    if r.eng != eng:
                        o.deps.add(r)
        for t in writes:
            if t.w is not None:
                o.deps.add(t.w)
            for r in t.r:
                o.deps.add(r)
        for t in reads:
            t.r.append(o)
        for t in writes:
            t.w = o
            t.r = []
        o.deps.discard(o)
        self.ops[eng].append(o)
        self.all_ops.append(o)
        return o

    def dma(self, queue, out, in_, reads, writes, key, group=False):
        if group:
            self.dma_group.add(key)
        return self.op(queue, lambda e: [e.dma_start(out=out, in_=in_)], reads, writes, dkey=key)

    def dma_multi(self, queue, pieces, reads, writes, key):
        return self.op(queue, lambda e: [e.dma_start(out=o, in_=i) for (o, i) in pieces], reads, writes, dkey=key,
                       nd=len(pieces))

    def fence(self):
        deps_r, deps_w = [], []
        o = self.op("sp", lambda e: e.nop(), reads=(), writes=())
        for t in self.tiles:
            if t.w is not None:
                o.deps.add(t.w)
            for r in t.r:
                o.deps.add(r)
        o.deps.discard(o)
        self.fence_op = o
        return o

    def finalize(self, es):
        nc = self.nc
        seen = {e: {} for e in ENGS}
        for o in self.all_ops:
            sn = seen[o.eng]
            need = {}
            for d in o.deps:
                if d.dkey is not None:
                    key = ("d", d.dkey)
                    val = self.dma_count[d.dkey] if d.dkey in self.dma_group else d.dval
                    if sn.get(key, 0) >= val:
                        continue
                    if need.get(key, (0, None))[0] < val:
                        need[key] = (val, d)
                else:
                    if d.eng == o.eng:
                        if o.eng == "pe" or (o.idx - d.idx) > 3:
                            continue
                    key = ("e", d.eng)
                    if sn.get(key, -1) >= d.idx:
                        continue
                    if need.get(key, (-1, None))[0] < d.idx:
                        need[key] = (d.idx, d)
            for key, (val, d) in need.items():
                sn[key] = val
                if key[0] == "e":
                    d.signal = True
                o.waits.append((key, d))
        for e in ENGS:
            cnt = 0
            for o in self.ops[e]:
                if o.signal:
                    cnt += 1
                    o.sigval = cnt
        esem = {e: es.enter_context(nc.semaphore(f"sem_{e}")) for e in ENGS}
        dsem = {k: es.enter_context(nc.semaphore(f"dsem_{k}")) for k in self.dma_count}
        block = es.enter_context(nc.Block())
        kb = self

        def run(ename, eng):
            for o in kb.ops[ename]:
                for key, d in o.waits:
                    if key[0] == "d":
                        val = kb.dma_count[d.dkey] if d.dkey in kb.dma_group else d.dval
                        eng.wait_ge(dsem[d.dkey], val)
                    else:
                        eng.wait_ge(esem[d.eng], d.sigval)
                ins = o.fn(eng)
                if o.dkey is not None:
                    for di in ins:
                        di.then_inc(dsem[o.dkey], 16)
                elif o.signal:
                    ins.then_inc(esem[ename], 1)

        @block.tensor
        def _(e):
            run("pe", e)

        @block.scalar
        def _(e):
            run("act", e)

        @block.vector
        def _(e):
            run("dve", e)

        @block.gpsimd
        def _(e):
            run("pool", e)

        @block.sync
        def _(e):
            run("sp", e)


class Prog:
    def __init__(self, layers, do_final, x_in_tokmajor=True):
        self.layers = layers
        self.do_final = do_final

    def build(self):
        nc = bass.Bass("TRN2", target_bir_lowering=False)
        self.nc = nc
        dt = nc.dram_tensor
        self.d = {}
        shapes = {
            "x": [L, D], "consts": [P, NCONST1], "consts2": [P, NCONST2], "vecs": [NVROWS, 128],
            "sb_w_qkv": [2, D, 3 * D], "sb_w_o": [2, D, D],
            "sg_w_in": [1, D, 2 * D], "sg_norm_g": [1, D], "sg_w_s": [1, 8, 128, 128],
            "sg_b": [1, 8, 128], "sg_w_o": [1, D, D],
            "ssm_w_in": [1, D, D], "ssm_lam_re": [1, 64, 64], "ssm_lam_im": [1, 64, 64],
            "ssm_log_dt": [1, 64], "ssm_b_re": [1, 64, 64, 16], "ssm_b_im": [1, 64, 64, 16],
            "ssm_c_re": [1, 64, 16, 64], "ssm_c_im": [1, 64, 16, 64], "ssm_w_glu": [1, D, 2 * D],
            "ffn_w_up": [4, D, 2 * DFF], "ffn_w_down": [4, DFF, D],
        }
        for n, s in shapes.items():
            self.d[n] = dt(n, s, F32, kind="ExternalInput").ap()
        self.y = dt("y", [L, D], F32, kind="ExternalOutput").ap()
        self.scr = dt("s5_scratch", [2, 64, 1024], F32, kind="Internal").ap()

        with ExitStack() as es:
            self.es = es
            kb = KB(nc)
            self.kb = kb
            self.ptr = (nc._sbuf_addr_for_side("left") + 63) // 64 * 64
            self.sb_end = nc._sbuf_addr_for_side("right")
            self.uid = 0
            sb = self.alloc
            self.X = sb("X", [P, KD, L], F32)
            self.Xd = kb.tiles_n("X", KD, 4)
            self.cst = sb("cst", [P, NCONST1], F32)
            self.cst_d = kb.tile("cst")
            self.cstb = sb("cstb", [P, 512], BF16)
            self.cstb_d = kb.tile("cstb")
            self.cols = sb("cols", [P, NVROWS], F32)
            self.cols_d = kb.tile("cols")
            self.NSLOT = 3
            self.slots = [sb(f"slot{i}", [P, 3072], BF16) for i in range(self.NSLOT)]
            self.slot_d = [kb.tile(f"slot{i}") for i in range(self.NSLOT)]
            self.slot_i = 0
            self.PF = es.enter_context(nc.psum_tensor("pf", [P, 6 * 512], F32))
            self.PB = es.enter_context(nc.psum_tensor("pb", [P, 2 * 1024], BF16))
            self.pf_d = [kb.tile(f"pf{i}", excl=True) for i in range(6)]
            self.pb_d = [kb.tile(f"pb{i}", excl=True) for i in range(2)]

            self.setup()
            self.load_x()
            for l in self.layers:
                m = l % 3
                if m == 0:
                    self.attention(l, l // 3)
                elif m == 1:
                    self.gmlp(l)
                else:
                    self.s5(l)
                if os.environ.get("NOFFN") != "1":
                    self.ffn(l)
            self.store(self.do_final)
            kb.finalize(es)
        return nc

    def alloc(self, name, shape, dtype):
        nbytes = int(np.prod(shape[1:])) * (4 if dtype == F32 else 2)
        off = (self.ptr + 63) // 64 * 64
        assert off + nbytes <= self.sb_end, f"SBUF overflow allocating {name}: {off + nbytes} > {self.sb_end}"
        self.ptr = off + nbytes
        self.uid += 1
        return self.nc.alloc_sbuf_tensor_at(f"{name}_{self.uid}", list(shape), dtype, offset=off)

    def pf(self, b, n=512, off=0):
        return self.PF[:, b * 512 + off: b * 512 + off + n]

    def pfx(self, b):
        if b < 6:
            return self.pf(b)
        return self.PB[:, (b - 6) * 1024:(b - 5) * 1024].bitcast(F32)

    def pfx_d(self, b):
        return self.pf_d[b] if b < 6 else self.pb_d[b - 6]

    def ident_f(self):
        return self.cst[:, C_IDENT:C_IDENT + 128]

    def col(self, r):
        return self.cols[:, r:r + 1]

    def phase_scope(self):
        self.kb.fence()
        ps = ExitStack()
        return ps

    def setup(self):
        kb, nc = self.kb, self.nc
        kb.dma("sp", self.cst[:], self.d["consts"], [], [self.cst_d], "cst")
        kb.dma("pool", self.cstb[:], self.d["consts"][:, 0:512], [], [self.cstb_d], "cstb")
        mark = self.ptr
        vst = self.alloc("vstage", [P, 7, 128], F32)
        if True:
            vd = kb.tile("vstage")
            kb.dma("sp", vst[:], self.d["vecs"].rearrange("(a p) c -> p a c", p=P), [], [vd], "vst")
            for a in range(7):
                b = a // 4
                o = (a % 4) * 128
                kb.op("pe", lambda e, a=a, b=b, o=o: e.transpose(self.pf(b, 128, o), vst[:, a, :], self.ident_f()),
                      [vd, self.cst_d], [self.pf_d[b]])
            kb.op("dve", lambda e: e.tensor_copy(out=self.cols[:, 0:512], in_=self.pf(0)), [self.pf_d[0]], [self.cols_d])
            kb.op("dve", lambda e: e.tensor_copy(out=self.cols[:, 512:896], in_=self.pf(1, 384)), [self.pf_d[1]], [self.cols_d])
            kb.fence()
        self.ptr = mark

    def load_x(self):
        kb, nc = self.kb, self.nc
        x = self.d["x"]
        mark = self.ptr
        xst = self.alloc("xst", [P, 2, D], F32)
        if True:
            xd = [kb.tile("xst0"), kb.tile("xst1")]
            for i in range(16):
                s = i % 2
                kb.dma("sp", xst[:, s, :], x[i * 128:(i + 1) * 128, :], [], [xd[s]], f"xst{s}")
                for half in range(2):
                    b = 2 * s + half
                    for kk in range(4):
                        k = half * 4 + kk
                        kb.op("pe", lambda e, s=s, k=k, b=b, kk=kk: e.transpose(
                            self.pf(b, 128, kk * 128), xst[:, s, k * 128:(k + 1) * 128], self.ident_f()),
                            [xd[s], self.cst_d], [self.pf_d[b]])
                    eng = "dve" if half == 0 else "act"
                    outap = self.X[:, half * 4:half * 4 + 4, i * 128:(i + 1) * 128]
                    inap = self.pf(b).rearrange("p (k t) -> p k t", k=4)
                    if eng == "dve":
                        fn = lambda e, outap=outap, inap=inap: e.tensor_copy(out=outap, in_=inap)
                    else:
                        fn = lambda e, outap=outap, inap=inap: e.activation(out=outap, in_=inap, func=AF.Copy)
                    kb.op(eng, fn, [self.pf_d[b]], [self.Xd[k][i // 4] for k in range(half * 4, half * 4 + 4)])
            kb.fence()
        self.ptr = mark

    def store(self, do_final):
        kb, nc = self.kb, self.nc
        mark = self.ptr
        ps = None
        if True:
            sbt = self.alloc
            xo = sbt("xo", [P, KD, 512], F32)
            xo_d = kb.tiles_n("xo", KD)
            yst = sbt("yst", [P, 2, D], F32)
            yd = [kb.tile("yst0"), kb.tile("yst1")]
            nt = self.norm_temps(ps) if do_final else None
            cnt = 0
            for tt in range(4):
                if do_final:
                    self.rmsnorm(OFF_FNG, tt, xo, 0, xo_d, nt)
                    src = lambda k, c: xo[:, k, c * 128:(c + 1) * 128]
                    srcd = lambda k: xo_d[k]
                else:
                    src = lambda k, c, tt=tt: self.X[:, k, tt * 512 + c * 128: tt * 512 + (c + 1) * 128]
                    srcd = lambda k, tt=tt: self.Xd[k][tt]
                for c in range(4):
                    i = tt * 4 + c
                    s = cnt % 2
                    cnt += 1
                    for half in range(2):
                        b = 2 * s + half
                        for kk in range(4):
                            k = half * 4 + kk
                            kb.op("pe", lambda e, k=k, c=c, b=b, kk=kk, src=src: e.transpose(
                                self.pf(b, 128, kk * 128), src(k, c), self.ident_f()),
                                [srcd(k), self.cst_d], [self.pf_d[b]])
                        outap = yst[:, s, half * 512:(half + 1) * 512]
                        if half == 0:
                            kb.op("dve", lambda e, outap=outap, b=b: e.tensor_copy(out=outap, in_=self.pf(b)),
                                  [self.pf_d[b]], [yd[s]])
                        else:
                            kb.op("act", lambda e, outap=outap, b=b: e.activation(out=outap, in_=self.pf(b), func=AF.Copy),
                                  [self.pf_d[b]], [yd[s]])
                    kb.dma("sp", self.y[i * 128:(i + 1) * 128, :], yst[:, s, :], [yd[s]], [], f"yst{s}")
            fin = kb.tile("fin")
            kb.op("sp", lambda e: e.nop(), [], [yd[0], yd[1], fin])
            kb.fence()
        self.ptr = mark

    def norm_temps(self, ps):
        nc, kb = self.nc, self.kb
        sq = self.alloc("n_sq", [P, KD, 512], BF16)
        r1 = self.alloc("n_r1", [P, 512], F32)
        rs = self.alloc("n_rs", [P, 512], F32)
        return dict(sq=sq, r1=r1, rs=rs, sq_d=kb.tile("n_sq"), r1_d=kb.tile("n_r1"), rs_d=kb.tile("n_rs"))

    def rmsnorm(self, goff, tt, out, ocol0, out_d, nt, bank=5, out_scale_eng="dve"):
        kb = self.kb
        X = self.X
        ts = slice(tt * 512, (tt + 1) * 512)
        xr = [self.Xd[k][tt] for k in range(KD)]
        kb.op("act", lambda e: e.activation(out=nt["sq"][:], in_=X[:, :, ts], func=AF.Square), xr, [nt["sq_d"]])
        ones_b = self.cstb[:, 384:512]

        def mm(e):
            ins = None
            for k in range(KD):
                ins = e.matmul(self.pf(bank), lhsT=ones_b, rhs=nt["sq"][:, k, :], start=(k == 0), stop=(k == KD - 1))
            return ins
        kb.op("pe", mm, [nt["sq_d"], self.cstb_d], [self.pf_d[bank]])
        kb.op("act", lambda e: e.activation(out=nt["r1"][:], in_=self.pf(bank), func=AF.Sqrt, bias=self.cst[:, C_EPS:C_EPS + 1],
                                            scale=1.0 / D), [self.pf_d[bank], self.cst_d], [nt["r1_d"]])
        kb.op("dve", lambda e: e.reciprocal(out=nt["rs"][:], in_=nt["r1"][:]), [nt["r1_d"]], [nt["rs_d"]])
        for k in range(KD):
            kb.op("dve", lambda e, k=k: e.scalar_tensor_tensor(
                out=out[:, k, ocol0:ocol0 + 512], in0=X[:, k, ts], scalar=self.col(goff + k), in1=nt["rs"][:],
                op0=ALU.mult, op1=ALU.mult), [self.Xd[k][tt], nt["rs_d"], self.cols_d], [out_d[k]])

    def next_slot(self):
        s = self.slot_i % self.NSLOT
        self.slot_i += 1
        return s

    def load_w(self, s, pieces):
        self.kb.dma_multi("pool", pieces, [], [self.slot_d[s]], f"slot{s}")

    def slot3(self, s, k, n):
        return self.slots[s][:, 0:k * n].rearrange("p (k n) -> p k n", k=k)

    def ffn(self, l):
        kb, nc = self.kb, self.nc
        wup = self.d["ffn_w_up"][l]
        wdn = self.d["ffn_w_down"][l]
        kb.fence()
        mark = self.ptr
        ps = None
        if True:
            sbt = self.alloc
            xn = sbt("f_xn", [P, KD, 1024], BF16)
            xn_d = [kb.tiles_n("f_xn", KD) for _ in range(2)]
            act = sbt("f_act", [P, NJ, 1024], BF16)
            act_d = kb.tiles_n("f_act", NJ, 2)
            hs = sbt("f_hs", [P, 2, 2, 2 + 1024], F32)
            hs_d = kb.tiles_n("f_hs", 2, 2, 2)
            hsh_d = kb.tiles_n("f_hsh", 2, 2)
            y0 = sbt("f_y0", [P, 3, 2, 512], F32)
            y0_d = kb.tiles_n("f_y0", 3, 2)
            sg = sbt("f_sg", [P, 3, 512], F32)
            sg_d = kb.tiles_n("f_sg", 3)
            halo = sbt("f_halo", [P, NJ, 2, 2], F32)
            halo_d = kb.tiles_n("f_halo", NJ)
            nt = self.norm_temps(ps)
            ucount = 0
            for half in range(2):
                for t2 in range(2):
                    self.rmsnorm(OFF_NG + (l * 2 + 1) * 8, half * 2 + t2, xn, t2 * 512, xn_d[t2], nt)
                pend = []

                def issue_up(j):
                    s = self.next_slot()
                    v = self.slot3(s, KD, 256)
                    self.load_w(s, [
                        (v[:, :, 0:128], wup[:, j * 128:(j + 1) * 128].rearrange("(k p) n -> p k n", p=P)),
                        (v[:, :, 128:256], wup[:, DFF + j * 128: DFF + (j + 1) * 128].rearrange("(k p) n -> p k n", p=P)),
                    ])
                    return s

                def issue_dn(dc):
                    s = self.next_slot()
                    v = self.slot3(s, NJ, 128)
                    self.load_w(s, [(v, wdn[:, dc * 128:(dc + 1) * 128].rearrange("(k p) n -> p k n", p=P))])
                    return s
                seq = [("u", j) for j in range(NJ)] + [("d", dc) for dc in range(KD)]
                PRE = self.NSLOT - 1
                slots_of = {}
                for q in range(min(PRE, len(seq))):
                    slots_of[q] = issue_up(seq[q][1]) if seq[q][0] == "u" else issue_dn(seq[q][1])
                pend_ffn = []
                for qi, (kind, j) in enumerate(seq):
                    s = slots_of[qi]
                    if kind == "u":
                        wv = self.slot3(s, KD, 256)
                        hb = j % 2
                        for ag in range(2):
                            if half == 0:
                                kb.op("pool", lambda e, hb=hb, ag=ag: e.memset(hs[:, hb, ag, 0:2], 0.0), [], [hsh_d[hb][ag]])
                            else:
                                kb.op("pool", lambda e, hb=hb, ag=ag, j=j: e.tensor_copy(out=hs[:, hb, ag, 0:2], in_=halo[:, j, ag, :]),
                                      [halo_d[j]], [hsh_d[hb][ag]])
                        for t2 in range(2):
                            pb = (ucount % 2) * 2
                            yb = ucount % 3
                            ucount += 1
                            for ag in range(2):
                                def mm(e, ag=ag, t2=t2, pb=pb, wv=wv):
                                    ins = None
                                    for k in range(KD):
                                        ins = e.matmul(self.pf(pb + ag), lhsT=wv[:, k, ag * 128:(ag + 1) * 128],
                                                       rhs=xn[:, k, t2 * 512:(t2 + 1) * 512], start=(k == 0), stop=(k == KD - 1))
                                    return ins
                                kb.op("pe", mm, [self.slot_d[s]] + xn_d[t2], [self.pf_d[pb + ag]])
                            c0 = 2 + t2 * 512
                            for ag in range(2):
                                kk = j + ag * NJ
                                w0 = self.col(OFF_CW + (l * 3 + 0) * 44 + kk)
                                w1 = self.col(OFF_CW + (l * 3 + 1) * 44 + kk)
                                w2 = self.col(OFF_CW + (l * 3 + 2) * 44 + kk)
                                bb = self.col(OFF_CB + l * 44 + kk)
                                kb.op("act", lambda e, hb=hb, ag=ag, c0=c0, pb=pb: e.activation(
                                    out=hs[:, hb, ag, c0:c0 + 512], in_=self.pf(pb + ag), func=AF.Copy),
                                    [self.pf_d[pb + ag]], [hs_d[hb][ag][t2]])
                                kb.op("act", lambda e, yb=yb, ag=ag, pb=pb, w2=w2, bb=bb: e.activation(
                                    out=y0[:, yb, ag, :], in_=self.pf(pb + ag), func=AF.Identity, bias=bb, scale=w2),
                                    [self.pf_d[pb + ag], self.cols_d], [y0_d[yb][ag]])
                                rd = [hs_d[hb][ag][t2], self.cols_d] + ([hsh_d[hb][ag]] if t2 == 0 else [hs_d[hb][ag][0]])
                                kb.op("dve", lambda e, yb=yb, ag=ag, hb=hb, c0=c0, w1=w1: e.scalar_tensor_tensor(
                                    out=y0[:, yb, ag, :], in0=hs[:, hb, ag, c0 - 1:c0 + 511], scalar=w1, in1=y0[:, yb, ag, :],
                                    op0=ALU.mult, op1=ALU.add), rd + [y0_d[yb][ag]], [y0_d[yb][ag]])
                                kb.op("dve", lambda e, yb=yb, ag=ag, hb=hb, c0=c0, w0=w0: e.scalar_tensor_tensor(
                                    out=y0[:, yb, ag, :], in0=hs[:, hb, ag, c0 - 2:c0 + 510], scalar=w0, in1=y0[:, yb, ag, :],
                                    op0=ALU.mult, op1=ALU.add), rd + [y0_d[yb][ag]], [y0_d[yb][ag]])
                            if pend_ffn:
                                pend_ffn.pop()()

                            def fin(yb=yb, j=j, t2=t2):
                                kb.op("act", lambda e: e.activation(out=sg[:, yb, :], in_=y0[:, yb, 1, :], func=AF.Silu),
                                      [y0_d[yb][1]], [sg_d[yb]])
                                kb.op("dve", lambda e: e.tensor_tensor(
                                    out=act[:, j, t2 * 512:(t2 + 1) * 512], in0=sg[:, yb, :], in1=y0[:, yb, 0, :], op=ALU.mult),
                                    [sg_d[yb], y0_d[yb][0]], [act_d[j][t2]])
                            pend_ffn.append(fin)
                        if half == 0:
                            kb.op("pool", lambda e, hb=hb, j=j: e.tensor_copy(out=halo[:, j, :, :], in_=hs[:, hb, :, 1024:1026]),
                                  [hs_d[hb][0][1], hs_d[hb][1][1]], [halo_d[j]])
                    else:
                        if pend_ffn:
                            pend_ffn.pop()()
                        dc = j
                        wv = self.slot3(s, NJ, 128)
                        for t2 in range(2):
                            bank = 4 + (t2 % 2)
                            def mm(e, t2=t2, bank=bank, wv=wv):
                                ins = None
                                for jj in range(NJ):
                                    ins = e.matmul(self.pf(bank), lhsT=wv[:, jj, :], rhs=act[:, jj, t2 * 512:(t2 + 1) * 512],
                                                   start=(jj == 0), stop=(jj == NJ - 1))
                                return ins
                            kb.op("pe", mm, [self.slot_d[s]] + [act_d[jj][t2] for jj in range(NJ)], [self.pf_d[bank]])
                            tt = half * 2 + t2
                            kb.op("dve", lambda e, dc=dc, tt=tt, bank=bank: e.tensor_tensor(
                                out=self.X[:, dc, tt * 512:(tt + 1) * 512], in0=self.X[:, dc, tt * 512:(tt + 1) * 512],
                                in1=self.pf(bank), op=ALU.add), [self.pf_d[bank], self.Xd[dc][tt]], [self.Xd[dc][tt]])
                    nq = qi + PRE
                    if nq < len(seq):
                        slots_of[nq] = issue_up(seq[nq][1]) if seq[nq][0] == "u" else issue_dn(seq[nq][1])
            kb.fence()
        self.ptr = mark

    def attention(self, l, ja):
        kb, nc = self.kb, self.nc
        wqkv = self.d["sb_w_qkv"][ja]
        wo = self.d["sb_w_o"][ja]
        kb.fence()
        mark = self.ptr
        ps = None
        if True:
            sbt = self.alloc
            xn = sbt("a_xn", [P, KD, L], BF16)
            xn_d = kb.tiles_n("a_xn", 4, KD)
            ao = sbt("a_o", [P, KD, L], BF16)
            ao_d = kb.tiles_n("a_o", KD, 4)
            qT = sbt("a_q", [P, L], BF16)
            kT = sbt("a_k", [P, L], BF16)
            vv = sbt("a_v", [P, 16, 128], BF16)
            q_d, k_d, v_d = kb.tile("a_q"), kb.tile("a_k"), kb.tile("a_v")
            NB = 4
            PW = 512
            mark2 = self.ptr
            nt = self.norm_temps(None)
            for tt in range(4):
                self.rmsnorm(OFF_NG + (l * 2 + 0) * 8, tt, xn, tt * 512, xn_d[tt], nt)
            kb.fence()
            self.ptr = mark2
            ee = sbt("a_e", [P, NB, PW], F32)
            spt = sbt("a_sp", [P, NB, PW], F32)
            lw = sbt("a_lw", [P, NB, PW], F32)
            ww = sbt("a_w", [P, NB, PW], BF16)
            wT = sbt("a_wT", [P, NB, PW], BF16)
            ones = sbt("a_ones", [P, PW], BF16)
            carry = sbt("a_carry", [P, 8], F32)
            e_d, sp_d, lw_d, w_d, wT_d = (kb.tiles_n(n, NB) for n in ("a_e", "a_sp", "a_lw", "a_w", "a_wT"))
            ones_d = kb.tile("a_ones")
            carry_d = kb.tiles_n("a_carry", 8)
            kb.op("pool", lambda e: e.memset(ones[:], 1.0), [], [ones_d])
            maskL_f = self.cst[:, C_MASKL:C_MASKL + 128]
            maskL_b = self.cstb[:, 128:256]
            ident_b = self.cstb[:, 0:128]
            pcount = 0
            ocount = 0
            ccount = 0
            gen = 0
            for pair in range(8):
                s = self.next_slot()
                wv = self.slot3(s, KD, 384)
                self.load_w(s, [(wv[:, :, c * 128:(c + 1) * 128],
                                 wqkv[:, c * D + pair * 128: c * D + (pair + 1) * 128].rearrange("(k p) n -> p k n", p=P))
                                for c in range(3)])
                for tt in range(4):
                    for c in range(2):
                        bank = 4 + (gen % 2)
                        gen += 1
                        def mm(e, c=c, tt=tt, bank=bank, wv=wv):
                            ins = None
                            for k in range(KD):
                                ins = e.matmul(self.pf(bank), lhsT=wv[:, k, c * 128:(c + 1) * 128],
                                               rhs=xn[:, k, tt * 512:(tt + 1) * 512], start=(k == 0), stop=(k == KD - 1))
                            return ins
                        kb.op("pe", mm, [self.slot_d[s]] + xn_d[tt], [self.pf_d[bank]])
                        dst = qT if c == 0 else kT
                        dd = q_d if c == 0 else k_d
                        sc = 0.125 if c == 0 else 1.0
                        kb.op("act", lambda e, dst=dst, tt=tt, bank=bank, sc=sc: e.activation(
                            out=dst[:, tt * 512:(tt + 1) * 512], in_=self.pf(bank), func=AF.Copy, scale=sc),
                            [self.pf_d[bank]], [dd])
                for g4 in range(4):
                    bank = 4 + (gen % 2)
                    gen += 1
                    def mmv(e, g4=g4, bank=bank, wv=wv):
                        ins = None
                        for c4 in range(4):
                            i = g4 * 4 + c4
                            for k in range(KD):
                                ins = e.matmul(self.pf(bank, 128, c4 * 128), lhsT=xn[:, k, i * 128:(i + 1) * 128],
                                               rhs=wv[:, k, 256:384], start=(k == 0), stop=(k == KD - 1))
                        return ins
                    kb.op("pe", mmv, [self.slot_d[s]] + xn_d[g4], [self.pf_d[bank]])
                    kb.op("dve", lambda e, g4=g4, bank=bank: e.tensor_copy(
                        out=vv[:, g4 * 4:(g4 + 1) * 4, :], in_=self.pf(bank).rearrange("p (c n) -> p c n", c=4)),
                        [self.pf_d[bank]], [v_d])
                items = []
                for hh in range(2):
                    for i in range(16):
                        t1 = (i + 1) * 128
                        pcs = []
                        ke = t1
                        while ke > 0:
                            ks = ((ke - 1) // PW) * PW
                            pcs.append((ks, ke))
                            ke = ks
                        ob = 4 + (ocount % 2)
                        ocount += 1
                        prev_cc = None
                        for pi, (ks, ke) in enumerate(pcs):
                            it = dict(hh=hh, i=i, ks=ks, ke=ke, first=(pi == 0), last=(pi == len(pcs) - 1), ob=ob,
                                      p=pcount, cin=prev_cc, cout=None)
                            pcount += 1
                            if pi < len(pcs) - 1:
                                it["cout"] = ccount % 8
                                ccount += 1
                            prev_cc = it["cout"]
                            items.append(it)

                def geo(it):
                    hh, i, ks, ke, p = it["hh"], it["i"], it["ks"], it["ke"], it["p"]
                    return 64 * hh, i * 128, (i + 1) * 128, ke - ks, p % NB, p % 4

                def f_z(it):
                    pb0, t0, t1, n, bi, zb = geo(it)
                    ks, ke = it["ks"], it["ke"]
                    zap = self.pf(zb, n)
                    kb.op("pe", lambda e: e.matmul(zap, lhsT=qT[pb0:pb0 + 64, t0:t1], rhs=kT[pb0:pb0 + 64, ks:ke], start=True, stop=True),
                          [q_d, k_d], [self.pf_d[zb]])

                def f_exp(it):
                    pb0, t0, t1, n, bi, zb = geo(it)
                    zap = self.pf(zb, n)
                    kb.op("act", lambda e: e.activation(out=ee[:, bi, 0:n], in_=zap, func=AF.Exp), [self.pf_d[zb]], [e_d[bi]])

                def f_mask(it):
                    pb0, t0, t1, n, bi, zb = geo(it)
                    if it["first"]:
                        kb.op("pool", lambda e: e.tensor_tensor(out=ee[:, bi, n - 128:n], in0=ee[:, bi, n - 128:n], in1=maskL_f, op=ALU.mult),
                              [e_d[bi], self.cst_d], [e_d[bi]])

                def f_ln(it):
                    pb0, t0, t1, n, bi, zb = geo(it)
                    kb.op("act", lambda e: e.activation(out=spt[:, bi, 0:n], in_=ee[:, bi, 0:n], func=AF.Ln, bias=1.0), [e_d[bi]], [sp_d[bi]])

                def f_scan(it):
                    pb0, t0, t1, n, bi, zb = geo(it)
                    cin, cout = it["cin"], it["cout"]
                    init = 0.0 if cin is None else carry[:, cin:cin + 1]
                    rd = [sp_d[bi], ones_d] + ([] if cin is None else [carry_d[cin]])
                    kb.op("dve", lambda e: e.tensor_tensor_scan(out=lw[:, bi, 0:n][:, ::-1], data0=ones[:, 0:n], data1=spt[:, bi, 0:n][:, ::-1],
                                                                initial=init, op0=ALU.mult, op1=ALU.add), rd, [lw_d[bi]])
                    if cout is not None:
                        kb.op("dve", lambda e: e.tensor_copy(out=carry[:, cout:cout + 1], in_=lw[:, bi, 0:1]), [lw_d[bi]], [carry_d[cout]])

                def f_er(it):
                    pb0, t0, t1, n, bi, zb = geo(it)
                    kb.op("act", lambda e: e.activation(out=lw[:, bi, 0:n], in_=lw[:, bi, 0:n], func=AF.Exp, scale=-1.0), [lw_d[bi]], [lw_d[bi]])

                def f_mult(it):
                    pb0, t0, t1, n, bi, zb = geo(it)
                    kb.op("dve", lambda e: e.tensor_tensor(out=ww[:, bi, 0:n], in0=ee[:, bi, 0:n], in1=lw[:, bi, 0:n], op=ALU.mult),
                          [e_d[bi], lw_d[bi]], [w_d[bi]])

                def f_tr(it):
                    pb0, t0, t1, n, bi, zb = geo(it)
                    tb = it["p"] % 2
                    nb_ = n // 128

                    def trs(e):
                        ins = None
                        for b in range(nb_):
                            ins = e.transpose(self.PB[:, tb * 1024 + b * 128: tb * 1024 + (b + 1) * 128], ww[:, bi, b * 128:(b + 1) * 128], ident_b)
                        return ins
                    kb.op("pe", trs, [w_d[bi], self.cstb_d], [self.pb_d[tb]])

                def f_copy(it):
                    pb0, t0, t1, n, bi, zb = geo(it)
                    tb = it["p"] % 2
                    kb.op("act", lambda e: e.activation(out=wT[:, bi, 0:n], in_=self.PB[:, tb * 1024: tb * 1024 + n], func=AF.Copy),
                          [self.pb_d[tb]], [wT_d[bi]])

                def f_mmo(it, pair=pair):
                    pb0, t0, t1, n, bi, zb = geo(it)
                    ob, i = it["ob"], it["i"]
                    nb_ = n // 128
                    kb0 = it["ks"] // 128
                    first, last = it["first"], it["last"]

                    def mmo(e):
                        ins = None
                        for b in range(nb_):
                            ins = e.matmul(self.pf(ob, 128), lhsT=vv[:, kb0 + b, :], rhs=wT[:, bi, b * 128:(b + 1) * 128],
                                           start=(first and b == 0), stop=(last and b == nb_ - 1))
                        return ins
                    kb.op("pe", mmo, [wT_d[bi], v_d], [self.pf_d[ob]])
                    if last:
                        kb.op("act", lambda e: e.activation(out=ao[pb0:pb0 + 64, pair, t0:t1], in_=self.PF[pb0:pb0 + 64, ob * 512: ob * 512 + 128],
                                                            func=AF.Copy), [self.pf_d[ob]], [ao_d[pair][i // 4]])
                sched = [(0, f_z), (0, f_exp), (0, f_mask), (2, f_er), (5, f_copy), (0, f_ln), (1, f_scan), (3, f_mult), (4, f_tr), (6, f_mmo)]
                nit = len(items)
                for step in range(nit + 6):
                    for lag, fn in sched:
                        j = step - lag
                        if 0 <= j < nit:
                            fn(items[j])
            self.proj_residual(ao, lambda k, tt: ao_d[k][tt], wo)
            kb.fence()
        self.ptr = mark

    def proj_residual(self, src, src_d, wdram):
        kb = self.kb
        for dc in range(KD):
            s = self.next_slot()
            wv = self.slot3(s, KD, 128)
            self.load_w(s, [(wv, wdram[:, dc * 128:(dc + 1) * 128].rearrange("(k p) n -> p k n", p=P))])
            for tt in range(4):
                bank = tt % 4

                def mm(e, tt=tt, bank=bank, wv=wv):
                    ins = None
                    for k in range(KD):
                        ins = e.matmul(self.pf(bank), lhsT=wv[:, k, :], rhs=src[:, k, tt * 512:(tt + 1) * 512],
                                       start=(k == 0), stop=(k == KD - 1))
                    return ins
                kb.op("pe", mm, [self.slot_d[s]] + [src_d(k, tt) for k in range(KD)], [self.pf_d[bank]])
                kb.op("dve", lambda e, dc=dc, tt=tt, bank=bank: e.tensor_tensor(
                    out=self.X[:, dc, tt * 512:(tt + 1) * 512], in0=self.X[:, dc, tt * 512:(tt + 1) * 512],
                    in1=self.pf(bank), op=ALU.add), [self.pf_d[bank], self.Xd[dc][tt]], [self.Xd[dc][tt]])

    def gmlp(self, l):
        kb, nc = self.kb, self.nc
        win = self.d["sg_w_in"][0]
        wo = self.d["sg_w_o"][0]
        kb.fence()
        mark = self.ptr
        sbt = self.alloc
        xn = sbt("g_xn", [P, KD, L], BF16)
        xn_d = kb.tiles_n("g_xn", 4, KD)
        vn = sbt("g_vn", [P, 16, D], BF16)
        vn_d = kb.tiles_n("g_vn", 16)
        wsT = sbt("g_wsT", [P, 8, 128], BF16)
        wsT_d = kb.tile("g_wsT")
        bsr = sbt("g_bsr", [1, D], BF16)
        bsr_d = kb.tile("g_bsr")
        ssq = sbt("g_ssq", [P, 16], F32)
        rstd = sbt("g_rstd", [P, 16], F32)
        ssq_d, rstd_d = kb.tile("g_ssq"), kb.tiles_n("g_rstd", 16)
        mark2 = self.ptr
        nt = self.norm_temps(None)
        for tt in range(4):
            self.rmsnorm(OFF_NG + (l * 2 + 0) * 8, tt, xn, tt * 512, xn_d[tt], nt)
        kb.fence()
        self.ptr = mark2
        wv_ = sbt("g_wv", [P, KD, D], BF16)
        wv_d = kb.tile("g_wv")
        vg = sbt("g_vg", [P, 2, D], F32)
        vg_d = kb.tiles_n("g_vg", 2)
        junk = sbt("g_junk", [P, D], BF16)
        junk_d = kb.tile("g_junk")
        gbc = sbt("g_gbc", [P, D], F32)
        gbc_d = kb.tile("g_gbc")
        grow = sbt("g_grow", [1, D], F32)
        grow_d = kb.tile("g_grow")
        wsf = sbt("g_wsf", [P, 8, 128], F32)
        wsf_d = kb.tile("g_wsf")
        kb.dma_multi("pool", [(wv_[:, :, c * 256:(c + 1) * 256],
                               win[:, D + c * 256: D + (c + 1) * 256].rearrange("(k p) n -> p k n", p=P)) for c in range(4)],
                     [], [wv_d], "g_wv")
        kb.dma("pool", bsr[:], self.d["sg_b"].rearrange("a g t -> a (g t)"), [], [bsr_d], "g_bsr")
        kb.dma("sp", grow[:], self.d["sg_norm_g"], [], [grow_d], "g_grow")
        kb.dma("sp", wsf[:], self.d["sg_w_s"][0].rearrange("g t s -> t g s"), [], [wsf_d], "g_wsf")
        kb.op("dve", lambda e: e.memset(ssq[:], 0.0), [], [ssq_d])
        ones_row_f = self.cst[0:1, C_ONES:C_ONES + 128]
        for nh in range(2):
            kb.op("pe", lambda e, nh=nh: e.matmul(self.pf(4 + nh), lhsT=ones_row_f, rhs=grow[0:1, nh * 512:(nh + 1) * 512],
                                                 start=True, stop=True), [grow_d, self.cst_d], [self.pf_d[4 + nh]])
            kb.op("dve", lambda e, nh=nh: e.tensor_copy(out=gbc[:, nh * 512:(nh + 1) * 512], in_=self.pf(4 + nh)),
                  [self.pf_d[4 + nh]], [gbc_d])
        trilT = self.cst[:, C_TRILT:C_TRILT + 128]
        for g in range(8):
            bank = 4 + g // 4
            kb.op("pe", lambda e, g=g, bank=bank: e.transpose(self.pf(bank, 128, (g % 4) * 128), wsf[:, g, :], self.ident_f()),
                  [wsf_d, self.cst_d], [self.pf_d[bank]])
            kb.op("dve", lambda e, g=g, bank=bank: e.tensor_tensor(out=wsT[:, g, :], in0=self.pf(bank, 128, (g % 4) * 128), in1=trilT,
                                                                   op=ALU.mult), [self.pf_d[bank], self.cst_d], [wsT_d])
        for i in range(16):
            b0 = 2 * (i % 2)
            vb = i % 2
            for nh in range(2):
                def mm(e, i=i, nh=nh, b0=b0):
                    ins = None
                    for k in range(KD):
                        ins = e.matmul(self.pf(b0 + nh), lhsT=xn[:, k, i * 128:(i + 1) * 128], rhs=wv_[:, k, nh * 512:(nh + 1) * 512],
                                       start=(k == 0), stop=(k == KD - 1))
                    return ins
                kb.op("pe", mm, [wv_d] + xn_d[i // 4], [self.pf_d[b0 + nh]])
            kb.op("act", lambda e, vb=vb, b0=b0: e.activation(out=vg[:, vb, :], in_=self.PF[:, b0 * 512: b0 * 512 + 1024],
                                                              func=AF.Gelu_apprx_tanh), [self.pf_d[b0], self.pf_d[b0 + 1]], [vg_d[vb]])
            kb.op("act", lambda e, vb=vb, i=i: e.activation(out=junk[:], in_=vg[:, vb, :], func=AF.Square, accum_out=ssq[:, i:i + 1]),
                  [vg_d[vb], ssq_d], [junk_d, rstd_d[i]])
            kb.op("act", lambda e, i=i: e.activation(out=rstd[:, i:i + 1], in_=ssq[:, i:i + 1], func=AF.Sqrt,
                                                     bias=self.cst[:, C_EPS:C_EPS + 1], scale=1.0 / D), [rstd_d[i], self.cst_d], [rstd_d[i]])
            kb.op("dve", lambda e, i=i: e.reciprocal(out=rstd[:, i:i + 1], in_=rstd[:, i:i + 1]), [rstd_d[i]], [rstd_d[i]])
            kb.op("dve", lambda e, i=i, vb=vb: e.scalar_tensor_tensor(out=vn[:, i, :], in0=vg[:, vb, :], scalar=rstd[:, i:i + 1], in1=gbc[:],
                                                                    op0=ALU.mult, op1=ALU.mult), [vg_d[vb], rstd_d[i], gbc_d], [vn_d[i]])
        kb.fence()
        self.ptr = mark2
        gated = sbt("g_gated", [P, KD, L], BF16)
        gated_d = kb.tiles_n("g_gated", KD, 4)
        ug = sbt("g_ug", [P, 2, 512], F32)
        ug_d = kb.tiles_n("g_ug", 2)
        ones_row_b = self.cstb[0:1, 384:512]
        cnt = 0
        for g in range(8):
            s = self.next_slot()
            wu = self.slot3(s, KD, 128)
            self.load_w(s, [(wu, win[:, g * 128:(g + 1) * 128].rearrange("(k p) n -> p k n", p=P))])
            for tt in range(4):
                sb_ = (cnt % 2) * 2
                ub_ = sb_ + 1
                ui = cnt % 2
                cnt += 1

                def mmsv(e, g=g, tt=tt, sb_=sb_):
                    ins = None
                    for c in range(4):
                        o = self.pf(sb_, 128, c * 128)
                        e.matmul(o, lhsT=vn[:, 4 * tt + c, g * 128:(g + 1) * 128], rhs=wsT[:, g, :], start=True, stop=False)
                        ins = e.matmul(o, lhsT=ones_row_b, rhs=bsr[0:1, g * 128:(g + 1) * 128], start=False, stop=True)
                    return ins
                kb.op("pe", mmsv, [vn_d[4 * tt + c] for c in range(4)] + [wsT_d, bsr_d, self.cstb_d], [self.pf_d[sb_]])

                def mmu(e, tt=tt, ub_=ub_, wu=wu):
                    ins = None
                    for k in range(KD):
                        ins = e.matmul(self.pf(ub_), lhsT=wu[:, k, :], rhs=xn[:, k, tt * 512:(tt + 1) * 512], start=(k == 0), stop=(k == KD - 1))
                    return ins
                kb.op("pe", mmu, [self.slot_d[s]] + xn_d[tt], [self.pf_d[ub_]])
                kb.op("act", lambda e, ui=ui, ub_=ub_: e.activation(out=ug[:, ui, :], in_=self.pf(ub_), func=AF.Gelu_apprx_tanh),
                      [self.pf_d[ub_]], [ug_d[ui]])
                kb.op("dve", lambda e, g=g, tt=tt, ui=ui, sb_=sb_: e.tensor_tensor(
                    out=gated[:, g, tt * 512:(tt + 1) * 512], in0=ug[:, ui, :], in1=self.pf(sb_), op=ALU.mult),
                    [ug_d[ui], self.pf_d[sb_]], [gated_d[g][tt]])
        self.proj_residual(gated, lambda k, tt: gated_d[k][tt], wo)
        kb.fence()
        self.ptr = mark

    def s5(self, l):
        kb, nc = self.kb, self.nc
        d = self.d
        win = d["ssm_w_in"][0]
        wglu = d["ssm_w_glu"][0]
        PI = math.pi
        kb.fence()
        mark = self.ptr
        sbt = self.alloc
        xn = sbt("s_xn", [P, KD, L], BF16)
        xn_d = kb.tiles_n("s_xn", 4, KD)
        yg = sbt("s_yg", [P, KD, L], BF16)
        yg_d = kb.tiles_n("s_yg", KD, 4)
        c2 = sbt("s_c2", [P, C2_REP], F32)
        c2_d = kb.tile("s_c2")
        bbt = sbt("s_bbt", [P, 2, 8, 64], BF16)
        bbt_d = kb.tile("s_bbt")
        ct = sbt("s_ct", [P, 2, 8, 64], BF16)
        ct_d = kb.tile("s_ct")
        svr = sbt("s_svr", [P, 32], F32)
        svt = sbt("s_svt", [P, 32], F32)
        svc = sbt("s_svc", [P, 32], F32)
        svs = sbt("s_svs", [P, 32], F32)
        svn = sbt("s_svn", [P, 32], F32)
        ctmp = sbt("s_ctmp", [P, 2], F32)
        ctmp_d = kb.tile("s_ctmp")
        sv_d = kb.tile("s_sv")
        carry = sbt("s_carry", [P, 2, 32], F32)
        carry_d = kb.tiles_n("s_carry", 32)
        kb.dma("sp", c2[:], d["consts2"][:, 0:C2_REP], [], [c2_d], "s_c2")
        kb.dma("pool", ct[:, 0, :, :], d["ssm_c_re"][0].rearrange("(q gl) h p -> (gl h) q p", q=8), [], [ct_d], "s_ct")
        kb.dma("pool", ct[:, 1, :, :], d["ssm_c_im"][0].rearrange("(q gl) h p -> (gl h) q p", q=8), [], [ct_d], "s_ct")
        mark2 = self.ptr
        nt = self.norm_temps(None)
        for tt in range(4):
            self.rmsnorm(OFF_NG + (l * 2 + 0) * 8, tt, xn, tt * 512, xn_d[tt], nt)
        kb.fence()
        self.ptr = mark2
        bn = sbt("s_bn", [64, 2, 1024], F32)
        bn_d = kb.tile("s_bn")
        kb.dma("sp", bn[:, 0, :], d["ssm_b_re"][0].rearrange("g p h -> g (p h)"), [], [bn_d], "s_bn")
        kb.dma("sp", bn[:, 1, :], d["ssm_b_im"][0].rearrange("g p h -> g (p h)"), [], [bn_d], "s_bn")
        NT = 26
        tp = sbt("s_tp", [64, NT, 64], F32)
        tp_d = kb.tiles_n("s_tp", NT)
        dtc = sbt("s_dt", [64, 2], F32)
        dt_d = kb.tile("s_dt")
        (LR, LI, DLR, DLI, MAG, SA, CA, SN, CS, AR, AI, T1, T2, DEN, RDEN, AM1, U1, U2, CR, CI) = range(20)
        T = lambda i: tp[:, i, :]
        kb.dma("sp", T(LR), d["ssm_lam_re"][0], [], [tp_d[LR]], "s_p0")
        kb.dma("sp", T(LI), d["ssm_lam_im"][0], [], [tp_d[LI]], "s_p1")
        kb.dma("sp", dtc[:, 0:1], d["ssm_log_dt"].rearrange("a g -> g a"), [], [dt_d], "s_p2")
        kb.op("act", lambda e: e.activation(out=dtc[:, 1:2], in_=dtc[:, 0:1], func=AF.Exp), [dt_d], [dt_d])
        dtcol = dtc[:, 1:2]

        def v1(eng, fn, r, w):
            kb.op(eng, fn, [tp_d[i] for i in r] + [dt_d, self.cst_d], [tp_d[i] for i in w])
        v1("dve", lambda e: e.tensor_scalar_min(out=T(LR), in0=T(LR), scalar1=-1e-4), [LR], [LR])
        v1("dve", lambda e: e.tensor_scalar_mul(out=T(DLR), in0=T(LR), scalar1=dtcol), [LR], [DLR])
        v1("dve", lambda e: e.tensor_scalar_mul(out=T(DLI), in0=T(LI), scalar1=dtcol), [LI], [DLI])
        v1("act", lambda e: e.activation(out=T(MAG), in_=T(DLR), func=AF.Exp), [DLR], [MAG])
        hpi64 = self.cst[0:64, C_HALFPI:C_HALFPI + 1]
        v1("dve", lambda e: e.tensor_scalar(out=T(T1), in0=T(DLI), scalar1=1.0 / (2 * PI), scalar2=MAGIC, op0=ALU.mult, op1=ALU.add), [DLI], [T1])
        v1("dve", lambda e: e.tensor_scalar_add(out=T(T1), in0=T(T1), scalar1=-MAGIC), [T1], [T1])
        v1("dve", lambda e: e.scalar_tensor_tensor(out=T(SA), in0=T(T1), scalar=-CW1, in1=T(DLI), op0=ALU.mult, op1=ALU.add), [T1, DLI], [SA])
        v1("dve", lambda e: e.scalar_tensor_tensor(out=T(SA), in0=T(T1), scalar=-CW2, in1=T(SA), op0=ALU.mult, op1=ALU.add), [T1, SA], [SA])
        v1("dve", lambda e: e.tensor_scalar(out=T(SA), in0=T(SA), scalar1=-PI_LO, scalar2=PI_LO, op0=ALU.max, op1=ALU.min), [SA], [SA])
        v1("dve", lambda e: e.scalar_tensor_tensor(out=T(CA), in0=T(SA), scalar=-1.0, in1=T(SA), op0=ALU.mult, op1=ALU.max), [SA], [CA])
        v1("act", lambda e: e.activation(out=T(SN), in_=T(SA), func=AF.Sin), [SA], [SN])
        v1("act", lambda e: e.activation(out=T(CS), in_=T(CA), func=AF.Sin, scale=-1.0, bias=hpi64), [CA], [CS])
        v1("dve", lambda e: e.tensor_tensor(out=T(AR), in0=T(MAG), in1=T(CS), op=ALU.mult), [MAG, CS], [AR])
        v1("dve", lambda e: e.tensor_tensor(out=T(AI), in0=T(MAG), in1=T(SN), op=ALU.mult), [MAG, SN], [AI])
        v1("dve", lambda e: e.tensor_tensor(out=T(T1), in0=T(LR), in1=T(LR), op=ALU.mult), [LR], [T1])
        v1("dve", lambda e: e.tensor_tensor(out=T(T2), in0=T(LI), in1=T(LI), op=ALU.mult), [LI], [T2])
        v1("dve", lambda e: e.tensor_tensor(out=T(DEN), in0=T(T1), in1=T(T2), op=ALU.add), [T1, T2], [DEN])
        v1("dve", lambda e: e.reciprocal(out=T(RDEN), in_=T(DEN)), [DEN], [RDEN])
        v1("dve", lambda e: e.tensor_scalar_add(out=T(AM1), in0=T(AR), scalar1=-1.0), [AR], [AM1])
        v1("dve", lambda e: e.tensor_tensor(out=T(U1), in0=T(AM1), in1=T(LR), op=ALU.mult), [AM1, LR], [U1])
        v1("dve", lambda e: e.tensor_tensor(out=T(U2), in0=T(AI), in1=T(LI), op=ALU.mult), [AI, LI], [U2])
        v1("dve", lambda e: e.tensor_tensor(out=T(U1), in0=T(U1), in1=T(U2), op=ALU.add), [U1, U2], [U1])
        v1("dve", lambda e: e.tensor_tensor(out=T(CR), in0=T(U1), in1=T(RDEN), op=ALU.mult), [U1, RDEN], [CR])
        v1("dve", lambda e: e.tensor_tensor(out=T(U1), in0=T(AI), in1=T(LR), op=ALU.mult), [AI, LR, CR], [U1])
        v1("dve", lambda e: e.tensor_tensor(out=T(U2), in0=T(AM1), in1=T(LI), op=ALU.mult), [AM1, LI], [U2])
        v1("dve", lambda e: e.tensor_tensor(out=T(U1), in0=T(U1), in1=T(U2), op=ALU.subtract), [U1, U2], [U1])
        v1("dve", lambda e: e.tensor_tensor(out=T(CI), in0=T(U1), in1=T(RDEN), op=ALU.mult), [U1, RDEN], [CI])
        bbn = sbt("s_bbn", [64, 2, 1024], F32)
        bbn_d = kb.tile("s_bbn")
        tmpn = sbt("s_tmpn", [64, 1024], F32)
        tmpn_d = kb.tile("s_tmpn")
        R_, I_ = 0, 1
        cb = lambda i: T(i).unsqueeze(2).broadcast_to([64, 64, 16])
        nat = lambda ri: bn[:, ri, :].rearrange("g (p h) -> g p h", h=16)
        outv = lambda t_: t_.rearrange("g (h p) -> g p h", h=16)
        kb.op("dve", lambda e: e.tensor_tensor(out=outv(bbn[:, R_, :]), in0=cb(CR), in1=nat(R_), op=ALU.mult), [tp_d[CR], bn_d], [bbn_d])
        kb.op("dve", lambda e: e.tensor_tensor(out=outv(tmpn[:]), in0=cb(CI), in1=nat(I_), op=ALU.mult), [tp_d[CI], bn_d], [tmpn_d])
        kb.op("dve", lambda e: e.tensor_tensor(out=bbn[:, R_, :], in0=bbn[:, R_, :], in1=tmpn[:], op=ALU.subtract), [bbn_d, tmpn_d], [bbn_d])
        kb.op("dve", lambda e: e.tensor_tensor(out=outv(bbn[:, I_, :]), in0=cb(CR), in1=nat(I_), op=ALU.mult), [tp_d[CR], bn_d], [bbn_d])
        kb.op("dve", lambda e: e.tensor_tensor(out=outv(tmpn[:]), in0=cb(CI), in1=nat(R_), op=ALU.mult), [tp_d[CI], bn_d, bbn_d], [tmpn_d])
        kb.op("dve", lambda e: e.tensor_tensor(out=bbn[:, I_, :], in0=bbn[:, I_, :], in1=tmpn[:], op=ALU.add), [bbn_d, tmpn_d], [bbn_d])
        scr_d = kb.tile("s_scr")
        kb.dma("sp", self.scr.rearrange("r g n -> g r n"), bbn[:], [bbn_d], [scr_d], "s_scr_w")
        for ri in range(2):
            kb.dma("pool", bbt[:, ri, :, :], self.scr[ri].rearrange("(q gl) (h p) -> (gl h) q p", q=8, h=16), [scr_d], [bbt_d], "s_scr_r")
        A5, K5, R5, B5, C5, S5_ = 20, 21, 22, 23, 24, 25
        v1("dve", lambda e: e.tensor_scalar_mul(out=T(A5), in0=T(DLI), scalar1=512.0), [DLI], [A5])
        v1("dve", lambda e: e.tensor_scalar(out=T(K5), in0=T(A5), scalar1=1.0 / (2 * PI), scalar2=MAGIC, op0=ALU.mult, op1=ALU.add), [A5], [K5])
        v1("dve", lambda e: e.tensor_scalar_add(out=T(K5), in0=T(K5), scalar1=-MAGIC), [K5], [K5])
        v1("dve", lambda e: e.scalar_tensor_tensor(out=T(R5), in0=T(K5), scalar=-CW1, in1=T(A5), op0=ALU.mult, op1=ALU.add), [K5, A5], [R5])
        v1("dve", lambda e: e.scalar_tensor_tensor(out=T(R5), in0=T(K5), scalar=-CW2, in1=T(R5), op0=ALU.mult, op1=ALU.add), [K5, R5], [R5])
        v1("dve", lambda e: e.tensor_scalar(out=T(R5), in0=T(R5), scalar1=-PI_LO, scalar2=PI_LO, op0=ALU.max, op1=ALU.min), [R5], [R5])
        v1("dve", lambda e: e.scalar_tensor_tensor(out=T(B5), in0=T(R5), scalar=-1.0, in1=T(R5), op0=ALU.mult, op1=ALU.max), [R5], [B5])
        v1("act", lambda e: e.activation(out=T(S5_), in_=T(R5), func=AF.Sin), [R5], [S5_])
        v1("act", lambda e: e.activation(out=T(C5), in_=T(B5), func=AF.Sin, scale=-1.0, bias=hpi64), [B5], [C5])
        dup = sbt("s_dup", [64, 4, 128], F32)
        dup_d = kb.tile("s_dup")
        for qi, (src, dst) in enumerate(((MAG, svr), (DLI, svt), (C5, svc), (S5_, svs))):
            kb.op("dve", lambda e, qi=qi, src=src: e.tensor_copy(out=dup[:, qi, 0:64], in_=T(src)), [tp_d[src]], [dup_d])
            kb.op("dve", lambda e, qi=qi, src=src: e.tensor_copy(out=dup[:, qi, 64:128], in_=T(src)), [tp_d[src]], [dup_d])
            bank = 2 + qi
            kb.op("pe", lambda e, qi=qi, bank=bank: e.transpose(self.pf(bank, 64), dup[:, qi, :], self.cst[0:64, C_IDENT:C_IDENT + 64]),
                  [dup_d, self.cst_d], [self.pf_d[bank]])
            kb.op("dve", lambda e, dst=dst, bank=bank: e.tensor_copy(out=dst[0:64, :], in_=self.PF[0:64, bank * 512: bank * 512 + 64: 2]),
                  [self.pf_d[bank]], [sv_d])
            kb.op("dve", lambda e, dst=dst, bank=bank: e.tensor_copy(out=dst[64:128, :], in_=self.PF[64:128, bank * 512 + 1: bank * 512 + 64: 2]),
                  [self.pf_d[bank]], [sv_d])
        kb.op("dve", lambda e: e.tensor_scalar_mul(out=svn[:], in0=svs[:], scalar1=-1.0), [sv_d], [sv_d])
        kb.op("dve", lambda e: e.memset(carry[:], 0.0), [], carry_d)
        kb.fence()
        self.ptr = mark2
        u32 = sbt("s_u32", [P, L], F32)
        u32_d = kb.tiles_n("s_u32", 4)
        ub = sbt("s_ub", [P, L], BF16)
        ub_d = kb.tiles_n("s_ub", 4)
        bwm = sbt("s_bwm", [P, 2, 4, 128], BF16)
        bwm_d = kb.tile("s_bwm")
        bwf = sbt("s_bwf", [P, 2, 128], F32)
        bwf_d = kb.tile("s_bwf")
        cw = sbt("s_cw", [P, 2, 4, 128], BF16)
        cw_d = kb.tile("s_cw")
        ctd, ctd_d = bwf, bwf_d
        tabs = sbt("s_tabs", [P, 2, 2, 512], F32)
        tabs_d = kb.tiles_n("s_tabs", 2, 2)
        rbt = sbt("s_rbt", [P, 512], F32)
        rbt_d = kb.tile("s_rbt")
        un = sbt("s_un", [P, 2, 512], F32)
        un_d = kb.tiles_n("s_un", 2)
        wk = sbt("s_wk", [P, 4, 512], F32)
        wk_d = kb.tiles_n("s_wk", 4)
        xrb = sbt("s_xr", [P, 2, 512], BF16)
        xr_d = kb.tiles_n("s_xr", 2)
        iota = c2[:, C2_IOTA:C2_IOTA + 512]
        bmask = c2[:, C2_BMASK:C2_BMASK + 128]
        hpi = self.cst[:, C_HALFPI:C_HALFPI + 1]
        W = lambda i: wk[:, i, :]
        ysum, ysum_d = wk[:, 0, :], wk_d[0]
        YB = [4, 5, 6, 7]

        def tables(j):
            par = j % 2
            tS, tC, tK = tabs[:, par, 0, :], tabs[:, par, 1, :], wk[:, 3, :]
            dS, dC = tabs_d[par]
            dK = wk_d[3]
            th = svt[:, j:j + 1]
            kb.op("dve", lambda e: e.tensor_scalar_mul(out=tC, in0=iota, scalar1=th), [c2_d, sv_d], [dC])
            kb.op("dve", lambda e: e.tensor_scalar(out=tK, in0=tC, scalar1=1.0 / (2 * PI), scalar2=MAGIC, op0=ALU.mult, op1=ALU.add), [dC], [dK])
            kb.op("dve", lambda e: e.tensor_scalar_add(out=tK, in0=tK, scalar1=-MAGIC), [dK], [dK])
            kb.op("dve", lambda e: e.scalar_tensor_tensor(out=tS, in0=tK, scalar=-CW1, in1=tC, op0=ALU.mult, op1=ALU.add), [dK, dC], [dS])
            kb.op("dve", lambda e: e.scalar_tensor_tensor(out=tS, in0=tK, scalar=-CW2, in1=tS, op0=ALU.mult, op1=ALU.add), [dK, dS], [dS])
            kb.op("dve", lambda e: e.tensor_scalar(out=tS, in0=tS, scalar1=-PI_LO, scalar2=PI_LO, op0=ALU.max, op1=ALU.min), [dS], [dS])
            kb.op("dve", lambda e: e.scalar_tensor_tensor(out=tC, in0=tS, scalar=-1.0, in1=tS, op0=ALU.mult, op1=ALU.max), [dS], [dC])
            kb.op("act", lambda e: e.activation(out=tS, in_=tS, func=AF.Sin), [dS], [dS])
            kb.op("act", lambda e: e.activation(out=tC, in_=tC, func=AF.Sin, scale=-1.0, bias=hpi), [dC, self.cst_d], [dC])

        def rho_table(j):
            rho = svr[:, j:j + 1]
            kb.op("act", lambda e: e.activation(out=rbt[:], in_=iota, func=AF.Identity, bias=rho, scale=0.0), [c2_d, sv_d], [rbt_d])

        ucount = [0]

        def stage_a(jj, tt, up):
            ts = slice(tt * 512, (tt + 1) * 512)
            for ri in range(2):
                bk = 2 * up + ri
                kb.op("pe", lambda e, ri=ri, bk=bk: e.matmul(self.pf(bk), lhsT=bwm[:, ri, jj, :], rhs=ub[:, ts], start=True, stop=True),
                      [bwm_d, ub_d[tt]], [self.pf_d[bk]])

        def stage_bcd(j, jj, tt, up):
            par = j % 2
            tS, tC, tK = tabs[:, par, 0, :], tabs[:, par, 1, :], rbt[:]
            dS, dC = tabs_d[par]
            dK = rbt_d
            brp, bip = self.pf(2 * up), self.pf(2 * up + 1)
            dbr, dbi = self.pf_d[2 * up], self.pf_d[2 * up + 1]
            YR, YI = un[:, 0, :], un[:, 1, :]

            def tt_(eng, o, od, a, ad, b, bd, op):
                kb.op(eng, lambda e: e.tensor_tensor(out=o, in0=a, in1=b, op=op), [ad, bd], [od])
            tt_("dve", W(0), wk_d[0], brp, dbr, tC, dC, ALU.mult)
            tt_("dve", W(1), wk_d[1], bip, dbi, tS, dS, ALU.mult)
            tt_("pool", W(0), wk_d[0], W(0), wk_d[0], W(1), wk_d[1], ALU.add)
            tt_("dve", W(2), wk_d[2], bip, dbi, tC, dC, ALU.mult)
            tt_("dve", W(3), wk_d[3], brp, dbr, tS, dS, ALU.mult)
            tt_("dve", W(2), wk_d[2], W(2), wk_d[2], W(3), wk_d[3], ALU.subtract)
            for ri, src in ((0, 0), (1, 2)):
                kb.op("dve", lambda e, ri=ri, src=src: e.tensor_tensor_scan(
                    out=un[:, ri, :], data0=tK, data1=W(src), initial=carry[:, ri, j:j + 1], op0=ALU.mult, op1=ALU.add),
                    [dK, wk_d[src], carry_d[j]], [un_d[ri]])
            if tt < 3:
                yrl, yil = un[:, 0, 511:512], un[:, 1, 511:512]
                cc, ss, ns = svc[:, j:j + 1], svs[:, j:j + 1], svn[:, j:j + 1]
                kb.op("dve", lambda e: e.tensor_scalar_mul(out=ctmp[:, 0:1], in0=yrl, scalar1=cc), [un_d[0], sv_d], [ctmp_d])
                kb.op("dve", lambda e: e.tensor_scalar_mul(out=ctmp[:, 1:2], in0=yrl, scalar1=ss), [un_d[0], sv_d, ctmp_d], [ctmp_d])
                kb.op("dve", lambda e: e.scalar_tensor_tensor(out=carry[:, 0, j:j + 1], in0=yil, scalar=ns, in1=ctmp[:, 0:1], op0=ALU.mult, op1=ALU.add),
                      [un_d[1], sv_d, ctmp_d], [carry_d[j]])
                kb.op("dve", lambda e: e.scalar_tensor_tensor(out=carry[:, 1, j:j + 1], in0=yil, scalar=cc, in1=ctmp[:, 1:2], op0=ALU.mult, op1=ALU.add),
                      [un_d[1], sv_d, ctmp_d, carry_d[j]], [carry_d[j]])
            tt_("dve", W(0), wk_d[0], tC, dC, YR, un_d[0], ALU.mult)
            tt_("dve", W(1), wk_d[1], tS, dS, YI, un_d[1], ALU.mult)
            kb.op("dve", lambda e: e.tensor_tensor(out=xrb[:, 0, :], in0=W(0), in1=W(1), op=ALU.subtract), [wk_d[0], wk_d[1]], [xr_d[0]])
            tt_("dve", W(3), wk_d[3], tS, dS, YR, un_d[0], ALU.mult)
            tt_("dve", W(2), wk_d[2], tC, dC, YI, un_d[1], ALU.mult)
            kb.op("dve", lambda e: e.tensor_tensor(out=xrb[:, 1, :], in0=W(3), in1=W(2), op=ALU.add), [wk_d[3], wk_d[2]], [xr_d[1]])
            yb = YB[tt]
            for ri in range(2):
                kb.op("pe", lambda e, ri=ri: e.matmul(self.pfx(yb), lhsT=cw[:, ri, jj, :], rhs=xrb[:, ri, :],
                                                     start=(jj == 0 and ri == 0), stop=(jj == 3 and ri == 1)),
                      [cw_d, xr_d[ri]], [self.pfx_d(yb)])

        for q in range(8):
            s = self.next_slot()
            wv = self.slot3(s, KD, 128)
            self.load_w(s, [(wv, win[:, q * 128:(q + 1) * 128].rearrange("(k p) n -> p k n", p=P))])
            for tt in range(4):
                bank = tt % 2

                def mm(e, tt=tt, bank=bank, wv=wv):
                    ins = None
                    for k in range(KD):
                        ins = e.matmul(self.pf(bank), lhsT=wv[:, k, :], rhs=xn[:, k, tt * 512:(tt + 1) * 512], start=(k == 0), stop=(k == KD - 1))
                    return ins
                kb.op("pe", mm, [self.slot_d[s]] + xn_d[tt], [self.pf_d[bank]])
                kb.op("dve", lambda e, tt=tt, bank=bank: e.tensor_copy(out=u32[:, tt * 512:(tt + 1) * 512], in_=self.pf(bank)),
                      [self.pf_d[bank]], [u32_d[tt]])
                kb.op("dve", lambda e, tt=tt, bank=bank: e.tensor_copy(out=ub[:, tt * 512:(tt + 1) * 512], in_=self.pf(bank)),
                      [self.pf_d[bank]], [ub_d[tt]])
            for ri in range(2):
                kb.op("dve", lambda e, ri=ri, q=q: e.tensor_tensor(
                    out=bwf[:, ri, :].rearrange("p (a n) -> p a n", a=2), in0=bbt[:, ri, q:q + 1, :].broadcast_to([P, 2, 64]),
                    in1=bmask.rearrange("p (a n) -> p a n", a=2), op=ALU.mult), [bbt_d, c2_d], [bwf_d])
                for jj in range(4):
                    kb.op("dve", lambda e, ri=ri, jj=jj: e.tensor_scalar_mul(
                        out=bwm[:, ri, jj, :], in0=bwf[:, ri, :], scalar1=c2[:, C2_RMASK + jj:C2_RMASK + jj + 1]), [bwf_d, c2_d], [bwm_d])
                kb.op("dve", lambda e, ri=ri, q=q: e.tensor_copy(
                    out=ctd[:, ri, :].rearrange("p (a n) -> p a n", a=2), in_=ct[:, ri, q:q + 1, :].broadcast_to([P, 2, 64])), [ct_d], [ctd_d])
                bank = ri
                kb.op("pe", lambda e, ri=ri, bank=bank: e.transpose(self.pf(bank, 128), ctd[:, ri, :], self.ident_f()),
                      [ctd_d, self.cst_d], [self.pf_d[bank]])
                for jj in range(4):
                    cm = c2[:, C2_CMASK + jj * 128: C2_CMASK + (jj + 1) * 128]
                    if ri == 0:
                        kb.op("dve", lambda e, jj=jj, bank=bank, cm=cm: e.tensor_tensor(out=cw[:, 0, jj, :], in0=self.pf(bank, 128), in1=cm, op=ALU.mult),
                              [self.pf_d[bank], c2_d], [cw_d])
                    else:
                        kb.op("dve", lambda e, jj=jj, bank=bank, cm=cm: e.scalar_tensor_tensor(
                            out=cw[:, 1, jj, :], in0=self.pf(bank, 128), scalar=-1.0, in1=cm, op0=ALU.mult, op1=ALU.mult),
                            [self.pf_d[bank], c2_d], [cw_d])
            units = [(jj, tt) for jj in range(4) for tt in range(4)]
            tables(4 * q)
            stage_a(units[0][0], units[0][1], ucount[0] % 2)
            for ui, (jj, tt) in enumerate(units):
                up = ucount[0] % 2
                ucount[0] += 1
                if ui + 1 < len(units):
                    stage_a(units[ui + 1][0], units[ui + 1][1], ucount[0] % 2)
                if tt == 0:
                    rho_table(4 * q + jj)
                stage_bcd(4 * q + jj, jj, tt, up)
                if tt == 1 and jj < 3:
                    tables(4 * q + jj + 1)
            for tt in range(4):
                yb = YB[tt]
                ts = slice(tt * 512, (tt + 1) * 512)
                kb.op("dve", lambda e, tt=tt, q=q, yb=yb, ts=ts: e.scalar_tensor_tensor(
                    out=ysum, in0=u32[:, ts], scalar=self.col(OFF_SSMD + q), in1=self.pfx(yb), op0=ALU.mult, op1=ALU.add),
                    [u32_d[tt], self.cols_d, self.pfx_d(yb)], [ysum_d])
                kb.op("act", lambda e, q=q, ts=ts: e.activation(out=yg[:, q, ts], in_=ysum, func=AF.Gelu_apprx_tanh), [ysum_d], [yg_d[q][tt]])
        kb.fence()
        self.ptr = mark2
        if os.environ.get("S5_STOP") in ("2", "3", "4", "5"):
            self.ptr = mark
            return
        sgm = sbt("s_sgm", [P, 2, 512], F32)
        sgm_d = kb.tiles_n("s_sgm", 2)
        gcnt = 0
        for dc in range(KD):
            s = self.next_slot()
            wv = self.slot3(s, KD, 256)
            self.load_w(s, [(wv[:, :, 0:128], wglu[:, dc * 128:(dc + 1) * 128].rearrange("(k p) n -> p k n", p=P)),
                            (wv[:, :, 128:256], wglu[:, D + dc * 128: D + (dc + 1) * 128].rearrange("(k p) n -> p k n", p=P))])
            for tt in range(4):
                b0 = (gcnt % 3) * 2
                gi = gcnt % 2
                gcnt += 1
                for ag in range(2):
                    def mm(e, ag=ag, tt=tt, b0=b0, wv=wv):
                        ins = None
                        for k in range(KD):
                            ins = e.matmul(self.pf(b0 + ag), lhsT=wv[:, k, ag * 128:(ag + 1) * 128], rhs=yg[:, k, tt * 512:(tt + 1) * 512],
                                           start=(k == 0), stop=(k == KD - 1))
                        return ins
                    kb.op("pe", mm, [self.slot_d[s]] + [yg_d[k][tt] for k in range(KD)], [self.pf_d[b0 + ag]])
                kb.op("act", lambda e, gi=gi, b0=b0: e.activation(out=sgm[:, gi, :], in_=self.pf(b0 + 1), func=AF.Sigmoid),
                      [self.pf_d[b0 + 1]], [sgm_d[gi]])
                kb.op("dve", lambda e, gi=gi, b0=b0: e.tensor_tensor(out=sgm[:, gi, :], in0=self.pf(b0), in1=sgm[:, gi, :], op=ALU.mult),
                      [self.pf_d[b0], sgm_d[gi]], [sgm_d[gi]])
                kb.op("dve", lambda e, dc=dc, tt=tt, gi=gi: e.tensor_tensor(
                    out=self.X[:, dc, tt * 512:(tt + 1) * 512], in0=self.X[:, dc, tt * 512:(tt + 1) * 512], in1=sgm[:, gi, :], op=ALU.add),
                    [sgm_d[gi], self.Xd[dc][tt]], [self.Xd[dc][tt]])
        kb.fence()
        self.ptr = mark


_CACHE = {}


def _pack_vecs(norm_g, final_norm_g, ffn_conv_w, ffn_conv_b, ssm_d):
    v = np.zeros((NVROWS, 128), np.float32)
    v[OFF_NG:OFF_NG + 64] = np.asarray(norm_g, np.float32).reshape(64, 128)
    v[OFF_FNG:OFF_FNG + 8] = np.asarray(final_norm_g, np.float32).reshape(8, 128)
    v[OFF_CW:OFF_CW + 528] = np.asarray(ffn_conv_w, np.float32).reshape(528, 128)
    v[OFF_CB:OFF_CB + 176] = np.asarray(ffn_conv_b, np.float32).reshape(176, 128)
    v[OFF_SSMD:OFF_SSMD + 8] = np.asarray(ssm_d, np.float32).reshape(8, 128)
    return v


def run_layers(x, inputs, layers, do_final, cores=NCORES):
    key = (tuple(layers), do_final)
    if key not in _CACHE:
        _CACHE[key] = Prog(layers, do_final).build()
    nc = _CACHE[key]
    consts, consts2 = _host_consts()
    vecs = _pack_vecs(inputs["norm_g"], inputs["final_norm_g"], inputs["ffn_conv_w"], inputs["ffn_conv_b"], inputs["ssm_d"])
    shared = {"consts": consts, "consts2": consts2, "vecs": vecs}
    for n in ("sb_w_qkv", "sb_w_o", "sg_w_in", "sg_norm_g", "sg_w_s", "sg_b", "sg_w_o", "ssm_w_in", "ssm_lam_re",
              "ssm_lam_im", "ssm_log_dt", "ssm_b_re", "ssm_b_im", "ssm_c_re", "ssm_c_im", "ssm_w_glu", "ffn_w_up",
              "ffn_w_down"):
        shared[n] = np.ascontiguousarray(np.asarray(inputs[n], np.float32))
    in_maps = []
    for c in range(cores):
        m = dict(shared)
        m["x"] = np.ascontiguousarray(np.asarray(x[c], np.float32))
        in_maps.append(m)
    res = run_bass_kernel_spmd(nc, in_maps, core_ids=list(range(cores)))
    return np.stack([np.asarray(r["y"]) for r in res.results], axis=0)


def kernel(**inputs):
    x = np.asarray(inputs["x"], np.float32)
    out = run_layers(x, inputs, [0, 1, 2, 3], True)
    return out.astype(np.float32)
```

```python
import math
import os
from contextlib import ExitStack

import numpy as np
import concourse.bass as bass
import concourse.mybir as mybir
from concourse.bass_utils import run_bass_kernel_spmd

F32 = mybir.dt.float32
BF16 = mybir.dt.bfloat16
AF = mybir.ActivationFunctionType
ALU = mybir.AluOpType

P = 128
L = 2048
D = 1024
KD = 8
DFF = 2816
NJ = 22
EPS = 1e-6
NCORES = 8

OFF_NG = 0
OFF_FNG = 64
OFF_CW = 72
OFF_CB = 600
OFF_SSMD = 776
NVROWS = 896

C_IDENT = 0
C_MASKL = 128
C_TRILT = 256
C_ONES = 384
C_EPS = 512
C_NEGPI = 513
C_HALFPI = 514
MAGIC = 12582912.0
CW1 = 6.28125
CW2 = 2.0 * math.pi - CW1
PI_LO = 3.1415925
HALFPI_LO = 1.5707962
C_MNEG = 640
NCONST1 = 768
C2_BMASK = 0
C2_CMASK = 128
C2_RMASK = 640
C2_IOTA = 704
C2_REP = 1216
NCONST2 = 2240


def _host_consts():
    c = np.zeros((P, NCONST1), np.float32)
    r = np.arange(P)
    c[:, C_IDENT:C_IDENT + 128] = np.eye(P, dtype=np.float32)
    c[:, C_MASKL:C_MASKL + 128] = (r[None, :] < r[:, None]).astype(np.float32)
    c[:, C_TRILT:C_TRILT + 128] = (r[:, None] <= r[None, :]).astype(np.float32)
    c[:, C_ONES:C_ONES + 128] = 1.0
    c[:, C_EPS] = EPS
    c[:, C_NEGPI] = -math.pi
    c[:, C_HALFPI] = HALFPI_LO
    c[:, C_MNEG:C_MNEG + 128] = np.where(r[None, :] >= r[:, None], -30000.0, 0.0).astype(np.float32)
    c2 = np.zeros((P, NCONST2), np.float32)
    for g in range(64):
        q, gl = divmod(g, 8)
        c2[g, C2_REP + q * 128 + gl * 16: C2_REP + q * 128 + gl * 16 + 16] = 1.0
    for row in range(P):
        gl8 = row // 16
        c2[row, C2_BMASK + (gl8 % 2) * 64: C2_BMASK + (gl8 % 2) * 64 + 64] = 1.0
        c2[row, C2_RMASK + gl8 // 2] = 1.0
    for m in range(4):
        for row in range(P):
            gl = row // 64
            g8 = 2 * m + gl
            c2[row, C2_CMASK + m * 128 + g8 * 16: C2_CMASK + m * 128 + g8 * 16 + 16] = 1.0
    c2[:, C2_IOTA:C2_IOTA + 512] = np.arange(512, dtype=np.float32)[None, :]
    return c, c2


class Dep:
    __slots__ = ("w", "r", "name", "excl")

    def __init__(self, name, w=None, excl=False):
        self.name = name
        self.w = w
        self.r = []
        self.excl = excl


class Op:
    __slots__ = ("eng", "fn", "idx", "deps", "waits", "signal", "sigval", "dkey", "dval", "gidx")


ENGS = ("pe", "act", "dve", "pool", "sp")


class KB:
    def __init__(self, nc):
        self.nc = nc
        self.ops = {e: [] for e in ENGS}
        self.all_ops = []
        self.tiles = []
        self.fence_op = None
        self.dma_count = {}
        self.dma_group = set()

    def tile(self, name, excl=False):
        t = Dep(name, self.fence_op, excl)
        self.tiles.append(t)
        return t

    def tiles_n(self, name, *dims):
        if len(dims) == 1:
            return [self.tile(f"{name}{i}") for i in range(dims[0])]
        return [self.tiles_n(f"{name}{i}_", *dims[1:]) for i in range(dims[0])]

    def op(self, eng, fn, reads=(), writes=(), dkey=None, nd=1):
        o = Op()
        o.eng = eng
        o.fn = fn
        o.idx = len(self.ops[eng])
        o.gidx = len(self.all_ops)
        o.deps = set()
        o.waits = []
        o.signal = False
        o.sigval = 0
        o.dkey = dkey
        o.dval = 0
        if dkey is not None:
            self.dma_count[dkey] = self.dma_count.get(dkey, 0) + 16 * nd
            o.dval = self.dma_count[dkey]
        for t in reads:
            if t.w is not None:
                o.deps.add(t.w)
            if t.excl:
                for r in t.r:
                    if r.eng != eng:
                        o.deps.add(r)
        for t in writes:
            if t.w is not None:
                o.deps.add(t.w)
            for r in t.r:
                o.deps.add(r)
        for t in reads:
            t.r.append(o)
        for t in writes:
            t.w = o
            t.r = []
        o.deps.discard(o)
        self.ops[eng].append(o)
        self.all_ops.append(o)
        return o

    def dma(self, queue, out, in_, reads, writes, key, group=False):
        if group:
            self.dma_group.add(key)
        return self.op(queue, lambda e: [e.dma_start(out=out, in_=in_)], reads, writes, dkey=key)

    def dma_multi(self, queue, pieces, reads, writes, key):
        return self.op(queue, lambda e: [e.dma_start(out=o, in_=i) for (o, i) in pieces], reads, writes, dkey=key,
                       nd=len(pieces))

    def fence(self):
        deps_r, deps_w = [], []
        o = self.op("sp", lambda e: e.nop(), reads=(), writes=())
        for t in self.tiles:
            if t.w is not None:
                o.deps.add(t.w)
            for r in t.r:
                o.deps.add(r)
        o.deps.discard(o)
        self.fence_op = o
        return o

    def finalize(self, es):
        nc = self.nc
        seen = {e: {} for e in ENGS}
        for o in self.all_ops:
            sn = seen[o.eng]
            need = {}
            for d in o.deps:
                if d.dkey is not None:
                    key = ("d", d.dkey)
                    val = self.dma_count[d.dkey] if d.dkey in self.dma_group else d.dval
                    if sn.get(key, 0) >= val:
                        continue
                    if need.get(key, (0, None))[0] < val:
                        need[key] = (val, d)
                else:
                    if d.eng == o.eng:
                        if o.eng == "pe" or (o.idx - d.idx) > 3:
                            continue
                    key = ("e", d.eng)
                    if sn.get(key, -1) >= d.idx:
                        continue
                    if need.get(key, (-1, None))[0] < d.idx:
                        need[key] = (d.idx, d)
            for key, (val, d) in need.items():
                sn[key] = val
                if key[0] == "e":
                    d.signal = True
                o.waits.append((key, d))
        for e in ENGS:
            cnt = 0
            for o in self.ops[e]:
                if o.signal:
                    cnt += 1
                    o.sigval = cnt
        esem = {e: es.enter_context(nc.semaphore(f"sem_{e}")) for e in ENGS}
        dsem = {k: es.enter_context(nc.semaphore(f"dsem_{k}")) for k in self.dma_count}
        block = es.enter_context(nc.Block())
        kb = self

        def run(ename, eng):
            for o in kb.ops[ename]:
                for key, d in o.waits:
                    if key[0] == "d":
                        val = kb.dma_count[d.dkey] if d.dkey in kb.dma_group else d.dval
                        eng.wait_ge(dsem[d.dkey], val)
                    else:
                        eng.wait_ge(esem[d.eng], d.sigval)
                ins = o.fn(eng)
                if o.dkey is not None:
                    for di in ins:
                        di.then_inc(dsem[o.dkey], 16)
                elif o.signal:
                    ins.then_inc(esem[ename], 1)

        @block.tensor
        def _(e):
            run("pe", e)

        @block.scalar
        def _(e):
            run("act", e)

        @block.vector
        def _(e):
            run("dve", e)

        @block.gpsimd
        def _(e):
            run("pool", e)

        @block.sync
        def _(e):
            run("sp", e)


class Prog:
    def __init__(self, layers, do_final, x_in_tokmajor=True):
        self.layers = layers
        self.do_final = do_final

    def build(self):
        nc = bass.Bass("TRN2", target_bir_lowering=False)
        self.nc = nc
        dt = nc.dram_tensor
        self.d = {}
        shapes = {
            "x": [L, D], "consts": [P, NCONST1], "consts2": [P, NCONST2], "vecs": [NVROWS, 128],
            "sb_w_qkv": [2, D, 3 * D], "sb_w_o": [2, D, D],
            "sg_w_in": [1, D, 2 * D], "sg_norm_g": [1, D], "sg_w_s": [1, 8, 128, 128],
            "sg_b": [1, 8, 128], "sg_w_o": [1, D, D],
            "ssm_w_in": [1, D, D], "ssm_lam_re": [1, 64, 64], "ssm_lam_im": [1, 64, 64],
            "ssm_log_dt": [1, 64], "ssm_b_re": [1, 64, 64, 16], "ssm_b_im": [1, 64, 64, 16],
            "ssm_c_re": [1, 64, 16, 64], "ssm_c_im": [1, 64, 16, 64], "ssm_w_glu": [1, D, 2 * D],
            "ffn_w_up": [4, D, 2 * DFF], "ffn_w_down": [4, DFF, D],
        }
        for n, s in shapes.items():
            self.d[n] = dt(n, s, F32, kind="ExternalInput").ap()
        self.y = dt("y", [L, D], F32, kind="ExternalOutput").ap()
        self.scr = dt("s5_scratch", [2, 64, 1024], F32, kind="Internal").ap()

        with ExitStack() as es:
            self.es = es
            kb = KB(nc)
            self.kb = kb
            self.ptr = (nc._sbuf_addr_for_side("left") + 63) // 64 * 64
            self.sb_end = nc._sbuf_addr_for_side("right")
            self.uid = 0
            sb = self.alloc
            self.X = sb("X", [P, KD, L], F32)
            self.Xd = kb.tiles_n("X", KD, 4)
            self.cst = sb("cst", [P, NCONST1], F32)
            self.cst_d = kb.tile("cst")
            self.cstb = sb("cstb", [P, 512], BF16)
            self.cstb_d = kb.tile("cstb")
            self.mneg = sb("mneg", [P, 128], BF16)
            self.cols = sb("cols", [P, NVROWS], F32)
            self.cols_d = kb.tile("cols")
            self.NSLOT = 3
            self.slots = [sb(f"slot{i}", [P, 3072], BF16) for i in range(self.NSLOT)]
            self.slot_d = [kb.tile(f"slot{i}") for i in range(self.NSLOT)]
            self.slot_i = 0
            self.PF = es.enter_context(nc.psum_tensor("pf", [P, 6 * 512], F32))
            self.PB = es.enter_context(nc.psum_tensor("pb", [P, 2 * 1024], BF16))
            self.pf_d = [kb.tile(f"pf{i}", excl=True) for i in range(6)]
            self.pb_d = [kb.tile(f"pb{i}", excl=True) for i in range(2)]

            self.setup()
            self.load_x()
            for l in self.layers:
                m = l % 3
                if m == 0:
                    self.attention(l, l // 3)
                elif m == 1:
                    self.gmlp(l)
                else:
                    self.s5(l)
                if os.environ.get("NOFFN") != "1":
                    self.ffn(l)
            self.store(self.do_final)
            kb.finalize(es)
        return nc

    def alloc(self, name, shape, dtype):
        nbytes = int(np.prod(shape[1:])) * (4 if dtype == F32 else 2)
        off = (self.ptr + 63) // 64 * 64
        assert off + nbytes <= self.sb_end, f"SBUF overflow allocating {name}: {off + nbytes} > {self.sb_end}"
        self.ptr = off + nbytes
        self.uid += 1
        return self.nc.alloc_sbuf_tensor_at(f"{name}_{self.uid}", list(shape), dtype, offset=off)

    def pf(self, b, n=512, off=0):
        return self.PF[:, b * 512 + off: b * 512 + off + n]

    def pfx(self, b):
        if b < 6:
            return self.pf(b)
        return self.PB[:, (b - 6) * 1024:(b - 5) * 1024].bitcast(F32)

    def pfx_d(self, b):
        return self.pf_d[b] if b < 6 else self.pb_d[b - 6]

    def ident_f(self):
        return self.cst[:, C_IDENT:C_IDENT + 128]

    def col(self, r):
        return self.cols[:, r:r + 1]

    def phase_scope(self):
        self.kb.fence()
        ps = ExitStack()
        return ps

    def setup(self):
        kb, nc = self.kb, self.nc
        kb.dma("sp", self.cst[:], self.d["consts"], [], [self.cst_d], "cst")
        kb.dma("pool", self.cstb[:], self.d["consts"][:, 0:512], [], [self.cstb_d], "cstb")
        kb.dma("pool", self.mneg[:], self.d["consts"][:, C_MNEG:C_MNEG + 128], [], [self.cstb_d], "cstb")
        mark = self.ptr
        vst = self.alloc("vstage", [P, 7, 128], F32)
        if True:
            vd = kb.tile("vstage")
            kb.dma("sp", vst[:], self.d["vecs"].rearrange("(a p) c -> p a c", p=P), [], [vd], "vst")
            for a in range(7):
                b = a // 4
                o = (a % 4) * 128
                kb.op("pe", lambda e, a=a, b=b, o=o: e.transpose(self.pf(b, 128, o), vst[:, a, :], self.ident_f()),
                      [vd, self.cst_d], [self.pf_d[b]])
            kb.op("dve", lambda e: e.tensor_copy(out=self.cols[:, 0:512], in_=self.pf(0)), [self.pf_d[0]], [self.cols_d])
            kb.op("dve", lambda e: e.tensor_copy(out=self.cols[:, 512:896], in_=self.pf(1, 384)), [self.pf_d[1]], [self.cols_d])
            kb.fence()
        self.ptr = mark

    def load_x(self):
        kb, nc = self.kb, self.nc
        x = self.d["x"]
        mark = self.ptr
        xst = self.alloc("xst", [P, 2, D], F32)
        if True:
            xd = [kb.tile("xst0"), kb.tile("xst1")]
            for i in range(16):
                s = i % 2
                kb.dma("sp", xst[:, s, :], x[i * 128:(i + 1) * 128, :], [], [xd[s]], f"xst{s}")
                for half in range(2):
                    b = 2 * s + half
                    for kk in range(4):
                        k = half * 4 + kk
                        kb.op("pe", lambda e, s=s, k=k, b=b, kk=kk: e.transpose(
                            self.pf(b, 128, kk * 128), xst[:, s, k * 128:(k + 1) * 128], self.ident_f()),
                            [xd[s], self.cst_d], [self.pf_d[b]])
                    eng = "dve" if half == 0 else "act"
                    outap = self.X[:, half * 4:half * 4 + 4, i * 128:(i + 1) * 128]
                    inap = self.pf(b).rearrange("p (k t) -> p k t", k=4)
                    if eng == "dve":
                        fn = lambda e, outap=outap, inap=inap: e.tensor_copy(out=outap, in_=inap)
                    else:
                        fn = lambda e, outap=outap, inap=inap: e.activation(out=outap, in_=inap, func=AF.Copy)
                    kb.op(eng, fn, [self.pf_d[b]], [self.Xd[k][i // 4] for k in range(half * 4, half * 4 + 4)])
            kb.fence()
        self.ptr = mark

    def store(self, do_final):
        kb, nc = self.kb, self.nc
        mark = self.ptr
        ps = None
        if True:
            sbt = self.alloc
            xo = sbt("xo", [P, KD, 512], F32)
            xo_d = kb.tiles_n("xo", KD)
            yst = sbt("yst", [P, 2, D], F32)
            yd = [kb.tile("yst0"), kb.tile("yst1")]
            nt = self.norm_temps(ps) if do_final else None
            cnt = 0
            for tt in range(4):
                if do_final:
                    self.rmsnorm(OFF_FNG, tt, xo, 0, xo_d, nt)
                    src = lambda k, c: xo[:, k, c * 128:(c + 1) * 128]
                    srcd = lambda k: xo_d[k]
                else:
                    src = lambda k, c, tt=tt: self.X[:, k, tt * 512 + c * 128: tt * 512 + (c + 1) * 128]
                    srcd = lambda k, tt=tt: self.Xd[k][tt]
                for c in range(4):
                    i = tt * 4 + c
                    s = cnt % 2
                    cnt += 1
                    for half in range(2):
                        b = 2 * s + half
                        for kk in range(4):
                            k = half * 4 + kk
                            kb.op("pe", lambda e, k=k, c=c, b=b, kk=kk, src=src: e.transpose(
                                self.pf(b, 128, kk * 128), src(k, c), self.ident_f()),
                                [srcd(k), self.cst_d], [self.pf_d[b]])
                        outap = yst[:, s, half * 512:(half + 1) * 512]
                        if half == 0:
                            kb.op("dve", lambda e, outap=outap, b=b: e.tensor_copy(out=outap, in_=self.pf(b)),
                                  [self.pf_d[b]], [yd[s]])
                        else:
                            kb.op("act", lambda e, outap=outap, b=b: e.activation(out=outap, in_=self.pf(b), func=AF.Copy),
                                  [self.pf_d[b]], [yd[s]])
                    kb.dma("sp", self.y[i * 128:(i + 1) * 128, :], yst[:, s, :], [yd[s]], [], f"yst{s}")
            fin = kb.tile("fin")
            kb.op("sp", lambda e: e.nop(), [], [yd[0], yd[1], fin])
            kb.fence()
        self.ptr = mark

    def norm_temps(self, ps):
        nc, kb = self.nc, self.kb
        sq = self.alloc("n_sq", [P, KD, 512], BF16)
        r1 = self.alloc("n_r1", [P, 512], F32)
        rs = self.alloc("n_rs", [P, 512], F32)
        return dict(sq=sq, r1=r1, rs=rs, sq_d=kb.tile("n_sq"), r1_d=kb.tile("n_r1"), rs_d=kb.tile("n_rs"))

    def rmsnorm(self, goff, tt, out, ocol0, out_d, nt, bank=5, out_scale_eng="dve"):
        kb = self.kb
        X = self.X
        ts = slice(tt * 512, (tt + 1) * 512)
        xr = [self.Xd[k][tt] for k in range(KD)]
        kb.op("act", lambda e: e.activation(out=nt["sq"][:], in_=X[:, :, ts], func=AF.Square), xr, [nt["sq_d"]])
        ones_b = self.cstb[:, 384:512]

        def mm(e):
            ins = None
            for k in range(KD):
                ins = e.matmul(self.pf(bank), lhsT=ones_b, rhs=nt["sq"][:, k, :], start=(k == 0), stop=(k == KD - 1))
            return ins
        kb.op("pe", mm, [nt["sq_d"], self.cstb_d], [self.pf_d[bank]])
        kb.op("act", lambda e: e.activation(out=nt["r1"][:], in_=self.pf(bank), func=AF.Sqrt, bias=self.cst[:, C_EPS:C_EPS + 1],
                                            scale=1.0 / D), [self.pf_d[bank], self.cst_d], [nt["r1_d"]])
        kb.op("dve", lambda e: e.reciprocal(out=nt["rs"][:], in_=nt["r1"][:]), [nt["r1_d"]], [nt["rs_d"]])
        for k in range(KD):
            kb.op("dve", lambda e, k=k: e.scalar_tensor_tensor(
                out=out[:, k, ocol0:ocol0 + 512], in0=X[:, k, ts], scalar=self.col(goff + k), in1=nt["rs"][:],
                op0=ALU.mult, op1=ALU.mult), [self.Xd[k][tt], nt["rs_d"], self.cols_d], [out_d[k]])

    def next_slot(self):
        s = self.slot_i % self.NSLOT
        self.slot_i += 1
        return s

    def load_w(self, s, pieces):
        self.kb.dma_multi("pool", pieces, [], [self.slot_d[s]], f"slot{s}")

    def slot3(self, s, k, n):
        return self.slots[s][:, 0:k * n].rearrange("p (k n) -> p k n", k=k)

    def ffn(self, l):
        kb, nc = self.kb, self.nc
        wup = self.d["ffn_w_up"][l]
        wdn = self.d["ffn_w_down"][l]
        kb.fence()
        mark = self.ptr
        ps = None
        if True:
            sbt = self.alloc
            xn = sbt("f_xn", [P, KD, 1024], BF16)
            xn_d = [kb.tiles_n("f_xn", KD) for _ in range(2)]
            act = sbt("f_act", [P, NJ, 1024], BF16)
            act_d = kb.tiles_n("f_act", NJ, 2)
            hs = sbt("f_hs", [P, 2, 2, 2 + 1024], F32)
            hs_d = kb.tiles_n("f_hs", 2, 2, 2)
            hsh_d = kb.tiles_n("f_hsh", 2, 2)
            y0 = sbt("f_y0", [P, 3, 2, 512], F32)
            y0_d = kb.tiles_n("f_y0", 3, 2)
            sg = sbt("f_sg", [P, 3, 512], F32)
            sg_d = kb.tiles_n("f_sg", 3)
            halo = sbt("f_halo", [P, NJ, 2, 2], F32)
            halo_d = kb.tiles_n("f_halo", NJ)
            nt = self.norm_temps(ps)
            ucount = 0
            for half in range(2):
                for t2 in range(2):
                    self.rmsnorm(OFF_NG + (l * 2 + 1) * 8, half * 2 + t2, xn, t2 * 512, xn_d[t2], nt)
                pend = []

                def issue_up(j):
                    s = self.next_slot()
                    v = self.slot3(s, KD, 256)
                    self.load_w(s, [
                        (v[:, :, 0:128], wup[:, j * 128:(j + 1) * 128].rearrange("(k p) n -> p k n", p=P)),
                        (v[:, :, 128:256], wup[:, DFF + j * 128: DFF + (j + 1) * 128].rearrange("(k p) n -> p k n", p=P)),
                    ])
                    return s

                def issue_dn(dc):
                    s = self.next_slot()
                    v = self.slot3(s, NJ, 128)
                    self.load_w(s, [(v, wdn[:, dc * 128:(dc + 1) * 128].rearrange("(k p) n -> p k n", p=P))])
                    return s
                seq = [("u", j) for j in range(NJ)] + [("d", dc) for dc in range(KD)]
                PRE = self.NSLOT - 1
                slots_of = {}
                for q in range(min(PRE, len(seq))):
                    slots_of[q] = issue_up(seq[q][1]) if seq[q][0] == "u" else issue_dn(seq[q][1])
                pend_ffn = []
                for qi, (kind, j) in enumerate(seq):
                    s = slots_of[qi]
                    if kind == "u":
                        wv = self.slot3(s, KD, 256)
                        hb = j % 2
                        for ag in range(2):
                            if half == 0:
                                kb.op("pool", lambda e, hb=hb, ag=ag: e.memset(hs[:, hb, ag, 0:2], 0.0), [], [hsh_d[hb][ag]])
                            else:
                                kb.op("pool", lambda e, hb=hb, ag=ag, j=j: e.tensor_copy(out=hs[:, hb, ag, 0:2], in_=halo[:, j, ag, :]),
                                      [halo_d[j]], [hsh_d[hb][ag]])
                        for t2 in range(2):
                            pb = (ucount % 2) * 2
                            yb = ucount % 3
                            ucount += 1
                            for ag in range(2):
                                def mm(e, ag=ag, t2=t2, pb=pb, wv=wv):
                                    ins = None
                                    for k in range(KD):
                                        ins = e.matmul(self.pf(pb + ag), lhsT=wv[:, k, ag * 128:(ag + 1) * 128],
                                                       rhs=xn[:, k, t2 * 512:(t2 + 1) * 512], start=(k == 0), stop=(k == KD - 1))
                                    return ins
                                kb.op("pe", mm, [self.slot_d[s]] + xn_d[t2], [self.pf_d[pb + ag]])
                            c0 = 2 + t2 * 512
                            for ag in range(2):
                                kk = j + ag * NJ
                                w0 = self.col(OFF_CW + (l * 3 + 0) * 44 + kk)
                                w1 = self.col(OFF_CW + (l * 3 + 1) * 44 + kk)
                                w2 = self.col(OFF_CW + (l * 3 + 2) * 44 + kk)
                                bb = self.col(OFF_CB + l * 44 + kk)
                                kb.op("act", lambda e, hb=hb, ag=ag, c0=c0, pb=pb: e.activation(
                                    out=hs[:, hb, ag, c0:c0 + 512], in_=self.pf(pb + ag), func=AF.Copy),
                                    [self.pf_d[pb + ag]], [hs_d[hb][ag][t2]])
                                kb.op("act", lambda e, yb=yb, ag=ag, pb=pb, w2=w2, bb=bb: e.activation(
                                    out=y0[:, yb, ag, :], in_=self.pf(pb + ag), func=AF.Identity, bias=bb, scale=w2),
                                    [self.pf_d[pb + ag], self.cols_d], [y0_d[yb][ag]])
                                rd = [hs_d[hb][ag][t2], self.cols_d] + ([hsh_d[hb][ag]] if t2 == 0 else [hs_d[hb][ag][0]])
                                kb.op("dve", lambda e, yb=yb, ag=ag, hb=hb, c0=c0, w1=w1: e.scalar_tensor_tensor(
                                    out=y0[:, yb, ag, :], in0=hs[:, hb, ag, c0 - 1:c0 + 511], scalar=w1, in1=y0[:, yb, ag, :],
                                    op0=ALU.mult, op1=ALU.add), rd + [y0_d[yb][ag]], [y0_d[yb][ag]])
                                kb.op("dve", lambda e, yb=yb, ag=ag, hb=hb, c0=c0, w0=w0: e.scalar_tensor_tensor(
                                    out=y0[:, yb, ag, :], in0=hs[:, hb, ag, c0 - 2:c0 + 510], scalar=w0, in1=y0[:, yb, ag, :],
                                    op0=ALU.mult, op1=ALU.add), rd + [y0_d[yb][ag]], [y0_d[yb][ag]])
                            if pend_ffn:
                                pend_ffn.pop()()

                            def fin(yb=yb, j=j, t2=t2):
                                kb.op("act", lambda e: e.activation(out=sg[:, yb, :], in_=y0[:, yb, 1, :], func=AF.Silu),
                                      [y0_d[yb][1]], [sg_d[yb]])
                                kb.op("dve", lambda e: e.tensor_tensor(
                                    out=act[:, j, t2 * 512:(t2 + 1) * 512], in0=sg[:, yb, :], in1=y0[:, yb, 0, :], op=ALU.mult),
                                    [sg_d[yb], y0_d[yb][0]], [act_d[j][t2]])
                            pend_ffn.append(fin)
                        if half == 0:
                            kb.op("pool", lambda e, hb=hb, j=j: e.tensor_copy(out=halo[:, j, :, :], in_=hs[:, hb, :, 1024:1026]),
                                  [hs_d[hb][0][1], hs_d[hb][1][1]], [halo_d[j]])
                    else:
                        if pend_ffn:
                            pend_ffn.pop()()
                        dc = j
                        wv = self.slot3(s, NJ, 128)
                        for t2 in range(2):
                            bank = 4 + (t2 % 2)
                            def mm(e, t2=t2, bank=bank, wv=wv):
                                ins = None
                                for jj in range(NJ):
                                    ins = e.matmul(self.pf(bank), lhsT=wv[:, jj, :], rhs=act[:, jj, t2 * 512:(t2 + 1) * 512],
                                                   start=(jj == 0), stop=(jj == NJ - 1))
                                return ins
                            kb.op("pe", mm, [self.slot_d[s]] + [act_d[jj][t2] for jj in range(NJ)], [self.pf_d[bank]])
                            tt = half * 2 + t2
                            kb.op("dve", lambda e, dc=dc, tt=tt, bank=bank: e.tensor_tensor(
                                out=self.X[:, dc, tt * 512:(tt + 1) * 512], in0=self.X[:, dc, tt * 512:(tt + 1) * 512],
                                in1=self.pf(bank), op=ALU.add), [self.pf_d[bank], self.Xd[dc][tt]], [self.Xd[dc][tt]])
                    nq = qi + PRE
                    if nq < len(seq):
                        slots_of[nq] = issue_up(seq[nq][1]) if seq[nq][0] == "u" else issue_dn(seq[nq][1])
            kb.fence()
        self.ptr = mark

    def attention(self, l, ja):
        kb, nc = self.kb, self.nc
        wqkv = self.d["sb_w_qkv"][ja]
        wo = self.d["sb_w_o"][ja]
        kb.fence()
        mark = self.ptr
        ps = None
        if True:
            sbt = self.alloc
            xn = sbt("a_xn", [P, KD, L], BF16)
            xn_d = kb.tiles_n("a_xn", 4, KD)
            ao = sbt("a_o", [P, KD, L], BF16)
            ao_d = kb.tiles_n("a_o", KD, 4)
            qT = sbt("a_q", [P, L], BF16)
            kT = sbt("a_k", [P, L], BF16)
            vv = sbt("a_v", [P, 16, 128], BF16)
            q_d, k_d, v_d = kb.tile("a_q"), kb.tile("a_k"), kb.tile("a_v")
            NB = 4
            PW = 512
            mark2 = self.ptr
            nt = self.norm_temps(None)
            for tt in range(4):
                self.rmsnorm(OFF_NG + (l * 2 + 0) * 8, tt, xn, tt * 512, xn_d[tt], nt)
            kb.fence()
            self.ptr = mark2
            ee = sbt("a_e", [P, NB, PW], F32)
            spt = sbt("a_sp", [P, NB, PW], F32)
            lw = sbt("a_lw", [P, NB, PW], F32)
            ww = sbt("a_w", [P, NB, PW], BF16)
            wT = sbt("a_wT", [P, NB, PW], BF16)
            ones = sbt("a_ones", [P, PW], BF16)
            carry = sbt("a_carry", [P, 8], F32)
            e_d, sp_d, lw_d, w_d, wT_d = (kb.tiles_n(n, NB) for n in ("a_e", "a_sp", "a_lw", "a_w", "a_wT"))
            ones_d = kb.tile("a_ones")
            carry_d = kb.tiles_n("a_carry", 8)
            kb.op("pool", lambda e: e.memset(ones[:], 1.0), [], [ones_d])
            maskL_f = self.cst[:, C_MASKL:C_MASKL + 128]
            maskL_b = self.cstb[:, 128:256]
            ident_b = self.cstb[:, 0:128]
            pcount = 0
            ocount = 0
            ccount = 0
            gen = 0
            for pair in range(8):
                s = self.next_slot()
                wv = self.slot3(s, KD, 384)
                self.load_w(s, [(wv[:, :, c * 128:(c + 1) * 128],
                                 wqkv[:, c * D + pair * 128: c * D + (pair + 1) * 128].rearrange("(k p) n -> p k n", p=P))
                                for c in range(3)])
                for tt in range(4):
                    for c in range(2):
                        bank = 4 + (gen % 2)
                        gen += 1
                        def mm(e, c=c, tt=tt, bank=bank, wv=wv):
                            ins = None
                            for k in range(KD):
                                ins = e.matmul(self.pf(bank), lhsT=wv[:, k, c * 128:(c + 1) * 128],
                                               rhs=xn[:, k, tt * 512:(tt + 1) * 512], start=(k == 0), stop=(k == KD - 1))
                            return ins
                        kb.op("pe", mm, [self.slot_d[s]] + xn_d[tt], [self.pf_d[bank]])
                        dst = qT if c == 0 else kT
                        dd = q_d if c == 0 else k_d
                        sc = 0.125 if c == 0 else 1.0
                        kb.op("act", lambda e, dst=dst, tt=tt, bank=bank, sc=sc: e.activation(
                            out=dst[:, tt * 512:(tt + 1) * 512], in_=self.pf(bank), func=AF.Copy, scale=sc),
                            [self.pf_d[bank]], [dd])
                for g4 in range(4):
                    bank = 4 + (gen % 2)
                    gen += 1
                    def mmv(e, g4=g4, bank=bank, wv=wv):
                        ins = None
                        for c4 in range(4):
                            i = g4 * 4 + c4
                            for k in range(KD):
                                ins = e.matmul(self.pf(bank, 128, c4 * 128), lhsT=xn[:, k, i * 128:(i + 1) * 128],
                                               rhs=wv[:, k, 256:384], start=(k == 0), stop=(k == KD - 1))
                        return ins
                    kb.op("pe", mmv, [self.slot_d[s]] + xn_d[g4], [self.pf_d[bank]])
                    kb.op("dve", lambda e, g4=g4, bank=bank: e.tensor_copy(
                        out=vv[:, g4 * 4:(g4 + 1) * 4, :], in_=self.pf(bank).rearrange("p (c n) -> p c n", c=4)),
                        [self.pf_d[bank]], [v_d])
                items = []
                for hh in range(2):
                    for i in range(16):
                        t1 = (i + 1) * 128
                        pcs = []
                        ke = t1
                        while ke > 0:
                            ks = ((ke - 1) // PW) * PW
                            pcs.append((ks, ke))
                            ke = ks
                        ob = 4 + (ocount % 2)
                        ocount += 1
                        prev_cc = None
                        for pi, (ks, ke) in enumerate(pcs):
                            it = dict(hh=hh, i=i, ks=ks, ke=ke, first=(pi == 0), last=(pi == len(pcs) - 1), ob=ob,
                                      p=pcount, cin=prev_cc, cout=None)
                            pcount += 1
                            if pi < len(pcs) - 1:
                                it["cout"] = ccount % 8
                                ccount += 1
                            prev_cc = it["cout"]
                            items.append(it)

                def geo(it):
                    hh, i, ks, ke, p = it["hh"], it["i"], it["ks"], it["ke"], it["p"]
                    return 64 * hh, i * 128, (i + 1) * 128, ke - ks, p % NB, p % 4

                def f_z(it):
                    pb0, t0, t1, n, bi, zb = geo(it)
                    ks, ke = it["ks"], it["ke"]
                    zap = self.pf(zb, n)
                    if it["first"]:
                        zdiag = self.pf(zb, 128, n - 128)

                        def mmz(e):
                            e.matmul(zap, lhsT=qT[pb0:pb0 + 64, t0:t1], rhs=kT[pb0:pb0 + 64, ks:ke], start=True, stop=False)
                            return e.matmul(zdiag, lhsT=ident_b, rhs=self.mneg[:], start=False, stop=True)
                        kb.op("pe", mmz, [q_d, k_d, self.cstb_d], [self.pf_d[zb]])
                    else:
                        kb.op("pe", lambda e: e.matmul(zap, lhsT=qT[pb0:pb0 + 64, t0:t1], rhs=kT[pb0:pb0 + 64, ks:ke], start=True, stop=True),
                              [q_d, k_d], [self.pf_d[zb]])

                def f_exp(it):
                    pb0, t0, t1, n, bi, zb = geo(it)
                    zap = self.pf(zb, n)
                    kb.op("act", lambda e: e.activation(out=ee[:, bi, 0:n], in_=zap, func=AF.Exp), [self.pf_d[zb]], [e_d[bi]])

                def f_mask(it):
                    pb0, t0, t1, n, bi, zb = geo(it)
                    pass

                def f_ln(it):
                    pb0, t0, t1, n, bi, zb = geo(it)
                    kb.op("act", lambda e: e.activation(out=spt[:, bi, 0:n], in_=ee[:, bi, 0:n], func=AF.Ln, bias=1.0), [e_d[bi]], [sp_d[bi]])

                def f_scan(it):
                    pb0, t0, t1, n, bi, zb = geo(it)
                    cin, cout = it["cin"], it["cout"]
                    init = 0.0 if cin is None else carry[:, cin:cin + 1]
                    rd = [sp_d[bi], ones_d] + ([] if cin is None else [carry_d[cin]])
                    kb.op("dve", lambda e: e.tensor_tensor_scan(out=lw[:, bi, 0:n][:, ::-1], data0=ones[:, 0:n], data1=spt[:, bi, 0:n][:, ::-1],
                                                                initial=init, op0=ALU.mult, op1=ALU.add), rd, [lw_d[bi]])
                    if cout is not None:
                        kb.op("dve", lambda e: e.tensor_copy(out=carry[:, cout:cout + 1], in_=lw[:, bi, 0:1]), [lw_d[bi]], [carry_d[cout]])

                def f_er(it):
                    pb0, t0, t1, n, bi, zb = geo(it)
                    kb.op("act", lambda e: e.activation(out=lw[:, bi, 0:n], in_=lw[:, bi, 0:n], func=AF.Exp, scale=-1.0), [lw_d[bi]], [lw_d[bi]])

                def f_mult(it):
                    pb0, t0, t1, n, bi, zb = geo(it)
                    kb.op("dve", lambda e: e.tensor_tensor(out=ww[:, bi, 0:n], in0=ee[:, bi, 0:n], in1=lw[:, bi, 0:n], op=ALU.mult),
                          [e_d[bi], lw_d[bi]], [w_d[bi]])

                def f_tr(it):
                    pb0, t0, t1, n, bi, zb = geo(it)
                    tb = it["p"] % 2
                    nb_ = n // 128

                    def trs(e):
                        ins = None
                        for b in range(nb_):
                            ins = e.transpose(self.PB[:, tb * 1024 + b * 128: tb * 1024 + (b + 1) * 128], ww[:, bi, b * 128:(b + 1) * 128], ident_b)
                        return ins
                    kb.op("pe", trs, [w_d[bi], self.cstb_d], [self.pb_d[tb]])

                def f_copy(it):
                    pb0, t0, t1, n, bi, zb = geo(it)
                    tb = it["p"] % 2
                    if it["p"] % 3 == 2:
                        kb.op("dve", lambda e: e.tensor_copy(out=wT[:, bi, 0:n], in_=self.PB[:, tb * 1024: tb * 1024 + n]),
                              [self.pb_d[tb]], [wT_d[bi]])
                    else:
                        kb.op("act", lambda e: e.activation(out=wT[:, bi, 0:n], in_=self.PB[:, tb * 1024: tb * 1024 + n], func=AF.Copy),
                              [self.pb_d[tb]], [wT_d[bi]])

                def f_mmo(it, pair=pair):
                    pb0, t0, t1, n, bi, zb = geo(it)
                    ob, i = it["ob"], it["i"]
                    nb_ = n // 128
                    kb0 = it["ks"] // 128
                    first, last = it["first"], it["last"]

                    def mmo(e):
                        ins = None
                        for b in range(nb_):
                            ins = e.matmul(self.pf(ob, 128), lhsT=vv[:, kb0 + b, :], rhs=wT[:, bi, b * 128:(b + 1) * 128],
                                           start=(first and b == 0), stop=(last and b == nb_ - 1))
                        return ins
                    kb.op("pe", mmo, [wT_d[bi], v_d], [self.pf_d[ob]])
                    if last:
                        kb.op("act", lambda e: e.activation(out=ao[pb0:pb0 + 64, pair, t0:t1], in_=self.PF[pb0:pb0 + 64, ob * 512: ob * 512 + 128],
                                                            func=AF.Copy), [self.pf_d[ob]], [ao_d[pair][i // 4]])
                sched = [(0, f_z), (0, f_exp), (0, f_mask), (2, f_er), (5, f_copy), (0, f_ln), (1, f_scan), (3, f_mult), (4, f_tr), (6, f_mmo)]
                nit = len(items)
                for step in range(nit + 6):
                    for lag, fn in sched:
                        j = step - lag
                        if 0 <= j < nit:
                            fn(items[j])
            self.proj_residual(ao, lambda k, tt: ao_d[k][tt], wo)
            kb.fence()
        self.ptr = mark

    def proj_residual(self, src, src_d, wdram):
        kb = self.kb
        for dc in range(KD):
            s = self.next_slot()
            wv = self.slot3(s, KD, 128)
            self.load_w(s, [(wv, wdram[:, dc * 128:(dc + 1) * 128].rearrange("(k p) n -> p k n", p=P))])
            for tt in range(4):
                bank = tt % 4

                def mm(e, tt=tt, bank=bank, wv=wv):
                    ins = None
                    for k in range(KD):
                        ins = e.matmul(self.pf(bank), lhsT=wv[:, k, :], rhs=src[:, k, tt * 512:(tt + 1) * 512],
                                       start=(k == 0), stop=(k == KD - 1))
                    return ins
                kb.op("pe", mm, [self.slot_d[s]] + [src_d(k, tt) for k in range(KD)], [self.pf_d[bank]])
                kb.op("dve", lambda e, dc=dc, tt=tt, bank=bank: e.tensor_tensor(
                    out=self.X[:, dc, tt * 512:(tt + 1) * 512], in0=self.X[:, dc, tt * 512:(tt + 1) * 512],
                    in1=self.pf(bank), op=ALU.add), [self.pf_d[bank], self.Xd[dc][tt]], [self.Xd[dc][tt]])

    def gmlp(self, l):
        kb, nc = self.kb, self.nc
        win = self.d["sg_w_in"][0]
        wo = self.d["sg_w_o"][0]
        kb.fence()
        mark = self.ptr
        sbt = self.alloc
        xn = sbt("g_xn", [P, KD, L], BF16)
        xn_d = kb.tiles_n("g_xn", 4, KD)
        vn = sbt("g_vn", [P, 16, D], BF16)
        vn_d = kb.tiles_n("g_vn", 16)
        wsT = sbt("g_wsT", [P, 8, 128], BF16)
        wsT_d = kb.tile("g_wsT")
        bsr = sbt("g_bsr", [1, D], BF16)
        bsr_d = kb.tile("g_bsr")
        ssq = sbt("g_ssq", [P, 16], F32)
        rstd = sbt("g_rstd", [P, 16], F32)
        ssq_d, rstd_d = kb.tile("g_ssq"), kb.tiles_n("g_rstd", 16)
        mark2 = self.ptr
        nt = self.norm_temps(None)
        for tt in range(4):
            self.rmsnorm(OFF_NG + (l * 2 + 0) * 8, tt, xn, tt * 512, xn_d[tt], nt)
        kb.fence()
        self.ptr = mark2
        wv_ = sbt("g_wv", [P, KD, D], BF16)
        wv_d = kb.tile("g_wv")
        vg = sbt("g_vg", [P, 2, D], F32)
        vg_d = kb.tiles_n("g_vg", 2)
        junk = sbt("g_junk", [P, D], BF16)
        junk_d = kb.tile("g_junk")
        gbc = sbt("g_gbc", [P, D], F32)
        gbc_d = kb.tile("g_gbc")
        grow = sbt("g_grow", [1, D], F32)
        grow_d = kb.tile("g_grow")
        wsf = sbt("g_wsf", [P, 8, 128], F32)
        wsf_d = kb.tile("g_wsf")
        kb.dma_multi("pool", [(wv_[:, :, c * 256:(c + 1) * 256],
                               win[:, D + c * 256: D + (c + 1) * 256].rearrange("(k p) n -> p k n", p=P)) for c in range(4)],
                     [], [wv_d], "g_wv")
        kb.dma("pool", bsr[:], self.d["sg_b"].rearrange("a g t -> a (g t)"), [], [bsr_d], "g_bsr")
        kb.dma("sp", grow[:], self.d["sg_norm_g"], [], [grow_d], "g_grow")
        kb.dma("sp", wsf[:], self.d["sg_w_s"][0].rearrange("g t s -> t g s"), [], [wsf_d], "g_wsf")
        kb.op("dve", lambda e: e.memset(ssq[:], 0.0), [], [ssq_d])
        ones_row_f = self.cst[0:1, C_ONES:C_ONES + 128]
        for nh in range(2):
            kb.op("pe", lambda e, nh=nh: e.matmul(self.pf(4 + nh), lhsT=ones_row_f, rhs=grow[0:1, nh * 512:(nh + 1) * 512],
                                                 start=True, stop=True), [grow_d, self.cst_d], [self.pf_d[4 + nh]])
            kb.op("dve", lambda e, nh=nh: e.tensor_copy(out=gbc[:, nh * 512:(nh + 1) * 512], in_=self.pf(4 + nh)),
                  [self.pf_d[4 + nh]], [gbc_d])
        trilT = self.cst[:, C_TRILT:C_TRILT + 128]
        for g in range(8):
            bank = 4 + g // 4
            kb.op("pe", lambda e, g=g, bank=bank: e.transpose(self.pf(bank, 128, (g % 4) * 128), wsf[:, g, :], self.ident_f()),
                  [wsf_d, self.cst_d], [self.pf_d[bank]])
            kb.op("dve", lambda e, g=g, bank=bank: e.tensor_tensor(out=wsT[:, g, :], in0=self.pf(bank, 128, (g % 4) * 128), in1=trilT,
                                                                   op=ALU.mult), [self.pf_d[bank], self.cst_d], [wsT_d])
        for i in range(16):
            b0 = 2 * (i % 2)
            vb = i % 2
            for nh in range(2):
                def mm(e, i=i, nh=nh, b0=b0):
                    ins = None
                    for k in range(KD):
                        ins = e.matmul(self.pf(b0 + nh), lhsT=xn[:, k, i * 128:(i + 1) * 128], rhs=wv_[:, k, nh * 512:(nh + 1) * 512],
                                       start=(k == 0), stop=(k == KD - 1))
                    return ins
                kb.op("pe", mm, [wv_d] + xn_d[i // 4], [self.pf_d[b0 + nh]])
            kb.op("act", lambda e, vb=vb, b0=b0: e.activation(out=vg[:, vb, :], in_=self.PF[:, b0 * 512: b0 * 512 + 1024],
                                                              func=AF.Gelu_apprx_tanh), [self.pf_d[b0], self.pf_d[b0 + 1]], [vg_d[vb]])
            kb.op("act", lambda e, vb=vb, i=i: e.activation(out=junk[:], in_=vg[:, vb, :], func=AF.Square, accum_out=ssq[:, i:i + 1]),
                  [vg_d[vb], ssq_d], [junk_d, rstd_d[i]])
            kb.op("act", lambda e, i=i: e.activation(out=rstd[:, i:i + 1], in_=ssq[:, i:i + 1], func=AF.Sqrt,
                                                     bias=self.cst[:, C_EPS:C_EPS + 1], scale=1.0 / D), [rstd_d[i], self.cst_d], [rstd_d[i]])
            kb.op("dve", lambda e, i=i: e.reciprocal(out=rstd[:, i:i + 1], in_=rstd[:, i:i + 1]), [rstd_d[i]], [rstd_d[i]])
            kb.op("dve", lambda e, i=i, vb=vb: e.scalar_tensor_tensor(out=vn[:, i, :], in0=vg[:, vb, :], scalar=rstd[:, i:i + 1], in1=gbc[:],
                                                                    op0=ALU.mult, op1=ALU.mult), [vg_d[vb], rstd_d[i], gbc_d], [vn_d[i]])
        kb.fence()
        self.ptr = mark2
        gated = sbt("g_gated", [P, KD, L], BF16)
        gated_d = kb.tiles_n("g_gated", KD, 4)
        ug = sbt("g_ug", [P, 2, 512], F32)
        ug_d = kb.tiles_n("g_ug", 2)
        ones_row_b = self.cstb[0:1, 384:512]
        cnt = 0
        for g in range(8):
            s = self.next_slot()
            wu = self.slot3(s, KD, 128)
            self.load_w(s, [(wu, win[:, g * 128:(g + 1) * 128].rearrange("(k p) n -> p k n", p=P))])
            for tt in range(4):
                sb_ = (cnt % 2) * 2
                ub_ = sb_ + 1
                ui = cnt % 2
                cnt += 1

                def mmsv(e, g=g, tt=tt, sb_=sb_):
                    ins = None
                    for c in range(4):
                        o = self.pf(sb_, 128, c * 128)
                        e.matmul(o, lhsT=vn[:, 4 * tt + c, g * 128:(g + 1) * 128], rhs=wsT[:, g, :], start=True, stop=False)
                        ins = e.matmul(o, lhsT=ones_row_b, rhs=bsr[0:1, g * 128:(g + 1) * 128], start=False, stop=True)
                    return ins
                kb.op("pe", mmsv, [vn_d[4 * tt + c] for c in range(4)] + [wsT_d, bsr_d, self.cstb_d], [self.pf_d[sb_]])

                def mmu(e, tt=tt, ub_=ub_, wu=wu):
                    ins = None
                    for k in range(KD):
                        ins = e.matmul(self.pf(ub_), lhsT=wu[:, k, :], rhs=xn[:, k, tt * 512:(tt + 1) * 512], start=(k == 0), stop=(k == KD - 1))
                    return ins
                kb.op("pe", mmu, [self.slot_d[s]] + xn_d[tt], [self.pf_d[ub_]])
                kb.op("act", lambda e, ui=ui, ub_=ub_: e.activation(out=ug[:, ui, :], in_=self.pf(ub_), func=AF.Gelu_apprx_tanh),
                      [self.pf_d[ub_]], [ug_d[ui]])
                kb.op("dve", lambda e, g=g, tt=tt, ui=ui, sb_=sb_: e.tensor_tensor(
                    out=gated[:, g, tt * 512:(tt + 1) * 512], in0=ug[:, ui, :], in1=self.pf(sb_), op=ALU.mult),
                    [ug_d[ui], self.pf_d[sb_]], [gated_d[g][tt]])
        self.proj_residual(gated, lambda k, tt: gated_d[k][tt], wo)
        kb.fence()
        self.ptr = mark

    def s5(self, l):
        kb, nc = self.kb, self.nc
        d = self.d
        win = d["ssm_w_in"][0]
        wglu = d["ssm_w_glu"][0]
        PI = math.pi
        kb.fence()
        mark = self.ptr
        sbt = self.alloc
        xn = sbt("s_xn", [P, KD, L], BF16)
        xn_d = kb.tiles_n("s_xn", 4, KD)
        yg = sbt("s_yg", [P, KD, L], BF16)
        yg_d = kb.tiles_n("s_yg", KD, 4)
        c2 = sbt("s_c2", [P, C2_REP], F32)
        c2_d = kb.tile("s_c2")
        bbt = sbt("s_bbt", [P, 2, 8, 64], BF16)
        bbt_d = kb.tile("s_bbt")
        ct = sbt("s_ct", [P, 2, 8, 64], BF16)
        ct_d = kb.tile("s_ct")
        svr = sbt("s_svr", [P, 32], F32)
        svt = sbt("s_svt", [P, 32], F32)
        svc = sbt("s_svc", [P, 32], F32)
        svs = sbt("s_svs", [P, 32], F32)
        svn = sbt("s_svn", [P, 32], F32)
        ctmp = sbt("s_ctmp", [P, 2], F32)
        ctmp_d = kb.tile("s_ctmp")
        sv_d = kb.tile("s_sv")
        carry = sbt("s_carry", [P, 2, 32], F32)
        carry_d = kb.tiles_n("s_carry", 32)
        kb.dma("sp", c2[:], d["consts2"][:, 0:C2_REP], [], [c2_d], "s_c2")
        kb.dma("pool", ct[:, 0, :, :], d["ssm_c_re"][0].rearrange("(q gl) h p -> (gl h) q p", q=8), [], [ct_d], "s_ct")
        kb.dma("pool", ct[:, 1, :, :], d["ssm_c_im"][0].rearrange("(q gl) h p -> (gl h) q p", q=8), [], [ct_d], "s_ct")
        mark2 = self.ptr
        nt = self.norm_temps(None)
        for tt in range(4):
            self.rmsnorm(OFF_NG + (l * 2 + 0) * 8, tt, xn, tt * 512, xn_d[tt], nt)
        kb.fence()
        self.ptr = mark2
        bn = sbt("s_bn", [64, 2, 1024], F32)
        bn_d = kb.tile("s_bn")
        kb.dma("sp", bn[:, 0, :], d["ssm_b_re"][0].rearrange("g p h -> g (p h)"), [], [bn_d], "s_bn")
        kb.dma("sp", bn[:, 1, :], d["ssm_b_im"][0].rearrange("g p h -> g (p h)"), [], [bn_d], "s_bn")
        NT = 26
        tp = sbt("s_tp", [64, NT, 64], F32)
        tp_d = kb.tiles_n("s_tp", NT)
        dtc = sbt("s_dt", [64, 2], F32)
        dt_d = kb.tile("s_dt")
        (LR, LI, DLR, DLI, MAG, SA, CA, SN, CS, AR, AI, T1, T2, DEN, RDEN, AM1, U1, U2, CR, CI) = range(20)
        T = lambda i: tp[:, i, :]
        kb.dma("sp", T(LR), d["ssm_lam_re"][0], [], [tp_d[LR]], "s_p0")
        kb.dma("sp", T(LI), d["ssm_lam_im"][0], [], [tp_d[LI]], "s_p1")
        kb.dma("sp", dtc[:, 0:1], d["ssm_log_dt"].rearrange("a g -> g a"), [], [dt_d], "s_p2")
        kb.op("act", lambda e: e.activation(out=dtc[:, 1:2], in_=dtc[:, 0:1], func=AF.Exp), [dt_d], [dt_d])
        dtcol = dtc[:, 1:2]

        def v1(eng, fn, r, w):
            kb.op(eng, fn, [tp_d[i] for i in r] + [dt_d, self.cst_d], [tp_d[i] for i in w])
        v1("dve", lambda e: e.tensor_scalar_min(out=T(LR), in0=T(LR), scalar1=-1e-4), [LR], [LR])
        v1("dve", lambda e: e.tensor_scalar_mul(out=T(DLR), in0=T(LR), scalar1=dtcol), [LR], [DLR])
        v1("dve", lambda e: e.tensor_scalar_mul(out=T(DLI), in0=T(LI), scalar1=dtcol), [LI], [DLI])
        v1("act", lambda e: e.activation(out=T(MAG), in_=T(DLR), func=AF.Exp), [DLR], [MAG])
        hpi64 = self.cst[0:64, C_HALFPI:C_HALFPI + 1]
        v1("dve", lambda e: e.tensor_scalar(out=T(T1), in0=T(DLI), scalar1=1.0 / (2 * PI), scalar2=MAGIC, op0=ALU.mult, op1=ALU.add), [DLI], [T1])
        v1("dve", lambda e: e.tensor_scalar_add(out=T(T1), in0=T(T1), scalar1=-MAGIC), [T1], [T1])
        v1("dve", lambda e: e.scalar_tensor_tensor(out=T(SA), in0=T(T1), scalar=-CW1, in1=T(DLI), op0=ALU.mult, op1=ALU.add), [T1, DLI], [SA])
        v1("dve", lambda e: e.scalar_tensor_tensor(out=T(SA), in0=T(T1), scalar=-CW2, in1=T(SA), op0=ALU.mult, op1=ALU.add), [T1, SA], [SA])
        v1("dve", lambda e: e.tensor_scalar(out=T(SA), in0=T(SA), scalar1=-PI_LO, scalar2=PI_LO, op0=ALU.max, op1=ALU.min), [SA], [SA])
        v1("dve", lambda e: e.scalar_tensor_tensor(out=T(CA), in0=T(SA), scalar=-1.0, in1=T(SA), op0=ALU.mult, op1=ALU.max), [SA], [CA])
        v1("act", lambda e: e.activation(out=T(SN), in_=T(SA), func=AF.Sin), [SA], [SN])
        v1("act", lambda e: e.activation(out=T(CS), in_=T(CA), func=AF.Sin, scale=-1.0, bias=hpi64), [CA], [CS])
        v1("dve", lambda e: e.tensor_tensor(out=T(AR), in0=T(MAG), in1=T(CS), op=ALU.mult), [MAG, CS], [AR])
        v1("dve", lambda e: e.tensor_tensor(out=T(AI), in0=T(MAG), in1=T(SN), op=ALU.mult), [MAG, SN], [AI])
        v1("dve", lambda e: e.tensor_tensor(out=T(T1), in0=T(LR), in1=T(LR), op=ALU.mult), [LR], [T1])
        v1("dve", lambda e: e.tensor_tensor(out=T(T2), in0=T(LI), in1=T(LI), op=ALU.mult), [LI], [T2])
        v1("dve", lambda e: e.tensor_tensor(out=T(DEN), in0=T(T1), in1=T(T2), op=ALU.add), [T1, T2], [DEN])
        v1("dve", lambda e: e.reciprocal(out=T(RDEN), in_=T(DEN)), [DEN], [RDEN])
        v1("dve", lambda e: e.tensor_scalar_add(out=T(AM1), in0=T(AR), scalar1=-1.0), [AR], [AM1])
        v1("dve", lambda e: e.tensor_tensor(out=T(U1), in0=T(AM1), in1=T(LR), op=ALU.mult), [AM1, LR], [U1])
        v1("dve", lambda e: e.tensor_tensor(out=T(U2), in0=T(AI), in1=T(LI), op=ALU.mult), [AI, LI], [U2])
        v1("dve", lambda e: e.tensor_tensor(out=T(U1), in0=T(U1), in1=T(U2), op=ALU.add), [U1, U2], [U1])
        v1("dve", lambda e: e.tensor_tensor(out=T(CR), in0=T(U1), in1=T(RDEN), op=ALU.mult), [U1, RDEN], [CR])
        v1("dve", lambda e: e.tensor_tensor(out=T(U1), in0=T(AI), in1=T(LR), op=ALU.mult), [AI, LR, CR], [U1])
        v1("dve", lambda e: e.tensor_tensor(out=T(U2), in0=T(AM1), in1=T(LI), op=ALU.mult), [AM1, LI], [U2])
        v1("dve", lambda e: e.tensor_tensor(out=T(U1), in0=T(U1), in1=T(U2), op=ALU.subtract), [U1, U2], [U1])
        v1("dve", lambda e: e.tensor_tensor(out=T(CI), in0=T(U1), in1=T(RDEN), op=ALU.mult), [U1, RDEN], [CI])
        bbn = sbt("s_bbn", [64, 2, 1024], F32)
        bbn_d = kb.tile("s_bbn")
        tmpn = sbt("s_tmpn", [64, 1024], F32)
        tmpn_d = kb.tile("s_tmpn")
        R_, I_ = 0, 1
        cb = lambda i: T(i).unsqueeze(2).broadcast_to([64, 64, 16])
        nat = lambda ri: bn[:, ri, :].rearrange("g (p h) -> g p h", h=16)
        outv = lambda t_: t_.rearrange("g (h p) -> g p h", h=16)
        kb.op("dve", lambda e: e.tensor_tensor(out=outv(bbn[:, R_, :]), in0=cb(CR), in1=nat(R_), op=ALU.mult), [tp_d[CR], bn_d], [bbn_d])
        kb.op("dve", lambda e: e.tensor_tensor(out=outv(tmpn[:]), in0=cb(CI), in1=nat(I_), op=ALU.mult), [tp_d[CI], bn_d], [tmpn_d])
        kb.op("dve", lambda e: e.tensor_tensor(out=bbn[:, R_, :], in0=bbn[:, R_, :], in1=tmpn[:], op=ALU.subtract), [bbn_d, tmpn_d], [bbn_d])
        kb.op("dve", lambda e: e.tensor_tensor(out=outv(bbn[:, I_, :]), in0=cb(CR), in1=nat(I_), op=ALU.mult), [tp_d[CR], bn_d], [bbn_d])
        kb.op("dve", lambda e: e.tensor_tensor(out=outv(tmpn[:]), in0=cb(CI), in1=nat(R_), op=ALU.mult), [tp_d[CI], bn_d, bbn_d], [tmpn_d])
        kb.op("dve", lambda e: e.tensor_tensor(out=bbn[:, I_, :], in0=bbn[:, I_, :], in1=tmpn[:], op=ALU.add), [bbn_d, tmpn_d], [bbn_d])
        scr_d = kb.tile("s_scr")
        kb.dma("sp", self.scr.rearrange("r g n -> g r n"), bbn[:], [bbn_d], [scr_d], "s_scr_w")
        for ri in range(2):
            kb.dma("pool", bbt[:, ri, :, :], self.scr[ri].rearrange("(q gl) (h p) -> (gl h) q p", q=8, h=16), [scr_d], [bbt_d], "s_scr_r")
        A5, K5, R5, B5, C5, S5_ = 20, 21, 22, 23, 24, 25
        v1("dve", lambda e: e.tensor_scalar_mul(out=T(A5), in0=T(DLI), scalar1=512.0), [DLI], [A5])
        v1("dve", lambda e: e.tensor_scalar(out=T(K5), in0=T(A5), scalar1=1.0 / (2 * PI), scalar2=MAGIC, op0=ALU.mult, op1=ALU.add), [A5], [K5])
        v1("dve", lambda e: e.tensor_scalar_add(out=T(K5), in0=T(K5), scalar1=-MAGIC), [K5], [K5])
        v1("dve", lambda e: e.scalar_tensor_tensor(out=T(R5), in0=T(K5), scalar=-CW1, in1=T(A5), op0=ALU.mult, op1=ALU.add), [K5, A5], [R5])
        v1("dve", lambda e: e.scalar_tensor_tensor(out=T(R5), in0=T(K5), scalar=-CW2, in1=T(R5), op0=ALU.mult, op1=ALU.add), [K5, R5], [R5])
        v1("dve", lambda e: e.tensor_scalar(out=T(R5), in0=T(R5), scalar1=-PI_LO, scalar2=PI_LO, op0=ALU.max, op1=ALU.min), [R5], [R5])
        v1("dve", lambda e: e.scalar_tensor_tensor(out=T(B5), in0=T(R5), scalar=-1.0, in1=T(R5), op0=ALU.mult, op1=ALU.max), [R5], [B5])
        v1("act", lambda e: e.activation(out=T(S5_), in_=T(R5), func=AF.Sin), [R5], [S5_])
        v1("act", lambda e: e.activation(out=T(C5), in_=T(B5), func=AF.Sin, scale=-1.0, bias=hpi64), [B5], [C5])
        dup = sbt("s_dup", [64, 4, 128], F32)
        dup_d = kb.tile("s_dup")
        for qi, (src, dst) in enumerate(((MAG, svr), (DLI, svt), (C5, svc), (S5_, svs))):
            kb.op("dve", lambda e, qi=qi, src=src: e.tensor_copy(out=dup[:, qi, 0:64], in_=T(src)), [tp_d[src]], [dup_d])
            kb.op("dve", lambda e, qi=qi, src=src: e.tensor_copy(out=dup[:, qi, 64:128], in_=T(src)), [tp_d[src]], [dup_d])
            bank = 2 + qi
            kb.op("pe", lambda e, qi=qi, bank=bank: e.transpose(self.pf(bank, 64), dup[:, qi, :], self.cst[0:64, C_IDENT:C_IDENT + 64]),
                  [dup_d, self.cst_d], [self.pf_d[bank]])
            kb.op("dve", lambda e, dst=dst, bank=bank: e.tensor_copy(out=dst[0:64, :], in_=self.PF[0:64, bank * 512: bank * 512 + 64: 2]),
                  [self.pf_d[bank]], [sv_d])
            kb.op("dve", lambda e, dst=dst, bank=bank: e.tensor_copy(out=dst[64:128, :], in_=self.PF[64:128, bank * 512 + 1: bank * 512 + 64: 2]),
                  [self.pf_d[bank]], [sv_d])
        kb.op("dve", lambda e: e.tensor_scalar_mul(out=svn[:], in0=svs[:], scalar1=-1.0), [sv_d], [sv_d])
        kb.op("dve", lambda e: e.memset(carry[:], 0.0), [], carry_d)
        kb.fence()
        self.ptr = mark2
        u32 = sbt("s_u32", [P, L], F32)
        u32_d = kb.tiles_n("s_u32", 4)
        ub = sbt("s_ub", [P, L], BF16)
        ub_d = kb.tiles_n("s_ub", 4)
        bwm = sbt("s_bwm", [P, 2, 4, 128], BF16)
        bwm_d = kb.tile("s_bwm")
        bwf = sbt("s_bwf", [P, 2, 128], F32)
        bwf_d = kb.tile("s_bwf")
        cw = sbt("s_cw", [P, 2, 4, 128], BF16)
        cw_d = kb.tile("s_cw")
        ctd, ctd_d = bwf, bwf_d
        tabs = sbt("s_tabs", [P, 2, 2, 512], F32)
        tabs_d = kb.tiles_n("s_tabs", 2, 2)
        rbt = sbt("s_rbt", [P, 512], F32)
        rbt_d = kb.tile("s_rbt")
        un = sbt("s_un", [P, 2, 512], F32)
        un_d = kb.tiles_n("s_un", 2)
        wk = sbt("s_wk", [P, 4, 512], F32)
        wk_d = kb.tiles_n("s_wk", 4)
        xrb = sbt("s_xr", [P, 2, 512], BF16)
        xr_d = kb.tiles_n("s_xr", 2)
        iota = c2[:, C2_IOTA:C2_IOTA + 512]
        bmask = c2[:, C2_BMASK:C2_BMASK + 128]
        hpi = self.cst[:, C_HALFPI:C_HALFPI + 1]
        W = lambda i: wk[:, i, :]
        ysum, ysum_d = wk[:, 0, :], wk_d[0]
        YB = [4, 5, 6, 7]

        def tables(j):
            par = j % 2
            tS, tC, tK = tabs[:, par, 0, :], tabs[:, par, 1, :], wk[:, 3, :]
            dS, dC = tabs_d[par]
            dK = wk_d[3]
            th = svt[:, j:j + 1]
            kb.op("dve", lambda e: e.tensor_scalar_mul(out=tC, in0=iota, scalar1=th), [c2_d, sv_d], [dC])
            kb.op("dve", lambda e: e.tensor_scalar(out=tK, in0=tC, scalar1=1.0 / (2 * PI), scalar2=MAGIC, op0=ALU.mult, op1=ALU.add), [dC], [dK])
            kb.op("dve", lambda e: e.tensor_scalar_add(out=tK, in0=tK, scalar1=-MAGIC), [dK], [dK])
            kb.op("dve", lambda e: e.scalar_tensor_tensor(out=tS, in0=tK, scalar=-CW1, in1=tC, op0=ALU.mult, op1=ALU.add), [dK, dC], [dS])
            kb.op("dve", lambda e: e.scalar_tensor_tensor(out=tS, in0=tK, scalar=-CW2, in1=tS, op0=ALU.mult, op1=ALU.add), [dK, dS], [dS])
            kb.op("dve", lambda e: e.tensor_scalar(out=tS, in0=tS, scalar1=-PI_LO, scalar2=PI_LO, op0=ALU.max, op1=ALU.min), [dS], [dS])
            kb.op("dve", lambda e: e.scalar_tensor_tensor(out=tC, in0=tS, scalar=-1.0, in1=tS, op0=ALU.mult, op1=ALU.max), [dS], [dC])
            kb.op("act", lambda e: e.activation(out=tS, in_=tS, func=AF.Sin), [dS], [dS])
            kb.op("act", lambda e: e.activation(out=tC, in_=tC, func=AF.Sin, scale=-1.0, bias=hpi), [dC, self.cst_d], [dC])

        def rho_table(j):
            rho = svr[:, j:j + 1]
            kb.op("act", lambda e: e.activation(out=rbt[:], in_=iota, func=AF.Identity, bias=rho, scale=0.0), [c2_d, sv_d], [rbt_d])

        ucount = [0]

        def stage_a(jj, tt, up):
            ts = slice(tt * 512, (tt + 1) * 512)
            for ri in range(2):
                bk = 2 * up + ri
                kb.op("pe", lambda e, ri=ri, bk=bk: e.matmul(self.pf(bk), lhsT=bwm[:, ri, jj, :], rhs=ub[:, ts], start=True, stop=True),
                      [bwm_d, ub_d[tt]], [self.pf_d[bk]])

        def stage_bcd(j, jj, tt, up):
            par = j % 2
            tS, tC, tK = tabs[:, par, 0, :], tabs[:, par, 1, :], rbt[:]
            dS, dC = tabs_d[par]
            dK = rbt_d
            brp, bip = self.pf(2 * up), self.pf(2 * up + 1)
            dbr, dbi = self.pf_d[2 * up], self.pf_d[2 * up + 1]
            YR, YI = un[:, 0, :], un[:, 1, :]

            def tt_(eng, o, od, a, ad, b, bd, op):
                kb.op(eng, lambda e: e.tensor_tensor(out=o, in0=a, in1=b, op=op), [ad, bd], [od])
            tt_("dve", W(0), wk_d[0], brp, dbr, tC, dC, ALU.mult)
            tt_("dve", W(1), wk_d[1], bip, dbi, tS, dS, ALU.mult)
            tt_("pool", W(0), wk_d[0], W(0), wk_d[0], W(1), wk_d[1], ALU.add)
            tt_("dve", W(2), wk_d[2], bip, dbi, tC, dC, ALU.mult)
            tt_("dve", W(3), wk_d[3], brp, dbr, tS, dS, ALU.mult)
            tt_("dve", W(2), wk_d[2], W(2), wk_d[2], W(3), wk_d[3], ALU.subtract)
            for ri, src in ((0, 0), (1, 2)):
                kb.op("dve", lambda e, ri=ri, src=src: e.tensor_tensor_scan(
                    out=un[:, ri, :], data0=tK, data1=W(src), initial=carry[:, ri, j:j + 1], op0=ALU.mult, op1=ALU.add),
                    [dK, wk_d[src], carry_d[j]], [un_d[ri]])
            if tt < 3:
                yrl, yil = un[:, 0, 511:512], un[:, 1, 511:512]
                cc, ss, ns = svc[:, j:j + 1], svs[:, j:j + 1], svn[:, j:j + 1]
                kb.op("dve", lambda e: e.tensor_scalar_mul(out=ctmp[:, 0:1], in0=yrl, scalar1=cc), [un_d[0], sv_d], [ctmp_d])
                kb.op("dve", lambda e: e.tensor_scalar_mul(out=ctmp[:, 1:2], in0=yrl, scalar1=ss), [un_d[0], sv_d, ctmp_d], [ctmp_d])
                kb.op("dve", lambda e: e.scalar_tensor_tensor(out=carry[:, 0, j:j + 1], in0=yil, scalar=ns, in1=ctmp[:, 0:1], op0=ALU.mult, op1=ALU.add),
                      [un_d[1], sv_d, ctmp_d], [carry_d[j]])
                kb.op("dve", lambda e: e.scalar_tensor_tensor(out=carry[:, 1, j:j + 1], in0=yil, scalar=cc, in1=ctmp[:, 1:2], op0=ALU.mult, op1=ALU.add),
                      [un_d[1], sv_d, ctmp_d, carry_d[j]], [carry_d[j]])
            tt_("dve", W(0), wk_d[0], tC, dC, YR, un_d[0], ALU.mult)
            tt_("dve", W(1), wk_d[1], tS, dS, YI, un_d[1], ALU.mult)
            kb.op("dve", lambda e: e.tensor_tensor(out=xrb[:, 0, :], in0=W(0), in1=W(1), op=ALU.subtract), [wk_d[0], wk_d[1]], [xr_d[0]])
            tt_("dve", W(3), wk_d[3], tS, dS, YR, un_d[0], ALU.mult)
            tt_("dve", W(2), wk_d[2], tC, dC, YI, un_d[1], ALU.mult)
            kb.op("dve", lambda e: e.tensor_tensor(out=xrb[:, 1, :], in0=W(3), in1=W(2), op=ALU.add), [wk_d[3], wk_d[2]], [xr_d[1]])
            yb = YB[tt]
            for ri in range(2):
                kb.op("pe", lambda e, ri=ri: e.matmul(self.pfx(yb), lhsT=cw[:, ri, jj, :], rhs=xrb[:, ri, :],
                                                     start=(jj == 0 and ri == 0), stop=(jj == 3 and ri == 1)),
                      [cw_d, xr_d[ri]], [self.pfx_d(yb)])

        for q in range(8):
            s = self.next_slot()
            wv = self.slot3(s, KD, 128)
            self.load_w(s, [(wv, win[:, q * 128:(q + 1) * 128].rearrange("(k p) n -> p k n", p=P))])
            for tt in range(4):
                bank = tt % 2

                def mm(e, tt=tt, bank=bank, wv=wv):
                    ins = None
                    for k in range(KD):
                        ins = e.matmul(self.pf(bank), lhsT=wv[:, k, :], rhs=xn[:, k, tt * 512:(tt + 1) * 512], start=(k == 0), stop=(k == KD - 1))
                    return ins
                kb.op("pe", mm, [self.slot_d[s]] + xn_d[tt], [self.pf_d[bank]])
                kb.op("dve", lambda e, tt=tt, bank=bank: e.tensor_copy(out=u32[:, tt * 512:(tt + 1) * 512], in_=self.pf(bank)),
                      [self.pf_d[bank]], [u32_d[tt]])
                kb.op("dve", lambda e, tt=tt, bank=bank: e.tensor_copy(out=ub[:, tt * 512:(tt + 1) * 512], in_=self.pf(bank)),
                      [self.pf_d[bank]], [ub_d[tt]])
            for ri in range(2):
                kb.op("dve", lambda e, ri=ri, q=q: e.tensor_tensor(
                    out=bwf[:, ri, :].rearrange("p (a n) -> p a n", a=2), in0=bbt[:, ri, q:q + 1, :].broadcast_to([P, 2, 64]),
                    in1=bmask.rearrange("p (a n) -> p a n", a=2), op=ALU.mult), [bbt_d, c2_d], [bwf_d])
                for jj in range(4):
                    kb.op("dve", lambda e, ri=ri, jj=jj: e.tensor_scalar_mul(
                        out=bwm[:, ri, jj, :], in0=bwf[:, ri, :], scalar1=c2[:, C2_RMASK + jj:C2_RMASK + jj + 1]), [bwf_d, c2_d], [bwm_d])
                kb.op("dve", lambda e, ri=ri, q=q: e.tensor_copy(
                    out=ctd[:, ri, :].rearrange("p (a n) -> p a n", a=2), in_=ct[:, ri, q:q + 1, :].broadcast_to([P, 2, 64])), [ct_d], [ctd_d])
                bank = ri
                kb.op("pe", lambda e, ri=ri, bank=bank: e.transpose(self.pf(bank, 128), ctd[:, ri, :], self.ident_f()),
                      [ctd_d, self.cst_d], [self.pf_d[bank]])
                for jj in range(4):
                    cm = c2[:, C2_CMASK + jj * 128: C2_CMASK + (jj + 1) * 128]
                    if ri == 0:
                        kb.op("dve", lambda e, jj=jj, bank=bank, cm=cm: e.tensor_tensor(out=cw[:, 0, jj, :], in0=self.pf(bank, 128), in1=cm, op=ALU.mult),
                              [self.pf_d[bank], c2_d], [cw_d])
                    else:
                        kb.op("dve", lambda e, jj=jj, bank=bank, cm=cm: e.scalar_tensor_tensor(
                            out=cw[:, 1, jj, :], in0=self.pf(bank, 128), scalar=-1.0, in1=cm, op0=ALU.mult, op1=ALU.mult),
                            [self.pf_d[bank], c2_d], [cw_d])
            units = [(jj, tt) for jj in range(4) for tt in range(4)]
            tables(4 * q)
            stage_a(units[0][0], units[0][1], ucount[0] % 2)
            for ui, (jj, tt) in enumerate(units):
                up = ucount[0] % 2
                ucount[0] += 1
                if ui + 1 < len(units):
                    stage_a(units[ui + 1][0], units[ui + 1][1], ucount[0] % 2)
                if tt == 0:
                    rho_table(4 * q + jj)
                stage_bcd(4 * q + jj, jj, tt, up)
                if tt == 1 and jj < 3:
                    tables(4 * q + jj + 1)
            for tt in range(4):
                yb = YB[tt]
                ts = slice(tt * 512, (tt + 1) * 512)
                kb.op("dve", lambda e, tt=tt, q=q, yb=yb, ts=ts: e.scalar_tensor_tensor(
                    out=ysum, in0=u32[:, ts], scalar=self.col(OFF_SSMD + q), in1=self.pfx(yb), op0=ALU.mult, op1=ALU.add),
                    [u32_d[tt], self.cols_d, self.pfx_d(yb)], [ysum_d])
                kb.op("act", lambda e, q=q, ts=ts: e.activation(out=yg[:, q, ts], in_=ysum, func=AF.Gelu_apprx_tanh), [ysum_d], [yg_d[q][tt]])
        kb.fence()
        self.ptr = mark2
        if os.environ.get("S5_STOP") in ("2", "3", "4", "5"):
            self.ptr = mark
            return
        sgm = sbt("s_sgm", [P, 2, 512], F32)
        sgm_d = kb.tiles_n("s_sgm", 2)
        gcnt = 0
        for dc in range(KD):
            s = self.next_slot()
            wv = self.slot3(s, KD, 256)
            self.load_w(s, [(wv[:, :, 0:128], wglu[:, dc * 128:(dc + 1) * 128].rearrange("(k p) n -> p k n", p=P)),
                            (wv[:, :, 128:256], wglu[:, D + dc * 128: D + (dc + 1) * 128].rearrange("(k p) n -> p k n", p=P))])
            for tt in range(4):
                b0 = (gcnt % 3) * 2
                gi = gcnt % 2
                gcnt += 1
                for ag in range(2):
                    def mm(e, ag=ag, tt=tt, b0=b0, wv=wv):
                        ins = None
                        for k in range(KD):
                            ins = e.matmul(self.pf(b0 + ag), lhsT=wv[:, k, ag * 128:(ag + 1) * 128], rhs=yg[:, k, tt * 512:(tt + 1) * 512],
                                           start=(k == 0), stop=(k == KD - 1))
                        return ins
                    kb.op("pe", mm, [self.slot_d[s]] + [yg_d[k][tt] for k in range(KD)], [self.pf_d[b0 + ag]])
                kb.op("act", lambda e, gi=gi, b0=b0: e.activation(out=sgm[:, gi, :], in_=self.pf(b0 + 1), func=AF.Sigmoid),
                      [self.pf_d[b0 + 1]], [sgm_d[gi]])
                kb.op("dve", lambda e, gi=gi, b0=b0: e.tensor_tensor(out=sgm[:, gi, :], in0=self.pf(b0), in1=sgm[:, gi, :], op=ALU.mult),
                      [self.pf_d[b0], sgm_d[gi]], [sgm_d[gi]])
                kb.op("dve", lambda e, dc=dc, tt=tt, gi=gi: e.tensor_tensor(
                    out=self.X[:, dc, tt * 512:(tt + 1) * 512], in0=self.X[:, dc, tt * 512:(tt + 1) * 512], in1=sgm[:, gi, :], op=ALU.add),
                    [sgm_d[gi], self.Xd[dc][tt]], [self.Xd[dc][tt]])
        kb.fence()
        self.ptr = mark


_CACHE = {}


def _pack_vecs(norm_g, final_norm_g, ffn_conv_w, ffn_conv_b, ssm_d):
    v = np.zeros((NVROWS, 128), np.float32)
    v[OFF_NG:OFF_NG + 64] = np.asarray(norm_g, np.float32).reshape(64, 128)
    v[OFF_FNG:OFF_FNG + 8] = np.asarray(final_norm_g, np.float32).reshape(8, 128)
    v[OFF_CW:OFF_CW + 528] = np.asarray(ffn_conv_w, np.float32).reshape(528, 128)
    v[OFF_CB:OFF_CB + 176] = np.asarray(ffn_conv_b, np.float32).reshape(176, 128)
    v[OFF_SSMD:OFF_SSMD + 8] = np.asarray(ssm_d, np.float32).reshape(8, 128)
    return v


def run_layers(x, inputs, layers, do_final, cores=NCORES):
    key = (tuple(layers), do_final)
    if key not in _CACHE:
        _CACHE[key] = Prog(layers, do_final).build()
    nc = _CACHE[key]
    consts, consts2 = _host_consts()
    vecs = _pack_vecs(inputs["norm_g"], inputs["final_norm_g"], inputs["ffn_conv_w"], inputs["ffn_conv_b"], inputs["ssm_d"])
    shared = {"consts": consts, "consts2": consts2, "vecs": vecs}
    for n in ("sb_w_qkv", "sb_w_o", "sg_w_in", "sg_norm_g", "sg_w_s", "sg_b", "sg_w_o", "ssm_w_in", "ssm_lam_re",
              "ssm_lam_im", "ssm_log_dt", "ssm_b_re", "ssm_b_im", "ssm_c_re", "ssm_c_im", "ssm_w_glu", "ffn_w_up",
              "ffn_w_down"):
        shared[n] = np.ascontiguousarray(np.asarray(inputs[n], np.float32))
    in_maps = []
    for c in range(cores):
        m = dict(shared)
        m["x"] = np.ascontiguousarray(np.asarray(x[c], np.float32))
        in_maps.append(m)
    res = run_bass_kernel_spmd(nc, in_maps, core_ids=list(range(cores)))
    return np.stack([np.asarray(r["y"]) for r in res.results], axis=0)


def kernel(**inputs):
    x = np.asarray(inputs["x"], np.float32)
    out = run_layers(x, inputs, [0, 1, 2, 3], True)
    return out.astype(np.float32)
```

```python
import math
import os
from contextlib import ExitStack

import numpy as np
import concourse.bass as bass
import concourse.mybir as mybir
from concourse.bass_utils import run_bass_kernel_spmd

F32 = mybir.dt.float32
BF16 = mybir.dt.bfloat16
AF = mybir.ActivationFunctionType
ALU = mybir.AluOpType

P = 128
L = 2048
D = 1024
KD = 8
DFF = 2816
NJ = 22
EPS = 1e-6
NCORES = 8

OFF_NG = 0
OFF_FNG = 64
OFF_CW = 72
OFF_CB = 600
OFF_SSMD = 776
NVROWS = 896

C_IDENT = 0
C_MASKL = 128
C_TRILT = 256
C_ONES = 384
C_EPS = 512
C_NEGPI = 513
C_HALFPI = 514
MAGIC = 12582912.0
CW1 = 6.28125
CW2 = 2.0 * math.pi - CW1
PI_LO = 3.1415925
HALFPI_LO = 1.5707962
C_MNEG = 640
NCONST1 = 768
C2_BMASK = 0
C2_CMASK = 128
C2_RMASK = 640
C2_IOTA = 704
C2_REP = 1216
NCONST2 = 2240


def _host_consts():
    c = np.zeros((P, NCONST1), np.float32)
    r = np.arange(P)
    c[:, C_IDENT:C_IDENT + 128] = np.eye(P, dtype=np.float32)
    c[:, C_MASKL:C_MASKL + 128] = (r[None, :] < r[:, None]).astype(np.float32)
    c[:, C_TRILT:C_TRILT + 128] = (r[:, None] <= r[None, :]).astype(np.float32)
    c[:, C_ONES:C_ONES + 128] = 1.0
    c[:, C_EPS] = EPS
    c[:, C_NEGPI] = -math.pi
    c[:, C_HALFPI] = HALFPI_LO
    c[:, C_MNEG:C_MNEG + 128] = np.where(r[None, :] >= r[:, None], -30000.0, 0.0).astype(np.float32)
    c2 = np.zeros((P, NCONST2), np.float32)
    for g in range(64):
        q, gl = divmod(g, 8)
        c2[g, C2_REP + q * 128 + gl * 16: C2_REP + q * 128 + gl * 16 + 16] = 1.0
    for row in range(P):
        gl8 = row // 16
        c2[row, C2_BMASK + (gl8 % 2) * 64: C2_BMASK + (gl8 % 2) * 64 + 64] = 1.0
        c2[row, C2_RMASK + gl8 // 2] = 1.0
    for m in range(4):
        for row in range(P):
            gl = row // 64
            g8 = 2 * m + gl
            c2[row, C2_CMASK + m * 128 + g8 * 16: C2_CMASK + m * 128 + g8 * 16 + 16] = 1.0
    c2[:, C2_IOTA:C2_IOTA + 512] = np.arange(512, dtype=np.float32)[None, :]
    return c, c2


class Dep:
    __slots__ = ("w", "r", "name", "excl")

    def __init__(self, name, w=None, excl=False):
        self.name = name
        self.w = w
        self.r = []
        self.excl = excl


class Op:
    __slots__ = ("eng", "fn", "idx", "deps", "waits", "signal", "sigval", "dkey", "dval", "gidx")


ENGS = ("pe", "act", "dve", "pool", "sp")


class KB:
    def __init__(self, nc):
        self.nc = nc
        self.ops = {e: [] for e in ENGS}
        self.all_ops = []
        self.tiles = []
        self.fence_op = None
        self.dma_count = {}
        self.dma_group = set()

    def tile(self, name, excl=False):
        t = Dep(name, self.fence_op, excl)
        self.tiles.append(t)
        return t

    def tiles_n(self, name, *dims):
        if len(dims) == 1:
            return [self.tile(f"{name}{i}") for i in range(dims[0])]
        return [self.tiles_n(f"{name}{i}_", *dims[1:]) for i in range(dims[0])]

    def op(self, eng, fn, reads=(), writes=(), dkey=None, nd=1):
        o = Op()
        o.eng = eng
        o.fn = fn
        o.idx = len(self.ops[eng])
        o.gidx = len(self.all_ops)
        o.deps = set()
        o.waits = []
        o.signal = False
        o.sigval = 0
        o.dkey = dkey
        o.dval = 0
        if dkey is not None:
            self.dma_count[dkey] = self.dma_count.get(dkey, 0) + 16 * nd
            o.dval = self.dma_count[dkey]
        for t in reads:
            if t.w is not None:
                o.deps.add(t.w)
            if t.excl:
                for r in t.r:
                    if r.eng != eng:
                        o.deps.add(r)
        for t in writes:
            if t.w is not None:
                o.deps.add(t.w)
            for r in t.r:
                o.deps.add(r)
        for t in reads:
            t.r.append(o)
        for t in writes:
            t.w = o
            t.r = []
        o.deps.discard(o)
        self.ops[eng].append(o)
        self.all_ops.append(o)
        return o

    def dma(self, queue, out, in_, reads, writes, key, group=False):
        if group:
            self.dma_group.add(key)
        return self.op(queue, lambda e: [e.dma_start(out=out, in_=in_)], reads, writes, dkey=key)

    def dma_multi(self, queue, pieces, reads, writes, key):
        return self.op(queue, lambda e: [e.dma_start(out=o, in_=i) for (o, i) in pieces], reads, writes, dkey=key,
                       nd=len(pieces))

    def fence(self):
        deps_r, deps_w = [], []
        o = self.op("sp", lambda e: e.nop(), reads=(), writes=())
        for t in self.tiles:
            if t.w is not None:
                o.deps.add(t.w)
            for r in t.r:
                o.deps.add(r)
        o.deps.discard(o)
        self.fence_op = o
        return o

    def finalize(self, es):
        nc = self.nc
        seen = {e: {} for e in ENGS}
        for o in self.all_ops:
            sn = seen[o.eng]
            need = {}
            for d in o.deps:
                if d.dkey is not None:
                    key = ("d", d.dkey)
                    val = self.dma_count[d.dkey] if d.dkey in self.dma_group else d.dval
                    if sn.get(key, 0) >= val:
                        continue
                    if need.get(key, (0, None))[0] < val:
                        need[key] = (val, d)
                else:
                    if d.eng == o.eng:
                        if o.eng == "pe" or (o.idx - d.idx) > 3:
                            continue
                    key = ("e", d.eng)
                    if sn.get(key, -1) >= d.idx:
                        continue
                    if need.get(key, (-1, None))[0] < d.idx:
                        need[key] = (d.idx, d)
            for key, (val, d) in need.items():
                sn[key] = val
                if key[0] == "e":
                    d.signal = True
                o.waits.append((key, d))
        for e in ENGS:
            cnt = 0
            for o in self.ops[e]:
                if o.signal:
                    cnt += 1
                    o.sigval = cnt
        esem = {e: es.enter_context(nc.semaphore(f"sem_{e}")) for e in ENGS}
        dsem = {k: es.enter_context(nc.semaphore(f"dsem_{k}")) for k in self.dma_count}
        block = es.enter_context(nc.Block())
        kb = self

        def run(ename, eng):
            for o in kb.ops[ename]:
                for key, d in o.waits:
                    if key[0] == "d":
                        val = kb.dma_count[d.dkey] if d.dkey in kb.dma_group else d.dval
                        eng.wait_ge(dsem[d.dkey], val)
                    else:
                        eng.wait_ge(esem[d.eng], d.sigval)
                ins = o.fn(eng)
                if o.dkey is not None:
                    for di in ins:
                        di.then_inc(dsem[o.dkey], 16)
                elif o.signal:
                    ins.then_inc(esem[ename], 1)

        @block.tensor
        def _(e):
            run("pe", e)

        @block.scalar
        def _(e):
            run("act", e)

        @block.vector
        def _(e):
            run("dve", e)

        @block.gpsimd
        def _(e):
            run("pool", e)

        @block.sync
        def _(e):
            run("sp", e)


class Prog:
    def __init__(self, layers, do_final, x_in_tokmajor=True):
        self.layers = layers
        self.do_final = do_final

    def build(self):
        nc = bass.Bass("TRN2", target_bir_lowering=False)
        self.nc = nc
        dt = nc.dram_tensor
        self.d = {}
        shapes = {
            "x": [L, D], "consts": [P, NCONST1], "consts2": [P, NCONST2], "vecs": [NVROWS, 128],
            "sb_w_qkv": [2, D, 3 * D], "sb_w_o": [2, D, D],
            "sg_w_in": [1, D, 2 * D], "sg_norm_g": [1, D], "sg_w_s": [1, 8, 128, 128],
            "sg_b": [1, 8, 128], "sg_w_o": [1, D, D],
            "ssm_w_in": [1, D, D], "ssm_lam_re": [1, 64, 64], "ssm_lam_im": [1, 64, 64],
            "ssm_log_dt": [1, 64], "ssm_b_re": [1, 64, 64, 16], "ssm_b_im": [1, 64, 64, 16],
            "ssm_c_re": [1, 64, 16, 64], "ssm_c_im": [1, 64, 16, 64], "ssm_w_glu": [1, D, 2 * D],
            "ffn_w_up": [4, D, 2 * DFF], "ffn_w_down": [4, DFF, D],
        }
        for n, s in shapes.items():
            self.d[n] = dt(n, s, F32, kind="ExternalInput").ap()
        self.y = dt("y", [L, D], F32, kind="ExternalOutput").ap()
        self.scr = dt("s5_scratch", [2, 64, 1024], F32, kind="Internal").ap()

        with ExitStack() as es:
            self.es = es
            kb = KB(nc)
            self.kb = kb
            self.ptr = (nc._sbuf_addr_for_side("left") + 63) // 64 * 64
            self.sb_end = nc._sbuf_addr_for_side("right")
            self.uid = 0
            sb = self.alloc
            self.X = sb("X", [P, KD, L], F32)
            self.Xd = kb.tiles_n("X", KD, 4)
            self.cst = sb("cst", [P, NCONST1], F32)
            self.cst_d = kb.tile("cst")
            self.cstb = sb("cstb", [P, 512], BF16)
            self.cstb_d = kb.tile("cstb")
            self.mneg = sb("mneg", [P, 128], BF16)
            self.cols = sb("cols", [P, NVROWS], F32)
            self.cols_d = kb.tile("cols")
            self.NSLOT = 3
            self.slots = [sb(f"slot{i}", [P, 3072], BF16) for i in range(self.NSLOT)]
            self.slot_d = [kb.tile(f"slot{i}") for i in range(self.NSLOT)]
            self.slot_i = 0
            self.PF = es.enter_context(nc.psum_tensor("pf", [P, 6 * 512], F32))
            self.PB = es.enter_context(nc.psum_tensor("pb", [P, 2 * 1024], BF16))
            self.pf_d = [kb.tile(f"pf{i}", excl=True) for i in range(6)]
            self.pb_d = [kb.tile(f"pb{i}", excl=True) for i in range(2)]

            self.setup()
            self.load_x()
            for l in self.layers:
                m = l % 3
                if m == 0:
                    self.attention(l, l // 3)
                elif m == 1:
                    self.gmlp(l)
                else:
                    self.s5(l)
                if os.environ.get("NOFFN") != "1":
                    self.ffn(l)
            self.store(self.do_final)
            kb.finalize(es)
        return nc

    def alloc(self, name, shape, dtype):
        nbytes = int(np.prod(shape[1:])) * (4 if dtype == F32 else 2)
        off = (self.ptr + 63) // 64 * 64
        assert off + nbytes <= self.sb_end, f"SBUF overflow allocating {name}: {off + nbytes} > {self.sb_end}"
        self.ptr = off + nbytes
        self.uid += 1
        return self.nc.alloc_sbuf_tensor_at(f"{name}_{self.uid}", list(shape), dtype, offset=off)

    def pf(self, b, n=512, off=0):
        return self.PF[:, b * 512 + off: b * 512 + off + n]

    def pfx(self, b):
        if b < 6:
            return self.pf(b)
        return self.PB[:, (b - 6) * 1024:(b - 5) * 1024].bitcast(F32)

    def pfx_d(self, b):
        return self.pf_d[b] if b < 6 else self.pb_d[b - 6]

    def ident_f(self):
        return self.cst[:, C_IDENT:C_IDENT + 128]

    def col(self, r):
        return self.cols[:, r:r + 1]

    def phase_scope(self):
        self.kb.fence()
        ps = ExitStack()
        return ps

    def setup(self):
        kb, nc = self.kb, self.nc
        kb.dma("sp", self.cst[:], self.d["consts"], [], [self.cst_d], "cst")
        kb.dma("pool", self.cstb[:], self.d["consts"][:, 0:512], [], [self.cstb_d], "cstb")
        kb.dma("pool", self.mneg[:], self.d["consts"][:, C_MNEG:C_MNEG + 128], [], [self.cstb_d], "cstb")
        mark = self.ptr
        vst = self.alloc("vstage", [P, 7, 128], F32)
        if True:
            vd = kb.tile("vstage")
            kb.dma("sp", vst[:], self.d["vecs"].rearrange("(a p) c -> p a c", p=P), [], [vd], "vst")
            for a in range(7):
                b = a // 4
                o = (a % 4) * 128
                kb.op("pe", lambda e, a=a, b=b, o=o: e.transpose(self.pf(b, 128, o), vst[:, a, :], self.ident_f()),
                      [vd, self.cst_d], [self.pf_d[b]])
            kb.op("dve", lambda e: e.tensor_copy(out=self.cols[:, 0:512], in_=self.pf(0)), [self.pf_d[0]], [self.cols_d])
            kb.op("dve", lambda e: e.tensor_copy(out=self.cols[:, 512:896], in_=self.pf(1, 384)), [self.pf_d[1]], [self.cols_d])
            kb.fence()
        self.ptr = mark

    def load_x(self):
        kb, nc = self.kb, self.nc
        x = self.d["x"]
        mark = self.ptr
        xst = self.alloc("xst", [P, 2, D], F32)
        if True:
            xd = [kb.tile("xst0"), kb.tile("xst1")]
            for i in range(16):
                s = i % 2
                kb.dma("sp", xst[:, s, :], x[i * 128:(i + 1) * 128, :], [], [xd[s]], f"xst{s}")
                for half in range(2):
                    b = 2 * s + half
                    for kk in range(4):
                        k = half * 4 + kk
                        kb.op("pe", lambda e, s=s, k=k, b=b, kk=kk: e.transpose(
                            self.pf(b, 128, kk * 128), xst[:, s, k * 128:(k + 1) * 128], self.ident_f()),
                            [xd[s], self.cst_d], [self.pf_d[b]])
                    eng = "dve" if half == 0 else "act"
                    outap = self.X[:, half * 4:half * 4 + 4, i * 128:(i + 1) * 128]
                    inap = self.pf(b).rearrange("p (k t) -> p k t", k=4)
                    if eng == "dve":
                        fn = lambda e, outap=outap, inap=inap: e.tensor_copy(out=outap, in_=inap)
                    else:
                        fn = lambda e, outap=outap, inap=inap: e.activation(out=outap, in_=inap, func=AF.Copy)
                    kb.op(eng, fn, [self.pf_d[b]], [self.Xd[k][i // 4] for k in range(half * 4, half * 4 + 4)])
            kb.fence()
        self.ptr = mark

    def store(self, do_final):
        kb, nc = self.kb, self.nc
        mark = self.ptr
        ps = None
        if True:
            sbt = self.alloc
            xo = sbt("xo", [P, KD, 512], F32)
            xo_d = kb.tiles_n("xo", KD)
            yst = sbt("yst", [P, 2, D], F32)
            yd = [kb.tile("yst0"), kb.tile("yst1")]
            nt = self.norm_temps(ps) if do_final else None
            cnt = 0
            for tt in range(4):
                if do_final:
                    self.rmsnorm(OFF_FNG, tt, xo, 0, xo_d, nt)
                    src = lambda k, c: xo[:, k, c * 128:(c + 1) * 128]
                    srcd = lambda k: xo_d[k]
                else:
                    src = lambda k, c, tt=tt: self.X[:, k, tt * 512 + c * 128: tt * 512 + (c + 1) * 128]
                    srcd = lambda k, tt=tt: self.Xd[k][tt]
                for c in range(4):
                    i = tt * 4 + c
                    s = cnt % 2
                    cnt += 1
                    for half in range(2):
                        b = 2 * s + half
                        for kk in range(4):
                            k = half * 4 + kk
                            kb.op("pe", lambda e, k=k, c=c, b=b, kk=kk, src=src: e.transpose(
                                self.pf(b, 128, kk * 128), src(k, c), self.ident_f()),
                                [srcd(k), self.cst_d], [self.pf_d[b]])
                        outap = yst[:, s, half * 512:(half + 1) * 512]
                        if half == 0:
                            kb.op("dve", lambda e, outap=outap, b=b: e.tensor_copy(out=outap, in_=self.pf(b)),
                                  [self.pf_d[b]], [yd[s]])
                        else:
                            kb.op("act", lambda e, outap=outap, b=b: e.activation(out=outap, in_=self.pf(b), func=AF.Copy),
                                  [self.pf_d[b]], [yd[s]])
                    kb.dma("sp", self.y[i * 128:(i + 1) * 128, :], yst[:, s, :], [yd[s]], [], f"yst{s}")
            fin = kb.tile("fin")
            kb.op("sp", lambda e: e.nop(), [], [yd[0], yd[1], fin])
            kb.fence()
        self.ptr = mark

    def norm_temps(self, ps):
        nc, kb = self.nc, self.kb
        sq = self.alloc("n_sq", [P, KD, 512], BF16)
        r1 = self.alloc("n_r1", [P, 512], F32)
        rs = self.alloc("n_rs", [P, 512], F32)
        return dict(sq=sq, r1=r1, rs=rs, sq_d=kb.tile("n_sq"), r1_d=kb.tile("n_r1"), rs_d=kb.tile("n_rs"))

    def rmsnorm(self, goff, tt, out, ocol0, out_d, nt, bank=5, out_scale_eng="dve"):
        kb = self.kb
        X = self.X
        ts = slice(tt * 512, (tt + 1) * 512)
        xr = [self.Xd[k][tt] for k in range(KD)]
        kb.op("act", lambda e: e.activation(out=nt["sq"][:], in_=X[:, :, ts], func=AF.Square), xr, [nt["sq_d"]])
        ones_b = self.cstb[:, 384:512]

        def mm(e):
            ins = None
            for k in range(KD):
                ins = e.matmul(self.pf(bank), lhsT=ones_b, rhs=nt["sq"][:, k, :], start=(k == 0), stop=(k == KD - 1))
            return ins
        kb.op("pe", mm, [nt["sq_d"], self.cstb_d], [self.pf_d[bank]])
        kb.op("act", lambda e: e.activation(out=nt["r1"][:], in_=self.pf(bank), func=AF.Sqrt, bias=self.cst[:, C_EPS:C_EPS + 1],
                                            scale=1.0 / D), [self.pf_d[bank], self.cst_d], [nt["r1_d"]])
        kb.op("dve", lambda e: e.reciprocal(out=nt["rs"][:], in_=nt["r1"][:]), [nt["r1_d"]], [nt["rs_d"]])
        for k in range(KD):
            kb.op("dve", lambda e, k=k: e.scalar_tensor_tensor(
                out=out[:, k, ocol0:ocol0 + 512], in0=X[:, k, ts], scalar=self.col(goff + k), in1=nt["rs"][:],
                op0=ALU.mult, op1=ALU.mult), [self.Xd[k][tt], nt["rs_d"], self.cols_d], [out_d[k]])

    def next_slot(self):
        s = self.slot_i % self.NSLOT
        self.slot_i += 1
        return s

    def load_w(self, s, pieces):
        self.kb.dma_multi("pool", pieces, [], [self.slot_d[s]], f"slot{s}")

    def slot3(self, s, k, n):
        return self.slots[s][:, 0:k * n].rearrange("p (k n) -> p k n", k=k)

    def ffn(self, l):
        kb, nc = self.kb, self.nc
        wup = self.d["ffn_w_up"][l]
        wdn = self.d["ffn_w_down"][l]
        kb.fence()
        mark = self.ptr
        ps = None
        if True:
            sbt = self.alloc
            xn = sbt("f_xn", [P, KD, 1024], BF16)
            xn_d = [kb.tiles_n("f_xn", KD) for _ in range(2)]
            act = sbt("f_act", [P, NJ, 1024], BF16)
            act_d = kb.tiles_n("f_act", NJ, 2)
            hs = sbt("f_hs", [P, 2, 2, 2 + 1024], F32)
            hs_d = kb.tiles_n("f_hs", 2, 2, 2)
            hsh_d = kb.tiles_n("f_hsh", 2, 2)
            y0 = sbt("f_y0", [P, 3, 2, 512], F32)
            y0_d = kb.tiles_n("f_y0", 3, 2)
            sg = sbt("f_sg", [P, 3, 512], F32)
            sg_d = kb.tiles_n("f_sg", 3)
            halo = sbt("f_halo", [P, NJ, 2, 2], F32)
            halo_d = kb.tiles_n("f_halo", NJ)
            nt = self.norm_temps(ps)
            ucount = 0
            for half in range(2):
                for t2 in range(2):
                    self.rmsnorm(OFF_NG + (l * 2 + 1) * 8, half * 2 + t2, xn, t2 * 512, xn_d[t2], nt)
                pend = []

                def issue_up(j):
                    s = self.next_slot()
                    v = self.slot3(s, KD, 256)
                    self.load_w(s, [
                        (v[:, :, 0:128], wup[:, j * 128:(j + 1) * 128].rearrange("(k p) n -> p k n", p=P)),
                        (v[:, :, 128:256], wup[:, DFF + j * 128: DFF + (j + 1) * 128].rearrange("(k p) n -> p k n", p=P)),
                    ])
                    return s

                def issue_dn(dc):
                    s = self.next_slot()
                    v = self.slot3(s, NJ, 128)
                    self.load_w(s, [(v, wdn[:, dc * 128:(dc + 1) * 128].rearrange("(k p) n -> p k n", p=P))])
                    return s
                seq = [("u", j) for j in range(NJ)] + [("d", dc) for dc in range(KD)]
                PRE = self.NSLOT - 1
                slots_of = {}
                for q in range(min(PRE, len(seq))):
                    slots_of[q] = issue_up(seq[q][1]) if seq[q][0] == "u" else issue_dn(seq[q][1])
                pend_ffn = []
                for qi, (kind, j) in enumerate(seq):
                    s = slots_of[qi]
                    if kind == "u":
                        wv = self.slot3(s, KD, 256)
                        hb = j % 2
                        for ag in range(2):
                            if half == 0:
                                kb.op("pool", lambda e, hb=hb, ag=ag: e.memset(hs[:, hb, ag, 0:2], 0.0), [], [hsh_d[hb][ag]])
                            else:
                                kb.op("pool", lambda e, hb=hb, ag=ag, j=j: e.tensor_copy(out=hs[:, hb, ag, 0:2], in_=halo[:, j, ag, :]),
                                      [halo_d[j]], [hsh_d[hb][ag]])
                        for t2 in range(2):
                            pb = (ucount % 2) * 2
                            yb = ucount % 3
                            ucount += 1
                            for ag in range(2):
                                def mm(e, ag=ag, t2=t2, pb=pb, wv=wv):
                                    ins = None
                                    for k in range(KD):
                                        ins = e.matmul(self.pf(pb + ag), lhsT=wv[:, k, ag * 128:(ag + 1) * 128],
                                                       rhs=xn[:, k, t2 * 512:(t2 + 1) * 512], start=(k == 0), stop=(k == KD - 1))
                                    return ins
                                kb.op("pe", mm, [self.slot_d[s]] + xn_d[t2], [self.pf_d[pb + ag]])
                            c0 = 2 + t2 * 512
                            for ag in range(2):
                                kk = j + ag * NJ
                                w0 = self.col(OFF_CW + (l * 3 + 0) * 44 + kk)
                                w1 = self.col(OFF_CW + (l * 3 + 1) * 44 + kk)
                                w2 = self.col(OFF_CW + (l * 3 + 2) * 44 + kk)
                                bb = self.col(OFF_CB + l * 44 + kk)
                                kb.op("act", lambda e, hb=hb, ag=ag, c0=c0, pb=pb: e.activation(
                                    out=hs[:, hb, ag, c0:c0 + 512], in_=self.pf(pb + ag), func=AF.Copy),
                                    [self.pf_d[pb + ag]], [hs_d[hb][ag][t2]])
                                kb.op("act", lambda e, yb=yb, ag=ag, pb=pb, w2=w2, bb=bb: e.activation(
                                    out=y0[:, yb, ag, :], in_=self.pf(pb + ag), func=AF.Identity, bias=bb, scale=w2),
                                    [self.pf_d[pb + ag], self.cols_d], [y0_d[yb][ag]])
                                rd = [hs_d[hb][ag][t2], self.cols_d] + ([hsh_d[hb][ag]] if t2 == 0 else [hs_d[hb][ag][0]])
                                kb.op("dve", lambda e, yb=yb, ag=ag, hb=hb, c0=c0, w1=w1: e.scalar_tensor_tensor(
                                    out=y0[:, yb, ag, :], in0=hs[:, hb, ag, c0 - 1:c0 + 511], scalar=w1, in1=y0[:, yb, ag, :],
                                    op0=ALU.mult, op1=ALU.add), rd + [y0_d[yb][ag]], [y0_d[yb][ag]])
                                kb.op("dve", lambda e, yb=yb, ag=ag, hb=hb, c0=c0, w0=w0: e.scalar_tensor_tensor(
                                    out=y0[:, yb, ag, :], in0=hs[:, hb, ag, c0 - 2:c0 + 510], scalar=w0, in1=y0[:, yb, ag, :],
                                    op0=ALU.mult, op1=ALU.add), rd + [y0_d[yb][ag]], [y0_d[yb][ag]])
                            if pend_ffn:
                                pend_ffn.pop()()

                            def fin(yb=yb, j=j, t2=t2):
                                kb.op("act", lambda e: e.activation(out=sg[:, yb, :], in_=y0[:, yb, 1, :], func=AF.Silu),
                                      [y0_d[yb][1]], [sg_d[yb]])
                                kb.op("dve", lambda e: e.tensor_tensor(
                                    out=act[:, j, t2 * 512:(t2 + 1) * 512], in0=sg[:, yb, :], in1=y0[:, yb, 0, :], op=ALU.mult),
                                    [sg_d[yb], y0_d[yb][0]], [act_d[j][t2]])
                            pend_ffn.append(fin)
                        if half == 0:
                            kb.op("pool", lambda e, hb=hb, j=j: e.tensor_copy(out=halo[:, j, :, :], in_=hs[:, hb, :, 1024:1026]),
                                  [hs_d[hb][0][1], hs_d[hb][1][1]], [halo_d[j]])
                    else:
                        if pend_ffn:
                            pend_ffn.pop()()
                        dc = j
                        wv = self.slot3(s, NJ, 128)
                        for t2 in range(2):
                            bank = 4 + (t2 % 2)
                            def mm(e, t2=t2, bank=bank, wv=wv):
                                ins = None
                                for jj in range(NJ):
                                    ins = e.matmul(self.pf(bank), lhsT=wv[:, jj, :], rhs=act[:, jj, t2 * 512:(t2 + 1) * 512],
                                                   start=(jj == 0), stop=(jj == NJ - 1))
                                return ins
                            kb.op("pe", mm, [self.slot_d[s]] + [act_d[jj][t2] for jj in range(NJ)], [self.pf_d[bank]])
                            tt = half * 2 + t2
                            kb.op("dve", lambda e, dc=dc, tt=tt, bank=bank: e.tensor_tensor(
                                out=self.X[:, dc, tt * 512:(tt + 1) * 512], in0=self.X[:, dc, tt * 512:(tt + 1) * 512],
                                in1=self.pf(bank), op=ALU.add), [self.pf_d[bank], self.Xd[dc][tt]], [self.Xd[dc][tt]])
                    nq = qi + PRE
                    if nq < len(seq):
                        slots_of[nq] = issue_up(seq[nq][1]) if seq[nq][0] == "u" else issue_dn(seq[nq][1])
            kb.fence()
        self.ptr = mark

    def attention(self, l, ja):
        kb, nc = self.kb, self.nc
        wqkv = self.d["sb_w_qkv"][ja]
        wo = self.d["sb_w_o"][ja]
        kb.fence()
        mark = self.ptr
        ps = None
        if True:
            sbt = self.alloc
            xn = sbt("a_xn", [P, KD, L], BF16)
            xn_d = kb.tiles_n("a_xn", 4, KD)
            ao = sbt("a_o", [P, KD, L], BF16)
            ao_d = kb.tiles_n("a_o", KD, 4)
            qT = sbt("a_q", [P, L], BF16)
            kT = sbt("a_k", [P, L], BF16)
            vv = sbt("a_v", [P, 16, 128], BF16)
            q_d, k_d, v_d = kb.tile("a_q"), kb.tile("a_k"), kb.tile("a_v")
            NB = 4
            PW = 512
            mark2 = self.ptr
            nt = self.norm_temps(None)
            for tt in range(4):
                self.rmsnorm(OFF_NG + (l * 2 + 0) * 8, tt, xn, tt * 512, xn_d[tt], nt)
            kb.fence()
            self.ptr = mark2
            ee = sbt("a_e", [P, NB, PW], F32)
            spt = sbt("a_sp", [P, NB, PW], F32)
            lw = sbt("a_lw", [P, NB, PW], F32)
            ww = sbt("a_w", [P, NB, PW], BF16)
            wT = sbt("a_wT", [P, NB, PW], BF16)
            ones = sbt("a_ones", [P, PW], BF16)
            carry = sbt("a_carry", [P, 8], F32)
            e_d, sp_d, lw_d, w_d, wT_d = (kb.tiles_n(n, NB) for n in ("a_e", "a_sp", "a_lw", "a_w", "a_wT"))
            ones_d = kb.tile("a_ones")
            carry_d = kb.tiles_n("a_carry", 8)
            kb.op("pool", lambda e: e.memset(ones[:], 1.0), [], [ones_d])
            maskL_f = self.cst[:, C_MASKL:C_MASKL + 128]
            maskL_b = self.cstb[:, 128:256]
            ident_b = self.cstb[:, 0:128]
            pcount = 0
            ocount = 0
            ccount = 0
            gen = 0
            for pair in range(8):
                s = self.next_slot()
                wv = self.slot3(s, KD, 384)
                self.load_w(s, [(wv[:, :, c * 128:(c + 1) * 128],
                                 wqkv[:, c * D + pair * 128: c * D + (pair + 1) * 128].rearrange("(k p) n -> p k n", p=P))
                                for c in range(3)])
                for tt in range(4):
                    for c in range(2):
                        bank = 4 + (gen % 2)
                        gen += 1
                        def mm(e, c=c, tt=tt, bank=bank, wv=wv):
                            ins = None
                            for k in range(KD):
                                ins = e.matmul(self.pf(bank), lhsT=wv[:, k, c * 128:(c + 1) * 128],
                                               rhs=xn[:, k, tt * 512:(tt + 1) * 512], start=(k == 0), stop=(k == KD - 1))
                            return ins
                        kb.op("pe", mm, [self.slot_d[s]] + xn_d[tt], [self.pf_d[bank]])
                        dst = qT if c == 0 else kT
                        dd = q_d if c == 0 else k_d
                        sc = 0.125 if c == 0 else 1.0
                        kb.op("act", lambda e, dst=dst, tt=tt, bank=bank, sc=sc: e.activation(
                            out=dst[:, tt * 512:(tt + 1) * 512], in_=self.pf(bank), func=AF.Copy, scale=sc),
                            [self.pf_d[bank]], [dd])
                for g4 in range(4):
                    bank = 4 + (gen % 2)
                    gen += 1
                    def mmv(e, g4=g4, bank=bank, wv=wv):
                        ins = None
                        for c4 in range(4):
                            i = g4 * 4 + c4
                            for k in range(KD):
                                ins = e.matmul(self.pf(bank, 128, c4 * 128), lhsT=xn[:, k, i * 128:(i + 1) * 128],
                                               rhs=wv[:, k, 256:384], start=(k == 0), stop=(k == KD - 1))
                        return ins
                    kb.op("pe", mmv, [self.slot_d[s]] + xn_d[g4], [self.pf_d[bank]])
                    kb.op("dve", lambda e, g4=g4, bank=bank: e.tensor_copy(
                        out=vv[:, g4 * 4:(g4 + 1) * 4, :], in_=self.pf(bank).rearrange("p (c n) -> p c n", c=4)),
                        [self.pf_d[bank]], [v_d])
                items = []
                for hh in range(2):
                    for i in range(16):
                        t1 = (i + 1) * 128
                        pcs = []
                        ke = t1
                        while ke > 0:
                            ks = ((ke - 1) // PW) * PW
                            pcs.append((ks, ke))
                            ke = ks
                        ob = 4 + ((hh * 4 + i // 4) % 2)
                        ocount += 1
                        prev_cc = None
                        for pi, (ks, ke) in enumerate(pcs):
                            it = dict(hh=hh, i=i, ks=ks, ke=ke, first=(pi == 0), last=(pi == len(pcs) - 1), ob=ob,
                                      p=pcount, cin=prev_cc, cout=None)
                            pcount += 1
                            if pi < len(pcs) - 1:
                                it["cout"] = ccount % 8
                                ccount += 1
                            prev_cc = it["cout"]
                            items.append(it)

                def geo(it):
                    hh, i, ks, ke, p = it["hh"], it["i"], it["ks"], it["ke"], it["p"]
                    return 64 * hh, i * 128, (i + 1) * 128, ke - ks, p % NB, p % 4

                def f_z(it):
                    pb0, t0, t1, n, bi, zb = geo(it)
                    ks, ke = it["ks"], it["ke"]
                    zap = self.pf(zb, n)
                    if it["first"]:
                        zdiag = self.pf(zb, 128, n - 128)

                        def mmz(e):
                            e.matmul(zap, lhsT=qT[pb0:pb0 + 64, t0:t1], rhs=kT[pb0:pb0 + 64, ks:ke], start=True, stop=False)
                            return e.matmul(zdiag, lhsT=ident_b, rhs=self.mneg[:], start=False, stop=True)
                        kb.op("pe", mmz, [q_d, k_d, self.cstb_d], [self.pf_d[zb]])
                    else:
                        kb.op("pe", lambda e: e.matmul(zap, lhsT=qT[pb0:pb0 + 64, t0:t1], rhs=kT[pb0:pb0 + 64, ks:ke], start=True, stop=True),
                              [q_d, k_d], [self.pf_d[zb]])

                def f_exp(it):
                    pb0, t0, t1, n, bi, zb = geo(it)
                    zap = self.pf(zb, n)
                    kb.op("act", lambda e: e.activation(out=ee[:, bi, 0:n], in_=zap, func=AF.Exp), [self.pf_d[zb]], [e_d[bi]])

                def f_mask(it):
                    pb0, t0, t1, n, bi, zb = geo(it)
                    pass

                def f_ln(it):
                    pb0, t0, t1, n, bi, zb = geo(it)
                    kb.op("act", lambda e: e.activation(out=spt[:, bi, 0:n], in_=ee[:, bi, 0:n], func=AF.Ln, bias=1.0), [e_d[bi]], [sp_d[bi]])

                def f_scan(it):
                    pb0, t0, t1, n, bi, zb = geo(it)
                    cin, cout = it["cin"], it["cout"]
                    init = 0.0 if cin is None else carry[:, cin:cin + 1]
                    rd = [sp_d[bi], ones_d] + ([] if cin is None else [carry_d[cin]])
                    kb.op("dve", lambda e: e.tensor_tensor_scan(out=lw[:, bi, 0:n][:, ::-1], data0=ones[:, 0:n], data1=spt[:, bi, 0:n][:, ::-1],
                                                                initial=init, op0=ALU.mult, op1=ALU.add), rd, [lw_d[bi]])
                    if cout is not None:
                        kb.op("dve", lambda e: e.tensor_copy(out=carry[:, cout:cout + 1], in_=lw[:, bi, 0:1]), [lw_d[bi]], [carry_d[cout]])

                def f_er(it):
                    pb0, t0, t1, n, bi, zb = geo(it)
                    kb.op("act", lambda e: e.activation(out=lw[:, bi, 0:n], in_=lw[:, bi, 0:n], func=AF.Exp, scale=-1.0), [lw_d[bi]], [lw_d[bi]])

                def f_mult(it):
                    pb0, t0, t1, n, bi, zb = geo(it)
                    kb.op("dve", lambda e: e.tensor_tensor(out=ww[:, bi, 0:n], in0=ee[:, bi, 0:n], in1=lw[:, bi, 0:n], op=ALU.mult),
                          [e_d[bi], lw_d[bi]], [w_d[bi]])

                def f_tr(it):
                    pb0, t0, t1, n, bi, zb = geo(it)
                    tb = it["p"] % 2
                    nb_ = n // 128

                    def trs(e):
                        ins = None
                        for b in range(nb_):
                            ins = e.transpose(self.PB[:, tb * 1024 + b * 128: tb * 1024 + (b + 1) * 128], ww[:, bi, b * 128:(b + 1) * 128], ident_b)
                        return ins
                    kb.op("pe", trs, [w_d[bi], self.cstb_d], [self.pb_d[tb]])

                def f_copy(it):
                    pb0, t0, t1, n, bi, zb = geo(it)
                    tb = it["p"] % 2
                    if it["p"] % 3 == 2:
                        kb.op("dve", lambda e: e.tensor_copy(out=wT[:, bi, 0:n], in_=self.PB[:, tb * 1024: tb * 1024 + n]),
                              [self.pb_d[tb]], [wT_d[bi]])
                    else:
                        kb.op("act", lambda e: e.activation(out=wT[:, bi, 0:n], in_=self.PB[:, tb * 1024: tb * 1024 + n], func=AF.Copy),
                              [self.pb_d[tb]], [wT_d[bi]])

                def f_mmo(it, pair=pair):
                    pb0, t0, t1, n, bi, zb = geo(it)
                    ob, i = it["ob"], it["i"]
                    nb_ = n // 128
                    kb0 = it["ks"] // 128
                    first, last = it["first"], it["last"]

                    def mmo(e):
                        ins = None
                        for b in range(nb_):
                            ins = e.matmul(self.pf(ob, 128, (i % 4) * 128), lhsT=vv[:, kb0 + b, :], rhs=wT[:, bi, b * 128:(b + 1) * 128],
                                           start=(first and b == 0), stop=(last and b == nb_ - 1))
                        return ins
                    kb.op("pe", mmo, [wT_d[bi], v_d], [self.pf_d[ob]])
                    if last and i % 4 == 3:
                        g0 = (i // 4) * 512
                        kb.op("act", lambda e: e.activation(out=ao[pb0:pb0 + 64, pair, g0:g0 + 512], in_=self.PF[pb0:pb0 + 64, ob * 512: ob * 512 + 512],
                                                            func=AF.Copy), [self.pf_d[ob]], [ao_d[pair][i // 4]])
                sched = [(0, f_z), (0, f_exp), (0, f_mask), (2, f_er), (5, f_copy), (0, f_ln), (1, f_scan), (3, f_mult), (4, f_tr), (6, f_mmo)]
                nit = len(items)
                for step in range(nit + 6):
                    for lag, fn in sched:
                        j = step - lag
                        if 0 <= j < nit:
                            fn(items[j])
            self.proj_residual(ao, lambda k, tt: ao_d[k][tt], wo)
            kb.fence()
        self.ptr = mark

    def proj_residual(self, src, src_d, wdram):
        kb = self.kb
        for dc in range(KD):
            s = self.next_slot()
            wv = self.slot3(s, KD, 128)
            self.load_w(s, [(wv, wdram[:, dc * 128:(dc + 1) * 128].rearrange("(k p) n -> p k n", p=P))])
            for tt in range(4):
                bank = tt % 4

                def mm(e, tt=tt, bank=bank, wv=wv):
                    ins = None
                    for k in range(KD):
                        ins = e.matmul(self.pf(bank), lhsT=wv[:, k, :], rhs=src[:, k, tt * 512:(tt + 1) * 512],
                                       start=(k == 0), stop=(k == KD - 1))
                    return ins
                kb.op("pe", mm, [self.slot_d[s]] + [src_d(k, tt) for k in range(KD)], [self.pf_d[bank]])
                kb.op("dve", lambda e, dc=dc, tt=tt, bank=bank: e.tensor_tensor(
                    out=self.X[:, dc, tt * 512:(tt + 1) * 512], in0=self.X[:, dc, tt * 512:(tt + 1) * 512],
                    in1=self.pf(bank), op=ALU.add), [self.pf_d[bank], self.Xd[dc][tt]], [self.Xd[dc][tt]])

    def gmlp(self, l):
        kb, nc = self.kb, self.nc
        win = self.d["sg_w_in"][0]
        wo = self.d["sg_w_o"][0]
        kb.fence()
        mark = self.ptr
        sbt = self.alloc
        xn = sbt("g_xn", [P, KD, L], BF16)
        xn_d = kb.tiles_n("g_xn", 4, KD)
        vn = sbt("g_vn", [P, 16, D], BF16)
        vn_d = kb.tiles_n("g_vn", 16)
        wsT = sbt("g_wsT", [P, 8, 128], BF16)
        wsT_d = kb.tile("g_wsT")
        bsr = sbt("g_bsr", [1, D], BF16)
        bsr_d = kb.tile("g_bsr")
        ssq = sbt("g_ssq", [P, 16], F32)
        rstd = sbt("g_rstd", [P, 16], F32)
        ssq_d, rstd_d = kb.tile("g_ssq"), kb.tiles_n("g_rstd", 16)
        mark2 = self.ptr
        nt = self.norm_temps(None)
        for tt in range(4):
            self.rmsnorm(OFF_NG + (l * 2 + 0) * 8, tt, xn, tt * 512, xn_d[tt], nt)
        kb.fence()
        self.ptr = mark2
        wv_ = sbt("g_wv", [P, KD, D], BF16)
        wv_d = kb.tile("g_wv")
        vg = sbt("g_vg", [P, 2, D], F32)
        vg_d = kb.tiles_n("g_vg", 2)
        junk = sbt("g_junk", [P, D], BF16)
        junk_d = kb.tile("g_junk")
        gbc = sbt("g_gbc", [P, D], F32)
        gbc_d = kb.tile("g_gbc")
        grow = sbt("g_grow", [1, D], F32)
        grow_d = kb.tile("g_grow")
        wsf = sbt("g_wsf", [P, 8, 128], F32)
        wsf_d = kb.tile("g_wsf")
        kb.dma_multi("pool", [(wv_[:, :, c * 256:(c + 1) * 256],
                               win[:, D + c * 256: D + (c + 1) * 256].rearrange("(k p) n -> p k n", p=P)) for c in range(4)],
                     [], [wv_d], "g_wv")
        kb.dma("pool", bsr[:], self.d["sg_b"].rearrange("a g t -> a (g t)"), [], [bsr_d], "g_bsr")
        kb.dma("sp", grow[:], self.d["sg_norm_g"], [], [grow_d], "g_grow")
        kb.dma("sp", wsf[:], self.d["sg_w_s"][0].rearrange("g t s -> t g s"), [], [wsf_d], "g_wsf")
        kb.op("dve", lambda e: e.memset(ssq[:], 0.0), [], [ssq_d])
        ones_row_f = self.cst[0:1, C_ONES:C_ONES + 128]
        for nh in range(2):
            kb.op("pe", lambda e, nh=nh: e.matmul(self.pf(4 + nh), lhsT=ones_row_f, rhs=grow[0:1, nh * 512:(nh + 1) * 512],
                                                 start=True, stop=True), [grow_d, self.cst_d], [self.pf_d[4 + nh]])
            kb.op("dve", lambda e, nh=nh: e.tensor_copy(out=gbc[:, nh * 512:(nh + 1) * 512], in_=self.pf(4 + nh)),
                  [self.pf_d[4 + nh]], [gbc_d])
        trilT = self.cst[:, C_TRILT:C_TRILT + 128]
        for g in range(8):
            bank = 4 + g // 4
            kb.op("pe", lambda e, g=g, bank=bank: e.transpose(self.pf(bank, 128, (g % 4) * 128), wsf[:, g, :], self.ident_f()),
                  [wsf_d, self.cst_d], [self.pf_d[bank]])
            kb.op("dve", lambda e, g=g, bank=bank: e.tensor_tensor(out=wsT[:, g, :], in0=self.pf(bank, 128, (g % 4) * 128), in1=trilT,
                                                                   op=ALU.mult), [self.pf_d[bank], self.cst_d], [wsT_d])
        for i in range(16):
            b0 = 2 * (i % 2)
            vb = i % 2
            for nh in range(2):
                def mm(e, i=i, nh=nh, b0=b0):
                    ins = None
                    for k in range(KD):
                        ins = e.matmul(self.pf(b0 + nh), lhsT=xn[:, k, i * 128:(i + 1) * 128], rhs=wv_[:, k, nh * 512:(nh + 1) * 512],
                                       start=(k == 0), stop=(k == KD - 1))
                    return ins
                kb.op("pe", mm, [wv_d] + xn_d[i // 4], [self.pf_d[b0 + nh]])
            kb.op("act", lambda e, vb=vb, b0=b0: e.activation(out=vg[:, vb, :], in_=self.PF[:, b0 * 512: b0 * 512 + 1024],
                                                              func=AF.Gelu_apprx_tanh), [self.pf_d[b0], self.pf_d[b0 + 1]], [vg_d[vb]])
            kb.op("act", lambda e, vb=vb, i=i: e.activation(out=junk[:], in_=vg[:, vb, :], func=AF.Square, accum_out=ssq[:, i:i + 1]),
                  [vg_d[vb], ssq_d], [junk_d, rstd_d[i]])
            kb.op("act", lambda e, i=i: e.activation(out=rstd[:, i:i + 1], in_=ssq[:, i:i + 1], func=AF.Sqrt,
                                                     bias=self.cst[:, C_EPS:C_EPS + 1], scale=1.0 / D), [rstd_d[i], self.cst_d], [rstd_d[i]])
            kb.op("dve", lambda e, i=i: e.reciprocal(out=rstd[:, i:i + 1], in_=rstd[:, i:i + 1]), [rstd_d[i]], [rstd_d[i]])
            kb.op("dve", lambda e, i=i, vb=vb: e.scalar_tensor_tensor(out=vn[:, i, :], in0=vg[:, vb, :], scalar=rstd[:, i:i + 1], in1=gbc[:],
                                                                    op0=ALU.mult, op1=ALU.mult), [vg_d[vb], rstd_d[i], gbc_d], [vn_d[i]])
        kb.fence()
        self.ptr = mark2
        gated = sbt("g_gated", [P, KD, L], BF16)
        gated_d = kb.tiles_n("g_gated", KD, 4)
        ug = sbt("g_ug", [P, 2, 512], F32)
        ug_d = kb.tiles_n("g_ug", 2)
        ones_row_b = self.cstb[0:1, 384:512]
        cnt = 0
        for g in range(8):
            s = self.next_slot()
            wu = self.slot3(s, KD, 128)
            self.load_w(s, [(wu, win[:, g * 128:(g + 1) * 128].rearrange("(k p) n -> p k n", p=P))])
            for tt in range(4):
                sb_ = (cnt % 2) * 2
                ub_ = sb_ + 1
                ui = cnt % 2
                cnt += 1

                def mmsv(e, g=g, tt=tt, sb_=sb_):
                    ins = None
                    for c in range(4):
                        o = self.pf(sb_, 128, c * 128)
                        e.matmul(o, lhsT=vn[:, 4 * tt + c, g * 128:(g + 1) * 128], rhs=wsT[:, g, :], start=True, stop=False)
                        ins = e.matmul(o, lhsT=ones_row_b, rhs=bsr[0:1, g * 128:(g + 1) * 128], start=False, stop=True)
                    return ins
                kb.op("pe", mmsv, [vn_d[4 * tt + c] for c in range(4)] + [wsT_d, bsr_d, self.cstb_d], [self.pf_d[sb_]])

                def mmu(e, tt=tt, ub_=ub_, wu=wu):
                    ins = None
                    for k in range(KD):
                        ins = e.matmul(self.pf(ub_), lhsT=wu[:, k, :], rhs=xn[:, k, tt * 512:(tt + 1) * 512], start=(k == 0), stop=(k == KD - 1))
                    return ins
                kb.op("pe", mmu, [self.slot_d[s]] + xn_d[tt], [self.pf_d[ub_]])
                kb.op("act", lambda e, ui=ui, ub_=ub_: e.activation(out=ug[:, ui, :], in_=self.pf(ub_), func=AF.Gelu_apprx_tanh),
                      [self.pf_d[ub_]], [ug_d[ui]])
                kb.op("dve", lambda e, g=g, tt=tt, ui=ui, sb_=sb_: e.tensor_tensor(
                    out=gated[:, g, tt * 512:(tt + 1) * 512], in0=ug[:, ui, :], in1=self.pf(sb_), op=ALU.mult),
                    [ug_d[ui], self.pf_d[sb_]], [gated_d[g][tt]])
        self.proj_residual(gated, lambda k, tt: gated_d[k][tt], wo)
        kb.fence()
        self.ptr = mark

    def s5(self, l):
        kb, nc = self.kb, self.nc
        d = self.d
        win = d["ssm_w_in"][0]
        wglu = d["ssm_w_glu"][0]
        PI = math.pi
        kb.fence()
        mark = self.ptr
        sbt = self.alloc
        xn = sbt("s_xn", [P, KD, L], BF16)
        xn_d = kb.tiles_n("s_xn", 4, KD)
        yg = sbt("s_yg", [P, KD, L], BF16)
        yg_d = kb.tiles_n("s_yg", KD, 4)
        c2 = sbt("s_c2", [P, C2_REP], F32)
        c2_d = kb.tile("s_c2")
        bbt = sbt("s_bbt", [P, 2, 8, 64], BF16)
        bbt_d = kb.tile("s_bbt")
        ct = sbt("s_ct", [P, 2, 8, 64], BF16)
        ct_d = kb.tile("s_ct")
        svr = sbt("s_svr", [P, 32], F32)
        svt = sbt("s_svt", [P, 32], F32)
        svc = sbt("s_svc", [P, 32], F32)
        svs = sbt("s_svs", [P, 32], F32)
        svn = sbt("s_svn", [P, 32], F32)
        ctmp = sbt("s_ctmp", [P, 2], F32)
        ctmp_d = kb.tile("s_ctmp")
        sv_d = kb.tile("s_sv")
        carry = sbt("s_carry", [P, 2, 32], F32)
        carry_d = kb.tiles_n("s_carry", 32)
        kb.dma("sp", c2[:], d["consts2"][:, 0:C2_REP], [], [c2_d], "s_c2")
        kb.dma("pool", ct[:, 0, :, :], d["ssm_c_re"][0].rearrange("(q gl) h p -> (gl h) q p", q=8), [], [ct_d], "s_ct")
        kb.dma("pool", ct[:, 1, :, :], d["ssm_c_im"][0].rearrange("(q gl) h p -> (gl h) q p", q=8), [], [ct_d], "s_ct")
        mark2 = self.ptr
        nt = self.norm_temps(None)
        for tt in range(4):
            self.rmsnorm(OFF_NG + (l * 2 + 0) * 8, tt, xn, tt * 512, xn_d[tt], nt)
        kb.fence()
        self.ptr = mark2
        bn = sbt("s_bn", [64, 2, 1024], F32)
        bn_d = kb.tile("s_bn")
        kb.dma("sp", bn[:, 0, :], d["ssm_b_re"][0].rearrange("g p h -> g (p h)"), [], [bn_d], "s_bn")
        kb.dma("sp", bn[:, 1, :], d["ssm_b_im"][0].rearrange("g p h -> g (p h)"), [], [bn_d], "s_bn")
        NT = 26
        tp = sbt("s_tp", [64, NT, 64], F32)
        tp_d = kb.tiles_n("s_tp", NT)
        dtc = sbt("s_dt", [64, 2], F32)
        dt_d = kb.tile("s_dt")
        (LR, LI, DLR, DLI, MAG, SA, CA, SN, CS, AR, AI, T1, T2, DEN, RDEN, AM1, U1, U2, CR, CI) = range(20)
        T = lambda i: tp[:, i, :]
        kb.dma("sp", T(LR), d["ssm_lam_re"][0], [], [tp_d[LR]], "s_p0")
        kb.dma("sp", T(LI), d["ssm_lam_im"][0], [], [tp_d[LI]], "s_p1")
        kb.dma("sp", dtc[:, 0:1], d["ssm_log_dt"].rearrange("a g -> g a"), [], [dt_d], "s_p2")
        kb.op("act", lambda e: e.activation(out=dtc[:, 1:2], in_=dtc[:, 0:1], func=AF.Exp), [dt_d], [dt_d])
        dtcol = dtc[:, 1:2]

        def v1(eng, fn, r, w):
            kb.op(eng, fn, [tp_d[i] for i in r] + [dt_d, self.cst_d], [tp_d[i] for i in w])
        v1("dve", lambda e: e.tensor_scalar_min(out=T(LR), in0=T(LR), scalar1=-1e-4), [LR], [LR])
        v1("dve", lambda e: e.tensor_scalar_mul(out=T(DLR), in0=T(LR), scalar1=dtcol), [LR], [DLR])
        v1("dve", lambda e: e.tensor_scalar_mul(out=T(DLI), in0=T(LI), scalar1=dtcol), [LI], [DLI])
        v1("act", lambda e: e.activation(out=T(MAG), in_=T(DLR), func=AF.Exp), [DLR], [MAG])
        hpi64 = self.cst[0:64, C_HALFPI:C_HALFPI + 1]
        v1("dve", lambda e: e.tensor_scalar(out=T(T1), in0=T(DLI), scalar1=1.0 / (2 * PI), scalar2=MAGIC, op0=ALU.mult, op1=ALU.add), [DLI], [T1])
        v1("dve", lambda e: e.tensor_scalar_add(out=T(T1), in0=T(T1), scalar1=-MAGIC), [T1], [T1])
        v1("dve", lambda e: e.scalar_tensor_tensor(out=T(SA), in0=T(T1), scalar=-CW1, in1=T(DLI), op0=ALU.mult, op1=ALU.add), [T1, DLI], [SA])
        v1("dve", lambda e: e.scalar_tensor_tensor(out=T(SA), in0=T(T1), scalar=-CW2, in1=T(SA), op0=ALU.mult, op1=ALU.add), [T1, SA], [SA])
        v1("dve", lambda e: e.tensor_scalar(out=T(SA), in0=T(SA), scalar1=-PI_LO, scalar2=PI_LO, op0=ALU.max, op1=ALU.min), [SA], [SA])
        v1("dve", lambda e: e.scalar_tensor_tensor(out=T(CA), in0=T(SA), scalar=-1.0, in1=T(SA), op0=ALU.mult, op1=ALU.max), [SA], [CA])
        v1("act", lambda e: e.activation(out=T(SN), in_=T(SA), func=AF.Sin), [SA], [SN])
        v1("act", lambda e: e.activation(out=T(CS), in_=T(CA), func=AF.Sin, scale=-1.0, bias=hpi64), [CA], [CS])
        v1("dve", lambda e: e.tensor_tensor(out=T(AR), in0=T(MAG), in1=T(CS), op=ALU.mult), [MAG, CS], [AR])
        v1("dve", lambda e: e.tensor_tensor(out=T(AI), in0=T(MAG), in1=T(SN), op=ALU.mult), [MAG, SN], [AI])
        v1("dve", lambda e: e.tensor_tensor(out=T(T1), in0=T(LR), in1=T(LR), op=ALU.mult), [LR], [T1])
        v1("dve", lambda e: e.tensor_tensor(out=T(T2), in0=T(LI), in1=T(LI), op=ALU.mult), [LI], [T2])
        v1("dve", lambda e: e.tensor_tensor(out=T(DEN), in0=T(T1), in1=T(T2), op=ALU.add), [T1, T2], [DEN])
        v1("dve", lambda e: e.reciprocal(out=T(RDEN), in_=T(DEN)), [DEN], [RDEN])
        v1("dve", lambda e: e.tensor_scalar_add(out=T(AM1), in0=T(AR), scalar1=-1.0), [AR], [AM1])
        v1("dve", lambda e: e.tensor_tensor(out=T(U1), in0=T(AM1), in1=T(LR), op=ALU.mult), [AM1, LR], [U1])
        v1("dve", lambda e: e.tensor_tensor(out=T(U2), in0=T(AI), in1=T(LI), op=ALU.mult), [AI, LI], [U2])
        v1("dve", lambda e: e.tensor_tensor(out=T(U1), in0=T(U1), in1=T(U2), op=ALU.add), [U1, U2], [U1])
        v1("dve", lambda e: e.tensor_tensor(out=T(CR), in0=T(U1), in1=T(RDEN), op=ALU.mult), [U1, RDEN], [CR])
        v1("dve", lambda e: e.tensor_tensor(out=T(U1), in0=T(AI), in1=T(LR), op=ALU.mult), [AI, LR, CR], [U1])
        v1("dve", lambda e: e.tensor_tensor(out=T(U2), in0=T(AM1), in1=T(LI), op=ALU.mult), [AM1, LI], [U2])
        v1("dve", lambda e: e.tensor_tensor(out=T(U1), in0=T(U1), in1=T(U2), op=ALU.subtract), [U1, U2], [U1])
        v1("dve", lambda e: e.tensor_tensor(out=T(CI), in0=T(U1), in1=T(RDEN), op=ALU.mult), [U1, RDEN], [CI])
        bbn = sbt("s_bbn", [64, 2, 1024], F32)
        bbn_d = kb.tile("s_bbn")
        tmpn = sbt("s_tmpn", [64, 1024], F32)
        tmpn_d = kb.tile("s_tmpn")
        R_, I_ = 0, 1
        cb = lambda i: T(i).unsqueeze(2).broadcast_to([64, 64, 16])
        nat = lambda ri: bn[:, ri, :].rearrange("g (p h) -> g p h", h=16)
        outv = lambda t_: t_.rearrange("g (h p) -> g p h", h=16)
        kb.op("dve", lambda e: e.tensor_tensor(out=outv(bbn[:, R_, :]), in0=cb(CR), in1=nat(R_), op=ALU.mult), [tp_d[CR], bn_d], [bbn_d])
        kb.op("dve", lambda e: e.tensor_tensor(out=outv(tmpn[:]), in0=cb(CI), in1=nat(I_), op=ALU.mult), [tp_d[CI], bn_d], [tmpn_d])
        kb.op("dve", lambda e: e.tensor_tensor(out=bbn[:, R_, :], in0=bbn[:, R_, :], in1=tmpn[:], op=ALU.subtract), [bbn_d, tmpn_d], [bbn_d])
        kb.op("dve", lambda e: e.tensor_tensor(out=outv(bbn[:, I_, :]), in0=cb(CR), in1=nat(I_), op=ALU.mult), [tp_d[CR], bn_d], [bbn_d])
        kb.op("dve", lambda e: e.tensor_tensor(out=outv(tmpn[:]), in0=cb(CI), in1=nat(R_), op=ALU.mult), [tp_d[CI], bn_d, bbn_d], [tmpn_d])
        kb.op("dve", lambda e: e.tensor_tensor(out=bbn[:, I_, :], in0=bbn[:, I_, :], in1=tmpn[:], op=ALU.add), [bbn_d, tmpn_d], [bbn_d])
        scr_d = kb.tile("s_scr")
        kb.dma("sp", self.scr.rearrange("r g n -> g r n"), bbn[:], [bbn_d], [scr_d], "s_scr_w")
        for ri in range(2):
            kb.dma("pool", bbt[:, ri, :, :], self.scr[ri].rearrange("(q gl) (h p) -> (gl h) q p", q=8, h=16), [scr_d], [bbt_d], "s_scr_r")
        A5, K5, R5, B5, C5, S5_ = 20, 21, 22, 23, 24, 25
        v1("dve", lambda e: e.tensor_scalar_mul(out=T(A5), in0=T(DLI), scalar1=512.0), [DLI], [A5])
        v1("dve", lambda e: e.tensor_scalar(out=T(K5), in0=T(A5), scalar1=1.0 / (2 * PI), scalar2=MAGIC, op0=ALU.mult, op1=ALU.add), [A5], [K5])
        v1("dve", lambda e: e.tensor_scalar_add(out=T(K5), in0=T(K5), scalar1=-MAGIC), [K5], [K5])
        v1("dve", lambda e: e.scalar_tensor_tensor(out=T(R5), in0=T(K5), scalar=-CW1, in1=T(A5), op0=ALU.mult, op1=ALU.add), [K5, A5], [R5])
        v1("dve", lambda e: e.scalar_tensor_tensor(out=T(R5), in0=T(K5), scalar=-CW2, in1=T(R5), op0=ALU.mult, op1=ALU.add), [K5, R5], [R5])
        v1("dve", lambda e: e.tensor_scalar(out=T(R5), in0=T(R5), scalar1=-PI_LO, scalar2=PI_LO, op0=ALU.max, op1=ALU.min), [R5], [R5])
        v1("dve", lambda e: e.scalar_tensor_tensor(out=T(B5), in0=T(R5), scalar=-1.0, in1=T(R5), op0=ALU.mult, op1=ALU.max), [R5], [B5])
        v1("act", lambda e: e.activation(out=T(S5_), in_=T(R5), func=AF.Sin), [R5], [S5_])
        v1("act", lambda e: e.activation(out=T(C5), in_=T(B5), func=AF.Sin, scale=-1.0, bias=hpi64), [B5], [C5])
        dup = sbt("s_dup", [64, 4, 128], F32)
        dup_d = kb.tile("s_dup")
        for qi, (src, dst) in enumerate(((MAG, svr), (DLI, svt), (C5, svc), (S5_, svs))):
            kb.op("dve", lambda e, qi=qi, src=src: e.tensor_copy(out=dup[:, qi, 0:64], in_=T(src)), [tp_d[src]], [dup_d])
            kb.op("dve", lambda e, qi=qi, src=src: e.tensor_copy(out=dup[:, qi, 64:128], in_=T(src)), [tp_d[src]], [dup_d])
            bank = 2 + qi
            kb.op("pe", lambda e, qi=qi, bank=bank: e.transpose(self.pf(bank, 64), dup[:, qi, :], self.cst[0:64, C_IDENT:C_IDENT + 64]),
                  [dup_d, self.cst_d], [self.pf_d[bank]])
            kb.op("dve", lambda e, dst=dst, bank=bank: e.tensor_copy(out=dst[0:64, :], in_=self.PF[0:64, bank * 512: bank * 512 + 64: 2]),
                  [self.pf_d[bank]], [sv_d])
            kb.op("dve", lambda e, dst=dst, bank=bank: e.tensor_copy(out=dst[64:128, :], in_=self.PF[64:128, bank * 512 + 1: bank * 512 + 64: 2]),
                  [self.pf_d[bank]], [sv_d])
        kb.op("dve", lambda e: e.tensor_scalar_mul(out=svn[:], in0=svs[:], scalar1=-1.0), [sv_d], [sv_d])
        kb.op("dve", lambda e: e.memset(carry[:], 0.0), [], carry_d)
        kb.fence()
        self.ptr = mark2
        u32 = sbt("s_u32", [P, L], F32)
        u32_d = kb.tiles_n("s_u32", 4)
        ub = sbt("s_ub", [P, L], BF16)
        ub_d = kb.tiles_n("s_ub", 4)
        bwm = sbt("s_bwm", [P, 2, 4, 128], BF16)
        bwm_d = kb.tile("s_bwm")
        bwf = sbt("s_bwf", [P, 2, 128], F32)
        bwf_d = kb.tile("s_bwf")
        cw = sbt("s_cw", [P, 2, 4, 128], BF16)
        cw_d = kb.tile("s_cw")
        ctd, ctd_d = bwf, bwf_d
        tabs = sbt("s_tabs", [P, 2, 2, 512], F32)
        tabs_d = kb.tiles_n("s_tabs", 2, 2)
        rbt = sbt("s_rbt", [P, 512], F32)
        rbt_d = kb.tile("s_rbt")
        un = sbt("s_un", [P, 2, 512], F32)
        un_d = kb.tiles_n("s_un", 2)
        wk = sbt("s_wk", [P, 4, 512], F32)
        wk_d = kb.tiles_n("s_wk", 4)
        xrb = sbt("s_xr", [P, 2, 512], BF16)
        xr_d = kb.tiles_n("s_xr", 2)
        iota = c2[:, C2_IOTA:C2_IOTA + 512]
        bmask = c2[:, C2_BMASK:C2_BMASK + 128]
        hpi = self.cst[:, C_HALFPI:C_HALFPI + 1]
        W = lambda i: wk[:, i, :]
        ysum, ysum_d = wk[:, 0, :], wk_d[0]
        YB = [4, 5, 6, 7]

        def tables(j):
            par = j % 2
            tS, tC, tK = tabs[:, par, 0, :], tabs[:, par, 1, :], wk[:, 3, :]
            dS, dC = tabs_d[par]
            dK = wk_d[3]
            th = svt[:, j:j + 1]
            kb.op("dve", lambda e: e.tensor_scalar_mul(out=tC, in0=iota, scalar1=th), [c2_d, sv_d], [dC])
            kb.op("dve", lambda e: e.tensor_scalar(out=tK, in0=tC, scalar1=1.0 / (2 * PI), scalar2=MAGIC, op0=ALU.mult, op1=ALU.add), [dC], [dK])
            kb.op("dve", lambda e: e.tensor_scalar_add(out=tK, in0=tK, scalar1=-MAGIC), [dK], [dK])
            kb.op("dve", lambda e: e.scalar_tensor_tensor(out=tS, in0=tK, scalar=-CW1, in1=tC, op0=ALU.mult, op1=ALU.add), [dK, dC], [dS])
            kb.op("dve", lambda e: e.scalar_tensor_tensor(out=tS, in0=tK, scalar=-CW2, in1=tS, op0=ALU.mult, op1=ALU.add), [dK, dS], [dS])
            kb.op("dve", lambda e: e.tensor_scalar(out=tS, in0=tS, scalar1=-PI_LO, scalar2=PI_LO, op0=ALU.max, op1=ALU.min), [dS], [dS])
            kb.op("dve", lambda e: e.scalar_tensor_tensor(out=tC, in0=tS, scalar=-1.0, in1=tS, op0=ALU.mult, op1=ALU.max), [dS], [dC])
            kb.op("act", lambda e: e.activation(out=tS, in_=tS, func=AF.Sin), [dS], [dS])
            kb.op("act", lambda e: e.activation(out=tC, in_=tC, func=AF.Sin, scale=-1.0, bias=hpi), [dC, self.cst_d], [dC])

        def rho_table(j):
            rho = svr[:, j:j + 1]
            kb.op("act", lambda e: e.activation(out=rbt[:], in_=iota, func=AF.Identity, bias=rho, scale=0.0), [c2_d, sv_d], [rbt_d])

        ucount = [0]

        def stage_a(jj, tt, up):
            ts = slice(tt * 512, (tt + 1) * 512)
            for ri in range(2):
                bk = 2 * up + ri
                kb.op("pe", lambda e, ri=ri, bk=bk: e.matmul(self.pf(bk), lhsT=bwm[:, ri, jj, :], rhs=ub[:, ts], start=True, stop=True),
                      [bwm_d, ub_d[tt]], [self.pf_d[bk]])

        def stage_bcd(j, jj, tt, up):
            par = j % 2
            tS, tC, tK = tabs[:, par, 0, :], tabs[:, par, 1, :], rbt[:]
            dS, dC = tabs_d[par]
            dK = rbt_d
            brp, bip = self.pf(2 * up), self.pf(2 * up + 1)
            dbr, dbi = self.pf_d[2 * up], self.pf_d[2 * up + 1]
            YR, YI = un[:, 0, :], un[:, 1, :]

            def tt_(eng, o, od, a, ad, b, bd, op):
                kb.op(eng, lambda e: e.tensor_tensor(out=o, in0=a, in1=b, op=op), [ad, bd], [od])
            tt_("dve", W(0), wk_d[0], brp, dbr, tC, dC, ALU.mult)
            tt_("dve", W(1), wk_d[1], bip, dbi, tS, dS, ALU.mult)
            tt_("pool", W(0), wk_d[0], W(0), wk_d[0], W(1), wk_d[1], ALU.add)
            tt_("dve", W(2), wk_d[2], bip, dbi, tC, dC, ALU.mult)
            tt_("dve", W(3), wk_d[3], brp, dbr, tS, dS, ALU.mult)
            tt_("dve", W(2), wk_d[2], W(2), wk_d[2], W(3), wk_d[3], ALU.subtract)
            for ri, src in ((0, 0), (1, 2)):
                kb.op("dve", lambda e, ri=ri, src=src: e.tensor_tensor_scan(
                    out=un[:, ri, :], data0=tK, data1=W(src), initial=carry[:, ri, j:j + 1], op0=ALU.mult, op1=ALU.add),
                    [dK, wk_d[src], carry_d[j]], [un_d[ri]])
            if tt < 3:
                yrl, yil = un[:, 0, 511:512], un[:, 1, 511:512]
                cc, ss, ns = svc[:, j:j + 1], svs[:, j:j + 1], svn[:, j:j + 1]
                kb.op("dve", lambda e: e.tensor_scalar_mul(out=ctmp[:, 0:1], in0=yrl, scalar1=cc), [un_d[0], sv_d], [ctmp_d])
                kb.op("dve", lambda e: e.tensor_scalar_mul(out=ctmp[:, 1:2], in0=yrl, scalar1=ss), [un_d[0], sv_d, ctmp_d], [ctmp_d])
                kb.op("dve", lambda e: e.scalar_tensor_tensor(out=carry[:, 0, j:j + 1], in0=yil, scalar=ns, in1=ctmp[:, 0:1], op0=ALU.mult, op1=ALU.add),
                      [un_d[1], sv_d, ctmp_d], [carry_d[j]])
                kb.op("dve", lambda e: e.scalar_tensor_tensor(out=carry[:, 1, j:j + 1], in0=yil, scalar=cc, in1=ctmp[:, 1:2], op0=ALU.mult, op1=ALU.add),
                      [un_d[1], sv_d, ctmp_d, carry_d[j]], [carry_d[j]])
            tt_("dve", W(0), wk_d[0], tC, dC, YR, un_d[0], ALU.mult)
            tt_("dve", W(1), wk_d[1], tS, dS, YI, un_d[1], ALU.mult)
            kb.op("dve", lambda e: e.tensor_tensor(out=xrb[:, 0, :], in0=W(0), in1=W(1), op=ALU.subtract), [wk_d[0], wk_d[1]], [xr_d[0]])
            tt_("dve", W(3), wk_d[3], tS, dS, YR, un_d[0], ALU.mult)
            tt_("dve", W(2), wk_d[2], tC, dC, YI, un_d[1], ALU.mult)
            kb.op("dve", lambda e: e.tensor_tensor(out=xrb[:, 1, :], in0=W(3), in1=W(2), op=ALU.add), [wk_d[3], wk_d[2]], [xr_d[1]])
            yb = YB[tt]
            for ri in range(2):
                kb.op("pe", lambda e, ri=ri: e.matmul(self.pfx(yb), lhsT=cw[:, ri, jj, :], rhs=xrb[:, ri, :],
                                                     start=(jj == 0 and ri == 0), stop=(jj == 3 and ri == 1)),
                      [cw_d, xr_d[ri]], [self.pfx_d(yb)])

        for q in range(8):
            s = self.next_slot()
            wv = self.slot3(s, KD, 128)
            self.load_w(s, [(wv, win[:, q * 128:(q + 1) * 128].rearrange("(k p) n -> p k n", p=P))])
            for tt in range(4):
                bank = tt % 2

                def mm(e, tt=tt, bank=bank, wv=wv):
                    ins = None
                    for k in range(KD):
                        ins = e.matmul(self.pf(bank), lhsT=wv[:, k, :], rhs=xn[:, k, tt * 512:(tt + 1) * 512], start=(k == 0), stop=(k == KD - 1))
                    return ins
                kb.op("pe", mm, [self.slot_d[s]] + xn_d[tt], [self.pf_d[bank]])
                kb.op("dve", lambda e, tt=tt, bank=bank: e.tensor_copy(out=u32[:, tt * 512:(tt + 1) * 512], in_=self.pf(bank)),
                      [self.pf_d[bank]], [u32_d[tt]])
                kb.op("dve", lambda e, tt=tt, bank=bank: e.tensor_copy(out=ub[:, tt * 512:(tt + 1) * 512], in_=self.pf(bank)),
                      [self.pf_d[bank]], [ub_d[tt]])
            for ri in range(2):
                kb.op("dve", lambda e, ri=ri, q=q: e.tensor_tensor(
                    out=bwf[:, ri, :].rearrange("p (a n) -> p a n", a=2), in0=bbt[:, ri, q:q + 1, :].broadcast_to([P, 2, 64]),
                    in1=bmask.rearrange("p (a n) -> p a n", a=2), op=ALU.mult), [bbt_d, c2_d], [bwf_d])
                for jj in range(4):
                    kb.op("dve", lambda e, ri=ri, jj=jj: e.tensor_scalar_mul(
                        out=bwm[:, ri, jj, :], in0=bwf[:, ri, :], scalar1=c2[:, C2_RMASK + jj:C2_RMASK + jj + 1]), [bwf_d, c2_d], [bwm_d])
                kb.op("dve", lambda e, ri=ri, q=q: e.tensor_copy(
                    out=ctd[:, ri, :].rearrange("p (a n) -> p a n", a=2), in_=ct[:, ri, q:q + 1, :].broadcast_to([P, 2, 64])), [ct_d], [ctd_d])
                bank = ri
                kb.op("pe", lambda e, ri=ri, bank=bank: e.transpose(self.pf(bank, 128), ctd[:, ri, :], self.ident_f()),
                      [ctd_d, self.cst_d], [self.pf_d[bank]])
                for jj in range(4):
                    cm = c2[:, C2_CMASK + jj * 128: C2_CMASK + (jj + 1) * 128]
                    if ri == 0:
                        kb.op("dve", lambda e, jj=jj, bank=bank, cm=cm: e.tensor_tensor(out=cw[:, 0, jj, :], in0=self.pf(bank, 128), in1=cm, op=ALU.mult),
                              [self.pf_d[bank], c2_d], [cw_d])
                    else:
                        kb.op("dve", lambda e, jj=jj, bank=bank, cm=cm: e.scalar_tensor_tensor(
                            out=cw[:, 1, jj, :], in0=self.pf(bank, 128), scalar=-1.0, in1=cm, op0=ALU.mult, op1=ALU.mult),
                            [self.pf_d[bank], c2_d], [cw_d])
            units = [(jj, tt) for jj in range(4) for tt in range(4)]
            tables(4 * q)
            stage_a(units[0][0], units[0][1], ucount[0] % 2)
            for ui, (jj, tt) in enumerate(units):
                up = ucount[0] % 2
                ucount[0] += 1
                if ui + 1 < len(units):
                    stage_a(units[ui + 1][0], units[ui + 1][1], ucount[0] % 2)
                if tt == 0:
                    rho_table(4 * q + jj)
                stage_bcd(4 * q + jj, jj, tt, up)
                if tt == 1 and jj < 3:
                    tables(4 * q + jj + 1)
            for tt in range(4):
                yb = YB[tt]
                ts = slice(tt * 512, (tt + 1) * 512)
                kb.op("dve", lambda e, tt=tt, q=q, yb=yb, ts=ts: e.scalar_tensor_tensor(
                    out=ysum, in0=u32[:, ts], scalar=self.col(OFF_SSMD + q), in1=self.pfx(yb), op0=ALU.mult, op1=ALU.add),
                    [u32_d[tt], self.cols_d, self.pfx_d(yb)], [ysum_d])
                kb.op("act", lambda e, q=q, ts=ts: e.activation(out=yg[:, q, ts], in_=ysum, func=AF.Gelu_apprx_tanh), [ysum_d], [yg_d[q][tt]])
        kb.fence()
        self.ptr = mark2
        if os.environ.get("S5_STOP") in ("2", "3", "4", "5"):
            self.ptr = mark
            return
        sgm = sbt("s_sgm", [P, 2, 512], F32)
        sgm_d = kb.tiles_n("s_sgm", 2)
        gcnt = 0
        for dc in range(KD):
            s = self.next_slot()
            wv = self.slot3(s, KD, 256)
            self.load_w(s, [(wv[:, :, 0:128], wglu[:, dc * 128:(dc + 1) * 128].rearrange("(k p) n -> p k n", p=P)),
                            (wv[:, :, 128:256], wglu[:, D + dc * 128: D + (dc + 1) * 128].rearrange("(k p) n -> p k n", p=P))])
            for tt in range(4):
                b0 = (gcnt % 3) * 2
                gi = gcnt % 2
                gcnt += 1
                for ag in range(2):
                    def mm(e, ag=ag, tt=tt, b0=b0, wv=wv):
                        ins = None
                        for k in range(KD):
                            ins = e.matmul(self.pf(b0 + ag), lhsT=wv[:, k, ag * 128:(ag + 1) * 128], rhs=yg[:, k, tt * 512:(tt + 1) * 512],
                                           start=(k == 0), stop=(k == KD - 1))
                        return ins
                    kb.op("pe", mm, [self.slot_d[s]] + [yg_d[k][tt] for k in range(KD)], [self.pf_d[b0 + ag]])
                kb.op("act", lambda e, gi=gi, b0=b0: e.activation(out=sgm[:, gi, :], in_=self.pf(b0 + 1), func=AF.Sigmoid),
                      [self.pf_d[b0 + 1]], [sgm_d[gi]])
                kb.op("dve", lambda e, gi=gi, b0=b0: e.tensor_tensor(out=sgm[:, gi, :], in0=self.pf(b0), in1=sgm[:, gi, :], op=ALU.mult),
                      [self.pf_d[b0], sgm_d[gi]], [sgm_d[gi]])
                kb.op("dve", lambda e, dc=dc, tt=tt, gi=gi: e.tensor_tensor(
                    out=self.X[:, dc, tt * 512:(tt + 1) * 512], in0=self.X[:, dc, tt * 512:(tt + 1) * 512], in1=sgm[:, gi, :], op=ALU.add),
                    [sgm_d[gi], self.Xd[dc][tt]], [self.Xd[dc][tt]])
        kb.fence()
        self.ptr = mark


_CACHE = {}


def _pack_vecs(norm_g, final_norm_g, ffn_conv_w, ffn_conv_b, ssm_d):
    v = np.zeros((NVROWS, 128), np.float32)
    v[OFF_NG:OFF_NG + 64] = np.asarray(norm_g, np.float32).reshape(64, 128)
    v[OFF_FNG:OFF_FNG + 8] = np.asarray(final_norm_g, np.float32).reshape(8, 128)
    v[OFF_CW:OFF_CW + 528] = np.asarray(ffn_conv_w, np.float32).reshape(528, 128)
    v[OFF_CB:OFF_CB + 176] = np.asarray(ffn_conv_b, np.float32).reshape(176, 128)
    v[OFF_SSMD:OFF_SSMD + 8] = np.asarray(ssm_d, np.float32).reshape(8, 128)
    return v


def run_layers(x, inputs, layers, do_final, cores=NCORES):
    key = (tuple(layers), do_final)
    if key not in _CACHE:
        _CACHE[key] = Prog(layers, do_final).build()
    nc = _CACHE[key]
    consts, consts2 = _host_consts()
    vecs = _pack_vecs(inputs["norm_g"], inputs["final_norm_g"], inputs["ffn_conv_w"], inputs["ffn_conv_b"], inputs["ssm_d"])
    shared = {"consts": consts, "consts2": consts2, "vecs": vecs}
    for n in ("sb_w_qkv", "sb_w_o", "sg_w_in", "sg_norm_g", "sg_w_s", "sg_b", "sg_w_o", "ssm_w_in", "ssm_lam_re",
              "ssm_lam_im", "ssm_log_dt", "ssm_b_re", "ssm_b_im", "ssm_c_re", "ssm_c_im", "ssm_w_glu", "ffn_w_up",
              "ffn_w_down"):
        shared[n] = np.ascontiguousarray(np.asarray(inputs[n], np.float32))
    in_maps = []
    for c in range(cores):
        m = dict(shared)
        m["x"] = np.ascontiguousarray(np.asarray(x[c], np.float32))
        in_maps.append(m)
    res = run_bass_kernel_spmd(nc, in_maps, core_ids=list(range(cores)))
    return np.stack([np.asarray(r["y"]) for r in res.results], axis=0)


def kernel(**inputs):
    x = np.asarray(inputs["x"], np.float32)
    out = run_layers(x, inputs, [0, 1, 2, 3], True)
    return out.astype(np.float32)
```

```python
import math
import os
from contextlib import ExitStack

import numpy as np
import concourse.bass as bass
import concourse.mybir as mybir
from concourse.bass_utils import run_bass_kernel_spmd

F32 = mybir.dt.float32
BF16 = mybir.dt.bfloat16
AF = mybir.ActivationFunctionType
ALU = mybir.AluOpType

P = 128
L = 2048
D = 1024
KD = 8
DFF = 2816
NJ = 22
EPS = 1e-6
NCORES = 8

OFF_NG = 0
OFF_FNG = 64
OFF_CW = 72
OFF_CB = 600
OFF_SSMD = 776
NVROWS = 896

C_IDENT = 0
C_MASKL = 128
C_TRILT = 256
C_ONES = 384
C_EPS = 512
C_NEGPI = 513
C_HALFPI = 514
MAGIC = 12582912.0
CW1 = 6.28125
CW2 = 2.0 * math.pi - CW1
PI_LO = 3.1415925
HALFPI_LO = 1.5707962
C_MNEG = 640
NCONST1 = 768
C2_BMASK = 0
C2_CMASK = 128
C2_RMASK = 640
C2_IOTA = 704
C2_REP = 1216
NCONST2 = 2240


def _host_consts():
    c = np.zeros((P, NCONST1), np.float32)
    r = np.arange(P)
    c[:, C_IDENT:C_IDENT + 128] = np.eye(P, dtype=np.float32)
    c[:, C_MASKL:C_MASKL + 128] = (r[None, :] < r[:, None]).astype(np.float32)
    c[:, C_TRILT:C_TRILT + 128] = (r[:, None] <= r[None, :]).astype(np.float32)
    c[:, C_ONES:C_ONES + 128] = 1.0
    c[:, C_EPS] = EPS
    c[:, C_NEGPI] = -math.pi
    c[:, C_HALFPI] = HALFPI_LO
    c[:, C_MNEG:C_MNEG + 128] = np.where(r[None, :] >= r[:, None], -30000.0, 0.0).astype(np.float32)
    c2 = np.zeros((P, NCONST2), np.float32)
    for g in range(64):
        q, gl = divmod(g, 8)
        c2[g, C2_REP + q * 128 + gl * 16: C2_REP + q * 128 + gl * 16 + 16] = 1.0
    for row in range(P):
        gl8 = row // 16
        c2[row, C2_BMASK + (gl8 % 2) * 64: C2_BMASK + (gl8 % 2) * 64 + 64] = 1.0
        c2[row, C2_RMASK + gl8 // 2] = 1.0
    for m in range(4):
        for row in range(P):
            gl = row // 64
            g8 = 2 * m + gl
            c2[row, C2_CMASK + m * 128 + g8 * 16: C2_CMASK + m * 128 + g8 * 16 + 16] = 1.0
    c2[:, C2_IOTA:C2_IOTA + 512] = np.arange(512, dtype=np.float32)[None, :]
    return c, c2


class Dep:
    __slots__ = ("w", "r", "name", "excl")

    def __init__(self, name, w=None, excl=False):
        self.name = name
        self.w = w
        self.r = []
        self.excl = excl


class Op:
    __slots__ = ("eng", "fn", "idx", "deps", "waits", "signal", "sigval", "dkey", "dval", "gidx")


ENGS = ("pe", "act", "dve", "pool", "sp")


class KB:
    def __init__(self, nc):
        self.nc = nc
        self.ops = {e: [] for e in ENGS}
        self.all_ops = []
        self.tiles = []
        self.fence_op = None
        self.dma_count = {}
        self.dma_group = set()

    def tile(self, name, excl=False):
        t = Dep(name, self.fence_op, excl)
        self.tiles.append(t)
        return t

    def tiles_n(self, name, *dims):
        if len(dims) == 1:
            return [self.tile(f"{name}{i}") for i in range(dims[0])]
        return [self.tiles_n(f"{name}{i}_", *dims[1:]) for i in range(dims[0])]

    def op(self, eng, fn, reads=(), writes=(), dkey=None, nd=1):
        o = Op()
        o.eng = eng
        o.fn = fn
        o.idx = len(self.ops[eng])
        o.gidx = len(self.all_ops)
        o.deps = set()
        o.waits = []
        o.signal = False
        o.sigval = 0
        o.dkey = dkey
        o.dval = 0
        if dkey is not None:
            self.dma_count[dkey] = self.dma_count.get(dkey, 0) + 16 * nd
            o.dval = self.dma_count[dkey]
        for t in reads:
            if t.w is not None:
                o.deps.add(t.w)
            if t.excl:
                for r in t.r:
                    if r.eng != eng:
                        o.deps.add(r)
        for t in writes:
            if t.w is not None:
                o.deps.add(t.w)
            for r in t.r:
                o.deps.add(r)
        for t in reads:
            t.r.append(o)
        for t in writes:
            t.w = o
            t.r = []
        o.deps.discard(o)
        self.ops[eng].append(o)
        self.all_ops.append(o)
        return o

    def dma(self, queue, out, in_, reads, writes, key, group=False):
        if group:
            self.dma_group.add(key)
        return self.op(queue, lambda e: [e.dma_start(out=out, in_=in_)], reads, writes, dkey=key)

    def dma_multi(self, queue, pieces, reads, writes, key):
        return self.op(queue, lambda e: [e.dma_start(out=o, in_=i) for (o, i) in pieces], reads, writes, dkey=key,
                       nd=len(pieces))

    def fence(self):
        deps_r, deps_w = [], []
        o = self.op("sp", lambda e: e.nop(), reads=(), writes=())
        for t in self.tiles:
            if t.w is not None:
                o.deps.add(t.w)
            for r in t.r:
                o.deps.add(r)
        o.deps.discard(o)
        self.fence_op = o
        return o

    def finalize(self, es):
        nc = self.nc
        seen = {e: {} for e in ENGS}
        for o in self.all_ops:
            sn = seen[o.eng]
            need = {}
            for d in o.deps:
                if d.dkey is not None:
                    key = ("d", d.dkey)
                    val = self.dma_count[d.dkey] if d.dkey in self.dma_group else d.dval
                    if sn.get(key, 0) >= val:
                        continue
                    if need.get(key, (0, None))[0] < val:
                        need[key] = (val, d)
                else:
                    if d.eng == o.eng:
                        if o.eng == "pe" or (o.idx - d.idx) > 3:
                            continue
                    key = ("e", d.eng)
                    if sn.get(key, -1) >= d.idx:
                        continue
                    if need.get(key, (-1, None))[0] < d.idx:
                        need[key] = (d.idx, d)
            for key, (val, d) in need.items():
                sn[key] = val
                if key[0] == "e":
                    d.signal = True
                o.waits.append((key, d))
        for e in ENGS:
            cnt = 0
            for o in self.ops[e]:
                if o.signal:
                    cnt += 1
                    o.sigval = cnt
        esem = {e: es.enter_context(nc.semaphore(f"sem_{e}")) for e in ENGS}
        dsem = {k: es.enter_context(nc.semaphore(f"dsem_{k}")) for k in self.dma_count}
        block = es.enter_context(nc.Block())
        kb = self

        def run(ename, eng):
            for o in kb.ops[ename]:
                for key, d in o.waits:
                    if key[0] == "d":
                        val = kb.dma_count[d.dkey] if d.dkey in kb.dma_group else d.dval
                        eng.wait_ge(dsem[d.dkey], val)
                    else:
                        eng.wait_ge(esem[d.eng], d.sigval)
                ins = o.fn(eng)
                if o.dkey is not None:
                    for di in ins:
                        di.then_inc(dsem[o.dkey], 16)
                elif o.signal:
                    ins.then_inc(esem[ename], 1)

        @block.tensor
        def _(e):
            run("pe", e)

        @block.scalar
        def _(e):
            run("act", e)

        @block.vector
        def _(e):
            run("dve", e)

        @block.gpsimd
        def _(e):
            run("pool", e)

        @block.sync
        def _(e):
            run("sp", e)


class Prog:
    def __init__(self, layers, do_final, x_in_tokmajor=True):
        self.layers = layers
        self.do_final = do_final

    def build(self):
        nc = bass.Bass("TRN2", target_bir_lowering=False)
        self.nc = nc
        dt = nc.dram_tensor
        self.d = {}
        shapes = {
            "x": [L, D], "consts": [P, NCONST1], "consts2": [P, NCONST2], "vecs": [NVROWS, 128],
            "sb_w_qkv": [2, D, 3 * D], "sb_w_o": [2, D, D],
            "sg_w_in": [1, D, 2 * D], "sg_norm_g": [1, D], "sg_w_s": [1, 8, 128, 128],
            "sg_b": [1, 8, 128], "sg_w_o": [1, D, D],
            "ssm_w_in": [1, D, D], "ssm_lam_re": [1, 64, 64], "ssm_lam_im": [1, 64, 64],
            "ssm_log_dt": [1, 64], "ssm_b_re": [1, 64, 64, 16], "ssm_b_im": [1, 64, 64, 16],
            "ssm_c_re": [1, 64, 16, 64], "ssm_c_im": [1, 64, 16, 64], "ssm_w_glu": [1, D, 2 * D],
            "ffn_w_up": [4, D, 2 * DFF], "ffn_w_down": [4, DFF, D],
        }
        for n, s in shapes.items():
            self.d[n] = dt(n, s, F32, kind="ExternalInput").ap()
        self.y = dt("y", [L, D], F32, kind="ExternalOutput").ap()
        self.scr = dt("s5_scratch", [2, 64, 1024], F32, kind="Internal").ap()

        with ExitStack() as es:
            self.es = es
            kb = KB(nc)
            self.kb = kb
            self.ptr = (nc._sbuf_addr_for_side("left") + 63) // 64 * 64
            self.sb_end = nc._sbuf_addr_for_side("right")
            self.uid = 0
            sb = self.alloc
            self.X = sb("X", [P, KD, L], F32)
            self.Xd = kb.tiles_n("X", KD, 4)
            self.cst = sb("cst", [P, NCONST1], F32)
            self.cst_d = kb.tile("cst")
            self.cstb = sb("cstb", [P, 512], BF16)
            self.cstb_d = kb.tile("cstb")
            self.mneg = sb("mneg", [P, 128], BF16)
            self.cols = sb("cols", [P, NVROWS], F32)
            self.cols_d = kb.tile("cols")
            self.NSLOT = 3
            self.slots = [sb(f"slot{i}", [P, 3072], BF16) for i in range(self.NSLOT)]
            self.slot_d = [kb.tile(f"slot{i}") for i in range(self.NSLOT)]
            self.slot_i = 0
            self.PF = es.enter_context(nc.psum_tensor("pf", [P, 6 * 512], F32))
            self.PB = es.enter_context(nc.psum_tensor("pb", [P, 2 * 1024], BF16))
            self.pf_d = [kb.tile(f"pf{i}", excl=True) for i in range(6)]
            self.pb_d = [kb.tile(f"pb{i}", excl=True) for i in range(2)]

            self.setup()
            self.load_x()
            for l in self.layers:
                m = l % 3
                if m == 0:
                    self.attention(l, l // 3)
                elif m == 1:
                    self.gmlp(l)
                else:
                    self.s5(l)
                if os.environ.get("NOFFN") != "1":
                    self.ffn(l)
            self.store(self.do_final)
            kb.finalize(es)
        return nc

    def alloc(self, name, shape, dtype):
        nbytes = int(np.prod(shape[1:])) * (4 if dtype == F32 else 2)
        off = (self.ptr + 63) // 64 * 64
        assert off + nbytes <= self.sb_end, f"SBUF overflow allocating {name}: {off + nbytes} > {self.sb_end}"
        self.ptr = off + nbytes
        self.uid += 1
        return self.nc.alloc_sbuf_tensor_at(f"{name}_{self.uid}", list(shape), dtype, offset=off)

    def pf(self, b, n=512, off=0):
        return self.PF[:, b * 512 + off: b * 512 + off + n]

    def pfx(self, b):
        if b < 6:
            return self.pf(b)
        return self.PB[:, (b - 6) * 1024:(b - 5) * 1024].bitcast(F32)

    def pfx_d(self, b):
        return self.pf_d[b] if b < 6 else self.pb_d[b - 6]

    def ident_f(self):
        return self.cst[:, C_IDENT:C_IDENT + 128]

    def col(self, r):
        return self.cols[:, r:r + 1]

    def phase_scope(self):
        self.kb.fence()
        ps = ExitStack()
        return ps

    def setup(self):
        kb, nc = self.kb, self.nc
        kb.dma("sp", self.cst[:], self.d["consts"], [], [self.cst_d], "cst")
        kb.dma("pool", self.cstb[:], self.d["consts"][:, 0:512], [], [self.cstb_d], "cstb")
        kb.dma("pool", self.mneg[:], self.d["consts"][:, C_MNEG:C_MNEG + 128], [], [self.cstb_d], "cstb")
        mark = self.ptr
        vst = self.alloc("vstage", [P, 7, 128], F32)
        if True:
            vd = kb.tile("vstage")
            kb.dma("sp", vst[:], self.d["vecs"].rearrange("(a p) c -> p a c", p=P), [], [vd], "vst")
            for a in range(7):
                b = a // 4
                o = (a % 4) * 128
                kb.op("pe", lambda e, a=a, b=b, o=o: e.transpose(self.pf(b, 128, o), vst[:, a, :], self.ident_f()),
                      [vd, self.cst_d], [self.pf_d[b]])
            kb.op("dve", lambda e: e.tensor_copy(out=self.cols[:, 0:512], in_=self.pf(0)), [self.pf_d[0]], [self.cols_d])
            kb.op("dve", lambda e: e.tensor_copy(out=self.cols[:, 512:896], in_=self.pf(1, 384)), [self.pf_d[1]], [self.cols_d])
            kb.fence()
        self.ptr = mark

    def load_x(self):
        kb, nc = self.kb, self.nc
        x = self.d["x"]
        mark = self.ptr
        xst = self.alloc("xst", [P, 2, D], F32)
        if True:
            xd = [kb.tile("xst0"), kb.tile("xst1")]
            for i in range(16):
                s = i % 2
                kb.dma("sp", xst[:, s, :], x[i * 128:(i + 1) * 128, :], [], [xd[s]], f"xst{s}")
                for half in range(2):
                    b = 2 * s + half
                    for kk in range(4):
                        k = half * 4 + kk
                        kb.op("pe", lambda e, s=s, k=k, b=b, kk=kk: e.transpose(
                            self.pf(b, 128, kk * 128), xst[:, s, k * 128:(k + 1) * 128], self.ident_f()),
                            [xd[s], self.cst_d], [self.pf_d[b]])
                    eng = "dve" if half == 0 else "act"
                    outap = self.X[:, half * 4:half * 4 + 4, i * 128:(i + 1) * 128]
                    inap = self.pf(b).rearrange("p (k t) -> p k t", k=4)
                    if eng == "dve":
                        fn = lambda e, outap=outap, inap=inap: e.tensor_copy(out=outap, in_=inap)
                    else:
                        fn = lambda e, outap=outap, inap=inap: e.activation(out=outap, in_=inap, func=AF.Copy)
                    kb.op(eng, fn, [self.pf_d[b]], [self.Xd[k][i // 4] for k in range(half * 4, half * 4 + 4)])
            kb.fence()
        self.ptr = mark

    def store(self, do_final):
        kb, nc = self.kb, self.nc
        mark = self.ptr
        ps = None
        if True:
            sbt = self.alloc
            xo = sbt("xo", [P, KD, 512], F32)
            xo_d = kb.tiles_n("xo", KD)
            yst = sbt("yst", [P, 2, D], F32)
            yd = [kb.tile("yst0"), kb.tile("yst1")]
            nt = self.norm_temps(ps) if do_final else None
            cnt = 0
            for tt in range(4):
                if do_final:
                    self.rmsnorm(OFF_FNG, tt, xo, 0, xo_d, nt)
                    src = lambda k, c: xo[:, k, c * 128:(c + 1) * 128]
                    srcd = lambda k: xo_d[k]
                else:
                    src = lambda k, c, tt=tt: self.X[:, k, tt * 512 + c * 128: tt * 512 + (c + 1) * 128]
                    srcd = lambda k, tt=tt: self.Xd[k][tt]
                for c in range(4):
                    i = tt * 4 + c
                    s = cnt % 2
                    cnt += 1
                    for half in range(2):
                        b = 2 * s + half
                        for kk in range(4):
                            k = half * 4 + kk
                            kb.op("pe", lambda e, k=k, c=c, b=b, kk=kk, src=src: e.transpose(
                                self.pf(b, 128, kk * 128), src(k, c), self.ident_f()),
                                [srcd(k), self.cst_d], [self.pf_d[b]])
                        outap = yst[:, s, half * 512:(half + 1) * 512]
                        if half == 0:
                            kb.op("dve", lambda e, outap=outap, b=b: e.tensor_copy(out=outap, in_=self.pf(b)),
                                  [self.pf_d[b]], [yd[s]])
                        else:
                            kb.op("act", lambda e, outap=outap, b=b: e.activation(out=outap, in_=self.pf(b), func=AF.Copy),
                                  [self.pf_d[b]], [yd[s]])
                    kb.dma("sp", self.y[i * 128:(i + 1) * 128, :], yst[:, s, :], [yd[s]], [], f"yst{s}")
            fin = kb.tile("fin")
            kb.op("sp", lambda e: e.nop(), [], [yd[0], yd[1], fin])
            kb.fence()
        self.ptr = mark

    def norm_temps(self, ps):
        nc, kb = self.nc, self.kb
        sq = self.alloc("n_sq", [P, KD, 512], BF16)
        r1 = self.alloc("n_r1", [P, 512], F32)
        rs = self.alloc("n_rs", [P, 512], F32)
        return dict(sq=sq, r1=r1, rs=rs, sq_d=kb.tile("n_sq"), r1_d=kb.tile("n_r1"), rs_d=kb.tile("n_rs"))

    def rmsnorm(self, goff, tt, out, ocol0, out_d, nt, bank=5, out_scale_eng="dve"):
        kb = self.kb
        X = self.X
        ts = slice(tt * 512, (tt + 1) * 512)
        xr = [self.Xd[k][tt] for k in range(KD)]
        kb.op("act", lambda e: e.activation(out=nt["sq"][:], in_=X[:, :, ts], func=AF.Square), xr, [nt["sq_d"]])
        ones_b = self.cstb[:, 384:512]

        def mm(e):
            ins = None
            for k in range(KD):
                ins = e.matmul(self.pf(bank), lhsT=ones_b, rhs=nt["sq"][:, k, :], start=(k == 0), stop=(k == KD - 1))
            return ins
        kb.op("pe", mm, [nt["sq_d"], self.cstb_d], [self.pf_d[bank]])
        kb.op("act", lambda e: e.activation(out=nt["r1"][:], in_=self.pf(bank), func=AF.Sqrt, bias=self.cst[:, C_EPS:C_EPS + 1],
                                            scale=1.0 / D), [self.pf_d[bank], self.cst_d], [nt["r1_d"]])
        kb.op("dve", lambda e: e.reciprocal(out=nt["rs"][:], in_=nt["r1"][:]), [nt["r1_d"]], [nt["rs_d"]])
        for k in range(KD):
            kb.op("dve", lambda e, k=k: e.scalar_tensor_tensor(
                out=out[:, k, ocol0:ocol0 + 512], in0=X[:, k, ts], scalar=self.col(goff + k), in1=nt["rs"][:],
                op0=ALU.mult, op1=ALU.mult), [self.Xd[k][tt], nt["rs_d"], self.cols_d], [out_d[k]])

    def next_slot(self):
        s = self.slot_i % self.NSLOT
        self.slot_i += 1
        return s

    def load_w(self, s, pieces):
        self.kb.dma_multi("pool", pieces, [], [self.slot_d[s]], f"slot{s}")

    def slot3(self, s, k, n):
        return self.slots[s][:, 0:k * n].rearrange("p (k n) -> p k n", k=k)

    def ffn(self, l):
        kb, nc = self.kb, self.nc
        wup = self.d["ffn_w_up"][l]
        wdn = self.d["ffn_w_down"][l]
        kb.fence()
        mark = self.ptr
        ps = None
        if True:
            sbt = self.alloc
            xn = sbt("f_xn", [P, KD, 1024], BF16)
            xn_d = [kb.tiles_n("f_xn", KD) for _ in range(2)]
            act = sbt("f_act", [P, NJ, 1024], BF16)
            act_d = kb.tiles_n("f_act", NJ, 2)
            hs = sbt("f_hs", [P, 2, 2, 2 + 1024], F32)
            hs_d = kb.tiles_n("f_hs", 2, 2, 2)
            hsh_d = kb.tiles_n("f_hsh", 2, 2)
            y0 = sbt("f_y0", [P, 3, 2, 512], F32)
            y0_d = kb.tiles_n("f_y0", 3, 2)
            sg = sbt("f_sg", [P, 3, 512], F32)
            sg_d = kb.tiles_n("f_sg", 3)
            halo = sbt("f_halo", [P, NJ, 2, 2], F32)
            halo_d = kb.tiles_n("f_halo", NJ)
            nt = self.norm_temps(ps)
            ucount = 0
            for half in range(2):
                for t2 in range(2):
                    self.rmsnorm(OFF_NG + (l * 2 + 1) * 8, half * 2 + t2, xn, t2 * 512, xn_d[t2], nt)
                pend = []

                def issue_up(j):
                    s = self.next_slot()
                    v = self.slot3(s, KD, 256)
                    self.load_w(s, [
                        (v[:, :, 0:128], wup[:, j * 128:(j + 1) * 128].rearrange("(k p) n -> p k n", p=P)),
                        (v[:, :, 128:256], wup[:, DFF + j * 128: DFF + (j + 1) * 128].rearrange("(k p) n -> p k n", p=P)),
                    ])
                    return s

                def issue_dn(dc):
                    s = self.next_slot()
                    v = self.slot3(s, NJ, 128)
                    self.load_w(s, [(v, wdn[:, dc * 128:(dc + 1) * 128].rearrange("(k p) n -> p k n", p=P))])
                    return s
                seq = [("u", j) for j in range(NJ)] + [("d", dc) for dc in range(KD)]
                PRE = self.NSLOT - 1
                slots_of = {}
                for q in range(min(PRE, len(seq))):
                    slots_of[q] = issue_up(seq[q][1]) if seq[q][0] == "u" else issue_dn(seq[q][1])
                pend_ffn = []
                for qi, (kind, j) in enumerate(seq):
                    s = slots_of[qi]
                    if kind == "u":
                        wv = self.slot3(s, KD, 256)
                        hb = j % 2
                        for ag in range(2):
                            if half == 0:
                                kb.op("pool", lambda e, hb=hb, ag=ag: e.memset(hs[:, hb, ag, 0:2], 0.0), [], [hsh_d[hb][ag]])
                            else:
                                kb.op("pool", lambda e, hb=hb, ag=ag, j=j: e.tensor_copy(out=hs[:, hb, ag, 0:2], in_=halo[:, j, ag, :]),
                                      [halo_d[j]], [hsh_d[hb][ag]])
                        for t2 in range(2):
                            pb = (ucount % 2) * 2
                            yb = ucount % 3
                            ucount += 1
                            for ag in range(2):
                                def mm(e, ag=ag, t2=t2, pb=pb, wv=wv):
                                    ins = None
                                    for k in range(KD):
                                        ins = e.matmul(self.pf(pb + ag), lhsT=wv[:, k, ag * 128:(ag + 1) * 128],
                                                       rhs=xn[:, k, t2 * 512:(t2 + 1) * 512], start=(k == 0), stop=(k == KD - 1))
                                    return ins
                                kb.op("pe", mm, [self.slot_d[s]] + xn_d[t2], [self.pf_d[pb + ag]])
                            c0 = 2 + t2 * 512
                            for ag in range(2):
                                kk = j + ag * NJ
                                w0 = self.col(OFF_CW + (l * 3 + 0) * 44 + kk)
                                w1 = self.col(OFF_CW + (l * 3 + 1) * 44 + kk)
                                w2 = self.col(OFF_CW + (l * 3 + 2) * 44 + kk)
                                bb = self.col(OFF_CB + l * 44 + kk)
                                kb.op("act", lambda e, hb=hb, ag=ag, c0=c0, pb=pb: e.activation(
                                    out=hs[:, hb, ag, c0:c0 + 512], in_=self.pf(pb + ag), func=AF.Copy),
                                    [self.pf_d[pb + ag]], [hs_d[hb][ag][t2]])
                                kb.op("act", lambda e, yb=yb, ag=ag, pb=pb, w2=w2, bb=bb: e.activation(
                                    out=y0[:, yb, ag, :], in_=self.pf(pb + ag), func=AF.Identity, bias=bb, scale=w2),
                                    [self.pf_d[pb + ag], self.cols_d], [y0_d[yb][ag]])
                                rd = [hs_d[hb][ag][t2], self.cols_d] + ([hsh_d[hb][ag]] if t2 == 0 else [hs_d[hb][ag][0]])
                                kb.op("dve", lambda e, yb=yb, ag=ag, hb=hb, c0=c0, w1=w1: e.scalar_tensor_tensor(
                                    out=y0[:, yb, ag, :], in0=hs[:, hb, ag, c0 - 1:c0 + 511], scalar=w1, in1=y0[:, yb, ag, :],
                                    op0=ALU.mult, op1=ALU.add), rd + [y0_d[yb][ag]], [y0_d[yb][ag]])
                                kb.op("dve", lambda e, yb=yb, ag=ag, hb=hb, c0=c0, w0=w0: e.scalar_tensor_tensor(
                                    out=y0[:, yb, ag, :], in0=hs[:, hb, ag, c0 - 2:c0 + 510], scalar=w0, in1=y0[:, yb, ag, :],
                                    op0=ALU.mult, op1=ALU.add), rd + [y0_d[yb][ag]], [y0_d[yb][ag]])
                            if pend_ffn:
                                pend_ffn.pop()()

                            def fin(yb=yb, j=j, t2=t2):
                                kb.op("act", lambda e: e.activation(out=sg[:, yb, :], in_=y0[:, yb, 1, :], func=AF.Silu),
                                      [y0_d[yb][1]], [sg_d[yb]])
                                kb.op("dve", lambda e: e.tensor_tensor(
                                    out=act[:, j, t2 * 512:(t2 + 1) * 512], in0=sg[:, yb, :], in1=y0[:, yb, 0, :], op=ALU.mult),
                                    [sg_d[yb], y0_d[yb][0]], [act_d[j][t2]])
                            pend_ffn.append(fin)
                        if half == 0:
                            kb.op("pool", lambda e, hb=hb, j=j: e.tensor_copy(out=halo[:, j, :, :], in_=hs[:, hb, :, 1024:1026]),
                                  [hs_d[hb][0][1], hs_d[hb][1][1]], [halo_d[j]])
                    else:
                        if pend_ffn:
                            pend_ffn.pop()()
                        dc = j
                        wv = self.slot3(s, NJ, 128)
                        for t2 in range(2):
                            bank = 4 + (t2 % 2)
                            def mm(e, t2=t2, bank=bank, wv=wv):
                                ins = None
                                for jj in range(NJ):
                                    ins = e.matmul(self.pf(bank), lhsT=wv[:, jj, :], rhs=act[:, jj, t2 * 512:(t2 + 1) * 512],
                                                   start=(jj == 0), stop=(jj == NJ - 1))
                                return ins
                            kb.op("pe", mm, [self.slot_d[s]] + [act_d[jj][t2] for jj in range(NJ)], [self.pf_d[bank]])
                            tt = half * 2 + t2
                            kb.op("dve", lambda e, dc=dc, tt=tt, bank=bank: e.tensor_tensor(
                                out=self.X[:, dc, tt * 512:(tt + 1) * 512], in0=self.X[:, dc, tt * 512:(tt + 1) * 512],
                                in1=self.pf(bank), op=ALU.add), [self.pf_d[bank], self.Xd[dc][tt]], [self.Xd[dc][tt]])
                    nq = qi + PRE
                    if nq < len(seq):
                        slots_of[nq] = issue_up(seq[nq][1]) if seq[nq][0] == "u" else issue_dn(seq[nq][1])
            kb.fence()
        self.ptr = mark

    def attention(self, l, ja):
        kb, nc = self.kb, self.nc
        wqkv = self.d["sb_w_qkv"][ja]
        wo = self.d["sb_w_o"][ja]
        kb.fence()
        mark = self.ptr
        ps = None
        if True:
            sbt = self.alloc
            xn = sbt("a_xn", [P, KD, L], BF16)
            xn_d = kb.tiles_n("a_xn", 4, KD)
            ao = sbt("a_o", [P, KD, L], BF16)
            ao_d = kb.tiles_n("a_o", KD, 4)
            qT = sbt("a_q", [P, L], BF16)
            kT = sbt("a_k", [P, L], BF16)
            vv = sbt("a_v", [P, 16, 128], BF16)
            q_d, k_d, v_d = kb.tile("a_q"), kb.tile("a_k"), kb.tile("a_v")
            NB = 4
            PW = 512
            mark2 = self.ptr
            nt = self.norm_temps(None)
            for tt in range(4):
                self.rmsnorm(OFF_NG + (l * 2 + 0) * 8, tt, xn, tt * 512, xn_d[tt], nt)
            kb.fence()
            self.ptr = mark2
            ee = sbt("a_e", [P, NB, PW], F32)
            spt = sbt("a_sp", [P, NB, PW], F32)
            lw = sbt("a_lw", [P, NB, PW], F32)
            ww = sbt("a_w", [P, NB, PW], BF16)
            wT = sbt("a_wT", [P, NB, PW], BF16)
            ones = sbt("a_ones", [P, PW], BF16)
            carry = sbt("a_carry", [P, 8], F32)
            e_d, sp_d, lw_d, w_d, wT_d = (kb.tiles_n(n, NB) for n in ("a_e", "a_sp", "a_lw", "a_w", "a_wT"))
            ones_d = kb.tile("a_ones")
            carry_d = kb.tiles_n("a_carry", 8)
            kb.op("pool", lambda e: e.memset(ones[:], 1.0), [], [ones_d])
            maskL_f = self.cst[:, C_MASKL:C_MASKL + 128]
            maskL_b = self.cstb[:, 128:256]
            ident_b = self.cstb[:, 0:128]
            pcount = 0
            ocount = 0
            ccount = 0
            gen = 0
            for pair in range(8):
                s = self.next_slot()
                wv = self.slot3(s, KD, 384)
                self.load_w(s, [(wv[:, :, c * 128:(c + 1) * 128],
                                 wqkv[:, c * D + pair * 128: c * D + (pair + 1) * 128].rearrange("(k p) n -> p k n", p=P))
                                for c in range(3)])
                for tt in range(4):
                    for c in range(2):
                        bank = 4 + (gen % 2)
                        gen += 1
                        def mm(e, c=c, tt=tt, bank=bank, wv=wv):
                            ins = None
                            for k in range(KD):
                                ins = e.matmul(self.pf(bank), lhsT=wv[:, k, c * 128:(c + 1) * 128],
                                               rhs=xn[:, k, tt * 512:(tt + 1) * 512], start=(k == 0), stop=(k == KD - 1))
                            return ins
                        kb.op("pe", mm, [self.slot_d[s]] + xn_d[tt], [self.pf_d[bank]])
                        dst = qT if c == 0 else kT
                        dd = q_d if c == 0 else k_d
                        sc = 0.125 if c == 0 else 1.0
                        kb.op("act", lambda e, dst=dst, tt=tt, bank=bank, sc=sc: e.activation(
                            out=dst[:, tt * 512:(tt + 1) * 512], in_=self.pf(bank), func=AF.Copy, scale=sc),
                            [self.pf_d[bank]], [dd])
                for g4 in range(4):
                    bank = 4 + (gen % 2)
                    gen += 1
                    def mmv(e, g4=g4, bank=bank, wv=wv):
                        ins = None
                        for c4 in range(4):
                            i = g4 * 4 + c4
                            for k in range(KD):
                                ins = e.matmul(self.pf(bank, 128, c4 * 128), lhsT=xn[:, k, i * 128:(i + 1) * 128],
                                               rhs=wv[:, k, 256:384], start=(k == 0), stop=(k == KD - 1))
                        return ins
                    kb.op("pe", mmv, [self.slot_d[s]] + xn_d[g4], [self.pf_d[bank]])
                    kb.op("dve", lambda e, g4=g4, bank=bank: e.tensor_copy(
                        out=vv[:, g4 * 4:(g4 + 1) * 4, :], in_=self.pf(bank).rearrange("p (c n) -> p c n", c=4)),
                        [self.pf_d[bank]], [v_d])
                items = []
                for hh in range(2):
                    for i in range(16):
                        t1 = (i + 1) * 128
                        pcs = []
                        ke = t1
                        while ke > 0:
                            ks = ((ke - 1) // PW) * PW
                            pcs.append((ks, ke))
                            ke = ks
                        ob = 4 + ((hh * 4 + i // 4) % 2)
                        ocount += 1
                        prev_cc = None
                        for pi, (ks, ke) in enumerate(pcs):
                            it = dict(hh=hh, i=i, ks=ks, ke=ke, first=(pi == 0), last=(pi == len(pcs) - 1), ob=ob,
                                      p=pcount, cin=prev_cc, cout=None)
                            pcount += 1
                            if pi < len(pcs) - 1:
                                it["cout"] = ccount % 8
                                ccount += 1
                            prev_cc = it["cout"]
                            items.append(it)

                def geo(it):
                    hh, i, ks, ke, p = it["hh"], it["i"], it["ks"], it["ke"], it["p"]
                    return 64 * hh, i * 128, (i + 1) * 128, ke - ks, p % NB, p % 4

                def f_z(it):
                    pb0, t0, t1, n, bi, zb = geo(it)
                    ks, ke = it["ks"], it["ke"]
                    zap = self.pf(zb, n)
                    if it["first"]:
                        zdiag = self.pf(zb, 128, n - 128)

                        def mmz(e):
                            e.matmul(zap, lhsT=qT[pb0:pb0 + 64, t0:t1], rhs=kT[pb0:pb0 + 64, ks:ke], start=True, stop=False)
                            return e.matmul(zdiag, lhsT=ident_b, rhs=self.mneg[:], start=False, stop=True)
                        kb.op("pe", mmz, [q_d, k_d, self.cstb_d], [self.pf_d[zb]])
                    else:
                        kb.op("pe", lambda e: e.matmul(zap, lhsT=qT[pb0:pb0 + 64, t0:t1], rhs=kT[pb0:pb0 + 64, ks:ke], start=True, stop=True),
                              [q_d, k_d], [self.pf_d[zb]])

                def f_exp(it):
                    pb0, t0, t1, n, bi, zb = geo(it)
                    zap = self.pf(zb, n)
                    kb.op("act", lambda e: e.activation(out=ee[:, bi, 0:n], in_=zap, func=AF.Exp), [self.pf_d[zb]], [e_d[bi]])

                def f_mask(it):
                    pb0, t0, t1, n, bi, zb = geo(it)
                    pass

                def f_ln(it):
                    pb0, t0, t1, n, bi, zb = geo(it)
                    kb.op("act", lambda e: e.activation(out=spt[:, bi, 0:n], in_=ee[:, bi, 0:n], func=AF.Ln, bias=1.0), [e_d[bi]], [sp_d[bi]])

                def f_scan(it):
                    pb0, t0, t1, n, bi, zb = geo(it)
                    cin, cout = it["cin"], it["cout"]
                    init = 0.0 if cin is None else carry[:, cin:cin + 1]
                    rd = [sp_d[bi], ones_d] + ([] if cin is None else [carry_d[cin]])
                    kb.op("dve", lambda e: e.tensor_tensor_scan(out=lw[:, bi, 0:n][:, ::-1], data0=ones[:, 0:n], data1=spt[:, bi, 0:n][:, ::-1],
                                                                initial=init, op0=ALU.mult, op1=ALU.add), rd, [lw_d[bi]])
                    if cout is not None:
                        kb.op("dve", lambda e: e.tensor_copy(out=carry[:, cout:cout + 1], in_=lw[:, bi, 0:1]), [lw_d[bi]], [carry_d[cout]])

                def f_er(it):
                    pb0, t0, t1, n, bi, zb = geo(it)
                    kb.op("act", lambda e: e.activation(out=lw[:, bi, 0:n], in_=lw[:, bi, 0:n], func=AF.Exp, scale=-1.0), [lw_d[bi]], [lw_d[bi]])

                def f_mult(it):
                    pb0, t0, t1, n, bi, zb = geo(it)
                    kb.op("dve", lambda e: e.tensor_tensor(out=ww[:, bi, 0:n], in0=ee[:, bi, 0:n], in1=lw[:, bi, 0:n], op=ALU.mult),
                          [e_d[bi], lw_d[bi]], [w_d[bi]])

                def f_tr(it):
                    pb0, t0, t1, n, bi, zb = geo(it)
                    tb = it["p"] % 2
                    nb_ = n // 128

                    def trs(e):
                        ins = None
                        for b in range(nb_):
                            ins = e.transpose(self.PB[:, tb * 1024 + b * 128: tb * 1024 + (b + 1) * 128], ww[:, bi, b * 128:(b + 1) * 128], ident_b)
                        return ins
                    kb.op("pe", trs, [w_d[bi], self.cstb_d], [self.pb_d[tb]])

                def f_copy(it):
                    pb0, t0, t1, n, bi, zb = geo(it)
                    tb = it["p"] % 2
                    if it["p"] % 3 == 2:
                        kb.op("dve", lambda e: e.tensor_copy(out=wT[:, bi, 0:n], in_=self.PB[:, tb * 1024: tb * 1024 + n]),
                              [self.pb_d[tb]], [wT_d[bi]])
                    else:
                        kb.op("act", lambda e: e.activation(out=wT[:, bi, 0:n], in_=self.PB[:, tb * 1024: tb * 1024 + n], func=AF.Copy),
                              [self.pb_d[tb]], [wT_d[bi]])

                def f_mmo(it, pair=pair):
                    pb0, t0, t1, n, bi, zb = geo(it)
                    ob, i = it["ob"], it["i"]
                    nb_ = n // 128
                    kb0 = it["ks"] // 128
                    first, last = it["first"], it["last"]

                    def mmo(e):
                        ins = None
                        for b in range(nb_):
                            ins = e.matmul(self.pf(ob, 128, (i % 4) * 128), lhsT=vv[:, kb0 + b, :], rhs=wT[:, bi, b * 128:(b + 1) * 128],
                                           start=(first and b == 0), stop=(last and b == nb_ - 1))
                        return ins
                    kb.op("pe", mmo, [wT_d[bi], v_d], [self.pf_d[ob]])
                    if last and i % 4 == 3:
                        g0 = (i // 4) * 512
                        kb.op("act", lambda e: e.activation(out=ao[pb0:pb0 + 64, pair, g0:g0 + 512], in_=self.PF[pb0:pb0 + 64, ob * 512: ob * 512 + 512],
                                                            func=AF.Copy), [self.pf_d[ob]], [ao_d[pair][i // 4]])
                sched = [(0, f_z), (0, f_exp), (2, f_er), (0, f_ln), (5, f_copy), (1, f_scan), (3, f_mult), (4, f_tr), (6, f_mmo)]
                nit = len(items)
                for step in range(nit + 6):
                    for lag, fn in sched:
                        j = step - lag
                        if 0 <= j < nit:
                            fn(items[j])
            self.proj_residual(ao, lambda k, tt: ao_d[k][tt], wo)
            kb.fence()
        self.ptr = mark

    def proj_residual(self, src, src_d, wdram):
        kb = self.kb
        for dc in range(KD):
            s = self.next_slot()
            wv = self.slot3(s, KD, 128)
            self.load_w(s, [(wv, wdram[:, dc * 128:(dc + 1) * 128].rearrange("(k p) n -> p k n", p=P))])
            for tt in range(4):
                bank = tt % 4

                def mm(e, tt=tt, bank=bank, wv=wv):
                    ins = None
                    for k in range(KD):
                        ins = e.matmul(self.pf(bank), lhsT=wv[:, k, :], rhs=src[:, k, tt * 512:(tt + 1) * 512],
                                       start=(k == 0), stop=(k == KD - 1))
                    return ins
                kb.op("pe", mm, [self.slot_d[s]] + [src_d(k, tt) for k in range(KD)], [self.pf_d[bank]])
                kb.op("dve", lambda e, dc=dc, tt=tt, bank=bank: e.tensor_tensor(
                    out=self.X[:, dc, tt * 512:(tt + 1) * 512], in0=self.X[:, dc, tt * 512:(tt + 1) * 512],
                    in1=self.pf(bank), op=ALU.add), [self.pf_d[bank], self.Xd[dc][tt]], [self.Xd[dc][tt]])

    def gmlp(self, l):
        kb, nc = self.kb, self.nc
        win = self.d["sg_w_in"][0]
        wo = self.d["sg_w_o"][0]
        kb.fence()
        mark = self.ptr
        sbt = self.alloc
        xn = sbt("g_xn", [P, KD, L], BF16)
        xn_d = kb.tiles_n("g_xn", 4, KD)
        vn = sbt("g_vn", [P, 16, D], BF16)
        vn_d = kb.tiles_n("g_vn", 16)
        wsT = sbt("g_wsT", [P, 8, 128], BF16)
        wsT_d = kb.tile("g_wsT")
        bsr = sbt("g_bsr", [1, D], BF16)
        bsr_d = kb.tile("g_bsr")
        ssq = sbt("g_ssq", [P, 16], F32)
        rstd = sbt("g_rstd", [P, 16], F32)
        ssq_d, rstd_d = kb.tile("g_ssq"), kb.tiles_n("g_rstd", 16)
        mark2 = self.ptr
        nt = self.norm_temps(None)
        for tt in range(4):
            self.rmsnorm(OFF_NG + (l * 2 + 0) * 8, tt, xn, tt * 512, xn_d[tt], nt)
        kb.fence()
        self.ptr = mark2
        wv_ = sbt("g_wv", [P, KD, D], BF16)
        wv_d = kb.tile("g_wv")
        vg = sbt("g_vg", [P, 2, D], F32)
        vg_d = kb.tiles_n("g_vg", 2)
        junk = sbt("g_junk", [P, D], BF16)
        junk_d = kb.tile("g_junk")
        gbc = sbt("g_gbc", [P, D], F32)
        gbc_d = kb.tile("g_gbc")
        grow = sbt("g_grow", [1, D], F32)
        grow_d = kb.tile("g_grow")
        wsf = sbt("g_wsf", [P, 8, 128], F32)
        wsf_d = kb.tile("g_wsf")
        kb.dma_multi("pool", [(wv_[:, :, c * 256:(c + 1) * 256],
                               win[:, D + c * 256: D + (c + 1) * 256].rearrange("(k p) n -> p k n", p=P)) for c in range(4)],
                     [], [wv_d], "g_wv")
        kb.dma("pool", bsr[:], self.d["sg_b"].rearrange("a g t -> a (g t)"), [], [bsr_d], "g_bsr")
        kb.dma("sp", grow[:], self.d["sg_norm_g"], [], [grow_d], "g_grow")
        kb.dma("sp", wsf[:], self.d["sg_w_s"][0].rearrange("g t s -> t g s"), [], [wsf_d], "g_wsf")
        kb.op("dve", lambda e: e.memset(ssq[:], 0.0), [], [ssq_d])
        ones_row_f = self.cst[0:1, C_ONES:C_ONES + 128]
        for nh in range(2):
            kb.op("pe", lambda e, nh=nh: e.matmul(self.pf(4 + nh), lhsT=ones_row_f, rhs=grow[0:1, nh * 512:(nh + 1) * 512],
                                                 start=True, stop=True), [grow_d, self.cst_d], [self.pf_d[4 + nh]])
            kb.op("dve", lambda e, nh=nh: e.tensor_copy(out=gbc[:, nh * 512:(nh + 1) * 512], in_=self.pf(4 + nh)),
                  [self.pf_d[4 + nh]], [gbc_d])
        trilT = self.cst[:, C_TRILT:C_TRILT + 128]
        for g in range(8):
            bank = 4 + g // 4
            kb.op("pe", lambda e, g=g, bank=bank: e.transpose(self.pf(bank, 128, (g % 4) * 128), wsf[:, g, :], self.ident_f()),
                  [wsf_d, self.cst_d], [self.pf_d[bank]])
            kb.op("dve", lambda e, g=g, bank=bank: e.tensor_tensor(out=wsT[:, g, :], in0=self.pf(bank, 128, (g % 4) * 128), in1=trilT,
                                                                   op=ALU.mult), [self.pf_d[bank], self.cst_d], [wsT_d])
        for i in range(16):
            b0 = 2 * (i % 2)
            vb = i % 2
            for nh in range(2):
                def mm(e, i=i, nh=nh, b0=b0):
                    ins = None
                    for k in range(KD):
                        ins = e.matmul(self.pf(b0 + nh), lhsT=xn[:, k, i * 128:(i + 1) * 128], rhs=wv_[:, k, nh * 512:(nh + 1) * 512],
                                       start=(k == 0), stop=(k == KD - 1))
                    return ins
                kb.op("pe", mm, [wv_d] + xn_d[i // 4], [self.pf_d[b0 + nh]])
            kb.op("act", lambda e, vb=vb, b0=b0: e.activation(out=vg[:, vb, :], in_=self.PF[:, b0 * 512: b0 * 512 + 1024],
                                                              func=AF.Gelu_apprx_tanh), [self.pf_d[b0], self.pf_d[b0 + 1]], [vg_d[vb]])
            kb.op("act", lambda e, vb=vb, i=i: e.activation(out=junk[:], in_=vg[:, vb, :], func=AF.Square, accum_out=ssq[:, i:i + 1]),
                  [vg_d[vb], ssq_d], [junk_d, rstd_d[i]])
            kb.op("act", lambda e, i=i: e.activation(out=rstd[:, i:i + 1], in_=ssq[:, i:i + 1], func=AF.Sqrt,
                                                     bias=self.cst[:, C_EPS:C_EPS + 1], scale=1.0 / D), [rstd_d[i], self.cst_d], [rstd_d[i]])
            kb.op("dve", lambda e, i=i: e.reciprocal(out=rstd[:, i:i + 1], in_=rstd[:, i:i + 1]), [rstd_d[i]], [rstd_d[i]])
            kb.op("dve", lambda e, i=i, vb=vb: e.scalar_tensor_tensor(out=vn[:, i, :], in0=vg[:, vb, :], scalar=rstd[:, i:i + 1], in1=gbc[:],
                                                                    op0=ALU.mult, op1=ALU.mult), [vg_d[vb], rstd_d[i], gbc_d], [vn_d[i]])
        kb.fence()
        self.ptr = mark2
        gated = sbt("g_gated", [P, KD, L], BF16)
        gated_d = kb.tiles_n("g_gated", KD, 4)
        ug = sbt("g_ug", [P, 2, 512], F32)
        ug_d = kb.tiles_n("g_ug", 2)
        ones_row_b = self.cstb[0:1, 384:512]
        cnt = 0
        for g in range(8):
            s = self.next_slot()
            wu = self.slot3(s, KD, 128)
            self.load_w(s, [(wu, win[:, g * 128:(g + 1) * 128].rearrange("(k p) n -> p k n", p=P))])
            for tt in range(4):
                sb_ = (cnt % 2) * 2
                ub_ = sb_ + 1
                ui = cnt % 2
                cnt += 1

                def mmsv(e, g=g, tt=tt, sb_=sb_):
                    ins = None
                    for c in range(4):
                        o = self.pf(sb_, 128, c * 128)
                        e.matmul(o, lhsT=vn[:, 4 * tt + c, g * 128:(g + 1) * 128], rhs=wsT[:, g, :], start=True, stop=False)
                        ins = e.matmul(o, lhsT=ones_row_b, rhs=bsr[0:1, g * 128:(g + 1) * 128], start=False, stop=True)
                    return ins
                kb.op("pe", mmsv, [vn_d[4 * tt + c] for c in range(4)] + [wsT_d, bsr_d, self.cstb_d], [self.pf_d[sb_]])

                def mmu(e, tt=tt, ub_=ub_, wu=wu):
                    ins = None
                    for k in range(KD):
                        ins = e.matmul(self.pf(ub_), lhsT=wu[:, k, :], rhs=xn[:, k, tt * 512:(tt + 1) * 512], start=(k == 0), stop=(k == KD - 1))
                    return ins
                kb.op("pe", mmu, [self.slot_d[s]] + xn_d[tt], [self.pf_d[ub_]])
                kb.op("act", lambda e, ui=ui, ub_=ub_: e.activation(out=ug[:, ui, :], in_=self.pf(ub_), func=AF.Gelu_apprx_tanh),
                      [self.pf_d[ub_]], [ug_d[ui]])
                kb.op("dve", lambda e, g=g, tt=tt, ui=ui, sb_=sb_: e.tensor_tensor(
                    out=gated[:, g, tt * 512:(tt + 1) * 512], in0=ug[:, ui, :], in1=self.pf(sb_), op=ALU.mult),
                    [ug_d[ui], self.pf_d[sb_]], [gated_d[g][tt]])
        self.proj_residual(gated, lambda k, tt: gated_d[k][tt], wo)
        kb.fence()
        self.ptr = mark

    def s5(self, l):
        kb, nc = self.kb, self.nc
        d = self.d
        win = d["ssm_w_in"][0]
        wglu = d["ssm_w_glu"][0]
        PI = math.pi
        kb.fence()
        mark = self.ptr
        sbt = self.alloc
        xn = sbt("s_xn", [P, KD, L], BF16)
        xn_d = kb.tiles_n("s_xn", 4, KD)
        yg = sbt("s_yg", [P, KD, L], BF16)
        yg_d = kb.tiles_n("s_yg", KD, 4)
        c2 = sbt("s_c2", [P, C2_REP], F32)
        c2_d = kb.tile("s_c2")
        bbt = sbt("s_bbt", [P, 2, 8, 64], BF16)
        bbt_d = kb.tile("s_bbt")
        ct = sbt("s_ct", [P, 2, 8, 64], BF16)
        ct_d = kb.tile("s_ct")
        svr = sbt("s_svr", [P, 32], F32)
        svt = sbt("s_svt", [P, 32], F32)
        svc = sbt("s_svc", [P, 32], F32)
        svs = sbt("s_svs", [P, 32], F32)
        svn = sbt("s_svn", [P, 32], F32)
        ctmp = sbt("s_ctmp", [P, 2], F32)
        ctmp_d = kb.tile("s_ctmp")
        sv_d = kb.tile("s_sv")
        carry = sbt("s_carry", [P, 2, 32], F32)
        carry_d = kb.tiles_n("s_carry", 32)
        kb.dma("sp", c2[:], d["consts2"][:, 0:C2_REP], [], [c2_d], "s_c2")
        kb.dma("pool", ct[:, 0, :, :], d["ssm_c_re"][0].rearrange("(q gl) h p -> (gl h) q p", q=8), [], [ct_d], "s_ct")
        kb.dma("pool", ct[:, 1, :, :], d["ssm_c_im"][0].rearrange("(q gl) h p -> (gl h) q p", q=8), [], [ct_d], "s_ct")
        mark2 = self.ptr
        nt = self.norm_temps(None)
        for tt in range(4):
            self.rmsnorm(OFF_NG + (l * 2 + 0) * 8, tt, xn, tt * 512, xn_d[tt], nt)
        kb.fence()
        self.ptr = mark2
        bn = sbt("s_bn", [64, 2, 1024], F32)
        bn_d = kb.tile("s_bn")
        kb.dma("sp", bn[:, 0, :], d["ssm_b_re"][0].rearrange("g p h -> g (p h)"), [], [bn_d], "s_bn")
        kb.dma("sp", bn[:, 1, :], d["ssm_b_im"][0].rearrange("g p h -> g (p h)"), [], [bn_d], "s_bn")
        NT = 26
        tp = sbt("s_tp", [64, NT, 64], F32)
        tp_d = kb.tiles_n("s_tp", NT)
        dtc = sbt("s_dt", [64, 2], F32)
        dt_d = kb.tile("s_dt")
        (LR, LI, DLR, DLI, MAG, SA, CA, SN, CS, AR, AI, T1, T2, DEN, RDEN, AM1, U1, U2, CR, CI) = range(20)
        T = lambda i: tp[:, i, :]
        kb.dma("sp", T(LR), d["ssm_lam_re"][0], [], [tp_d[LR]], "s_p0")
        kb.dma("sp", T(LI), d["ssm_lam_im"][0], [], [tp_d[LI]], "s_p1")
        kb.dma("sp", dtc[:, 0:1], d["ssm_log_dt"].rearrange("a g -> g a"), [], [dt_d], "s_p2")
        kb.op("act", lambda e: e.activation(out=dtc[:, 1:2], in_=dtc[:, 0:1], func=AF.Exp), [dt_d], [dt_d])
        dtcol = dtc[:, 1:2]

        def v1(eng, fn, r, w):
            kb.op(eng, fn, [tp_d[i] for i in r] + [dt_d, self.cst_d], [tp_d[i] for i in w])
        v1("dve", lambda e: e.tensor_scalar_min(out=T(LR), in0=T(LR), scalar1=-1e-4), [LR], [LR])
        v1("dve", lambda e: e.tensor_scalar_mul(out=T(DLR), in0=T(LR), scalar1=dtcol), [LR], [DLR])
        v1("dve", lambda e: e.tensor_scalar_mul(out=T(DLI), in0=T(LI), scalar1=dtcol), [LI], [DLI])
        v1("act", lambda e: e.activation(out=T(MAG), in_=T(DLR), func=AF.Exp), [DLR], [MAG])
        hpi64 = self.cst[0:64, C_HALFPI:C_HALFPI + 1]
        v1("dve", lambda e: e.tensor_scalar(out=T(T1), in0=T(DLI), scalar1=1.0 / (2 * PI), scalar2=MAGIC, op0=ALU.mult, op1=ALU.add), [DLI], [T1])
        v1("dve", lambda e: e.tensor_scalar_add(out=T(T1), in0=T(T1), scalar1=-MAGIC), [T1], [T1])
        v1("dve", lambda e: e.scalar_tensor_tensor(out=T(SA), in0=T(T1), scalar=-CW1, in1=T(DLI), op0=ALU.mult, op1=ALU.add), [T1, DLI], [SA])
        v1("dve", lambda e: e.scalar_tensor_tensor(out=T(SA), in0=T(T1), scalar=-CW2, in1=T(SA), op0=ALU.mult, op1=ALU.add), [T1, SA], [SA])
        v1("dve", lambda e: e.tensor_scalar(out=T(SA), in0=T(SA), scalar1=-PI_LO, scalar2=PI_LO, op0=ALU.max, op1=ALU.min), [SA], [SA])
        v1("dve", lambda e: e.scalar_tensor_tensor(out=T(CA), in0=T(SA), scalar=-1.0, in1=T(SA), op0=ALU.mult, op1=ALU.max), [SA], [CA])
        v1("act", lambda e: e.activation(out=T(SN), in_=T(SA), func=AF.Sin), [SA], [SN])
        v1("act", lambda e: e.activation(out=T(CS), in_=T(CA), func=AF.Sin, scale=-1.0, bias=hpi64), [CA], [CS])
        v1("dve", lambda e: e.tensor_tensor(out=T(AR), in0=T(MAG), in1=T(CS), op=ALU.mult), [MAG, CS], [AR])
        v1("dve", lambda e: e.tensor_tensor(out=T(AI), in0=T(MAG), in1=T(SN), op=ALU.mult), [MAG, SN], [AI])
        v1("dve", lambda e: e.tensor_tensor(out=T(T1), in0=T(LR), in1=T(LR), op=ALU.mult), [LR], [T1])
        v1("dve", lambda e: e.tensor_tensor(out=T(T2), in0=T(LI), in1=T(LI), op=ALU.mult), [LI], [T2])
        v1("dve", lambda e: e.tensor_tensor(out=T(DEN), in0=T(T1), in1=T(T2), op=ALU.add), [T1, T2], [DEN])
        v1("dve", lambda e: e.reciprocal(out=T(RDEN), in_=T(DEN)), [DEN], [RDEN])
        v1("dve", lambda e: e.tensor_scalar_add(out=T(AM1), in0=T(AR), scalar1=-1.0), [AR], [AM1])
        v1("dve", lambda e: e.tensor_tensor(out=T(U1), in0=T(AM1), in1=T(LR), op=ALU.mult), [AM1, LR], [U1])
        v1("dve", lambda e: e.tensor_tensor(out=T(U2), in0=T(AI), in1=T(LI), op=ALU.mult), [AI, LI], [U2])
        v1("dve", lambda e: e.tensor_tensor(out=T(U1), in0=T(U1), in1=T(U2), op=ALU.add), [U1, U2], [U1])
        v1("dve", lambda e: e.tensor_tensor(out=T(CR), in0=T(U1), in1=T(RDEN), op=ALU.mult), [U1, RDEN], [CR])
        v1("dve", lambda e: e.tensor_tensor(out=T(U1), in0=T(AI), in1=T(LR), op=ALU.mult), [AI, LR, CR], [U1])
        v1("dve", lambda e: e.tensor_tensor(out=T(U2), in0=T(AM1), in1=T(LI), op=ALU.mult), [AM1, LI], [U2])
        v1("dve", lambda e: e.tensor_tensor(out=T(U1), in0=T(U1), in1=T(U2), op=ALU.subtract), [U1, U2], [U1])
        v1("dve", lambda e: e.tensor_tensor(out=T(CI), in0=T(U1), in1=T(RDEN), op=ALU.mult), [U1, RDEN], [CI])
        bbn = sbt("s_bbn", [64, 2, 1024], F32)
        bbn_d = kb.tile("s_bbn")
        tmpn = sbt("s_tmpn", [64, 1024], F32)
        tmpn_d = kb.tile("s_tmpn")
        R_, I_ = 0, 1
        cb = lambda i: T(i).unsqueeze(2).broadcast_to([64, 64, 16])
        nat = lambda ri: bn[:, ri, :].rearrange("g (p h) -> g p h", h=16)
        outv = lambda t_: t_.rearrange("g (h p) -> g p h", h=16)
        kb.op("dve", lambda e: e.tensor_tensor(out=outv(bbn[:, R_, :]), in0=cb(CR), in1=nat(R_), op=ALU.mult), [tp_d[CR], bn_d], [bbn_d])
        kb.op("dve", lambda e: e.tensor_tensor(out=outv(tmpn[:]), in0=cb(CI), in1=nat(I_), op=ALU.mult), [tp_d[CI], bn_d], [tmpn_d])
        kb.op("dve", lambda e: e.tensor_tensor(out=bbn[:, R_, :], in0=bbn[:, R_, :], in1=tmpn[:], op=ALU.subtract), [bbn_d, tmpn_d], [bbn_d])
        kb.op("dve", lambda e: e.tensor_tensor(out=outv(bbn[:, I_, :]), in0=cb(CR), in1=nat(I_), op=ALU.mult), [tp_d[CR], bn_d], [bbn_d])
        kb.op("dve", lambda e: e.tensor_tensor(out=outv(tmpn[:]), in0=cb(CI), in1=nat(R_), op=ALU.mult), [tp_d[CI], bn_d, bbn_d], [tmpn_d])
        kb.op("dve", lambda e: e.tensor_tensor(out=bbn[:, I_, :], in0=bbn[:, I_, :], in1=tmpn[:], op=ALU.add), [bbn_d, tmpn_d], [bbn_d])
        scr_d = kb.tile("s_scr")
        kb.dma("sp", self.scr.rearrange("r g n -> g r n"), bbn[:], [bbn_d], [scr_d], "s_scr_w")
        for ri in range(2):
            kb.dma("pool", bbt[:, ri, :, :], self.scr[ri].rearrange("(q gl) (h p) -> (gl h) q p", q=8, h=16), [scr_d], [bbt_d], "s_scr_r")
        A5, K5, R5, B5, C5, S5_ = 20, 21, 22, 23, 24, 25
        v1("dve", lambda e: e.tensor_scalar_mul(out=T(A5), in0=T(DLI), scalar1=512.0), [DLI], [A5])
        v1("dve", lambda e: e.tensor_scalar(out=T(K5), in0=T(A5), scalar1=1.0 / (2 * PI), scalar2=MAGIC, op0=ALU.mult, op1=ALU.add), [A5], [K5])
        v1("dve", lambda e: e.tensor_scalar_add(out=T(K5), in0=T(K5), scalar1=-MAGIC), [K5], [K5])
        v1("dve", lambda e: e.scalar_tensor_tensor(out=T(R5), in0=T(K5), scalar=-CW1, in1=T(A5), op0=ALU.mult, op1=ALU.add), [K5, A5], [R5])
        v1("dve", lambda e: e.scalar_tensor_tensor(out=T(R5), in0=T(K5), scalar=-CW2, in1=T(R5), op0=ALU.mult, op1=ALU.add), [K5, R5], [R5])
        v1("dve", lambda e: e.tensor_scalar(out=T(R5), in0=T(R5), scalar1=-PI_LO, scalar2=PI_LO, op0=ALU.max, op1=ALU.min), [R5], [R5])
        v1("dve", lambda e: e.scalar_tensor_tensor(out=T(B5), in0=T(R5), scalar=-1.0, in1=T(R5), op0=ALU.mult, op1=ALU.max), [R5], [B5])
        v1("act", lambda e: e.activation(out=T(S5_), in_=T(R5), func=AF.Sin), [R5], [S5_])
        v1("act", lambda e: e.activation(out=T(C5), in_=T(B5), func=AF.Sin, scale=-1.0, bias=hpi64), [B5], [C5])
        dup = sbt("s_dup", [64, 4, 128], F32)
        dup_d = kb.tile("s_dup")
        for qi, (src, dst) in enumerate(((MAG, svr), (DLI, svt), (C5, svc), (S5_, svs))):
            kb.op("dve", lambda e, qi=qi, src=src: e.tensor_copy(out=dup[:, qi, 0:64], in_=T(src)), [tp_d[src]], [dup_d])
            kb.op("dve", lambda e, qi=qi, src=src: e.tensor_copy(out=dup[:, qi, 64:128], in_=T(src)), [tp_d[src]], [dup_d])
            bank = 2 + qi
            kb.op("pe", lambda e, qi=qi, bank=bank: e.transpose(self.pf(bank, 64), dup[:, qi, :], self.cst[0:64, C_IDENT:C_IDENT + 64]),
                  [dup_d, self.cst_d], [self.pf_d[bank]])
            kb.op("dve", lambda e, dst=dst, bank=bank: e.tensor_copy(out=dst[0:64, :], in_=self.PF[0:64, bank * 512: bank * 512 + 64: 2]),
                  [self.pf_d[bank]], [sv_d])
            kb.op("dve", lambda e, dst=dst, bank=bank: e.tensor_copy(out=dst[64:128, :], in_=self.PF[64:128, bank * 512 + 1: bank * 512 + 64: 2]),
                  [self.pf_d[bank]], [sv_d])
        kb.op("dve", lambda e: e.tensor_scalar_mul(out=svn[:], in0=svs[:], scalar1=-1.0), [sv_d], [sv_d])
        kb.op("dve", lambda e: e.memset(carry[:], 0.0), [], carry_d)
        kb.fence()
        self.ptr = mark2
        u32 = sbt("s_u32", [P, L], F32)
        u32_d = kb.tiles_n("s_u32", 4)
        ub = sbt("s_ub", [P, L], BF16)
        ub_d = kb.tiles_n("s_ub", 4)
        bwm = sbt("s_bwm", [P, 2, 4, 128], BF16)
        bwm_d = kb.tile("s_bwm")
        bwf = sbt("s_bwf", [P, 2, 128], F32)
        bwf_d = kb.tile("s_bwf")
        cw = sbt("s_cw", [P, 2, 4, 128], BF16)
        cw_d = kb.tile("s_cw")
        ctd, ctd_d = bwf, bwf_d
        tabs = sbt("s_tabs", [P, 2, 2, 512], F32)
        tabs_d = kb.tiles_n("s_tabs", 2, 2)
        rbt = sbt("s_rbt", [P, 512], F32)
        rbt_d = kb.tile("s_rbt")
        un = sbt("s_un", [P, 2, 512], F32)
        un_d = kb.tiles_n("s_un", 2)
        wk = sbt("s_wk", [P, 4, 512], F32)
        wk_d = kb.tiles_n("s_wk", 4)
        xrb = sbt("s_xr", [P, 2, 512], BF16)
        xr_d = kb.tiles_n("s_xr", 2)
        iota = c2[:, C2_IOTA:C2_IOTA + 512]
        bmask = c2[:, C2_BMASK:C2_BMASK + 128]
        hpi = self.cst[:, C_HALFPI:C_HALFPI + 1]
        W = lambda i: wk[:, i, :]
        ysum, ysum_d = wk[:, 0, :], wk_d[0]
        YB = [4, 5, 6, 7]

        def tables(j):
            par = j % 2
            tS, tC, tK = tabs[:, par, 0, :], tabs[:, par, 1, :], wk[:, 3, :]
            dS, dC = tabs_d[par]
            dK = wk_d[3]
            th = svt[:, j:j + 1]
            kb.op("dve", lambda e: e.tensor_scalar_mul(out=tC, in0=iota, scalar1=th), [c2_d, sv_d], [dC])
            kb.op("dve", lambda e: e.tensor_scalar(out=tK, in0=tC, scalar1=1.0 / (2 * PI), scalar2=MAGIC, op0=ALU.mult, op1=ALU.add), [dC], [dK])
            kb.op("dve", lambda e: e.tensor_scalar_add(out=tK, in0=tK, scalar1=-MAGIC), [dK], [dK])
            kb.op("dve", lambda e: e.scalar_tensor_tensor(out=tS, in0=tK, scalar=-CW1, in1=tC, op0=ALU.mult, op1=ALU.add), [dK, dC], [dS])
            kb.op("dve", lambda e: e.scalar_tensor_tensor(out=tS, in0=tK, scalar=-CW2, in1=tS, op0=ALU.mult, op1=ALU.add), [dK, dS], [dS])
            kb.op("dve", lambda e: e.tensor_scalar(out=tS, in0=tS, scalar1=-PI_LO, scalar2=PI_LO, op0=ALU.max, op1=ALU.min), [dS], [dS])
            kb.op("dve", lambda e: e.scalar_tensor_tensor(out=tC, in0=tS, scalar=-1.0, in1=tS, op0=ALU.mult, op1=ALU.max), [dS], [dC])
            kb.op("act", lambda e: e.activation(out=tS, in_=tS, func=AF.Sin), [dS], [dS])
            kb.op("act", lambda e: e.activation(out=tC, in_=tC, func=AF.Sin, scale=-1.0, bias=hpi), [dC, self.cst_d], [dC])

        def rho_table(j):
            rho = svr[:, j:j + 1]
            kb.op("act", lambda e: e.activation(out=rbt[:], in_=iota, func=AF.Identity, bias=rho, scale=0.0), [c2_d, sv_d], [rbt_d])

        ucount = [0]

        def stage_a(jj, tt, up):
            ts = slice(tt * 512, (tt + 1) * 512)
            for ri in range(2):
                bk = 2 * up + ri
                kb.op("pe", lambda e, ri=ri, bk=bk: e.matmul(self.pf(bk), lhsT=bwm[:, ri, jj, :], rhs=ub[:, ts], start=True, stop=True),
                      [bwm_d, ub_d[tt]], [self.pf_d[bk]])

        def stage_bcd(j, jj, tt, up):
            par = j % 2
            tS, tC, tK = tabs[:, par, 0, :], tabs[:, par, 1, :], rbt[:]
            dS, dC = tabs_d[par]
            dK = rbt_d
            brp, bip = self.pf(2 * up), self.pf(2 * up + 1)
            dbr, dbi = self.pf_d[2 * up], self.pf_d[2 * up + 1]
            YR, YI = un[:, 0, :], un[:, 1, :]

            def tt_(eng, o, od, a, ad, b, bd, op):
                kb.op(eng, lambda e: e.tensor_tensor(out=o, in0=a, in1=b, op=op), [ad, bd], [od])
            tt_("dve", W(0), wk_d[0], brp, dbr, tC, dC, ALU.mult)
            tt_("dve", W(1), wk_d[1], bip, dbi, tS, dS, ALU.mult)
            tt_("pool", W(0), wk_d[0], W(0), wk_d[0], W(1), wk_d[1], ALU.add)
            tt_("dve", W(2), wk_d[2], bip, dbi, tC, dC, ALU.mult)
            tt_("dve", W(3), wk_d[3], brp, dbr, tS, dS, ALU.mult)
            tt_("dve", W(2), wk_d[2], W(2), wk_d[2], W(3), wk_d[3], ALU.subtract)
            for ri, src in ((0, 0), (1, 2)):
                kb.op("dve", lambda e, ri=ri, src=src: e.tensor_tensor_scan(
                    out=un[:, ri, :], data0=tK, data1=W(src), initial=carry[:, ri, j:j + 1], op0=ALU.mult, op1=ALU.add),
                    [dK, wk_d[src], carry_d[j]], [un_d[ri]])
            if tt < 3:
                yrl, yil = un[:, 0, 511:512], un[:, 1, 511:512]
                cc, ss, ns = svc[:, j:j + 1], svs[:, j:j + 1], svn[:, j:j + 1]
                kb.op("dve", lambda e: e.tensor_scalar_mul(out=ctmp[:, 0:1], in0=yrl, scalar1=cc), [un_d[0], sv_d], [ctmp_d])
                kb.op("dve", lambda e: e.tensor_scalar_mul(out=ctmp[:, 1:2], in0=yrl, scalar1=ss), [un_d[0], sv_d, ctmp_d], [ctmp_d])
                kb.op("dve", lambda e: e.scalar_tensor_tensor(out=carry[:, 0, j:j + 1], in0=yil, scalar=ns, in1=ctmp[:, 0:1], op0=ALU.mult, op1=ALU.add),
                      [un_d[1], sv_d, ctmp_d], [carry_d[j]])
                kb.op("dve", lambda e: e.scalar_tensor_tensor(out=carry[:, 1, j:j + 1], in0=yil, scalar=cc, in1=ctmp[:, 1:2], op0=ALU.mult, op1=ALU.add),
                      [un_d[1], sv_d, ctmp_d, carry_d[j]], [carry_d[j]])
            tt_("dve", W(0), wk_d[0], tC, dC, YR, un_d[0], ALU.mult)
            tt_("dve", W(1), wk_d[1], tS, dS, YI, un_d[1], ALU.mult)
            kb.op("dve", lambda e: e.tensor_tensor(out=xrb[:, 0, :], in0=W(0), in1=W(1), op=ALU.subtract), [wk_d[0], wk_d[1]], [xr_d[0]])
            tt_("dve", W(3), wk_d[3], tS, dS, YR, un_d[0], ALU.mult)
            tt_("dve", W(2), wk_d[2], tC, dC, YI, un_d[1], ALU.mult)
            kb.op("dve", lambda e: e.tensor_tensor(out=xrb[:, 1, :], in0=W(3), in1=W(2), op=ALU.add), [wk_d[3], wk_d[2]], [xr_d[1]])
            yb = YB[tt]
            for ri in range(2):
                kb.op("pe", lambda e, ri=ri: e.matmul(self.pfx(yb), lhsT=cw[:, ri, jj, :], rhs=xrb[:, ri, :],
                                                     start=(jj == 0 and ri == 0), stop=(jj == 3 and ri == 1)),
                      [cw_d, xr_d[ri]], [self.pfx_d(yb)])

        for q in range(8):
            s = self.next_slot()
            wv = self.slot3(s, KD, 128)
            self.load_w(s, [(wv, win[:, q * 128:(q + 1) * 128].rearrange("(k p) n -> p k n", p=P))])
            for tt in range(4):
                bank = tt % 2

                def mm(e, tt=tt, bank=bank, wv=wv):
                    ins = None
                    for k in range(KD):
                        ins = e.matmul(self.pf(bank), lhsT=wv[:, k, :], rhs=xn[:, k, tt * 512:(tt + 1) * 512], start=(k == 0), stop=(k == KD - 1))
                    return ins
                kb.op("pe", mm, [self.slot_d[s]] + xn_d[tt], [self.pf_d[bank]])
                kb.op("dve", lambda e, tt=tt, bank=bank: e.tensor_copy(out=u32[:, tt * 512:(tt + 1) * 512], in_=self.pf(bank)),
                      [self.pf_d[bank]], [u32_d[tt]])
                kb.op("dve", lambda e, tt=tt, bank=bank: e.tensor_copy(out=ub[:, tt * 512:(tt + 1) * 512], in_=self.pf(bank)),
                      [self.pf_d[bank]], [ub_d[tt]])
            for ri in range(2):
                kb.op("dve", lambda e, ri=ri, q=q: e.tensor_tensor(
                    out=bwf[:, ri, :].rearrange("p (a n) -> p a n", a=2), in0=bbt[:, ri, q:q + 1, :].broadcast_to([P, 2, 64]),
                    in1=bmask.rearrange("p (a n) -> p a n", a=2), op=ALU.mult), [bbt_d, c2_d], [bwf_d])
                for jj in range(4):
                    kb.op("dve", lambda e, ri=ri, jj=jj: e.tensor_scalar_mul(
                        out=bwm[:, ri, jj, :], in0=bwf[:, ri, :], scalar1=c2[:, C2_RMASK + jj:C2_RMASK + jj + 1]), [bwf_d, c2_d], [bwm_d])
                kb.op("dve", lambda e, ri=ri, q=q: e.tensor_copy(
                    out=ctd[:, ri, :].rearrange("p (a n) -> p a n", a=2), in_=ct[:, ri, q:q + 1, :].broadcast_to([P, 2, 64])), [ct_d], [ctd_d])
                bank = ri
                kb.op("pe", lambda e, ri=ri, bank=bank: e.transpose(self.pf(bank, 128), ctd[:, ri, :], self.ident_f()),
                      [ctd_d, self.cst_d], [self.pf_d[bank]])
                for jj in range(4):
                    cm = c2[:, C2_CMASK + jj * 128: C2_CMASK + (jj + 1) * 128]
                    if ri == 0:
                        kb.op("dve", lambda e, jj=jj, bank=bank, cm=cm: e.tensor_tensor(out=cw[:, 0, jj, :], in0=self.pf(bank, 128), in1=cm, op=ALU.mult),
                              [self.pf_d[bank], c2_d], [cw_d])
                    else:
                        kb.op("dve", lambda e, jj=jj, bank=bank, cm=cm: e.scalar_tensor_tensor(
                            out=cw[:, 1, jj, :], in0=self.pf(bank, 128), scalar=-1.0, in1=cm, op0=ALU.mult, op1=ALU.mult),
                            [self.pf_d[bank], c2_d], [cw_d])
            units = [(jj, tt) for jj in range(4) for tt in range(4)]
            tables(4 * q)
            stage_a(units[0][0], units[0][1], ucount[0] % 2)
            for ui, (jj, tt) in enumerate(units):
                up = ucount[0] % 2
                ucount[0] += 1
                if ui + 1 < len(units):
                    stage_a(units[ui + 1][0], units[ui + 1][1], ucount[0] % 2)
                if tt == 0:
                    rho_table(4 * q + jj)
                stage_bcd(4 * q + jj, jj, tt, up)
                if tt == 1 and jj < 3:
                    tables(4 * q + jj + 1)
            for tt in range(4):
                yb = YB[tt]
                ts = slice(tt * 512, (tt + 1) * 512)
                kb.op("dve", lambda e, tt=tt, q=q, yb=yb, ts=ts: e.scalar_tensor_tensor(
                    out=ysum, in0=u32[:, ts], scalar=self.col(OFF_SSMD + q), in1=self.pfx(yb), op0=ALU.mult, op1=ALU.add),
                    [u32_d[tt], self.cols_d, self.pfx_d(yb)], [ysum_d])
                kb.op("act", lambda e, q=q, ts=ts: e.activation(out=yg[:, q, ts], in_=ysum, func=AF.Gelu_apprx_tanh), [ysum_d], [yg_d[q][tt]])
        kb.fence()
        self.ptr = mark2
        if os.environ.get("S5_STOP") in ("2", "3", "4", "5"):
            self.ptr = mark
            return
        sgm = sbt("s_sgm", [P, 2, 512], F32)
        sgm_d = kb.tiles_n("s_sgm", 2)
        gcnt = 0
        for dc in range(KD):
            s = self.next_slot()
            wv = self.slot3(s, KD, 256)
            self.load_w(s, [(wv[:, :, 0:128], wglu[:, dc * 128:(dc + 1) * 128].rearrange("(k p) n -> p k n", p=P)),
                            (wv[:, :, 128:256], wglu[:, D + dc * 128: D + (dc + 1) * 128].rearrange("(k p) n -> p k n", p=P))])
            for tt in range(4):
                b0 = (gcnt % 3) * 2
                gi = gcnt % 2
                gcnt += 1
                for ag in range(2):
                    def mm(e, ag=ag, tt=tt, b0=b0, wv=wv):
                        ins = None
                        for k in range(KD):
                            ins = e.matmul(self.pf(b0 + ag), lhsT=wv[:, k, ag * 128:(ag + 1) * 128], rhs=yg[:, k, tt * 512:(tt + 1) * 512],
                                           start=(k == 0), stop=(k == KD - 1))
                        return ins
                    kb.op("pe", mm, [self.slot_d[s]] + [yg_d[k][tt] for k in range(KD)], [self.pf_d[b0 + ag]])
                kb.op("act", lambda e, gi=gi, b0=b0: e.activation(out=sgm[:, gi, :], in_=self.pf(b0 + 1), func=AF.Sigmoid),
                      [self.pf_d[b0 + 1]], [sgm_d[gi]])
                kb.op("dve", lambda e, gi=gi, b0=b0: e.tensor_tensor(out=sgm[:, gi, :], in0=self.pf(b0), in1=sgm[:, gi, :], op=ALU.mult),
                      [self.pf_d[b0], sgm_d[gi]], [sgm_d[gi]])
                kb.op("dve", lambda e, dc=dc, tt=tt, gi=gi: e.tensor_tensor(
                    out=self.X[:, dc, tt * 512:(tt + 1) * 512], in0=self.X[:, dc, tt * 512:(tt + 1) * 512], in1=sgm[:, gi, :], op=ALU.add),
                    [sgm_d[gi], self.Xd[dc][tt]], [self.Xd[dc][tt]])
        kb.fence()
        self.ptr = mark


_CACHE = {}


def _pack_vecs(norm_g, final_norm_g, ffn_conv_w, ffn_conv_b, ssm_d):
    v = np.zeros((NVROWS, 128), np.float32)
    v[OFF_NG:OFF_NG + 64] = np.asarray(norm_g, np.float32).reshape(64, 128)
    v[OFF_FNG:OFF_FNG + 8] = np.asarray(final_norm_g, np.float32).reshape(8, 128)
    v[OFF_CW:OFF_CW + 528] = np.asarray(ffn_conv_w, np.float32).reshape(528, 128)
    v[OFF_CB:OFF_CB + 176] = np.asarray(ffn_conv_b, np.float32).reshape(176, 128)
    v[OFF_SSMD:OFF_SSMD + 8] = np.asarray(ssm_d, np.float32).reshape(8, 128)
    return v


def run_layers(x, inputs, layers, do_final, cores=NCORES):
    key = (tuple(layers), do_final)
    if key not in _CACHE:
        _CACHE[key] = Prog(layers, do_final).build()
    nc = _CACHE[key]
    consts, consts2 = _host_consts()
    vecs = _pack_vecs(inputs["norm_g"], inputs["final_norm_g"], inputs["ffn_conv_w"], inputs["ffn_conv_b"], inputs["ssm_d"])
    shared = {"consts": consts, "consts2": consts2, "vecs": vecs}
    for n in ("sb_w_qkv", "sb_w_o", "sg_w_in", "sg_norm_g", "sg_w_s", "sg_b", "sg_w_o", "ssm_w_in", "ssm_lam_re",
              "ssm_lam_im", "ssm_log_dt", "ssm_b_re", "ssm_b_im", "ssm_c_re", "ssm_c_im", "ssm_w_glu", "ffn_w_up",
              "ffn_w_down"):
        shared[n] = np.ascontiguousarray(np.asarray(inputs[n], np.float32))
    in_maps = []
    for c in range(cores):
        m = dict(shared)
        m["x"] = np.ascontiguousarray(np.asarray(x[c], np.float32))
        in_maps.append(m)
    res = run_bass_kernel_spmd(nc, in_maps, core_ids=list(range(cores)))
    return np.stack([np.asarray(r["y"]) for r in res.results], axis=0)


def kernel(**inputs):
    x = np.asarray(inputs["x"], np.float32)
    out = run_layers(x, inputs, [0, 1, 2, 3], True)
    return out.astype(np.float32)
```
